# Optimizing a Trainium2 kernel written in Bass

```python
import jax
import jax.numpy as jnp
from jax import lax
import numpy as np

D_MODEL = 1024
BATCH = 8
SEQ = 4096
DEPTH = 4

CTX_LEN = 256
GRID_W = 64
D_FF = 2816
GLA_HEADS = 4
GLA_DK = 128
GLA_DV = 256
GLA_KEY = GLA_HEADS * GLA_DK
GLA_VAL = GLA_HEADS * GLA_DV
GLA_GATE_RANK = 16
GLA_GATE_TEMP = 16.0
GLA_CHUNK = 64
CONV_WIDTH = 1024
MLA_HEADS = 8
MLA_Q_RANK = 512
MLA_KV_RANK = 256
MLA_NOPE = 128
MLA_ROPE = 64
MLA_V = 128
ATTN_BLOCK = 128
ROPE_THETA = 10000.0
N_BRANCH = 3
N_MOD = 9
LN_EPS = 1e-6
DEEPNORM_ALPHA = (2.0 * DEPTH) ** 0.25
DEEPNORM_BETA = (8.0 * DEPTH) ** -0.25
IN_SIZES = (GLA_KEY, GLA_KEY, GLA_VAL, GLA_VAL, GLA_GATE_RANK, GLA_GATE_RANK,
            CONV_WIDTH, CONV_WIDTH, CONV_WIDTH,
            MLA_Q_RANK, MLA_KV_RANK, MLA_ROPE,
            N_BRANCH * D_MODEL)
IN_WIDTH = GLA_KEY * 2 + GLA_VAL * 2 + GLA_GATE_RANK * 2 + CONV_WIDTH * 3 + MLA_Q_RANK + MLA_KV_RANK + MLA_ROPE + N_BRANCH * D_MODEL

kernel_name = "hybrid_gla_shortconv_mla_dit_trunk"


def _layer_norm(x, w, b):
    xf = x.astype(jnp.float32)
    mu = jnp.mean(xf, axis=-1, keepdims=True)
    var = jnp.mean(jnp.square(xf - mu), axis=-1, keepdims=True)
    return ((xf - mu) * lax.rsqrt(var + LN_EPS) * w + b).astype(x.dtype)


def _rms_norm(x, w):
    xf = x.astype(jnp.float32)
    return (xf * lax.rsqrt(jnp.mean(jnp.square(xf), axis=-1, keepdims=True) + LN_EPS) * w).astype(x.dtype)


def _modulate(x, shift, scale):
    return x * (1.0 + scale) + shift


def _swiglu(h, w13, w2):
    a, g = jnp.split(h @ w13, 2, axis=-1)
    return (jax.nn.silu(a) * g) @ w2


def _split(z, sizes):
    idx, acc = [], 0
    for s in sizes[:-1]:
        acc += s
        idx.append(acc)
    return jnp.split(z, idx, axis=-1)


def _heads(t, n):
    b, l, _ = t.shape
    return t.reshape(b, l, n, -1).transpose(0, 2, 1, 3)


def _merge(t):
    b, h, l, d = t.shape
    return t.transpose(0, 2, 1, 3).reshape(b, l, h * d)


def _grid_positions(n):
    rows_n = n // GRID_W
    rows = jnp.repeat(jnp.arange(rows_n, dtype=jnp.int32), GRID_W)
    cols = jnp.tile(jnp.arange(GRID_W, dtype=jnp.int32), rows_n)
    return rows, cols


def _rope_1d(x, pos):
    half = x.shape[-1] // 2
    inv_freq = ROPE_THETA ** (-jnp.arange(half, dtype=jnp.float32) / half)
    ang = pos.astype(jnp.float32)[:, None] * inv_freq
    cos = jnp.cos(ang).astype(x.dtype)
    sin = jnp.sin(ang).astype(x.dtype)
    x1, x2 = x[..., :half], x[..., half:]
    return jnp.concatenate([x1 * cos - x2 * sin, x1 * sin + x2 * cos], axis=-1)


def _rope_2d(x, rows, cols):
    h = x.shape[-1] // 2
    return jnp.concatenate([_rope_1d(x[..., :h], rows), _rope_1d(x[..., h:], cols)], axis=-1)


def _log_decay(glr, w, b):
    return jax.nn.log_sigmoid((glr @ w + b).astype(jnp.float32)) / GLA_GATE_TEMP


def _gla_scan(q, k, v, g, s0):
    b_, h, l, _ = q.shape
    dv = v.shape[-1]
    n = l // GLA_CHUNK

    def to_chunks(t):
        return jnp.moveaxis(t.reshape(b_, h, n, GLA_CHUNK, t.shape[-1]), 2, 0).astype(jnp.float32)

    mask = jnp.tril(jnp.ones((GLA_CHUNK, GLA_CHUNK), dtype=bool))

    def step(s, inp):
        qc, kc, vc, gc = inp
        bcum = jnp.cumsum(gc, axis=-2)
        b_last = bcum[..., -1:, :]
        qe = qc * jnp.exp(bcum)
        ke = kc * jnp.exp(-bcum)
        a = jnp.where(mask, jnp.einsum('bhid,bhjd->bhij', qe, ke), 0.0)
        o = jnp.einsum('bhij,bhjv->bhiv', a, vc) + jnp.einsum('bhid,bhdv->bhiv', qe, s)
        s = jnp.exp(b_last[..., 0, :])[..., None] * s + jnp.einsum('bhjd,bhjv->bhdv', kc * jnp.exp(b_last - bcum), vc)
        return s, o

    s, o = lax.scan(step, s0, (to_chunks(q), to_chunks(k), to_chunks(v), to_chunks(g)))
    o = jnp.moveaxis(o, 0, 2).reshape(b_, h, l, dv)
    return o.astype(v.dtype), s


def _gla_bidir(q, k, v, g_fwd, g_bwd, s_fwd0, s_bwd0):
    flip = lambda t: jnp.flip(t, axis=2)
    o_f, s_f = _gla_scan(q, k, v, g_fwd, s_fwd0)
    o_b, s_b = _gla_scan(flip(q), flip(k), flip(v), flip(g_bwd), s_bwd0)
    return o_f + flip(o_b), s_f, s_b


def _attend(q, k, v):
    s = jnp.einsum('bhqd,bhkd->bhqk', q, k).astype(jnp.float32) * (q.shape[-1] ** -0.5)
    p = jax.nn.softmax(s, axis=-1).astype(v.dtype)
    return jnp.einsum('bhqk,bhkd->bhqd', p, v)


def _attend_blocks(q, k, v):
    b, h, l, d = q.shape
    nb = l // ATTN_BLOCK
    qb = jnp.moveaxis(q.reshape(b, h, nb, ATTN_BLOCK, d), 2, 0)
    ob = lax.map(lambda qi: _attend(qi, k, v), qb)
    return jnp.moveaxis(ob, 0, 2).reshape(b, h, l, v.shape[-1])


def _short_conv(u, w):
    up = jnp.pad(u, ((0, 0), (1, 1), (0, 0)))
    return up[:, :-2] * w[0] + up[:, 1:-1] * w[1] + up[:, 2:] * w[2]


def _mixer(hx, hc, rows, cols, w_in, b_gate, gla_decay_w, gla_decay_b, gla_norm, gla_proj,
           conv_w, conv_proj, mla_q_norm, mla_w_uq, mla_kv_norm, mla_w_ukv, mla_proj, w_out,
           with_ctx_out):
    zx = _split(hx @ w_in, IN_SIZES)
    zc = _split(hc @ w_in, IN_SIZES)

    def gla_in(z):
        lf = _log_decay(z[4], gla_decay_w[0], gla_decay_b[0])
        lb = _log_decay(z[5], gla_decay_w[1], gla_decay_b[1])
        return (_heads(z[0], GLA_HEADS) * (GLA_DK ** -0.5), _heads(z[1], GLA_HEADS), _heads(z[2], GLA_HEADS),
                _heads(lf, GLA_HEADS), _heads(lb, GLA_HEADS))

    def gla_out(o, r):
        o = _rms_norm(o, gla_norm.reshape(GLA_HEADS, 1, GLA_DV))
        return (_merge(o) * jax.nn.silu(r)) @ gla_proj

    qc, kc, vc, lfc, lbc = gla_in(zc)
    s_init = jnp.zeros(qc.shape[:2] + (GLA_DK, GLA_DV), jnp.float32)
    o_gla_c, s_fwd_ctx, s_bwd_ctx = _gla_bidir(qc, kc, vc, lfc, lbc, s_init, s_init)
    qx, kx, vx, lfx, lbx = gla_in(zx)
    o_gla_x, _, _ = _gla_bidir(qx, kx, vx, lfx, lbx, s_fwd_ctx, s_bwd_ctx)
    a_x = gla_out(o_gla_x, zx[3])

    b_x = (zx[6] * _short_conv(zx[7] * zx[8], conv_w)) @ conv_proj

    def mla_q(cq):
        return _heads(_rms_norm(cq, mla_q_norm) @ mla_w_uq, MLA_HEADS)

    def mla_kv(ckv, k_rope):
        kv = _heads(_rms_norm(ckv, mla_kv_norm) @ mla_w_ukv, MLA_HEADS)
        k_nope, v = kv[..., :MLA_NOPE], kv[..., MLA_NOPE:]
        k_rope = jnp.broadcast_to(k_rope[:, None], k_nope.shape[:3] + (MLA_ROPE,))
        return jnp.concatenate([k_nope, k_rope], axis=-1), v

    q_lat = mla_q(zx[9])
    q_lat = jnp.concatenate([q_lat[..., :MLA_NOPE], _rope_2d(q_lat[..., MLA_NOPE:], rows, cols)], axis=-1)
    k_lat, v_lat = mla_kv(zx[10], _rope_2d(zx[11], rows, cols))
    k_ctx, v_ctx = mla_kv(zc[10], zc[11])
    k_all = jnp.concatenate([k_ctx, k_lat], axis=2)
    v_all = jnp.concatenate([v_ctx, v_lat], axis=2)
    c_x = _merge(_attend_blocks(q_lat, k_all, v_all)) @ mla_proj

    def merge(a, b, c_, zg):
        ga, gb, gc = jnp.split(jax.nn.sigmoid(zg + b_gate), N_BRANCH, axis=-1)
        return (ga * a + gb * b + gc * c_) @ w_out

    out_x = merge(a_x, b_x, c_x, zx[12])
    if not with_ctx_out:
        return out_x, None
    a_c = gla_out(o_gla_c, zc[3])
    b_c = (zc[6] * _short_conv(zc[7] * zc[8], conv_w)) @ conv_proj
    c_c = _merge(_attend(mla_q(zc[9]), k_ctx, v_ctx)) @ mla_proj
    out_c = merge(a_c, b_c, c_c, zc[12])
    return out_x, out_c


def _ffn_step(h, m, i, w13, w2, lw, lb):
    shift, scale, gate = m[3 * i], m[3 * i + 1], m[3 * i + 2]
    return _layer_norm(DEEPNORM_ALPHA * h + 0.5 * gate * _swiglu(_modulate(h, shift, scale), w13, w2), lw, lb)


def setup_inputs(seed: int = 0) -> dict:
    key = jax.random.key(seed)
    ks = jax.random.split(key, 32)
    nrm = lambda k, shape, s: jax.random.normal(k, shape, jnp.float32) * s
    d = D_MODEL
    return {
        "x": nrm(ks[0], (BATCH, SEQ, d), 1.0),
        "c": nrm(ks[1], (BATCH, d), 1.0),
        "ctx": nrm(ks[2], (BATCH, CTX_LEN, d), 1.0),
        "c_ctx": nrm(ks[3], (d,), 1.0),
        "ada_w": nrm(ks[4], (DEPTH, d, N_MOD * d), d ** -0.5),
        "ada_b": nrm(ks[5], (DEPTH, N_MOD * d), 0.01),
        "ln_w": 1.0 + nrm(ks[6], (DEPTH, 3, d), 0.01),
        "ln_b": nrm(ks[7], (DEPTH, 3, d), 0.01),
        "ffn_w13": nrm(ks[8], (DEPTH, 2, d, 2 * D_FF), d ** -0.5),
        "ffn_w2": nrm(ks[9], (DEPTH, 2, D_FF, d), D_FF ** -0.5 * DEEPNORM_BETA),
        "w_in": nrm(ks[10], (DEPTH, d, IN_WIDTH), d ** -0.5),
        "b_gate": nrm(ks[11], (DEPTH, N_BRANCH * d), 0.01),
        "gla_decay_w": nrm(ks[12], (DEPTH, 2, GLA_GATE_RANK, GLA_KEY), GLA_GATE_RANK ** -0.5),
        "gla_decay_b": nrm(ks[13], (DEPTH, 2, GLA_KEY), 0.01),
        "gla_norm": 1.0 + nrm(ks[14], (DEPTH, GLA_VAL), 0.01),
        "gla_proj": nrm(ks[15], (DEPTH, GLA_VAL, d), GLA_VAL ** -0.5),
        "conv_w": nrm(ks[16], (DEPTH, 3, CONV_WIDTH), 3 ** -0.5),
        "conv_proj": nrm(ks[17], (DEPTH, CONV_WIDTH, d), CONV_WIDTH ** -0.5),
        "mla_q_norm": 1.0 + nrm(ks[18], (DEPTH, MLA_Q_RANK), 0.01),
        "mla_w_uq": nrm(ks[19], (DEPTH, MLA_Q_RANK, MLA_HEADS * (MLA_NOPE + MLA_ROPE)), MLA_Q_RANK ** -0.5),
        "mla_kv_norm": 1.0 + nrm(ks[20], (DEPTH, MLA_KV_RANK), 0.01),
        "mla_w_ukv": nrm(ks[21], (DEPTH, MLA_KV_RANK, MLA_HEADS * (MLA_NOPE + MLA_V)), MLA_KV_RANK ** -0.5),
        "mla_proj": nrm(ks[22], (DEPTH, MLA_HEADS * MLA_V, d), (MLA_HEADS * MLA_V) ** -0.5),
        "w_out": nrm(ks[23], (DEPTH, d, d), d ** -0.5 * DEEPNORM_BETA),
    }


def reference(x, c, ctx, c_ctx, ada_w, ada_b, ln_w, ln_b, ffn_w13, ffn_w2, w_in, b_gate,
              gla_decay_w, gla_decay_b, gla_norm, gla_proj, conv_w, conv_proj,
              mla_q_norm, mla_w_uq, mla_kv_norm, mla_w_ukv, mla_proj, w_out):
    rows, cols = _grid_positions(x.shape[1])
    sc = jax.nn.silu(c)
    scc = jax.nn.silu(c_ctx)
    for l in range(DEPTH):
        last = l == DEPTH - 1
        mod_x = jnp.split((sc @ ada_w[l] + ada_b[l])[:, None, :], N_MOD, axis=-1)
        mod_c = jnp.split((scc @ ada_w[l] + ada_b[l])[None, None, :], N_MOD, axis=-1)
        x = _ffn_step(x, mod_x, 0, ffn_w13[l, 0], ffn_w2[l, 0], ln_w[l, 0], ln_b[l, 0])
        ctx = _ffn_step(ctx, mod_c, 0, ffn_w13[l, 0], ffn_w2[l, 0], ln_w[l, 0], ln_b[l, 0])
        mx, mc = _mixer(_modulate(x, mod_x[3], mod_x[4]), _modulate(ctx, mod_c[3], mod_c[4]), rows, cols,
                        w_in[l], b_gate[l], gla_decay_w[l], gla_decay_b[l], gla_norm[l], gla_proj[l],
                        conv_w[l], conv_proj[l], mla_q_norm[l], mla_w_uq[l], mla_kv_norm[l], mla_w_ukv[l],
                        mla_proj[l], w_out[l], not last)
        x = _layer_norm(DEEPNORM_ALPHA * x + mod_x[5] * mx, ln_w[l, 1], ln_b[l, 1])
        x = _ffn_step(x, mod_x, 2, ffn_w13[l, 1], ffn_w2[l, 1], ln_w[l, 2], ln_b[l, 2])
        if not last:
            ctx = _layer_norm(DEEPNORM_ALPHA * ctx + mod_c[5] * mc, ln_w[l, 1], ln_b[l, 1])
            ctx = _ffn_step(ctx, mod_c, 2, ffn_w13[l, 1], ffn_w2[l, 1], ln_w[l, 2], ln_b[l, 2])
    return x
```

```python
import contextlib
import numpy as np
import concourse.bass as bass
import concourse.mybir as mybir
from concourse.bass_utils import run_bass_kernel_spmd

F32 = mybir.dt.float32
BF16 = mybir.dt.bfloat16
AF = mybir.ActivationFunctionType
ALU = mybir.AluOpType
AX = mybir.AxisListType

D = 1024
L = 4096
CTX = 256
NT = L + CTX
DEPTH = 4
DFF = 2816
NJ = DFF // 128
ALPHA = (2.0 * DEPTH) ** 0.25
EPS = 1e-6
INW = 10080

SAME_ENGINE_SYNC = False
EPOCH = 8192
N_DMA_SEMS = 8


class Tok:
    __slots__ = ("ws", "r", "name")

    def __init__(self, name=""):
        self.ws = []
        self.r = []
        self.name = name


class Op:
    __slots__ = ("eng", "idx", "fn", "waits", "sig", "cnt", "is_dma", "dsem", "dcnt", "prev_dma")

    def __init__(self, eng, idx, fn, is_dma):
        self.eng = eng
        self.idx = idx
        self.fn = fn
        self.waits = []
        self.sig = False
        self.cnt = 0
        self.is_dma = is_dma
        self.dsem = None
        self.dcnt = 0
        self.prev_dma = None


class Sched:
    ENGS = ("pe", "act", "dve", "pool", "sp")

    def __init__(self, nc):
        self.nc = nc
        self.ops = {e: [] for e in self.ENGS}
        self.seen = {e: {f: -1 for f in self.ENGS} for e in self.ENGS}
        self.seen_dma = {e: {} for e in self.ENGS}
        self.dma_n = {e: 0 for e in self.ENGS}
        self.dma_last = {}

    def _add(self, eng, fn, reads, writes, is_dma, join=False):
        lst = self.ops[eng]
        op = Op(eng, len(lst), fn, is_dma)
        deps = []
        for t in reads:
            deps.extend(t.ws)
        for t in writes:
            if not join:
                deps.extend(t.ws)
            deps.extend(t.r)
        for p in deps:
            self._dep(op, p)
        for t in writes:
            if join:
                t.ws.append(op)
            else:
                t.ws = [op]
            t.r = []
        for t in reads:
            t.r.append(op)
        if is_dma:
            k = self.dma_n[eng]
            self.dma_n[eng] += 1
            slot = k % N_DMA_SEMS
            op.dsem = (eng, slot)
            op.dcnt = 16 * (k // N_DMA_SEMS + 1)
            op.prev_dma = self.dma_last.get((eng, slot))
            self.dma_last[(eng, slot)] = op
            if op.prev_dma is not None:
                self._dep(op, op.prev_dma)
        lst.append(op)
        return op

    def _dep(self, op, p):
        e = op.eng
        if p.is_dma:
            sd = self.seen_dma[e]
            if sd.get(p.dsem, 0) >= p.dcnt:
                return
            sd[p.dsem] = p.dcnt
            op.waits.append(p)
        else:
            if p.eng == e and (e == "pe" or not SAME_ENGINE_SYNC):
                return
            if self.seen[e][p.eng] >= p.idx:
                return
            self.seen[e][p.eng] = p.idx
            p.sig = True
            op.waits.append(p)

    def op(self, eng, fn, reads=(), writes=()):
        return self._add(eng, fn, reads, writes, False)

    def dma_barrier(self, eng, queues=("sp", "act", "pool")):
        op = Op(eng, len(self.ops[eng]), lambda e: e.nop(), False)
        for (q, slot), p in self.dma_last.items():
            if q in queues:
                self._dep(op, p)
        self.ops[eng].append(op)
        return op

    def dma(self, eng, fn, reads=(), writes=(), join=False):
        return self._add(eng, fn, reads, writes, True, join)

    def full_barrier(self):
        lasts = {e: (self.ops[e][-1] if self.ops[e] else None) for e in self.ENGS}
        dl = list(self.dma_last.values())
        for e in self.ENGS:
            op = Op(e, len(self.ops[e]), lambda en: en.nop(), False)
            for f in self.ENGS:
                p = lasts[f]
                if p is None or f == e:
                    continue
                if p.is_dma:
                    q = None
                    for cand in reversed(self.ops[f]):
                        if not cand.is_dma:
                            q = cand
                            break
                    p = q
                if p is not None:
                    self._dep(op, p)
            for p in dl:
                self._dep(op, p)
            self.ops[e].append(op)

    def emit(self, es):
        nc = self.nc
        nsig = {}
        for e in self.ENGS:
            c = 0
            for op in self.ops[e]:
                if (not op.is_dma) and op.sig:
                    c += 1
                    op.cnt = c
            nsig[e] = c
        csems = {}
        for e in self.ENGS:
            n_ep = (nsig[e] + EPOCH - 1) // EPOCH
            csems[e] = [es.enter_context(nc.semaphore(f"c_{e}_{i}")) for i in range(n_ep)]
        dsems = {}
        for e in self.ENGS:
            if self.dma_n[e]:
                for s in range(min(N_DMA_SEMS, self.dma_n[e])):
                    dsems[(e, s)] = es.enter_context(nc.semaphore(f"d_{e}_{s}"))
        fin = es.enter_context(nc.semaphore("fin"))
        block = es.enter_context(nc.Block())
        engs = {"pe": (block.tensor, nc.tensor), "act": (block.scalar, nc.scalar), "dve": (block.vector, nc.vector),
                "pool": (block.gpsimd, nc.gpsimd), "sp": (block.sync, nc.sync)}

        def run(e, eng):
            for op in self.ops[e]:
                for p in op.waits:
                    if p.is_dma:
                        eng.wait_ge(dsems[p.dsem], p.dcnt)
                    else:
                        ep = (p.cnt - 1) // EPOCH
                        eng.wait_ge(csems[p.eng][ep], (p.cnt - 1) % EPOCH + 1)
                inst = op.fn(eng)
                if op.is_dma:
                    inst.then_inc(dsems[op.dsem], 16)
                elif op.sig:
                    ep = (op.cnt - 1) // EPOCH
                    inst.then_inc(csems[e][ep], 1)

        for e in self.ENGS:
            if not self.ops[e]:
                continue
            deco, _ = engs[e]

            def f(eng, e=e):
                run(e, eng)

            deco(f)


class Ctx:
    pass


def build_program(layers, n_ffn_only=False, debug=None):
    nc = bass.Bass("TRN2", target_bir_lowering=False)
    es = contextlib.ExitStack()
    S = Sched(nc)
    NL = len(layers)

    def din(name, shape, dt=F32):
        return nc.dram_tensor(name, list(shape), dt, kind="ExternalInput").ap()

    def dint(name, shape, dt):
        return nc.dram_tensor(name, list(shape), dt, kind="Internal").ap()

    ARENA_WORDS = 50 * 1024
    arena = es.enter_context(nc.sbuf_tensor("arena", [128, ARENA_WORDS], F32))
    astate = {"off": 0, "peak": 0}

    def sb(name, shape, dt):
        n = 1
        for d_ in shape[1:]:
            n *= d_
        isz = 4 if dt == F32 else 2
        words = (n * isz + 63) // 64 * 16
        off = astate["off"]
        assert off + words <= ARENA_WORDS, f"arena overflow at {name}: {off}+{words}"
        astate["off"] = off + words
        astate["peak"] = max(astate["peak"], astate["off"])
        v = arena[0:shape[0], off:off + (n * isz + 3) // 4]
        if dt != F32:
            v = v.bitcast(dt)
            v = v[:, 0:n]
        if len(shape) == 3:
            v = v.rearrange("p (a b) -> p a b", a=shape[1])
        elif len(shape) == 4:
            v = v.rearrange("p (a b c) -> p a b c", a=shape[1], b=shape[2])
        elif len(shape) == 5:
            v = v.rearrange("p (a b c d) -> p a b c d", a=shape[1], b=shape[2], c=shape[3])
        elif len(shape) == 6:
            v = v.rearrange("p (a b c d e) -> p a b c d e", a=shape[1], b=shape[2], c=shape[3], d=shape[4])
        return v

    def amark():
        return astate["off"]

    def arelease(m):
        S.full_barrier()
        astate["off"] = m

    x_in = din("x", [L, D])
    ctx_in = din("ctx", [CTX, D])
    c_in = din("c", [D])
    cctx_in = din("c_ctx", [D])
    ada_w = din("ada_w", [NL, D, 9 * D])
    ada_b = din("ada_b", [NL, 9 * D])
    ln_w = din("ln_w", [NL, 3, D])
    ln_b = din("ln_b", [NL, 3, D])
    ffn_w13 = din("ffn_w13", [NL, 2, D, 2 * DFF])
    ffn_w2 = din("ffn_w2", [NL, 2, DFF, D])
    ident_in = din("ident_in", [128, 128])
    w_in = din("w_in", [NL, D, INW])
    b_gate = din("b_gate", [NL, 3 * D])
    gdw_in = din("gla_decay_w", [NL, 2, 16, 512])
    gdb_in = din("gla_decay_b", [NL, 2, 512])
    gnorm_in = din("gla_norm", [NL, D])
    gla_proj = din("gla_proj", [NL, D, D])
    conv_w_in = din("conv_w", [NL, 3, D])
    conv_proj = din("conv_proj", [NL, D, D])
    qnorm_in = din("mla_q_norm", [NL, 512])
    w_uq = din("mla_w_uq", [NL, 512, 1536])
    kvnorm_in = din("mla_kv_norm", [NL, 256])
    w_ukv = din("mla_w_ukv", [NL, 256, 2048])
    mla_proj = din("mla_proj", [NL, D, D])
    w_out = din("w_out", [NL, D, D])
    rope_cs = din("rope_cs", [128, 2, L])
    gmask_in = din("gmask", [128, 4, 128])
    y_out = nc.dram_tensor("y", [L, D], F32, kind="ExternalOutput").ap()
    emit_ctx = (layers[-1] != DEPTH - 1)
    if emit_ctx:
        yc_out = nc.dram_tensor("yc", [CTX, D], F32, kind="ExternalOutput").ap()

    XT = dint("XT", [D, NT], F32)
    W13S = dint("W13S", [NL, 2, 2, NJ, 128, 8 * 128], BF16)
    W2S = dint("W2S", [NL, 2, 8, 128, NJ * 128], BF16)

    NBLK = 73
    WIS = dint("WIS", [NL, NBLK, 128, 8 * 128], BF16)
    WVS = dint("WVS", [NL, 2, 128, 8 * 512], BF16)
    PRJS = dint("PRJS", [NL, 4, 8, 128, 8 * 128], BF16)
    WUQS = dint("WUQS", [NL, 16, 128, 4 * 128], BF16)
    WUKS = dint("WUKS", [NL, 8, 128, 2 * 128], BF16)
    WUKVV = dint("WUKVV", [NL, 128, 2 * 1024], BF16)
    QT = dint("QT", [512, NT], BF16)
    KT = dint("KT", [512, NT], BF16)
    RT = dint("RT", [1024, NT], BF16)
    GP = dint("GP", [2, NT, 512], BF16)
    UT = dint("UT", [1024, NT], BF16)
    KRT = dint("KRT", [128, NT], BF16)
    VG = dint("VG", [NT, 1024], BF16)
    QNT = dint("QNT", [1024, NT], BF16)
    QRT = dint("QRT", [512, NT], BF16)
    KNT = dint("KNT", [1024, NT], BF16)
    VM = dint("VM", [8, 128, NT // 128, 128], BF16)
    OF = dint("OF", [1024, NT], F32)
    AN = dint("AN", [1024, NT], BF16)
    CT = dint("CT", [1024, NT], BF16)

    ident = sb("ident", [128, 128], F32)
    ones_bf = sb("ones_bf", [128, 128], BF16)
    t_const = Tok()
    S.dma("sp", lambda e: e.dma_start(out=ident[:], in_=ident_in), writes=[t_const])
    S.op("pool", lambda e: e.memset(ones_bf[:], 1.0), writes=[t_const])

    psum = [es.enter_context(nc.psum_tensor(f"ps{i}", [128, 512], F32)) for i in range(8)]
    t_ps = [Tok(f"ps{i}") for i in range(8)]

    modT = sb("modT", [128, NL, 72, 2], F32)
    mods = sb("mods", [128, NL, 3, 3, 8, 2], F32)
    lnw_sb = sb("lnw_sb", [128, NL, 3, 8], F32)
    lnb_sb = sb("lnb_sb", [128, NL, 3, 8], F32)
    adab_sb = sb("adab_sb", [128, NL, 72], F32)
    sc_sb = sb("sc_sb", [128, 8, 2], F32)
    sc_tmp = sb("sc_tmp", [128, 8, 2], F32)
    t_small = Tok()
    t_mod = Tok()
    bgate_sb = sb("bgate_sb", [128, NL, 24], F32)
    gnorm_sb = sb("gnorm_sb", [128, NL, 8], F32)
    convw_sb = sb("convw_sb", [128, NL, 3, 8], F32)
    qnorm_sb = sb("qnorm_sb", [128, NL, 4], F32)
    kvnorm_sb = sb("kvnorm_sb", [128, NL, 2], F32)
    gmask_f = sb("gmask_f", [128, 4, 128], F32)
    gmask = sb("gmask", [128, 4, 128], BF16)
    ident_bf = sb("ident_bf", [128, 128], BF16)

    def ld1(out_ap, in_ap):
        def f(e):
            with nc.allow_non_contiguous_dma(reason="tiny vector loads"):
                return e.dma_start(out=out_ap, in_=in_ap)
        S.dma("sp", f, writes=[t_small])

    ld1(sc_tmp[:, :, 0], c_in.rearrange("(k p) -> p k", p=128))
    ld1(sc_tmp[:, :, 1], cctx_in.rearrange("(k p) -> p k", p=128))
    for l in range(NL):
        ld1(adab_sb[:, l, :], ada_b[l].rearrange("(k p) -> p k", p=128))
        for s in range(3):
            ld1(lnw_sb[:, l, s, :], ln_w[l, s].rearrange("(k p) -> p k", p=128))
            ld1(lnb_sb[:, l, s, :], ln_b[l, s].rearrange("(k p) -> p k", p=128))
    for l in range(NL):
        ld1(bgate_sb[:, l, :], b_gate[l].rearrange("(k p) -> p k", p=128))
        ld1(gnorm_sb[:, l, :], gnorm_in[l].rearrange("(k p) -> p k", p=128))
        for k3 in range(3):
            ld1(convw_sb[:, l, k3, :], conv_w_in[l, k3].rearrange("(k p) -> p k", p=128))
        ld1(qnorm_sb[:, l, :], qnorm_in[l].rearrange("(k p) -> p k", p=128))
        ld1(kvnorm_sb[:, l, :], kvnorm_in[l].rearrange("(k p) -> p k", p=128))
    ld1(gmask_f[:], gmask_in)
    S.op("dve", lambda e: e.tensor_copy(out=gmask[:], in_=gmask_f[:]), reads=[t_small], writes=[t_small])
    S.op("dve", lambda e: e.tensor_copy(out=ident_bf[:], in_=ident[:]), reads=[t_const], writes=[t_const])
    S.op("act", lambda e: e.activation(out=sc_sb[:], in_=sc_tmp[:], func=AF.Silu), reads=[t_small], writes=[t_mod])

    m_pre = amark()
    ACW = 1152
    adaw_buf = [sb(f"adaw{i}", [128, 8, ACW], F32) for i in range(2)]
    t_adaw = [Tok(), Tok()]
    piece = 0
    for l in range(NL):
        for pc in range(9 * D // ACW):
            b = piece % 2
            buf = adaw_buf[b]
            src = ada_w[l, :, pc * ACW:(pc + 1) * ACW].rearrange("(k p) c -> p k c", p=128)
            S.dma("sp", lambda e, buf=buf, src=src: e.dma_start(out=buf[:], in_=src), writes=[t_adaw[b]])
            pb = piece % 8
            for mi in range(ACW // 128):
                i = pc * (ACW // 128) + mi

                def mm(e, buf=buf, mi=mi, pb=pb, mcol=mi):
                    r = None
                    for kc in range(8):
                        r = e.matmul(psum[pb][:, mcol * 2:mcol * 2 + 2], lhsT=buf[:, kc, mi * 128:(mi + 1) * 128],
                                     rhs=sc_sb[:, kc, :], start=(kc == 0), stop=(kc == 7))
                    return r
                S.op("pe", mm, reads=[t_adaw[b], t_mod], writes=[t_ps[pb]])
            i0 = pc * (ACW // 128)
            n = ACW // 128

            def ev(e, l=l, i0=i0, n=n, pb=pb):
                return e.tensor_tensor(out=modT[:, l, i0:i0 + n, :],
                                       in0=psum[pb][:, 0:2 * n].rearrange("p (i w) -> p i w", w=2),
                                       in1=adab_sb[:, l, i0:i0 + n].unsqueeze(2).to_broadcast([128, n, 2]),
                                       op=ALU.add)
            S.op("dve", ev, reads=[t_ps[pb], t_small], writes=[t_mod])
            piece += 1
    for l in range(NL):
        for s in range(3):
            base = 3 * s * 8
            S.op("dve", lambda e, l=l, s=s, base=base: e.tensor_scalar_add(
                out=mods[:, l, s, 0], in0=modT[:, l, base + 8:base + 16, :], scalar1=1.0), reads=[t_mod], writes=[t_mod])
            S.op("dve", lambda e, l=l, s=s, base=base: e.tensor_copy(
                out=mods[:, l, s, 1], in_=modT[:, l, base:base + 8, :]), reads=[t_mod], writes=[t_mod])
            gsc = (1.0 / ALPHA) if s == 1 else (0.5 / ALPHA)
            S.op("dve", lambda e, l=l, s=s, base=base, gsc=gsc: e.tensor_scalar_mul(
                out=mods[:, l, s, 2], in0=modT[:, l, base + 16:base + 24, :], scalar1=gsc), reads=[t_mod], writes=[t_mod])

    PW = 512
    wld = [sb(f"wld{i}", [128, 8 * PW], F32) for i in range(2)]
    wcv = [sb(f"wcv{i}", [128, 8 * PW], BF16) for i in range(2)]
    t_wld = [Tok(), Tok()]
    t_wcv = [Tok(), Tok()]
    t_wscr = Tok("wscr")
    prep_n = [0]
    cast_engs = ["act", "dve", "pool"]

    def prep(src, KC, width, dst, mb):
        cw = (8 * PW // KC) // mb * mb
        for c0 in range(0, width, cw):
            w_ = min(cw, width - c0)
            nb = w_ // mb
            i = prep_n[0] % 2
            ce = cast_engs[prep_n[0] % 3]
            prep_n[0] += 1
            ld_v = wld[i][:, 0:KC * w_].rearrange("p (k c) -> p k c", k=KC)
            S.dma("sp", lambda e, ld_v=ld_v, c0=c0, w_=w_: e.dma_start(
                out=ld_v, in_=src[:, c0:c0 + w_].rearrange("(k p) c -> p k c", p=128)), writes=[t_wld[i]])
            cv_v = wcv[i][:, 0:KC * w_].rearrange("p (b k m) -> p b k m", b=nb, k=KC)
            in_v = wld[i][:, 0:KC * w_].rearrange("p (k b m) -> p b k m", k=KC, b=nb)
            if ce == "act":
                S.op("act", lambda e, cv_v=cv_v, in_v=in_v: e.copy(out=cv_v, in_=in_v), reads=[t_wld[i]], writes=[t_wcv[i]])
            else:
                S.op(ce, lambda e, cv_v=cv_v, in_v=in_v: e.tensor_copy(out=cv_v, in_=in_v), reads=[t_wld[i]], writes=[t_wcv[i]])
            st_v = wcv[i][:, 0:KC * w_].rearrange("p (b f) -> p b f", b=nb)
            b0 = c0 // mb
            S.dma("sp", lambda e, st_v=st_v, b0=b0, nb=nb: e.dma_start(
                out=dst[b0:b0 + nb].rearrange("b p f -> p b f"), in_=st_v), reads=[t_wcv[i]], writes=[t_wscr])

    def cast(ce, out_ap, in_ap, neg=False):
        i = None
        if neg:
            if ce == "act":
                return lambda e: e.activation(out=out_ap, in_=in_ap, func=AF.Copy, scale=-1.0)
            return lambda e: e.tensor_scalar_mul(out=out_ap, in0=in_ap, scalar1=-1.0)
        if ce == "act":
            return lambda e: e.copy(out=out_ap, in_=in_ap)
        return lambda e: e.tensor_copy(out=out_ap, in_=in_ap)

    def prep_custom(KC, src_cols, pieces, stores):
        raise NotImplementedError

    def prep_special(src, KC, w_, ops, nout, dst):
        i = prep_n[0] % 2
        prep_n[0] += 1
        ld_v = wld[i][:, 0:KC * w_].rearrange("p (k c) -> p k c", k=KC)
        S.dma("sp", lambda e: e.dma_start(out=ld_v, in_=src.rearrange("(k p) c -> p k c", p=128)), writes=[t_wld[i]])
        cv = wcv[i][:, 0:nout * KC * 128].rearrange("p (b k m) -> p b k m", b=nout, k=KC)
        for n_, (o_ap, i_ap, neg) in enumerate(ops(ld_v, cv)):
            ce = cast_engs[n_ % 3]
            S.op(ce, cast(ce, o_ap, i_ap, neg), reads=[t_wld[i]], writes=[t_wcv[i]])
        S.dma("sp", lambda e: e.dma_start(out=dst.rearrange("b p f -> p b f"),
                                          in_=wcv[i][:, 0:nout * KC * 128].rearrange("p (b f) -> p b f", b=nout)),
              reads=[t_wcv[i]], writes=[t_wscr])

    def rot_ops(o_blk, i_cols):
        ov = o_blk.rearrange("p k (g two s) -> p k g two s", g=2, two=2)
        iv = i_cols.rearrange("p k (g two s) -> p k g two s", g=2, two=2)
        return [(ov[:, :, :, 0, :], iv[:, :, :, 1, :], True), (ov[:, :, :, 1, :], iv[:, :, :, 0, :], False)]

    def prep_layer(li):
        for f in range(2):
            for ag in range(2):
                prep(ffn_w13[li, f, :, ag * DFF:(ag + 1) * DFF], 8, DFF, W13S[li, f, ag], 128)
            prep(ffn_w2[li, f], NJ, D, W2S[li, f], 128)
        wl = w_in[li]
        prep(wl[:, 0:512], 8, 512, WIS[li, 0:4], 128)
        prep(wl[:, 512:1024], 8, 512, WIS[li, 4:8], 128)
        prep(wl[:, 2048:3072], 8, 1024, WIS[li, 8:16], 128)
        prep(wl[:, 4128:5152], 8, 1024, WIS[li, 16:24], 128)
        prep(wl[:, 5152:6176], 8, 1024, WIS[li, 24:32], 128)
        prep(wl[:, 6176:6688], 8, 512, WIS[li, 32:36], 128)
        prep(wl[:, 6688:6944], 8, 256, WIS[li, 36:38], 128)
        prep(wl[:, 3104:4128], 8, 1024, WIS[li, 38:46], 128)
        prep(wl[:, 7008:10080], 8, 3072, WIS[li, 46:70], 128)
        prep(wl[:, 1024:2048], 8, 1024, WVS[li], 512)
        prep_special(wl[:, 3072:3104], 8, 32, lambda ld, cv: [(cv[:, 0, :, 0:32], ld, False)], 1, WIS[li, 70:71])
        prep_special(wl[:, 6944:7008], 8, 64,
                     lambda ld, cv: [(cv[:, 0, :, 0:64], ld, False), (cv[:, 0, :, 64:128], ld, False)]
                     + rot_ops(cv[:, 1, :, 0:64], ld) + rot_ops(cv[:, 1, :, 64:128], ld), 2, WIS[li, 71:73])
        for pi, pw in enumerate([gla_proj, conv_proj, mla_proj, w_out]):
            prep(pw[li], 8, 1024, PRJS[li, pi], 128)
        for a in range(4):
            def uq_ops(ld, cv):
                ops = [(cv[:, 0, :, :], ld[:, :, 0:128], False), (cv[:, 1, :, :], ld[:, :, 192:320], False),
                       (cv[:, 2, :, 0:64], ld[:, :, 128:192], False), (cv[:, 2, :, 64:128], ld[:, :, 320:384], False)]
                ops += rot_ops(cv[:, 3, :, 0:64], ld[:, :, 128:192]) + rot_ops(cv[:, 3, :, 64:128], ld[:, :, 320:384])
                return ops
            i = prep_n[0] % 2
            prep_n[0] += 1
            ld_v = wld[i][:, 0:4 * 384].rearrange("p (k c) -> p k c", k=4)
            S.dma("sp", lambda e, ld_v=ld_v, a=a: e.dma_start(
                out=ld_v, in_=w_uq[li, :, a * 384:(a + 1) * 384].rearrange("(k p) c -> p k c", p=128)), writes=[t_wld[i]])
            cv = wcv[i][:, 0:4 * 512].rearrange("p (b k m) -> p b k m", b=4, k=4)
            for n_, (o_ap, i_ap, neg) in enumerate(uq_ops(ld_v, cv)):
                ce = cast_engs[n_ % 3]
                S.op(ce, cast(ce, o_ap, i_ap, neg), reads=[t_wld[i]], writes=[t_wcv[i]])
            cvf = wcv[i][:, 0:4 * 512].rearrange("p (b f) -> p b f", b=4)
            S.dma("sp", lambda e, cvf=cvf, a=a: e.dma_start(out=WUQS[li, 2 * a:2 * a + 2].rearrange("b p f -> p b f"),
                                                        in_=cvf[:, 0:2, :]), reads=[t_wcv[i]], writes=[t_wscr])
            S.dma("sp", lambda e, cvf=cvf, a=a: e.dma_start(out=WUQS[li, 8 + a], in_=cvf[:, 2, :]),
                  reads=[t_wcv[i]], writes=[t_wscr])
            S.dma("sp", lambda e, cvf=cvf, a=a: e.dma_start(out=WUQS[li, 12 + a], in_=cvf[:, 3, :]),
                  reads=[t_wcv[i]], writes=[t_wscr])
        i = prep_n[0] % 2
        prep_n[0] += 1
        ld_v = wld[i][:, 0:2 * 2048].rearrange("p (k h two m) -> p k h two m", k=2, h=8, two=2)
        S.dma("sp", lambda e, i=i: e.dma_start(out=wld[i][:, 0:4096].rearrange("p (k c) -> p k c", k=2),
                                                in_=w_ukv[li].rearrange("(k p) c -> p k c", p=128)), writes=[t_wld[i]])
        cvk = wcv[i][:, 0:2048].rearrange("p (h k m) -> p h k m", h=8, k=2)
        cvv = wcv[i][:, 2048:4096].rearrange("p (k h m) -> p k h m", k=2, h=8)
        for kk in range(2):
            ce = cast_engs[kk]
            S.op(ce, cast(ce, cvk[:, :, kk, :], ld_v[:, kk, :, 0, :]), reads=[t_wld[i]], writes=[t_wcv[i]])
            ce = cast_engs[kk + 1]
            S.op(ce, cast(ce, cvv[:, kk, :, :], ld_v[:, kk, :, 1, :]), reads=[t_wld[i]], writes=[t_wcv[i]])
        S.dma("sp", lambda e, i=i: e.dma_start(out=WUKS[li].rearrange("b p f -> p b f"),
                                                in_=wcv[i][:, 0:2048].rearrange("p (b f) -> p b f", b=8)),
              reads=[t_wcv[i]], writes=[t_wscr])
        S.dma("sp", lambda e, i=i: e.dma_start(out=WUKVV[li], in_=wcv[i][:, 2048:4096]),
              reads=[t_wcv[i]], writes=[t_wscr])

    for li in range(NL):
        prep_layer(li)

    arelease(m_pre)

    xin_sb = [sb(f"xin{i}", [128, D], F32) for i in range(2)]
    xT_sb = [sb(f"xTs{i}", [128, 8, 128], F32) for i in range(2)]
    t_xin = [Tok(), Tok()]
    t_xT = [Tok(), Tok()]
    t_XT = [Tok(f"XT{i}") for i in range(9)]
    for tt in range(NT // 128):
        i = tt % 2
        src = x_in[tt * 128:(tt + 1) * 128, :] if tt < L // 128 else ctx_in[(tt - L // 128) * 128:(tt - L // 128 + 1) * 128, :]
        S.dma("sp", lambda e, i=i, src=src: e.dma_start(out=xin_sb[i][:], in_=src), writes=[t_xin[i]])
        for half in range(2):
            pb = 2 * (tt % 2) + half

            def tr(e, i=i, half=half, pb=pb):
                r = None
                for q in range(4):
                    kc = half * 4 + q
                    r = e.transpose(psum[pb][:, q * 128:(q + 1) * 128], xin_sb[i][:, kc * 128:(kc + 1) * 128], ident[:])
                return r
            S.op("pe", tr, reads=[t_xin[i], t_const], writes=[t_ps[pb]])
            ce = "dve" if half == 0 else "act"

            def cp(e, i=i, half=half, pb=pb, ce=ce):
                o = xT_sb[i][:, half * 4:half * 4 + 4, :]
                src_ = psum[pb][:].rearrange("p (k t) -> p k t", k=4)
                return e.tensor_copy(out=o, in_=src_) if ce == "dve" else e.copy(out=o, in_=src_)
            S.op(ce, cp, reads=[t_ps[pb]], writes=[t_xT[i]])
        S.dma("sp", lambda e, i=i, tt=tt: e.dma_start(
            out=XT[:, tt * 128:(tt + 1) * 128].rearrange("(k p) t -> p k t", p=128), in_=xT_sb[i][:]),
            reads=[t_xT[i]], writes=[t_XT[tt // 4]])
    S.dma_barrier("sp")

    T = 512
    xt = [sb(f"xt{i}", [128, 8, T], F32) for i in range(2)]
    t_xt = [Tok(), Tok()]
    hb = sb("hb", [128, 8, T], BF16)
    t_hb = Tok()
    ub = sb("ub", [128, 8, T], BF16)
    usq = sb("usq", [128, 8, T], BF16)
    t_ub = Tok()
    t_usq = Tok()
    stat = sb("stat", [128, 4, T], F32)
    t_stat = Tok()
    tmpn = [sb(f"tmpn{i}", [128, T], F32) for i in range(2)]
    t_tmpn = [Tok(), Tok()]
    wblk = [sb(f"wblk{i}", [128, 8 * 128], BF16) for i in range(3)]
    t_wblk = [Tok() for _ in range(3)]
    cnt = {"xt": 0, "w13": 0, "w2": 0, "sil": 0, "tmpn": 0, "ps": 0, "wblk": 0}
    ffn_bufs = {}

    def ffn_alloc():
        ffn_bufs["hid"] = sb("hid", [128, NJ, T], BF16)
        ffn_bufs["sil"] = [sb(f"sil{i}", [128, T], BF16) for i in range(2)]
        ffn_bufs["w13b"] = [sb(f"w13b{i}", [128, 2, 8, 128], BF16) for i in range(3)]
        ffn_bufs["w2b"] = [sb(f"w2b{i}", [128, NJ, 128], BF16) for i in range(2)]

    t_hid = [Tok() for _ in range(NJ)]
    t_sil = [Tok(), Tok()]
    t_w13b = [Tok() for _ in range(3)]
    t_w2b = [Tok(), Tok()]

    tiles = [(i * T, T) for i in range(L // T)] + [(L, CTX)]

    dbg = {}

    def dump(name, ap, reads, dt=BF16):
        if debug is None or name not in debug or name in dbg:
            return
        dbg[name] = 1
        shp = list(ap.shape)
        d_ = nc.dram_tensor("dbg_" + name, shp, dt, kind="ExternalOutput").ap()
        S.dma("sp", lambda e: e.dma_start(out=d_, in_=ap), reads=reads)

    def dsnap(name, ap, dt=BF16):
        if debug is None or name not in debug:
            return
        S.dma_barrier("sp")
        d_ = nc.dram_tensor("dbg_" + name, list(ap.shape), dt, kind="ExternalOutput").ap()
        S.dma("sp", lambda e: e.dma_start(out=d_, in_=ap))
        S.dma_barrier("sp")

    def snapshot(name):
        if debug is None or name not in debug:
            return
        S.dma_barrier("sp")
        d_ = nc.dram_tensor("dbg_" + name, [D, NT], F32, kind="ExternalOutput").ap()
        S.dma("sp", lambda e: e.dma_start(out=d_, in_=XT))
        S.dma_barrier("sp")

    def load_xt(t0, tw):
        i = cnt["xt"] % 2
        cnt["xt"] += 1
        S.dma("sp", lambda e: e.dma_start(out=xt[i][:, :, 0:tw],
                                          in_=XT[:, t0:t0 + tw].rearrange("(k p) t -> p k t", p=128)),
              reads=[t_XT[t0 // 512]], writes=[t_xt[i]])
        return i

    def layer_norm_store(li, s, xi, t0, tw, which):
        S.op("pool", lambda e: e.tensor_copy(out=ub[:, :, 0:tw], in_=xt[xi][:, :, 0:tw]), reads=[t_xt[xi]], writes=[t_ub])
        S.op("act", lambda e: e.activation(out=usq[:, :, 0:tw], in_=xt[xi][:, :, 0:tw], func=AF.Square),
             reads=[t_xt[xi]], writes=[t_usq])
        p1, p2 = 6, 7
        eps_eff = EPS / (ALPHA * ALPHA)

        def mm1(e):
            r = None
            for kc in range(8):
                r = e.matmul(psum[p1][:, 0:tw], lhsT=ones_bf[:], rhs=ub[:, kc, 0:tw], start=(kc == 0), stop=(kc == 7))
            return r

        def mm2(e):
            r = None
            for kc in range(8):
                r = e.matmul(psum[p2][:, 0:tw], lhsT=ones_bf[:], rhs=usq[:, kc, 0:tw], start=(kc == 0), stop=(kc == 7))
            return r
        S.op("pe", mm1, reads=[t_ub, t_const], writes=[t_ps[p1]])
        S.op("pe", mm2, reads=[t_usq, t_const], writes=[t_ps[p2]])
        S.op("act", lambda e: e.activation(out=stat[:, 0, 0:tw], in_=psum[p1][:, 0:tw], func=AF.Copy, scale=1.0 / D),
             reads=[t_ps[p1]], writes=[t_stat])
        S.op("dve", lambda e: e.tensor_tensor(out=stat[:, 1, 0:tw], in0=stat[:, 0, 0:tw], in1=stat[:, 0, 0:tw], op=ALU.mult),
             reads=[t_stat], writes=[t_stat])
        S.op("dve", lambda e: e.scalar_tensor_tensor(out=stat[:, 1, 0:tw], in0=psum[p2][:, 0:tw], scalar=1.0 / D,
                                                     in1=stat[:, 1, 0:tw], op0=ALU.mult, op1=ALU.subtract),
             reads=[t_stat, t_ps[p2]], writes=[t_stat])
        S.op("dve", lambda e: e.tensor_scalar_add(out=stat[:, 1, 0:tw], in0=stat[:, 1, 0:tw], scalar1=eps_eff),
             reads=[t_stat], writes=[t_stat])
        S.op("act", lambda e: e.activation(out=stat[:, 1, 0:tw], in_=stat[:, 1, 0:tw], func=AF.Sqrt),
             reads=[t_stat], writes=[t_stat])
        S.op("dve", lambda e: e.reciprocal(out=stat[:, 2, 0:tw], in_=stat[:, 1, 0:tw]),
             reads=[t_stat], writes=[t_stat])
        S.op("dve", lambda e: e.scalar_tensor_tensor(out=stat[:, 3, 0:tw], in0=stat[:, 0, 0:tw], scalar=-1.0,
                                                     in1=stat[:, 2, 0:tw], op0=ALU.mult, op1=ALU.mult),
             reads=[t_stat], writes=[t_stat])
        for kc in range(8):
            ti = cnt["tmpn"] % 2
            cnt["tmpn"] += 1
            S.op("dve", lambda e, kc=kc, ti=ti: e.tensor_tensor(out=tmpn[ti][:, 0:tw], in0=xt[xi][:, kc, 0:tw],
                                                                in1=stat[:, 2, 0:tw], op=ALU.mult),
                 reads=[t_xt[xi], t_stat], writes=[t_tmpn[ti]])
            S.op("pool", lambda e, kc=kc, ti=ti: e.tensor_tensor(out=tmpn[ti][:, 0:tw], in0=tmpn[ti][:, 0:tw],
                                                                 in1=stat[:, 3, 0:tw], op=ALU.add),
                 reads=[t_tmpn[ti], t_stat], writes=[t_tmpn[ti]])
            S.op("act", lambda e, kc=kc, ti=ti: e.activation(out=xt[xi][:, kc, 0:tw], in_=tmpn[ti][:, 0:tw], func=AF.Identity,
                                                             scale=lnw_sb[:, li, s, kc:kc + 1], bias=lnb_sb[:, li, s, kc:kc + 1]),
                 reads=[t_tmpn[ti], t_small], writes=[t_xt[xi]])
        S.dma("sp", lambda e: e.dma_start(out=XT[:, t0:t0 + tw].rearrange("(k p) t -> p k t", p=128), in_=xt[xi][:, :, 0:tw]),
              reads=[t_xt[xi]], writes=[t_XT[t0 // 512]])

    def ffn_tile(li, f, s, t0, tw, xi, prefetch):
        which = 0 if t0 < L else 1
        hid, sil, w13b, w2b = ffn_bufs["hid"], ffn_bufs["sil"], ffn_bufs["w13b"], ffn_bufs["w2b"]
        for kc in range(8):
            S.op("act", lambda e, kc=kc: e.activation(out=hb[:, kc, 0:tw], in_=xt[xi][:, kc, 0:tw], func=AF.Identity,
                                                      scale=mods[:, li, s, 0, kc, which:which + 1],
                                                      bias=mods[:, li, s, 1, kc, which:which + 1]),
                 reads=[t_xt[xi], t_mod], writes=[t_hb])
        for j in range(NJ):
            wi = cnt["w13"] % 3
            cnt["w13"] += 1
            for ag in range(2):
                S.dma("sp", lambda e, wi=wi, ag=ag, j=j: e.dma_start(
                    out=w13b[wi][:, ag].rearrange("p k m -> p (k m)"), in_=W13S[li, f, ag, j]),
                    writes=[t_w13b[wi]], join=(ag == 1))
            pa = (cnt["ps"] % 3) * 2
            cnt["ps"] += 1
            for ag in range(2):
                def mm(e, wi=wi, ag=ag, pa=pa):
                    r = None
                    for kc in range(8):
                        r = e.matmul(psum[pa + ag][:, 0:tw], lhsT=w13b[wi][:, ag, kc, :], rhs=hb[:, kc, 0:tw],
                                     start=(kc == 0), stop=(kc == 7))
                    return r
                S.op("pe", mm, reads=[t_w13b[wi], t_hb], writes=[t_ps[pa + ag]])
            si = cnt["sil"] % 2
            cnt["sil"] += 1
            S.op("act", lambda e, si=si, pa=pa: e.activation(out=sil[si][:, 0:tw], in_=psum[pa][:, 0:tw], func=AF.Silu),
                 reads=[t_ps[pa]], writes=[t_sil[si]])
            S.op("dve", lambda e, si=si, pa=pa, j=j: e.tensor_tensor(out=hid[:, j, 0:tw], in0=psum[pa + 1][:, 0:tw],
                                                                     in1=sil[si][:, 0:tw], op=ALU.mult),
                 reads=[t_ps[pa + 1], t_sil[si]], writes=[t_hid[j]])
        dump("hb", hb[:], [t_hb])
        dump("hid", hid[:], t_hid)
        nxt = prefetch()
        for m in range(8):
            wi = cnt["w2"] % 2
            cnt["w2"] += 1
            S.dma("sp", lambda e, wi=wi, m=m: e.dma_start(out=w2b[wi][:].rearrange("p j n -> p (j n)"), in_=W2S[li, f, m]),
                  writes=[t_w2b[wi]])
            pa = (cnt["ps"] % 3) * 2
            cnt["ps"] += 1

            def mm(e, wi=wi, pa=pa):
                r = None
                for j in range(NJ):
                    r = e.matmul(psum[pa][:, 0:tw], lhsT=w2b[wi][:, j, :], rhs=hid[:, j, 0:tw],
                                 start=(j == 0), stop=(j == NJ - 1))
                return r
            S.op("pe", mm, reads=[t_w2b[wi]] + t_hid, writes=[t_ps[pa]])
            S.op("dve", lambda e, m=m, pa=pa: e.scalar_tensor_tensor(
                out=xt[xi][:, m, 0:tw], in0=psum[pa][:, 0:tw], scalar=mods[:, li, s, 2, m, which:which + 1],
                in1=xt[xi][:, m, 0:tw], op0=ALU.mult, op1=ALU.add),
                reads=[t_ps[pa], t_xt[xi], t_mod, t_hb], writes=[t_xt[xi]])
        dump("u", xt[xi][:], [t_xt[xi]], F32)
        layer_norm_store(li, s, xi, t0, tw, which)
        return nxt

    def ffn_phase(li, f, tile_list):
        s = 0 if f == 0 else 2
        m0 = amark()
        ffn_alloc()
        nxt = load_xt(*tile_list[0])
        for ti_, (t0, tw) in enumerate(tile_list):
            def prefetch(ti_=ti_):
                if ti_ + 1 < len(tile_list):
                    return load_xt(*tile_list[ti_ + 1])
                return None
            nxt = ffn_tile(li, f, s, t0, tw, nxt, prefetch)
        arelease(m0)

    def lin(wap, KC, M, rhs, rtoks, tw, evac):
        wi = cnt["wblk"] % 3
        cnt["wblk"] += 1
        S.dma("sp", lambda e: e.dma_start(out=wblk[wi][:, 0:KC * 128], in_=wap), writes=[t_wblk[wi]])
        pb = cnt["ps"] % 6
        cnt["ps"] += 1
        wv_ = wblk[wi][:, 0:KC * 128].rearrange("p (k m) -> p k m", k=KC)

        def mm(e):
            r = None
            for kc in range(KC):
                r = e.matmul(psum[pb][0:M, 0:tw], lhsT=wv_[:, kc, 0:M], rhs=rhs[:, kc, 0:tw], start=(kc == 0), stop=(kc == KC - 1))
            return r
        S.op("pe", mm, reads=[t_wblk[wi]] + list(rtoks), writes=[t_ps[pb]])
        evac(pb)
        return pb

    def ev_copy(eng, out_ap, wtok, M, tw, scale=None, func=None):
        def evac(pb):
            src = psum[pb][0:M, 0:tw]
            if eng == "act":
                if func is not None or scale is not None:
                    S.op("act", lambda e: e.activation(out=out_ap, in_=src, func=(func or AF.Copy), scale=(scale or 1.0)),
                         reads=[t_ps[pb]], writes=[wtok])
                else:
                    S.op("act", lambda e: e.copy(out=out_ap, in_=src), reads=[t_ps[pb]], writes=[wtok])
            else:
                if scale is not None:
                    S.op(eng, lambda e: e.tensor_scalar_mul(out=out_ap, in0=src, scalar1=scale), reads=[t_ps[pb]], writes=[wtok])
                else:
                    S.op(eng, lambda e: e.tensor_copy(out=out_ap, in_=src), reads=[t_ps[pb]], writes=[wtok])
        return evac

    m1b = {}

    def m1_alloc():
        m1b["wv"] = sb("wv", [128, 8, 1024], BF16)
        m1b["wkvv"] = sb("wkvv", [128, 2, 1024], BF16)
        m1b["dwst"] = sb("dwst", [33, 2, 512], F32)
        m1b["dw"] = sb("dw", [33, 2, 512], BF16)
        m1b["glrT"] = sb("glrT", [33, T], BF16)
        m1b["st8"] = [sb(f"st8_{i}", [128, 8, T], BF16) for i in range(2)]
        m1b["z7"] = [sb(f"z7_{i}", [128, T], F32) for i in range(2)]
        m1b["cqf"] = sb("cqf", [128, 4, T], F32)
        m1b["cqn"] = sb("cqn", [128, 4, T], BF16)
        m1b["ckvn"] = sb("ckvn", [128, 2, T], BF16)
        m1b["cs"] = sb("cs", [128, 2, T], F32)
        m1b["etmp"] = [sb(f"etmp{i}", [128, 512], F32) for i in range(2)]
        m1b["gst"] = sb("gst", [128, 2, 4, 512], BF16)
        m1b["vst"] = sb("vst", [128, 4, 1024], BF16)
        m1b["rtmp"] = [sb(f"rtmp{i}", [128, T], F32) for i in range(2)]
        for k_ in ["wv", "wkvv", "dw", "glrT", "cqf", "cqn", "ckvn", "cs", "gst", "vst"]:
            m1b["t_" + k_] = Tok(k_)
        m1b["t_st8"] = [Tok(), Tok()]
        m1b["t_z7"] = [Tok(), Tok()]
        m1b["t_etmp"] = [Tok(), Tok()]
        m1b["t_rtmp"] = [Tok(), Tok()]
        m1b["n"] = {"st8": 0, "z7": 0, "etmp": 0, "rtmp": 0}

    def rms_apply(src_f, KC, tw, nfeat, norm_ap_fn, out_bf, t_src, t_out):
        S.op("act", lambda e: e.activation(out=usq[:, 0:KC, 0:tw], in_=src_f[:, 0:KC, 0:tw], func=AF.Square),
             reads=[t_src], writes=[t_usq])
        p2 = 7

        def mm2(e):
            r = None
            for kc in range(KC):
                r = e.matmul(psum[p2][:, 0:tw], lhsT=ones_bf[:], rhs=usq[:, kc, 0:tw], start=(kc == 0), stop=(kc == KC - 1))
            return r
        S.op("pe", mm2, reads=[t_usq, t_const], writes=[t_ps[p2]])
        S.op("dve", lambda e: e.tensor_scalar(out=stat[:, 1, 0:tw], in0=psum[p2][:, 0:tw], scalar1=1.0 / nfeat, scalar2=EPS,
                                              op0=ALU.mult, op1=ALU.add), reads=[t_ps[p2]], writes=[t_stat])
        S.op("act", lambda e: e.activation(out=stat[:, 1, 0:tw], in_=stat[:, 1, 0:tw], func=AF.Sqrt), reads=[t_stat], writes=[t_stat])
        S.op("dve", lambda e: e.reciprocal(out=stat[:, 2, 0:tw], in_=stat[:, 1, 0:tw]), reads=[t_stat], writes=[t_stat])
        for kc in range(KC):
            S.op("dve", lambda e, kc=kc: e.scalar_tensor_tensor(out=out_bf[:, kc, 0:tw], in0=src_f[:, kc, 0:tw], scalar=norm_ap_fn(kc),
                                                                in1=stat[:, 2, 0:tw], op0=ALU.mult, op1=ALU.mult),
                 reads=[t_src, t_stat, t_small], writes=[t_out])

    def rope_pair(wA, wB, KC, rhs, rtoks, tw, isx, out_ap, wtok):
        if not isx:
            lin(wA, KC, 128, rhs, rtoks, tw, ev_copy("act", out_ap, wtok, 128, tw))
            return
        n = m1b["n"]
        r1 = n["rtmp"] % 2
        n["rtmp"] += 1
        cs = m1b["cs"]
        rt, t_rt = m1b["rtmp"][r1], m1b["t_rtmp"][r1]

        def evA(pb):
            S.op("dve", lambda e: e.tensor_tensor(out=rt[:, 0:tw], in0=psum[pb][:, 0:tw], in1=cs[:, 0, 0:tw], op=ALU.mult),
                 reads=[t_ps[pb], m1b["t_cs"]], writes=[t_rt])

        def evB(pb):
            ti = cnt["tmpn"] % 2
            cnt["tmpn"] += 1
            S.op("dve", lambda e: e.tensor_tensor(out=tmpn[ti][:, 0:tw], in0=psum[pb][:, 0:tw], in1=cs[:, 1, 0:tw], op=ALU.mult),
                 reads=[t_ps[pb], m1b["t_cs"]], writes=[t_tmpn[ti]])
            S.op("pool", lambda e: e.tensor_tensor(out=out_ap, in0=rt[:, 0:tw], in1=tmpn[ti][:, 0:tw], op=ALU.add),
                 reads=[t_rt, t_tmpn[ti]], writes=[wtok])
        lin(wA, KC, 128, rhs, rtoks, tw, evA)
        lin(wB, KC, 128, rhs, rtoks, tw, evB)

    def m1_tile(li, t0, tw, xi, prefetch):
        which = 0 if t0 < L else 1
        isx = t0 < L
        n = m1b["n"]
        ns = tw // 128
        for kc in range(8):
            S.op("act", lambda e, kc=kc: e.activation(out=hb[:, kc, 0:tw], in_=xt[xi][:, kc, 0:tw], func=AF.Identity,
                                                      scale=mods[:, li, 1, 0, kc, which:which + 1],
                                                      bias=mods[:, li, 1, 1, kc, which:which + 1]),
                 reads=[t_xt[xi], t_mod], writes=[t_hb])
        if isx:
            S.dma("sp", lambda e: e.dma_start(out=m1b["cs"][:, :, 0:tw], in_=rope_cs[:, :, t0:t0 + tw]), writes=[m1b["t_cs"]])

        def st8_next():
            i = n["st8"] % 2
            n["st8"] += 1
            return m1b["st8"][i], m1b["t_st8"][i]
        stq, t_stq = st8_next()
        for h in range(4):
            lin(WIS[li, h], 8, 128, hb, [t_hb], tw, ev_copy("act", stq[:, h, 0:tw], t_stq, 128, tw, scale=128.0 ** -0.5))
        for h in range(4):
            lin(WIS[li, 4 + h], 8, 128, hb, [t_hb], tw, ev_copy("dve", stq[:, 4 + h, 0:tw], t_stq, 128, tw))
        S.dma("sp", lambda e: e.dma_start(out=QT[:, t0:t0 + tw].rearrange("(h p) t -> p h t", p=128), in_=stq[:, 0:4, 0:tw]),
              reads=[t_stq], writes=[])
        S.dma("sp", lambda e: e.dma_start(out=KT[:, t0:t0 + tw].rearrange("(h p) t -> p h t", p=128), in_=stq[:, 4:8, 0:tw]),
              reads=[t_stq], writes=[])
        str_, t_str = st8_next()
        for c_ in range(8):
            zi = n["z7"] % 2
            n["z7"] += 1
            z7b, t_z7b = m1b["z7"][zi], m1b["t_z7"][zi]

            def evr(pb, c_=c_, z7b=z7b, t_z7b=t_z7b):
                S.op("act", lambda e: e.activation(out=z7b[:, 0:tw], in_=psum[pb][:, 0:tw], func=AF.Silu), reads=[t_ps[pb]], writes=[t_z7b])
                S.op("dve", lambda e: e.tensor_scalar_mul(out=str_[:, c_, 0:tw], in0=z7b[:, 0:tw], scalar1=gnorm_sb[:, li, c_:c_ + 1]),
                     reads=[t_z7b, t_small], writes=[t_str])
            lin(WIS[li, 8 + c_], 8, 128, hb, [t_hb], tw, evr)
        S.dma("sp", lambda e: e.dma_start(out=RT[:, t0:t0 + tw].rearrange("(h p) t -> p h t", p=128), in_=str_[:, :, 0:tw]),
              reads=[t_str], writes=[])
        stu, t_stu = st8_next()
        for c_ in range(8):
            zi = n["z7"] % 2
            n["z7"] += 1
            z7b, t_z7b = m1b["z7"][zi], m1b["t_z7"][zi]
            lin(WIS[li, 16 + c_], 8, 128, hb, [t_hb], tw, ev_copy("act", z7b[:, 0:tw], t_z7b, 128, tw))

            def ev8(pb, c_=c_, z7b=z7b, t_z7b=t_z7b):
                S.op("dve", lambda e: e.tensor_tensor(out=stu[:, c_, 0:tw], in0=psum[pb][:, 0:tw], in1=z7b[:, 0:tw], op=ALU.mult),
                     reads=[t_ps[pb], t_z7b], writes=[t_stu])
            lin(WIS[li, 24 + c_], 8, 128, hb, [t_hb], tw, ev8)
        S.dma("sp", lambda e: e.dma_start(out=UT[:, t0:t0 + tw].rearrange("(h p) t -> p h t", p=128), in_=stu[:, :, 0:tw]),
              reads=[t_stu], writes=[])
        glrT, dw, gst = m1b["glrT"], m1b["dw"], m1b["gst"]
        lin(WIS[li, 70], 8, 32, hb, [t_hb], tw, ev_copy("dve", glrT[0:32, 0:tw], m1b["t_glrT"], 32, tw))
        for sub in range(ns):
            for d_ in range(2):
                pb = cnt["ps"] % 6
                cnt["ps"] += 1
                S.op("pe", lambda e, sub=sub, d_=d_, pb=pb: e.matmul(psum[pb][:, :], lhsT=glrT[0:33, sub * 128:(sub + 1) * 128],
                                                                     rhs=dw[0:33, d_, :], start=True, stop=True),
                     reads=[m1b["t_glrT"], m1b["t_dw"]], writes=[t_ps[pb]])
                ei = n["etmp"] % 2
                n["etmp"] += 1
                et, t_et = m1b["etmp"][ei], m1b["t_etmp"][ei]
                S.op("act", lambda e, pb=pb, et=et: e.activation(out=et[:], in_=psum[pb][:, :], func=AF.Exp, scale=-1.0),
                     reads=[t_ps[pb]], writes=[t_et])
                S.op("act", lambda e, et=et, sub=sub, d_=d_: e.activation(out=gst[:, d_, sub, :], in_=et[:], func=AF.Ln, bias=1.0),
                     reads=[t_et], writes=[m1b["t_gst"]])
        for d_ in range(2):
            S.dma("sp", lambda e, d_=d_: e.dma_start(out=GP[d_, t0:t0 + tw, :].rearrange("(s p) f -> p s f", p=128),
                                                     in_=gst[:, d_, 0:ns, :]), reads=[m1b["t_gst"]], writes=[])
        wv, vst = m1b["wv"], m1b["vst"]
        for sub in range(ns):
            for half in range(2):
                pb = cnt["ps"] % 6
                cnt["ps"] += 1

                def mmv(e, sub=sub, half=half, pb=pb):
                    r = None
                    for kc in range(8):
                        r = e.matmul(psum[pb][:, :], lhsT=hb[:, kc, sub * 128:(sub + 1) * 128], rhs=wv[:, kc, half * 512:(half + 1) * 512],
                                     start=(kc == 0), stop=(kc == 7))
                    return r
                S.op("pe", mmv, reads=[t_hb, m1b["t_wv"]], writes=[t_ps[pb]])
                eng = "act" if half == 0 else "dve"
                ev_copy(eng, vst[:, sub, half * 512:(half + 1) * 512], m1b["t_vst"], 128, 512)(pb)
        S.dma("sp", lambda e: e.dma_start(out=VG[t0:t0 + tw, :].rearrange("(s p) f -> p s f", p=128), in_=vst[:, 0:ns, :]),
              reads=[m1b["t_vst"]], writes=[])
        stk, t_stk = st8_next()
        rope_pair(WIS[li, 71], WIS[li, 72], 8, hb, [t_hb], tw, isx, stk[:, 0, 0:tw], t_stk)
        S.dma("sp", lambda e: e.dma_start(out=KRT[:, t0:t0 + tw], in_=stk[:, 0, 0:tw]), reads=[t_stk], writes=[])
        cqf, cqn, ckvn = m1b["cqf"], m1b["cqn"], m1b["ckvn"]
        for c_ in range(4):
            lin(WIS[li, 32 + c_], 8, 128, hb, [t_hb], tw, ev_copy("act", cqf[:, c_, 0:tw], m1b["t_cqf"], 128, tw))
        rms_apply(cqf, 4, tw, 512, lambda kc: qnorm_sb[:, li, kc:kc + 1], cqn, m1b["t_cqf"], m1b["t_cqn"])
        for c_ in range(2):
            lin(WIS[li, 36 + c_], 8, 128, hb, [t_hb], tw, ev_copy("act", cqf[:, c_, 0:tw], m1b["t_cqf"], 128, tw))
        rms_apply(cqf, 2, tw, 256, lambda kc: kvnorm_sb[:, li, kc:kc + 1], ckvn, m1b["t_cqf"], m1b["t_ckvn"])
        nxt = prefetch()
        stn, t_stn = st8_next()
        for h in range(8):
            lin(WUQS[li, h], 4, 128, cqn, [m1b["t_cqn"]], tw, ev_copy("act" if h % 2 else "dve", stn[:, h, 0:tw], t_stn, 128, tw))
        S.dma("sp", lambda e: e.dma_start(out=QNT[:, t0:t0 + tw].rearrange("(h p) t -> p h t", p=128), in_=stn[:, :, 0:tw]),
              reads=[t_stn], writes=[])
        stp, t_stp = st8_next()
        for a in range(4):
            rope_pair(WUQS[li, 8 + a], WUQS[li, 12 + a], 4, cqn, [m1b["t_cqn"]], tw, isx, stp[:, a, 0:tw], t_stp)
        S.dma("sp", lambda e: e.dma_start(out=QRT[:, t0:t0 + tw].rearrange("(h p) t -> p h t", p=128), in_=stp[:, 0:4, 0:tw]),
              reads=[t_stp], writes=[])
        stkn, t_stkn = st8_next()
        for h in range(8):
            lin(WUKS[li, h], 2, 128, ckvn, [m1b["t_ckvn"]], tw, ev_copy("act" if h % 2 else "dve", stkn[:, h, 0:tw], t_stkn, 128, tw))
        S.dma("sp", lambda e: e.dma_start(out=KNT[:, t0:t0 + tw].rearrange("(h p) t -> p h t", p=128), in_=stkn[:, :, 0:tw]),
              reads=[t_stkn], writes=[])
        wkvv = m1b["wkvv"]
        for sub in range(ns):
            for half in range(2):
                pb = cnt["ps"] % 6
                cnt["ps"] += 1

                def mmv2(e, sub=sub, half=half, pb=pb):
                    r = None
                    for kc in range(2):
                        r = e.matmul(psum[pb][:, :], lhsT=ckvn[:, kc, sub * 128:(sub + 1) * 128], rhs=wkvv[:, kc, half * 512:(half + 1) * 512],
                                     start=(kc == 0), stop=(kc == 1))
                    return r
                S.op("pe", mmv2, reads=[m1b["t_ckvn"], m1b["t_wkvv"]], writes=[t_ps[pb]])
                eng = "act" if half == 0 else "dve"
                ev_copy(eng, vst[:, sub, half * 512:(half + 1) * 512], m1b["t_vst"], 128, 512)(pb)
        c0 = t0 // 128
        for sub in range(ns):
            S.dma("sp", lambda e, sub=sub: e.dma_start(out=VM[:, :, c0 + sub, :].rearrange("h p m -> p h m"),
                                                       in_=vst[:, sub, :].rearrange("p (h m) -> p h m", h=8)),
                  reads=[m1b["t_vst"]], writes=[])
        return nxt

    def mixer_in_phase(li, tile_list):
        m0 = amark()
        m1_alloc()
        S.dma("sp", lambda e: e.dma_start(out=m1b["wv"][:, :, 0:512], in_=WVS[li, 0].rearrange("p (k c) -> p k c", k=8)),
              writes=[m1b["t_wv"]])
        S.dma("sp", lambda e: e.dma_start(out=m1b["wv"][:, :, 512:1024], in_=WVS[li, 1].rearrange("p (k c) -> p k c", k=8)),
              writes=[m1b["t_wv"]], join=True)
        S.dma("sp", lambda e: e.dma_start(out=m1b["wkvv"][:].rearrange("p k c -> p (k c)"), in_=WUKVV[li]), writes=[m1b["t_wkvv"]])
        dwst, dw = m1b["dwst"], m1b["dw"]
        S.op("pool", lambda e: e.memset(dwst[:], 0.0), writes=[m1b["t_dw"]])
        S.op("pool", lambda e: e.memset(m1b["glrT"][32:33, :], 1.0), writes=[m1b["t_glrT"]])
        S.dma("sp", lambda e: e.dma_start(out=dwst[0:16, 0, :], in_=gdw_in[li, 0]), writes=[m1b["t_dw"]])
        S.dma("sp", lambda e: e.dma_start(out=dwst[16:32, 1, :], in_=gdw_in[li, 1]), writes=[m1b["t_dw"]])
        S.dma("sp", lambda e: e.dma_start(out=dwst[32:33, :, :], in_=gdb_in[li:li + 1]), writes=[m1b["t_dw"]])
        S.op("dve", lambda e: e.tensor_copy(out=dw[:], in_=dwst[:]), reads=[m1b["t_dw"]], writes=[m1b["t_dw"]])
        nxt = load_xt(*tile_list[0])
        for ti_, (t0, tw) in enumerate(tile_list):
            def prefetch(ti_=ti_):
                if ti_ + 1 < len(tile_list):
                    return load_xt(*tile_list[ti_ + 1])
                return None
            nxt = m1_tile(li, t0, tw, nxt, prefetch)
        arelease(m0)

    def gla_phase(li):
        m0 = amark()
        ld = []
        for i in range(3):
            b_ = dict(gp=sb(f"g_gp{i}", [128, 512], BF16), q=sb(f"g_q{i}", [128, 4, 128], BF16), k=sb(f"g_k{i}", [128, 4, 128], BF16),
                      v=sb(f"g_v{i}", [128, 1024], BF16), of=sb(f"g_of{i}", [128, 8, 128], F32), r=sb(f"g_r{i}", [128, 8, 128], BF16),
                      t_in=Tok(), t_of=Tok())
            ld.append(b_)
        wk = []
        for i in range(2):
            b_ = dict(eq=sb(f"g_eq{i}", [128, 4, 128], F32), ek=sb(f"g_ek{i}", [128, 4, 128], F32), qe=sb(f"g_qe{i}", [128, 4, 128], BF16),
                      ke=sb(f"g_ke{i}", [128, 4, 128], BF16), kd=sb(f"g_kd{i}", [128, 4, 128], BF16), At=sb(f"g_At{i}", [128, 4, 128], BF16),
                      kdt=sb(f"g_kdt{i}", [128, 4, 128], BF16), t_e=Tok(), t_qe=Tok(), t_ke=Tok(), t_kd=Tok(), t_At=Tok(), t_kdt=Tok())
            wk.append(b_)
        NR = 4
        Sr = [sb(f"g_S{i}", [128, 4, 256], F32) for i in range(NR)]
        t_Sr = [[Tok() for _ in range(4)] for _ in range(NR)]
        Sbr = [sb(f"g_Sb{i}", [128, 4, 256], BF16) for i in range(NR)]
        t_Sbr = [Tok() for _ in range(NR)]
        gst_ = {"n": 0}
        ost = [sb(f"g_ost{i}", [128, 8, 128], F32) for i in range(2)]
        t_ost = [Tok(), Tok()]
        sq = sb("g_sq", [128, 8, 128], BF16)
        t_sq = Tok()
        rst = sb("g_rst", [128, 4, 128], F32)
        t_rst = Tok()
        t1b = sb("g_t1", [128, 8, 128], F32)
        t_t1 = Tok()
        anb = [sb(f"g_an{i}", [128, 8, 128], BF16) for i in range(2)]
        t_an = [Tok(), Tok()]
        B_BC, B_A, B_T, B_ST, B_O0, B_O1 = 0, 0, 1, 1, 2, 3
        B_U = [(4, 5), (6, 7)]
        psT_bf = psum[B_T][:].bitcast(BF16)

        def loads(d_, t0, i):
            b_ = ld[i]
            S.dma("sp", lambda e: e.dma_start(out=b_["gp"][:], in_=GP[d_, t0:t0 + 128, :]), writes=[b_["t_in"]])
            S.dma("sp", lambda e: e.dma_start(out=b_["q"][:], in_=QT[:, t0:t0 + 128].rearrange("(h p) t -> p h t", p=128)),
                  writes=[b_["t_in"]], join=True)
            S.dma("sp", lambda e: e.dma_start(out=b_["k"][:], in_=KT[:, t0:t0 + 128].rearrange("(h p) t -> p h t", p=128)),
                  writes=[b_["t_in"]], join=True)
            S.dma("sp", lambda e: e.dma_start(out=b_["v"][:], in_=VG[t0:t0 + 128, :]), writes=[b_["t_in"]], join=True)
            if d_ == 1:
                S.dma("sp", lambda e: e.dma_start(out=b_["of"][:], in_=OF[:, t0:t0 + 128].rearrange("(k p) t -> p k t", p=128)),
                      writes=[b_["t_of"]])
                S.dma("sp", lambda e: e.dma_start(out=b_["r"][:], in_=RT[:, t0:t0 + 128].rearrange("(k p) t -> p k t", p=128)),
                      writes=[b_["t_of"]], join=True)

        def prologue(d_, i, wi_):
            b_, w_ = ld[i], wk[wi_]
            esel = 63 if d_ == 0 else 0

            def mm_bc(e):
                r = None
                for h in range(4):
                    r = e.matmul(psum[B_BC][:, h * 128:(h + 1) * 128], lhsT=b_["gp"][:, h * 128:(h + 1) * 128], rhs=gmask[:, d_, :],
                                 start=True, stop=True)
                return r
            S.op("pe", mm_bc, reads=[b_["t_in"], t_small], writes=[t_ps[B_BC]])
            bcv = psum[B_BC][:].rearrange("p (h t) -> p h t", h=4)
            S.op("act", lambda e: e.activation(out=w_["eq"][:], in_=bcv, func=AF.Exp), reads=[t_ps[B_BC]], writes=[w_["t_e"]])
            S.op("act", lambda e: e.activation(out=w_["ek"][:], in_=bcv, func=AF.Exp, scale=-1.0), reads=[t_ps[B_BC]], writes=[w_["t_e"]])
            S.op("dve", lambda e: e.tensor_tensor(out=w_["qe"][:], in0=b_["q"][:], in1=w_["eq"][:], op=ALU.mult),
                 reads=[b_["t_in"], w_["t_e"]], writes=[w_["t_qe"]])
            S.op("dve", lambda e: e.tensor_tensor(out=w_["ke"][:], in0=b_["k"][:], in1=w_["ek"][:], op=ALU.mult),
                 reads=[b_["t_in"], w_["t_e"]], writes=[w_["t_ke"]])
            S.op("pool", lambda e: e.tensor_tensor(
                out=w_["kd"][:].rearrange("p h (c j) -> p h c j", c=2), in0=w_["ke"][:].rearrange("p h (c j) -> p h c j", c=2),
                in1=w_["eq"][:, :, esel::64].unsqueeze(3).to_broadcast([128, 4, 2, 64]), op=ALU.mult),
                reads=[w_["t_ke"], w_["t_e"]], writes=[w_["t_kd"]])

            def mm_a(e):
                r = None
                for h in range(4):
                    r = e.matmul(psum[B_A][:, h * 128:(h + 1) * 128], lhsT=w_["ke"][:, h, :], rhs=w_["qe"][:, h, :], start=True, stop=True)
                return r
            S.op("pe", mm_a, reads=[w_["t_ke"], w_["t_qe"]], writes=[t_ps[B_A]])
            S.op("dve", lambda e: e.tensor_tensor(out=w_["At"][:], in0=psum[B_A][:].rearrange("p (h t) -> p h t", h=4),
                                                  in1=gmask[:, 2 + d_, :].unsqueeze(1).to_broadcast([128, 4, 128]), op=ALU.mult),
                 reads=[t_ps[B_A], t_small], writes=[w_["t_At"]])

            def mm_t(e):
                r = None
                for h in range(4):
                    r = e.transpose(psT_bf[:, h * 128:(h + 1) * 128], w_["kd"][:, h, :], ident_bf[:])
                return r
            S.op("pe", mm_t, reads=[w_["t_kd"], t_const], writes=[t_ps[B_T]])
            S.op("act", lambda e: e.copy(out=w_["kdt"][:], in_=psT_bf[:, 0:512].rearrange("p (h t) -> p h t", h=4)),
                 reads=[t_ps[B_T]], writes=[w_["t_kdt"]])

        def chain(d_, t0, i, oi):
            b_, w_ = ld[i], wk[oi]
            chunks = (0, 1) if d_ == 0 else (1, 0)
            esel = 63 if d_ == 0 else 0

            def oreg(h, c2):
                return psum[B_O0 + h // 2][:, ((h % 2) * 2 + c2) * 128:((h % 2) * 2 + c2 + 1) * 128]

            def mm_intra(e):
                r = None
                for h in range(4):
                    for c2 in range(2):
                        r = e.matmul(oreg(h, c2), lhsT=b_["v"][:, h * 256 + c2 * 128:h * 256 + (c2 + 1) * 128], rhs=w_["At"][:, h, :],
                                     start=(h % 2 == 0 and c2 == 0), stop=False, skip_group_check=True)
                return r
            n0 = gst_["n"]
            gst_["n"] += 2
            for ci, c in enumerate(chunks):
                nn = n0 + ci
                ub0, ub1 = B_U[nn % 2]

                def mm_upd(e, c=c, ub0=ub0, ub1=ub1):
                    r = None
                    for h in range(4):
                        r = e.matmul(psum[(ub0, ub1)[h // 2]][:, (h % 2) * 256:(h % 2 + 1) * 256], lhsT=w_["kdt"][c * 64:(c + 1) * 64, h, :],
                                     rhs=b_["v"][c * 64:(c + 1) * 64, h * 256:(h + 1) * 256], start=True, stop=True)
                    return r
                S.op("pe", mm_upd, reads=[w_["t_kdt"], b_["t_in"]], writes=[t_ps[ub0], t_ps[ub1]])
                cur, prv = nn % NR, (nn - 1) % NR
                for h in range(4):
                    S.op("dve", lambda e, h=h, c=c, cur=cur, prv=prv, ub0=ub0, ub1=ub1: e.scalar_tensor_tensor(
                        out=Sr[cur][:, h, :], in0=Sr[prv][:, h, :], scalar=w_["eq"][:, h, c * 64 + esel:c * 64 + esel + 1],
                        in1=psum[(ub0, ub1)[h // 2]][:, (h % 2) * 256:(h % 2 + 1) * 256], op0=ALU.mult, op1=ALU.add),
                        reads=[t_Sr[prv][h], w_["t_e"], t_ps[(ub0, ub1)[h // 2]]], writes=[t_Sr[cur][h]])
                S.op("act", lambda e, cur=cur: e.copy(out=Sbr[cur][:], in_=Sr[cur][:]), reads=t_Sr[cur], writes=[t_Sbr[cur]])
            S.op("pe", mm_intra, reads=[b_["t_in"], w_["t_At"]], writes=[t_ps[B_O0], t_ps[B_O1]])
            for ci, c in enumerate(chunks):
                prv = (n0 + ci - 1) % NR

                def mm_inter(e, c=c, ci=ci, prv=prv):
                    r = None
                    for h in range(4):
                        for c2 in range(2):
                            r = e.matmul(oreg(h, c2)[:, c * 64:(c + 1) * 64], lhsT=Sbr[prv][:, h, c2 * 128:(c2 + 1) * 128],
                                         rhs=w_["qe"][:, h, c * 64:(c + 1) * 64], start=False, stop=(ci == 1), skip_group_check=True)
                    return r
                S.op("pe", mm_inter, reads=[t_Sbr[prv], w_["t_qe"]], writes=[t_ps[B_O0], t_ps[B_O1]])
            o_t, t_o = ost[oi], t_ost[oi]
            if d_ == 0:
                S.op("act", lambda e: e.copy(out=o_t[:, 0:4, :], in_=psum[B_O0][:].rearrange("p (k t) -> p k t", k=4)),
                     reads=[t_ps[B_O0]], writes=[t_o])
                S.op("dve", lambda e: e.tensor_copy(out=o_t[:, 4:8, :], in_=psum[B_O1][:].rearrange("p (k t) -> p k t", k=4)),
                     reads=[t_ps[B_O1]], writes=[t_o])
                S.dma("sp", lambda e: e.dma_start(out=OF[:, t0:t0 + 128].rearrange("(k p) t -> p k t", p=128), in_=o_t[:]),
                      reads=[t_o], writes=[])
                return
            for half in range(2):
                S.op("dve", lambda e, half=half: e.tensor_tensor(
                    out=o_t[:, half * 4:half * 4 + 4, :], in0=psum[B_O0 + half][:].rearrange("p (k t) -> p k t", k=4),
                    in1=b_["of"][:, half * 4:half * 4 + 4, :], op=ALU.add), reads=[t_ps[B_O0 + half], b_["t_of"]], writes=[t_o])
            S.op("act", lambda e: e.activation(out=sq[:], in_=o_t[:], func=AF.Square), reads=[t_o], writes=[t_sq])

            def mm_st(e):
                r = None
                for h in range(4):
                    for c2 in range(2):
                        r = e.matmul(psum[B_ST][:, h * 128:(h + 1) * 128], lhsT=ones_bf[:], rhs=sq[:, h * 2 + c2, :],
                                     start=(h == 0 and c2 == 0), stop=(c2 == 1), skip_group_check=True)
                return r
            S.op("pe", mm_st, reads=[t_sq, t_const], writes=[t_ps[B_ST]])
            S.op("dve", lambda e: e.tensor_scalar(out=rst[:], in0=psum[B_ST][:].rearrange("p (h t) -> p h t", h=4), scalar1=1.0 / 256,
                                                  scalar2=EPS, op0=ALU.mult, op1=ALU.add), reads=[t_ps[B_ST]], writes=[t_rst])
            S.op("act", lambda e: e.activation(out=rst[:], in_=rst[:], func=AF.Sqrt), reads=[t_rst], writes=[t_rst])
            S.op("dve", lambda e: e.reciprocal(out=rst[:], in_=rst[:]), reads=[t_rst], writes=[t_rst])
            S.op("dve", lambda e: e.tensor_tensor(out=t1b[:].rearrange("p (h c) t -> p h c t", c=2),
                                                  in0=o_t[:].rearrange("p (h c) t -> p h c t", c=2),
                                                  in1=rst[:].unsqueeze(2).to_broadcast([128, 4, 2, 128]), op=ALU.mult),
                 reads=[t_o, t_rst], writes=[t_t1])
            a_t, t_a = anb[oi], t_an[oi]
            S.op("dve", lambda e: e.tensor_tensor(out=a_t[:], in0=t1b[:], in1=b_["r"][:], op=ALU.mult),
                 reads=[t_t1, b_["t_of"]], writes=[t_a])
            S.dma("sp", lambda e: e.dma_start(out=AN[:, t0:t0 + 128].rearrange("(k p) t -> p k t", p=128), in_=a_t[:]),
                  reads=[t_a], writes=[])

        for d_ in range(2):
            gst_["n"] = 0
            S.op("dve", lambda e: e.memset(Sr[NR - 1][:], 0.0), writes=t_Sr[NR - 1])
            S.op("pool", lambda e: e.memset(Sbr[NR - 1][:], 0.0), writes=[t_Sbr[NR - 1]])
            seq = [L, L + 128] + [i * 128 for i in range(L // 128)]
            if d_ == 1:
                seq = [L + 128, L] + [i * 128 for i in reversed(range(L // 128))]
            loads(d_, seq[0], 0)
            if len(seq) > 1:
                loads(d_, seq[1], 1)
            prologue(d_, 0, 0)
            for n_, t0 in enumerate(seq):
                if n_ + 2 < len(seq):
                    loads(d_, seq[n_ + 2], (n_ + 2) % 3)
                if n_ + 1 < len(seq):
                    prologue(d_, (n_ + 1) % 3, (n_ + 1) % 2)
                chain(d_, t0, n_ % 3, n_ % 2)
            if d_ == 0:
                S.full_barrier()
        arelease(m0)

    def attn_phase(li, qtiles):
        m0 = amark()
        kr2 = [sb(f"a_kr{i}", [128, NT], BF16) for i in range(2)]
        t_kr = Tok()
        kn = [sb(f"a_kn{i}", [128, NT], BF16) for i in range(2)]
        vh = [sb(f"a_vh{i}", [128, NT // 128, 128], BF16) for i in range(2)]
        t_kv = [Tok(), Tok()]
        qn = [sb(f"a_qn{i}", [128, T], BF16) for i in range(2)]
        qr = [sb(f"a_qr{i}", [128, T], BF16) for i in range(2)]
        t_q = [Tok(), Tok()]
        NP = 6
        pT = [sb(f"a_pT{i}", [128, T], BF16) for i in range(NP)]
        t_pT = [Tok() for _ in range(NP)]
        accd = [sb(f"a_accd{i}", [128, T], F32) for i in range(2)]
        accp = [sb(f"a_accp{i}", [128, T], F32) for i in range(2)]
        t_accd = [Tok(), Tok()]
        t_accp = [Tok(), Tok()]
        dhi = [sb(f"a_dhi{i}", [128, T], BF16) for i in range(2)]
        dlo = [sb(f"a_dlo{i}", [128, T], BF16) for i in range(2)]
        t_dh = [Tok(), Tok()]
        rden = sb("a_rden", [128, T], F32)
        t_rden = Tok()
        cst = [sb(f"a_cst{i}", [128, T], BF16) for i in range(2)]
        t_cst = [Tok(), Tok()]
        SCALE = 192.0 ** -0.5
        st = {"q": 0, "p": 0, "s": 0, "o": 0}

        S.op("pool", lambda e: e.memset(kr2[0][64:128, :], 0.0), writes=[t_kr])
        S.op("pool", lambda e: e.memset(kr2[1][0:64, :], 0.0), writes=[t_kr])
        S.dma("sp", lambda e: e.dma_start(out=kr2[0][0:64, :], in_=KRT[0:64, :]), writes=[t_kr])
        S.dma("sp", lambda e: e.dma_start(out=kr2[1][64:128, :], in_=KRT[64:128, :]), writes=[t_kr])

        def load_head(h, i):
            S.dma("sp", lambda e: e.dma_start(out=kn[i][:], in_=KNT[h * 128:(h + 1) * 128, :]), writes=[t_kv[i]])
            S.dma("sp", lambda e: e.dma_start(out=vh[i][:], in_=VM[h]), writes=[t_kv[i]], join=True)

        def load_q(h, t0, tw):
            i = st["q"] % 2
            st["q"] += 1
            S.dma("sp", lambda e: e.dma_start(out=qn[i][:, 0:tw], in_=QNT[h * 128:(h + 1) * 128, t0:t0 + tw]), writes=[t_q[i]])
            S.dma("sp", lambda e: e.dma_start(out=qr[i][:, 0:tw], in_=QRT[(h // 2) * 128:(h // 2 + 1) * 128, t0:t0 + tw]), writes=[t_q[i]], join=True)
            return i

        def attend(h, hi, t0, tw, qi):
            hp = h % 2
            chunks = list(range(NT // 128)) if t0 < L else list(range(L // 128, NT // 128))
            oset = st["o"] % 2
            st["o"] += 1
            B_O, B_D = 4 + oset, 6 + oset
            nch = len(chunks)
            sbank = {}
            a_d, a_p = accd[oset], accp[oset]
            t_ad, t_ap = t_accd[oset], t_accp[oset]
            used = {"dve": False, "pool": False, "pe": False}

            def qk(ci):
                c = chunks[ci]
                pb = st["s"] % 4
                st["s"] += 1
                sbank[ci] = pb

                def mm(e):
                    e.matmul(psum[pb][:, 0:tw], lhsT=kn[hi][:, c * 128:(c + 1) * 128], rhs=qn[qi][:, 0:tw], start=True, stop=False)
                    return e.matmul(psum[pb][:, 0:tw], lhsT=kr2[hp][:, c * 128:(c + 1) * 128], rhs=qr[qi][:, 0:tw], start=False, stop=True)
                S.op("pe", mm, reads=[t_kv[hi], t_kr, t_q[qi]], writes=[t_ps[pb]])

            def pv(ci):
                c = chunks[ci]
                pb = sbank[ci]
                pi = st["p"] % NP
                st["p"] += 1
                S.op("act", lambda e: e.activation(out=pT[pi][:, 0:tw], in_=psum[pb][:, 0:tw], func=AF.Exp, scale=SCALE),
                     reads=[t_ps[pb]], writes=[t_pT[pi]])
                S.op("pe", lambda e: e.matmul(psum[B_O][:, 0:tw], lhsT=vh[hi][:, c, :], rhs=pT[pi][:, 0:tw], start=(ci == 0), stop=(ci == nch - 1)),
                     reads=[t_kv[hi], t_pT[pi]], writes=[t_ps[B_O]])
                if ci % 4 == 3:
                    first = not used["pe"]
                    used["pe"] = True
                    S.op("pe", lambda e: e.matmul(psum[B_D][:, 0:tw], lhsT=ones_bf[:], rhs=pT[pi][:, 0:tw], start=first, stop=False),
                         reads=[t_pT[pi], t_const], writes=[t_ps[B_D]])
                    return
                eng = "dve"
                a_, t_a_ = a_d, t_ad
                if not used[eng]:
                    used[eng] = True
                    S.op(eng, lambda e: e.tensor_copy(out=a_[:, 0:tw], in_=pT[pi][:, 0:tw]), reads=[t_pT[pi]], writes=[t_a_])
                else:
                    S.op(eng, lambda e: e.tensor_tensor(out=a_[:, 0:tw], in0=a_[:, 0:tw], in1=pT[pi][:, 0:tw], op=ALU.add),
                         reads=[t_pT[pi], t_a_], writes=[t_a_])
            LOOK = 2
            for ci in range(min(LOOK, nch)):
                qk(ci)
            for ci in range(nch):
                if ci + LOOK < nch:
                    qk(ci + LOOK)
                pv(ci)
            if used["pool"]:
                S.op("dve", lambda e: e.tensor_tensor(out=a_d[:, 0:tw], in0=a_d[:, 0:tw], in1=a_p[:, 0:tw], op=ALU.add),
                     reads=[t_ad, t_ap], writes=[t_ad])
            S.op("dve", lambda e: e.tensor_copy(out=dhi[oset][:, 0:tw], in_=a_d[:, 0:tw]), reads=[t_ad], writes=[t_dh[oset]])
            S.op("dve", lambda e: e.tensor_tensor(out=dlo[oset][:, 0:tw], in0=a_d[:, 0:tw], in1=dhi[oset][:, 0:tw], op=ALU.subtract),
                 reads=[t_ad, t_dh[oset]], writes=[t_dh[oset]])

            def mmd(e):
                e.matmul(psum[B_D][:, 0:tw], lhsT=ones_bf[:], rhs=dhi[oset][:, 0:tw], start=(not used["pe"]), stop=False)
                return e.matmul(psum[B_D][:, 0:tw], lhsT=ones_bf[:], rhs=dlo[oset][:, 0:tw], start=False, stop=True)
            S.op("pe", mmd, reads=[t_dh[oset], t_const], writes=[t_ps[B_D]])
            S.op("dve", lambda e: e.reciprocal(out=rden[:, 0:tw], in_=psum[B_D][:, 0:tw]), reads=[t_ps[B_D]], writes=[t_rden])
            ci_ = oset
            S.op("dve", lambda e: e.tensor_tensor(out=cst[ci_][:, 0:tw], in0=psum[B_O][:, 0:tw], in1=rden[:, 0:tw], op=ALU.mult),
                 reads=[t_ps[B_O], t_rden], writes=[t_cst[ci_]])
            S.dma("sp", lambda e: e.dma_start(out=CT[h * 128:(h + 1) * 128, t0:t0 + tw], in_=cst[ci_][:, 0:tw]),
                  reads=[t_cst[ci_]], writes=[])

        load_head(0, 0)
        work = [(h, t0, tw) for h in range(8) for (t0, tw) in qtiles]
        nq = load_q(*work[0])
        for wi_, (h, t0, tw) in enumerate(work):
            hi = h % 2
            if t0 == qtiles[0][0] and h + 1 < 8:
                load_head(h + 1, 1 - hi)
            qi = nq
            if wi_ + 1 < len(work):
                nq = load_q(*work[wi_ + 1])
            attend(h, hi, t0, tw, qi)
        arelease(m0)

    def merge_phase(li, tile_list):
        m0 = amark()
        anb = [sb(f"m_an{i}", [128, 8, T], BF16) for i in range(2)]
        ctb = [sb(f"m_ct{i}", [128, 8, T], BF16) for i in range(2)]
        utb = [sb(f"m_ut{i}", [128, 8, T + 2], BF16) for i in range(2)]
        t_in = [Tok(), Tok()]
        bnb = sb("m_bn", [128, 8, T], BF16)
        t_bn = Tok()
        mbb = sb("m_mb", [128, 8, T], BF16)
        t_mb = Tok()
        gts = [sb(f"m_g{i}", [128, T], F32) for i in range(3)]
        t_g = [Tok() for _ in range(3)]
        acc = [sb(f"m_acc{i}", [128, T], F32) for i in range(2)]
        t_acc = [Tok(), Tok()]
        tt_ = [sb(f"m_t{i}", [128, T], F32) for i in range(2)]
        t_tt = [Tok(), Tok()]
        cvt = [sb(f"m_cv{i}", [128, T], F32) for i in range(2)]
        t_cv = [Tok(), Tok()]
        st = {"in": 0, "g": 0, "acc": 0, "t": 0, "cv": 0}

        def load_in(t0, tw):
            i = st["in"] % 2
            st["in"] += 1
            S.dma("sp", lambda e: e.dma_start(out=anb[i][:, :, 0:tw], in_=AN[:, t0:t0 + tw].rearrange("(k p) t -> p k t", p=128)),
                  writes=[t_in[i]])
            S.dma("sp", lambda e: e.dma_start(out=ctb[i][:, :, 0:tw], in_=CT[:, t0:t0 + tw].rearrange("(k p) t -> p k t", p=128)),
                  writes=[t_in[i]], join=True)
            first = (t0 == 0 or t0 == L)
            lastt = (t0 + tw == L or t0 + tw == NT)
            lo = t0 if first else t0 - 1
            hi_ = t0 + tw if lastt else t0 + tw + 1
            o0 = 1 if first else 0
            S.dma("sp", lambda e: e.dma_start(out=utb[i][:, :, o0:o0 + hi_ - lo], in_=UT[:, lo:hi_].rearrange("(k p) t -> p k t", p=128)),
                  writes=[t_in[i]], join=True)
            if first:
                S.op("pool", lambda e: e.memset(utb[i][:, :, 0:1], 0.0), writes=[t_in[i]])
            if lastt:
                S.op("pool", lambda e: e.memset(utb[i][:, :, tw + 1:tw + 2], 0.0), writes=[t_in[i]])
            return i

        def m4_tile(t0, tw, xi, ii, prefetch):
            which = 0 if t0 < L else 1
            an_, ct_, ut_ = anb[ii], ctb[ii], utb[ii]
            for kc in range(8):
                S.op("act", lambda e, kc=kc: e.activation(out=hb[:, kc, 0:tw], in_=xt[xi][:, kc, 0:tw], func=AF.Identity,
                                                          scale=mods[:, li, 1, 0, kc, which:which + 1],
                                                          bias=mods[:, li, 1, 1, kc, which:which + 1]),
                     reads=[t_xt[xi], t_mod], writes=[t_hb])
            for c_ in range(8):
                vi = st["cv"] % 2
                st["cv"] += 1
                cv_, t_cv_ = cvt[vi], t_cv[vi]
                S.op("pool", lambda e, c_=c_, cv_=cv_: e.tensor_scalar_mul(out=cv_[:, 0:tw], in0=ut_[:, c_, 1:tw + 1],
                                                                           scalar1=convw_sb[:, li, 1, c_:c_ + 1]),
                     reads=[t_in[ii], t_small], writes=[t_cv_])
                S.op("dve", lambda e, c_=c_, cv_=cv_: e.scalar_tensor_tensor(out=cv_[:, 0:tw], in0=ut_[:, c_, 0:tw],
                                                                              scalar=convw_sb[:, li, 0, c_:c_ + 1], in1=cv_[:, 0:tw],
                                                                              op0=ALU.mult, op1=ALU.add),
                     reads=[t_in[ii], t_small, t_cv_], writes=[t_cv_])
                S.op("dve", lambda e, c_=c_, cv_=cv_: e.scalar_tensor_tensor(out=cv_[:, 0:tw], in0=ut_[:, c_, 2:tw + 2],
                                                                              scalar=convw_sb[:, li, 2, c_:c_ + 1], in1=cv_[:, 0:tw],
                                                                              op0=ALU.mult, op1=ALU.add),
                     reads=[t_in[ii], t_small, t_cv_], writes=[t_cv_])

                def ev6(pb, c_=c_, cv_=cv_, t_cv_=t_cv_):
                    S.op("dve", lambda e: e.tensor_tensor(out=bnb[:, c_, 0:tw], in0=psum[pb][:, 0:tw], in1=cv_[:, 0:tw], op=ALU.mult),
                         reads=[t_ps[pb], t_cv_], writes=[t_bn])
                lin(WIS[li, 38 + c_], 8, 128, hb, [t_hb], tw, ev6)
            srcs = [(an_, [t_in[ii]]), (bnb, [t_bn]), (ct_, [t_in[ii]])]
            def do_m(m):
                ai = st["acc"] % 2
                st["acc"] += 1
                acc_, t_acc_ = acc[ai], t_acc[ai]
                for br in range(3):
                    gi = st["g"] % 3
                    st["g"] += 1
                    g_, t_g_ = gts[gi], t_g[gi]

                    def evg(pb, g_=g_, t_g_=t_g_, br=br):
                        S.op("act", lambda e: e.activation(out=g_[:, 0:tw], in_=psum[pb][:, 0:tw], func=AF.Sigmoid,
                                                           bias=bgate_sb[:, li, br * 8 + m:br * 8 + m + 1]),
                             reads=[t_ps[pb], t_small], writes=[t_g_])
                    lin(WIS[li, 46 + br * 8 + m], 8, 128, hb, [t_hb], tw, evg)

                    def evp(pb, g_=g_, t_g_=t_g_, br=br):
                        if br == 0:
                            S.op("dve", lambda e: e.tensor_tensor(out=acc_[:, 0:tw], in0=psum[pb][:, 0:tw], in1=g_[:, 0:tw], op=ALU.mult),
                                 reads=[t_ps[pb], t_g_], writes=[t_acc_])
                            return
                        ti = st["t"] % 2
                        st["t"] += 1
                        S.op("dve", lambda e: e.tensor_tensor(out=tt_[ti][:, 0:tw], in0=psum[pb][:, 0:tw], in1=g_[:, 0:tw], op=ALU.mult),
                             reads=[t_ps[pb], t_g_], writes=[t_tt[ti]])
                        if br == 1:
                            S.op("pool", lambda e: e.tensor_tensor(out=acc_[:, 0:tw], in0=acc_[:, 0:tw], in1=tt_[ti][:, 0:tw], op=ALU.add),
                                 reads=[t_tt[ti], t_acc_], writes=[t_acc_])
                        else:
                            S.op("pool", lambda e: e.tensor_tensor(out=mbb[:, m, 0:tw], in0=acc_[:, 0:tw], in1=tt_[ti][:, 0:tw], op=ALU.add),
                                 reads=[t_tt[ti], t_acc_], writes=[t_mb])
                    src, stoks = srcs[br]
                    lin(PRJS[li, br, m], 8, 128, src, stoks, tw, evp)
            for m in range(8):
                do_m(m)
            nxt = prefetch()
            for m in range(8):
                def evo(pb, m=m):
                    S.op("dve", lambda e: e.scalar_tensor_tensor(
                        out=xt[xi][:, m, 0:tw], in0=psum[pb][:, 0:tw], scalar=mods[:, li, 1, 2, m, which:which + 1],
                        in1=xt[xi][:, m, 0:tw], op0=ALU.mult, op1=ALU.add),
                        reads=[t_ps[pb], t_xt[xi], t_mod, t_hb], writes=[t_xt[xi]])
                lin(PRJS[li, 3, m], 8, 128, mbb, [t_mb], tw, evo)
            layer_norm_store(li, 1, xi, t0, tw, which)
            return nxt

        nx = load_xt(*tile_list[0])
        ni = load_in(*tile_list[0])
        for ti_, (t0, tw) in enumerate(tile_list):
            def prefetch(ti_=ti_):
                if ti_ + 1 < len(tile_list):
                    return load_xt(*tile_list[ti_ + 1]), load_in(*tile_list[ti_ + 1])
                return None, None
            nx, ni = m4_tile(t0, tw, nx, ni, prefetch)
        arelease(m0)

    if debug is not None and "mod" in debug:
        d_ = nc.dram_tensor("dbg_mod", [128, NL * 72 * 2], F32, kind="ExternalOutput").ap()
        S.dma("sp", lambda e: e.dma_start(out=d_, in_=modT[:].rearrange("p l i w -> p (l i w)")), reads=[t_mod])

    snapshot("x0")
    for li, l in enumerate(layers):
        last = (l == DEPTH - 1)
        ffn_phase(li, 0, tiles)
        snapshot("ffn1")
        if not n_ffn_only:
            mixer_in_phase(li, tiles)
            gla_phase(li)
            dsnap("OF", OF, F32)
            dsnap("AN", AN)
            attn_phase(li, tiles[:-1] if last else tiles)
            dsnap("CT", CT)
            merge_phase(li, tiles[:-1] if last else tiles)
            snapshot("mix")
            for nm_, ap_ in [("QT", QT), ("KT", KT), ("RT", RT), ("UT", UT), ("GP", GP), ("VG", VG), ("KRT", KRT),
                             ("QNT", QNT), ("QRT", QRT), ("KNT", KNT), ("VM", VM)]:
                dsnap(nm_, ap_)
        ffn_phase(li, 1, tiles[:-1] if last else tiles)

    t_y = Tok()
    for tt in range((NT if emit_ctx else L) // 128):
        i = tt % 2
        S.dma("sp", lambda e, i=i, tt=tt: e.dma_start(
            out=xT_sb[i][:], in_=XT[:, tt * 128:(tt + 1) * 128].rearrange("(k p) t -> p k t", p=128)),
            reads=[t_XT[tt // 4]], writes=[t_xT[i]])
        for half in range(2):
            pb = 2 * (tt % 2) + half

            def tr(e, i=i, half=half, pb=pb):
                r = None
                for q in range(4):
                    kc = half * 4 + q
                    r = e.transpose(psum[pb][:, q * 128:(q + 1) * 128], xT_sb[i][:, kc, :], ident[:])
                return r
            S.op("pe", tr, reads=[t_xT[i], t_const], writes=[t_ps[pb]])
            ce = "dve" if half == 0 else "act"

            def cp(e, i=i, half=half, pb=pb, ce=ce):
                o = xin_sb[i][:, half * 512:(half + 1) * 512]
                return e.tensor_copy(out=o, in_=psum[pb][:]) if ce == "dve" else e.copy(out=o, in_=psum[pb][:])
            S.op(ce, cp, reads=[t_ps[pb]], writes=[t_xin[i]])
        dst_ = y_out[tt * 128:(tt + 1) * 128, :] if tt < L // 128 else yc_out[(tt - L // 128) * 128:(tt - L // 128 + 1) * 128, :]
        S.dma("sp", lambda e, i=i, dst_=dst_: e.dma_start(out=dst_, in_=xin_sb[i][:]),
              reads=[t_xin[i]], writes=[t_y])
    S.dma_barrier("sp")

    S.emit(es)
    es.close()
    return nc


def make_consts():
    t = np.arange(L)
    rows, cols = t // 64, t % 64
    inv = 10000.0 ** (-np.arange(16, dtype=np.float32) / 16.0)
    cs = np.zeros((128, 2, L), np.float32)
    for p in range(128):
        d_ = p % 64
        pos = rows if d_ < 32 else cols
        ang = pos.astype(np.float32) * inv[d_ % 16]
        cs[p, 0] = np.cos(ang)
        cs[p, 1] = np.sin(ang)
    j = np.arange(128)[:, None]
    i = np.arange(128)[None, :]
    same = (j // 64) == (i // 64)
    gm = np.zeros((128, 4, 128), np.float32)
    gm[:, 0] = np.where(same & (j <= i), -1.0 / 16.0, 0.0)
    gm[:, 1] = np.where(same & (j >= i), -1.0 / 16.0, 0.0)
    gm[:, 2] = np.where(same & (j <= i), 1.0, 0.0)
    gm[:, 3] = np.where(same & (j >= i), 1.0, 0.0)
    return {"ident_in": np.eye(128, dtype=np.float32), "rope_cs": cs, "gmask": gm}


PER_LAYER = ["ada_w", "ada_b", "ln_w", "ln_b", "ffn_w13", "ffn_w2", "w_in", "b_gate", "gla_decay_w", "gla_decay_b",
             "gla_norm", "gla_proj", "conv_w", "conv_proj", "mla_q_norm", "mla_w_uq", "mla_kv_norm", "mla_w_ukv",
             "mla_proj", "w_out"]


def make_in_maps(inputs, layers, n_cores, xs=None, ctxs=None):
    consts = make_consts()
    shared = {k: np.ascontiguousarray(np.asarray(inputs[k])[layers]) for k in PER_LAYER}
    shared["c_ctx"] = np.ascontiguousarray(np.asarray(inputs["c_ctx"]))
    shared.update(consts)
    maps = []
    for b in range(n_cores):
        m = dict(shared)
        m["x"] = np.ascontiguousarray(np.asarray(inputs["x"][b]) if xs is None else xs[b])
        m["ctx"] = np.ascontiguousarray(np.asarray(inputs["ctx"][b]) if ctxs is None else ctxs[b])
        m["c"] = np.ascontiguousarray(np.asarray(inputs["c"][b]))
        maps.append(m)
    return maps


FUSED = True
_PROG = {}


def _prog(layers):
    key = tuple(layers)
    if key not in _PROG:
        _PROG[key] = build_program(list(layers))
    return _PROG[key]


def kernel(**inputs):
    inputs = {k: np.asarray(v) for k, v in inputs.items()}
    n = 8
    if FUSED:
        nc = _prog(range(DEPTH))
        maps = make_in_maps(inputs, list(range(DEPTH)), n)
        res = run_bass_kernel_spmd(nc, maps, core_ids=list(range(n)))
        return np.stack([np.asarray(r["y"]) for r in res.results], axis=0).astype(np.float32)
    xs = [inputs["x"][b] for b in range(n)]
    cs = [inputs["ctx"][b] for b in range(n)]
    for l in range(DEPTH):
        nc = _prog([l]) if l == DEPTH - 1 else _prog([0])
        maps = make_in_maps(inputs, [l], n, xs=xs, ctxs=cs)
        res = run_bass_kernel_spmd(nc, maps, core_ids=list(range(n)))
        xs = [np.asarray(r["y"]) for r in res.results]
        if l != DEPTH - 1:
            cs = [np.asarray(r["yc"]) for r in res.results]
    return np.stack(xs, axis=0).astype(np.float32)
```

```python
import contextlib
import numpy as np
import concourse.bass as bass
import concourse.mybir as mybir
from concourse.bass_utils import run_bass_kernel_spmd

F32 = mybir.dt.float32
BF16 = mybir.dt.bfloat16
AF = mybir.ActivationFunctionType
ALU = mybir.AluOpType
AX = mybir.AxisListType

D = 1024
L = 4096
CTX = 256
NT = L + CTX
DEPTH = 4
DFF = 2816
NJ = DFF // 128
ALPHA = (2.0 * DEPTH) ** 0.25
EPS = 1e-6
INW = 10080

SAME_ENGINE_SYNC = False
EPOCH = 8192
N_DMA_SEMS = 8


class Tok:
    __slots__ = ("ws", "r", "name")

    def __init__(self, name=""):
        self.ws = []
        self.r = []
        self.name = name


class Op:
    __slots__ = ("eng", "idx", "fn", "waits", "sig", "cnt", "is_dma", "dsem", "dcnt", "prev_dma")

    def __init__(self, eng, idx, fn, is_dma):
        self.eng = eng
        self.idx = idx
        self.fn = fn
        self.waits = []
        self.sig = False
        self.cnt = 0
        self.is_dma = is_dma
        self.dsem = None
        self.dcnt = 0
        self.prev_dma = None


class Sched:
    ENGS = ("pe", "act", "dve", "pool", "sp")

    def __init__(self, nc):
        self.nc = nc
        self.ops = {e: [] for e in self.ENGS}
        self.seen = {e: {f: -1 for f in self.ENGS} for e in self.ENGS}
        self.seen_dma = {e: {} for e in self.ENGS}
        self.dma_n = {e: 0 for e in self.ENGS}
        self.dma_last = {}

    def _add(self, eng, fn, reads, writes, is_dma, join=False):
        lst = self.ops[eng]
        op = Op(eng, len(lst), fn, is_dma)
        deps = []
        for t in reads:
            deps.extend(t.ws)
        for t in writes:
            if not join:
                deps.extend(t.ws)
            deps.extend(t.r)
        for p in deps:
            self._dep(op, p)
        for t in writes:
            if join:
                t.ws.append(op)
            else:
                t.ws = [op]
            t.r = []
        for t in reads:
            t.r.append(op)
        if is_dma:
            k = self.dma_n[eng]
            self.dma_n[eng] += 1
            slot = k % N_DMA_SEMS
            op.dsem = (eng, slot)
            op.dcnt = 16 * (k // N_DMA_SEMS + 1)
            op.prev_dma = self.dma_last.get((eng, slot))
            self.dma_last[(eng, slot)] = op
            if op.prev_dma is not None:
                self._dep(op, op.prev_dma)
        lst.append(op)
        return op

    def _dep(self, op, p):
        e = op.eng
        if p.is_dma:
            sd = self.seen_dma[e]
            if sd.get(p.dsem, 0) >= p.dcnt:
                return
            sd[p.dsem] = p.dcnt
            op.waits.append(p)
        else:
            if p.eng == e and not op.is_dma and (e == "pe" or not SAME_ENGINE_SYNC):
                return
            if self.seen[e][p.eng] >= p.idx:
                return
            self.seen[e][p.eng] = p.idx
            p.sig = True
            op.waits.append(p)

    def op(self, eng, fn, reads=(), writes=()):
        return self._add(eng, fn, reads, writes, False)

    def dma_barrier(self, eng, queues=("sp", "act", "pool")):
        op = Op(eng, len(self.ops[eng]), lambda e: e.nop(), False)
        for (q, slot), p in self.dma_last.items():
            if q in queues:
                self._dep(op, p)
        self.ops[eng].append(op)
        return op

    def dma(self, eng, fn, reads=(), writes=(), join=False):
        return self._add(eng, fn, reads, writes, True, join)

    def full_barrier(self):
        lasts = {e: (self.ops[e][-1] if self.ops[e] else None) for e in self.ENGS}
        dl = list(self.dma_last.values())
        for e in self.ENGS:
            op = Op(e, len(self.ops[e]), lambda en: en.nop(), False)
            for f in self.ENGS:
                p = lasts[f]
                if p is None or f == e:
                    continue
                if p.is_dma:
                    q = None
                    for cand in reversed(self.ops[f]):
                        if not cand.is_dma:
                            q = cand
                            break
                    p = q
                if p is not None:
                    self._dep(op, p)
            for p in dl:
                self._dep(op, p)
            self.ops[e].append(op)

    def emit(self, es):
        nc = self.nc
        nsig = {}
        for e in self.ENGS:
            c = 0
            for op in self.ops[e]:
                if (not op.is_dma) and op.sig:
                    c += 1
                    op.cnt = c
            nsig[e] = c
        csems = {}
        for e in self.ENGS:
            n_ep = (nsig[e] + EPOCH - 1) // EPOCH
            csems[e] = [es.enter_context(nc.semaphore(f"c_{e}_{i}")) for i in range(n_ep)]
        dsems = {}
        for e in self.ENGS:
            if self.dma_n[e]:
                for s in range(min(N_DMA_SEMS, self.dma_n[e])):
                    dsems[(e, s)] = es.enter_context(nc.semaphore(f"d_{e}_{s}"))
        fin = es.enter_context(nc.semaphore("fin"))
        block = es.enter_context(nc.Block())
        engs = {"pe": (block.tensor, nc.tensor), "act": (block.scalar, nc.scalar), "dve": (block.vector, nc.vector),
                "pool": (block.gpsimd, nc.gpsimd), "sp": (block.sync, nc.sync)}

        def run(e, eng):
            for op in self.ops[e]:
                for p in op.waits:
                    if p.is_dma:
                        eng.wait_ge(dsems[p.dsem], p.dcnt)
                    else:
                        ep = (p.cnt - 1) // EPOCH
                        eng.wait_ge(csems[p.eng][ep], (p.cnt - 1) % EPOCH + 1)
                inst = op.fn(eng)
                if op.is_dma:
                    inst.then_inc(dsems[op.dsem], 16)
                elif op.sig:
                    ep = (op.cnt - 1) // EPOCH
                    inst.then_inc(csems[e][ep], 1)

        for e in self.ENGS:
            if not self.ops[e]:
                continue
            deco, _ = engs[e]

            def f(eng, e=e):
                run(e, eng)

            deco(f)


class Ctx:
    pass


def build_program(layers, n_ffn_only=False, debug=None):
    nc = bass.Bass("TRN2", target_bir_lowering=False)
    es = contextlib.ExitStack()
    S = Sched(nc)
    NL = len(layers)

    def din(name, shape, dt=F32):
        return nc.dram_tensor(name, list(shape), dt, kind="ExternalInput").ap()

    def dint(name, shape, dt):
        return nc.dram_tensor(name, list(shape), dt, kind="Internal").ap()

    ARENA_WORDS = 52992
    arena = es.enter_context(nc.sbuf_tensor("arena", [128, ARENA_WORDS], F32))
    astate = {"off": 0, "peak": 0}

    def sb(name, shape, dt):
        n = 1
        for d_ in shape[1:]:
            n *= d_
        isz = 4 if dt == F32 else 2
        words = (n * isz + 63) // 64 * 16
        off = astate["off"]
        assert off + words <= ARENA_WORDS, f"arena overflow at {name}: {off}+{words}"
        astate["off"] = off + words
        astate["peak"] = max(astate["peak"], astate["off"])
        v = arena[0:shape[0], off:off + (n * isz + 3) // 4]
        if dt != F32:
            v = v.bitcast(dt)
            v = v[:, 0:n]
        if len(shape) == 3:
            v = v.rearrange("p (a b) -> p a b", a=shape[1])
        elif len(shape) == 4:
            v = v.rearrange("p (a b c) -> p a b c", a=shape[1], b=shape[2])
        elif len(shape) == 5:
            v = v.rearrange("p (a b c d) -> p a b c d", a=shape[1], b=shape[2], c=shape[3])
        elif len(shape) == 6:
            v = v.rearrange("p (a b c d e) -> p a b c d e", a=shape[1], b=shape[2], c=shape[3], d=shape[4])
        return v

    def amark():
        return astate["off"]

    def arelease(m):
        S.full_barrier()
        astate["off"] = m

    x_in = din("x", [L, D])
    ctx_in = din("ctx", [CTX, D])
    c_in = din("c", [D])
    cctx_in = din("c_ctx", [D])
    ada_w = din("ada_w", [NL, D, 9 * D])
    ada_b = din("ada_b", [NL, 9 * D])
    ln_w = din("ln_w", [NL, 3, D])
    ln_b = din("ln_b", [NL, 3, D])
    ffn_w13 = din("ffn_w13", [NL, 2, D, 2 * DFF])
    ffn_w2 = din("ffn_w2", [NL, 2, DFF, D])
    ident_in = din("ident_in", [128, 128])
    w_in = din("w_in", [NL, D, INW])
    b_gate = din("b_gate", [NL, 3 * D])
    gdw_in = din("gla_decay_w", [NL, 2, 16, 512])
    gdb_in = din("gla_decay_b", [NL, 2, 512])
    gnorm_in = din("gla_norm", [NL, D])
    gla_proj = din("gla_proj", [NL, D, D])
    conv_w_in = din("conv_w", [NL, 3, D])
    conv_proj = din("conv_proj", [NL, D, D])
    qnorm_in = din("mla_q_norm", [NL, 512])
    w_uq = din("mla_w_uq", [NL, 512, 1536])
    kvnorm_in = din("mla_kv_norm", [NL, 256])
    w_ukv = din("mla_w_ukv", [NL, 256, 2048])
    mla_proj = din("mla_proj", [NL, D, D])
    w_out = din("w_out", [NL, D, D])
    rope_cs = din("rope_cs", [128, 2, L])
    gmask_in = din("gmask", [128, 4, 128])
    y_out = nc.dram_tensor("y", [L, D], F32, kind="ExternalOutput").ap()
    emit_ctx = (layers[-1] != DEPTH - 1)
    if emit_ctx:
        yc_out = nc.dram_tensor("yc", [CTX, D], F32, kind="ExternalOutput").ap()

    XT = dint("XT", [D, NT], F32)
    W13S = dint("W13S", [NL, 2, 2, NJ, 128, 8 * 128], BF16)
    W2S = dint("W2S", [NL, 2, 8, 128, NJ * 128], BF16)

    NBLK = 73
    WIS = dint("WIS", [NL, NBLK, 128, 8 * 128], BF16)
    WVS = dint("WVS", [NL, 2, 128, 8 * 512], BF16)
    PRJS = dint("PRJS", [NL, 4, 8, 128, 8 * 128], BF16)
    WUQS = dint("WUQS", [NL, 16, 128, 4 * 128], BF16)
    WUKS = dint("WUKS", [NL, 8, 128, 2 * 128], BF16)
    WUKVV = dint("WUKVV", [NL, 128, 2 * 1024], BF16)
    QT = dint("QT", [512, NT], BF16)
    KT = dint("KT", [512, NT], BF16)
    RT = dint("RT", [1024, NT], BF16)
    GP = dint("GP", [2, NT, 512], BF16)
    UT = dint("UT", [1024, NT], BF16)
    KRT = dint("KRT", [128, NT], BF16)
    VG = dint("VG", [NT, 1024], BF16)
    QNT = dint("QNT", [1024, NT], BF16)
    QRT = dint("QRT", [512, NT], BF16)
    KNT = dint("KNT", [1024, NT], BF16)
    VM = dint("VM", [8, 128, NT // 128, 128], BF16)
    OF = dint("OF", [1024, NT], F32)
    AN = dint("AN", [1024, NT], BF16)
    CT = dint("CT", [1024, NT], BF16)

    ident = sb("ident", [128, 128], F32)
    ones_bf = sb("ones_bf", [128, 128], BF16)
    t_const = Tok()
    S.dma("sp", lambda e: e.dma_start(out=ident[:], in_=ident_in), writes=[t_const])
    S.op("pool", lambda e: e.memset(ones_bf[:], 1.0), writes=[t_const])

    psum = [es.enter_context(nc.psum_tensor(f"ps{i}", [128, 512], F32)) for i in range(8)]
    t_ps = [Tok(f"ps{i}") for i in range(8)]

    modT = sb("modT", [128, NL, 72, 2], F32)
    mods = sb("mods", [128, NL, 3, 3, 8, 2], F32)
    lnw_sb = sb("lnw_sb", [128, NL, 3, 8], F32)
    lnb_sb = sb("lnb_sb", [128, NL, 3, 8], F32)
    adab_sb = sb("adab_sb", [128, NL, 72], F32)
    sc_sb = sb("sc_sb", [128, 8, 2], F32)
    sc_tmp = sb("sc_tmp", [128, 8, 2], F32)
    t_small = Tok()
    t_mod = Tok()
    bgate_sb = sb("bgate_sb", [128, NL, 24], F32)
    gnorm_sb = sb("gnorm_sb", [128, NL, 8], F32)
    convw_sb = sb("convw_sb", [128, NL, 3, 8], F32)
    qnorm_sb = sb("qnorm_sb", [128, NL, 4], F32)
    kvnorm_sb = sb("kvnorm_sb", [128, NL, 2], F32)
    gmask_f = sb("gmask_f", [128, 4, 128], F32)
    gmask = sb("gmask", [128, 4, 128], BF16)
    ident_bf = sb("ident_bf", [128, 128], BF16)

    def ld1(out_ap, in_ap):
        def f(e):
            with nc.allow_non_contiguous_dma(reason="tiny vector loads"):
                return e.dma_start(out=out_ap, in_=in_ap)
        S.dma("sp", f, writes=[t_small])

    ld1(sc_tmp[:, :, 0], c_in.rearrange("(k p) -> p k", p=128))
    ld1(sc_tmp[:, :, 1], cctx_in.rearrange("(k p) -> p k", p=128))
    for l in range(NL):
        ld1(adab_sb[:, l, :], ada_b[l].rearrange("(k p) -> p k", p=128))
        for s in range(3):
            ld1(lnw_sb[:, l, s, :], ln_w[l, s].rearrange("(k p) -> p k", p=128))
            ld1(lnb_sb[:, l, s, :], ln_b[l, s].rearrange("(k p) -> p k", p=128))
    for l in range(NL):
        ld1(bgate_sb[:, l, :], b_gate[l].rearrange("(k p) -> p k", p=128))
        ld1(gnorm_sb[:, l, :], gnorm_in[l].rearrange("(k p) -> p k", p=128))
        for k3 in range(3):
            ld1(convw_sb[:, l, k3, :], conv_w_in[l, k3].rearrange("(k p) -> p k", p=128))
        ld1(qnorm_sb[:, l, :], qnorm_in[l].rearrange("(k p) -> p k", p=128))
        ld1(kvnorm_sb[:, l, :], kvnorm_in[l].rearrange("(k p) -> p k", p=128))
    ld1(gmask_f[:], gmask_in)
    S.op("dve", lambda e: e.tensor_copy(out=gmask[:], in_=gmask_f[:]), reads=[t_small], writes=[t_small])
    S.op("dve", lambda e: e.tensor_copy(out=ident_bf[:], in_=ident[:]), reads=[t_const], writes=[t_const])
    S.op("act", lambda e: e.activation(out=sc_sb[:], in_=sc_tmp[:], func=AF.Silu), reads=[t_small], writes=[t_mod])

    m_pre = amark()
    ACW = 1152
    adaw_buf = [sb(f"adaw{i}", [128, 8, ACW], F32) for i in range(2)]
    t_adaw = [Tok(), Tok()]
    piece = 0
    for l in range(NL):
        for pc in range(9 * D // ACW):
            b = piece % 2
            buf = adaw_buf[b]
            src = ada_w[l, :, pc * ACW:(pc + 1) * ACW].rearrange("(k p) c -> p k c", p=128)
            S.dma("sp", lambda e, buf=buf, src=src: e.dma_start(out=buf[:], in_=src), writes=[t_adaw[b]])
            pb = piece % 8
            for mi in range(ACW // 128):
                i = pc * (ACW // 128) + mi

                def mm(e, buf=buf, mi=mi, pb=pb, mcol=mi):
                    r = None
                    for kc in range(8):
                        r = e.matmul(psum[pb][:, mcol * 2:mcol * 2 + 2], lhsT=buf[:, kc, mi * 128:(mi + 1) * 128],
                                     rhs=sc_sb[:, kc, :], start=(kc == 0), stop=(kc == 7))
                    return r
                S.op("pe", mm, reads=[t_adaw[b], t_mod], writes=[t_ps[pb]])
            i0 = pc * (ACW // 128)
            n = ACW // 128

            def ev(e, l=l, i0=i0, n=n, pb=pb):
                return e.tensor_tensor(out=modT[:, l, i0:i0 + n, :],
                                       in0=psum[pb][:, 0:2 * n].rearrange("p (i w) -> p i w", w=2),
                                       in1=adab_sb[:, l, i0:i0 + n].unsqueeze(2).to_broadcast([128, n, 2]),
                                       op=ALU.add)
            S.op("dve", ev, reads=[t_ps[pb], t_small], writes=[t_mod])
            piece += 1
    for l in range(NL):
        for s in range(3):
            base = 3 * s * 8
            S.op("dve", lambda e, l=l, s=s, base=base: e.tensor_scalar_add(
                out=mods[:, l, s, 0], in0=modT[:, l, base + 8:base + 16, :], scalar1=1.0), reads=[t_mod], writes=[t_mod])
            S.op("dve", lambda e, l=l, s=s, base=base: e.tensor_copy(
                out=mods[:, l, s, 1], in_=modT[:, l, base:base + 8, :]), reads=[t_mod], writes=[t_mod])
            gsc = (1.0 / ALPHA) if s == 1 else (0.5 / ALPHA)
            S.op("dve", lambda e, l=l, s=s, base=base, gsc=gsc: e.tensor_scalar_mul(
                out=mods[:, l, s, 2], in0=modT[:, l, base + 16:base + 24, :], scalar1=gsc), reads=[t_mod], writes=[t_mod])

    PW = 512
    wld = [sb(f"wld{i}", [128, 8 * PW], F32) for i in range(2)]
    wcv = [sb(f"wcv{i}", [128, 8 * PW], BF16) for i in range(2)]
    t_wld = [Tok(), Tok()]
    t_wcv = [Tok(), Tok()]
    t_wscr = Tok("wscr")
    prep_n = [0]
    cast_engs = ["act", "dve", "pool"]

    def prep(src, KC, width, dst, mb):
        cw = (8 * PW // KC) // mb * mb
        for c0 in range(0, width, cw):
            w_ = min(cw, width - c0)
            nb = w_ // mb
            i = prep_n[0] % 2
            ce = cast_engs[prep_n[0] % 3]
            prep_n[0] += 1
            ld_v = wld[i][:, 0:KC * w_].rearrange("p (k c) -> p k c", k=KC)
            S.dma("sp", lambda e, ld_v=ld_v, c0=c0, w_=w_: e.dma_start(
                out=ld_v, in_=src[:, c0:c0 + w_].rearrange("(k p) c -> p k c", p=128)), writes=[t_wld[i]])
            cv_v = wcv[i][:, 0:KC * w_].rearrange("p (b k m) -> p b k m", b=nb, k=KC)
            in_v = wld[i][:, 0:KC * w_].rearrange("p (k b m) -> p b k m", k=KC, b=nb)
            if ce == "act":
                S.op("act", lambda e, cv_v=cv_v, in_v=in_v: e.copy(out=cv_v, in_=in_v), reads=[t_wld[i]], writes=[t_wcv[i]])
            else:
                S.op(ce, lambda e, cv_v=cv_v, in_v=in_v: e.tensor_copy(out=cv_v, in_=in_v), reads=[t_wld[i]], writes=[t_wcv[i]])
            st_v = wcv[i][:, 0:KC * w_].rearrange("p (b f) -> p b f", b=nb)
            b0 = c0 // mb
            S.dma("sp", lambda e, st_v=st_v, b0=b0, nb=nb: e.dma_start(
                out=dst[b0:b0 + nb].rearrange("b p f -> p b f"), in_=st_v), reads=[t_wcv[i]], writes=[t_wscr])

    def cast(ce, out_ap, in_ap, neg=False):
        i = None
        if neg:
            if ce == "act":
                return lambda e: e.activation(out=out_ap, in_=in_ap, func=AF.Copy, scale=-1.0)
            return lambda e: e.tensor_scalar_mul(out=out_ap, in0=in_ap, scalar1=-1.0)
        if ce == "act":
            return lambda e: e.copy(out=out_ap, in_=in_ap)
        return lambda e: e.tensor_copy(out=out_ap, in_=in_ap)

    def prep_custom(KC, src_cols, pieces, stores):
        raise NotImplementedError

    def prep_special(src, KC, w_, ops, nout, dst):
        i = prep_n[0] % 2
        prep_n[0] += 1
        ld_v = wld[i][:, 0:KC * w_].rearrange("p (k c) -> p k c", k=KC)
        S.dma("sp", lambda e: e.dma_start(out=ld_v, in_=src.rearrange("(k p) c -> p k c", p=128)), writes=[t_wld[i]])
        cv = wcv[i][:, 0:nout * KC * 128].rearrange("p (b k m) -> p b k m", b=nout, k=KC)
        for n_, (o_ap, i_ap, neg) in enumerate(ops(ld_v, cv)):
            ce = cast_engs[n_ % 3]
            S.op(ce, cast(ce, o_ap, i_ap, neg), reads=[t_wld[i]], writes=[t_wcv[i]])
        S.dma("sp", lambda e: e.dma_start(out=dst.rearrange("b p f -> p b f"),
                                          in_=wcv[i][:, 0:nout * KC * 128].rearrange("p (b f) -> p b f", b=nout)),
              reads=[t_wcv[i]], writes=[t_wscr])

    def rot_ops(o_blk, i_cols):
        ov = o_blk.rearrange("p k (g two s) -> p k g two s", g=2, two=2)
        iv = i_cols.rearrange("p k (g two s) -> p k g two s", g=2, two=2)
        return [(ov[:, :, :, 0, :], iv[:, :, :, 1, :], True), (ov[:, :, :, 1, :], iv[:, :, :, 0, :], False)]

    def prep_layer(li):
        for f in range(2):
            for ag in range(2):
                prep(ffn_w13[li, f, :, ag * DFF:(ag + 1) * DFF], 8, DFF, W13S[li, f, ag], 128)
            prep(ffn_w2[li, f], NJ, D, W2S[li, f], 128)
        wl = w_in[li]
        prep(wl[:, 0:512], 8, 512, WIS[li, 0:4], 128)
        prep(wl[:, 512:1024], 8, 512, WIS[li, 4:8], 128)
        prep(wl[:, 2048:3072], 8, 1024, WIS[li, 8:16], 128)
        prep(wl[:, 4128:5152], 8, 1024, WIS[li, 16:24], 128)
        prep(wl[:, 5152:6176], 8, 1024, WIS[li, 24:32], 128)
        prep(wl[:, 6176:6688], 8, 512, WIS[li, 32:36], 128)
        prep(wl[:, 6688:6944], 8, 256, WIS[li, 36:38], 128)
        prep(wl[:, 3104:4128], 8, 1024, WIS[li, 38:46], 128)
        prep(wl[:, 7008:10080], 8, 3072, WIS[li, 46:70], 128)
        prep(wl[:, 1024:2048], 8, 1024, WVS[li], 512)
        prep_special(wl[:, 3072:3104], 8, 32, lambda ld, cv: [(cv[:, 0, :, 0:32], ld, False)], 1, WIS[li, 70:71])
        prep_special(wl[:, 6944:7008], 8, 64,
                     lambda ld, cv: [(cv[:, 0, :, 0:64], ld, False), (cv[:, 0, :, 64:128], ld, False)]
                     + rot_ops(cv[:, 1, :, 0:64], ld) + rot_ops(cv[:, 1, :, 64:128], ld), 2, WIS[li, 71:73])
        for pi, pw in enumerate([gla_proj, conv_proj, mla_proj, w_out]):
            prep(pw[li], 8, 1024, PRJS[li, pi], 128)
        for a in range(4):
            def uq_ops(ld, cv):
                ops = [(cv[:, 0, :, :], ld[:, :, 0:128], False), (cv[:, 1, :, :], ld[:, :, 192:320], False),
                       (cv[:, 2, :, 0:64], ld[:, :, 128:192], False), (cv[:, 2, :, 64:128], ld[:, :, 320:384], False)]
                ops += rot_ops(cv[:, 3, :, 0:64], ld[:, :, 128:192]) + rot_ops(cv[:, 3, :, 64:128], ld[:, :, 320:384])
                return ops
            i = prep_n[0] % 2
            prep_n[0] += 1
            ld_v = wld[i][:, 0:4 * 384].rearrange("p (k c) -> p k c", k=4)
            S.dma("sp", lambda e, ld_v=ld_v, a=a: e.dma_start(
                out=ld_v, in_=w_uq[li, :, a * 384:(a + 1) * 384].rearrange("(k p) c -> p k c", p=128)), writes=[t_wld[i]])
            cv = wcv[i][:, 0:4 * 512].rearrange("p (b k m) -> p b k m", b=4, k=4)
            for n_, (o_ap, i_ap, neg) in enumerate(uq_ops(ld_v, cv)):
                ce = cast_engs[n_ % 3]
                S.op(ce, cast(ce, o_ap, i_ap, neg), reads=[t_wld[i]], writes=[t_wcv[i]])
            cvf = wcv[i][:, 0:4 * 512].rearrange("p (b f) -> p b f", b=4)
            S.dma("sp", lambda e, cvf=cvf, a=a: e.dma_start(out=WUQS[li, 2 * a:2 * a + 2].rearrange("b p f -> p b f"),
                                                        in_=cvf[:, 0:2, :]), reads=[t_wcv[i]], writes=[t_wscr])
            S.dma("sp", lambda e, cvf=cvf, a=a: e.dma_start(out=WUQS[li, 8 + a], in_=cvf[:, 2, :]),
                  reads=[t_wcv[i]], writes=[t_wscr])
            S.dma("sp", lambda e, cvf=cvf, a=a: e.dma_start(out=WUQS[li, 12 + a], in_=cvf[:, 3, :]),
                  reads=[t_wcv[i]], writes=[t_wscr])
        i = prep_n[0] % 2
        prep_n[0] += 1
        ld_v = wld[i][:, 0:2 * 2048].rearrange("p (k h two m) -> p k h two m", k=2, h=8, two=2)
        S.dma("sp", lambda e, i=i: e.dma_start(out=wld[i][:, 0:4096].rearrange("p (k c) -> p k c", k=2),
                                                in_=w_ukv[li].rearrange("(k p) c -> p k c", p=128)), writes=[t_wld[i]])
        cvk = wcv[i][:, 0:2048].rearrange("p (h k m) -> p h k m", h=8, k=2)
        cvv = wcv[i][:, 2048:4096].rearrange("p (k h m) -> p k h m", k=2, h=8)
        for kk in range(2):
            ce = cast_engs[kk]
            S.op(ce, cast(ce, cvk[:, :, kk, :], ld_v[:, kk, :, 0, :]), reads=[t_wld[i]], writes=[t_wcv[i]])
            ce = cast_engs[kk + 1]
            S.op(ce, cast(ce, cvv[:, kk, :, :], ld_v[:, kk, :, 1, :]), reads=[t_wld[i]], writes=[t_wcv[i]])
        S.dma("sp", lambda e, i=i: e.dma_start(out=WUKS[li].rearrange("b p f -> p b f"),
                                                in_=wcv[i][:, 0:2048].rearrange("p (b f) -> p b f", b=8)),
              reads=[t_wcv[i]], writes=[t_wscr])
        S.dma("sp", lambda e, i=i: e.dma_start(out=WUKVV[li], in_=wcv[i][:, 2048:4096]),
              reads=[t_wcv[i]], writes=[t_wscr])

    for li in range(NL):
        prep_layer(li)

    arelease(m_pre)

    xin_sb = [sb(f"xin{i}", [128, D], F32) for i in range(2)]
    xT_sb = [sb(f"xTs{i}", [128, 8, 128], F32) for i in range(2)]
    t_xin = [Tok(), Tok()]
    t_xT = [Tok(), Tok()]
    t_XT = [Tok(f"XT{i}") for i in range(9)]
    for tt in range(NT // 128):
        i = tt % 2
        src = x_in[tt * 128:(tt + 1) * 128, :] if tt < L // 128 else ctx_in[(tt - L // 128) * 128:(tt - L // 128 + 1) * 128, :]
        S.dma("sp", lambda e, i=i, src=src: e.dma_start(out=xin_sb[i][:], in_=src), writes=[t_xin[i]])
        for half in range(2):
            pb = 2 * (tt % 2) + half

            def tr(e, i=i, half=half, pb=pb):
                r = None
                for q in range(4):
                    kc = half * 4 + q
                    r = e.transpose(psum[pb][:, q * 128:(q + 1) * 128], xin_sb[i][:, kc * 128:(kc + 1) * 128], ident[:])
                return r
            S.op("pe", tr, reads=[t_xin[i], t_const], writes=[t_ps[pb]])
            ce = "dve" if half == 0 else "act"

            def cp(e, i=i, half=half, pb=pb, ce=ce):
                o = xT_sb[i][:, half * 4:half * 4 + 4, :]
                src_ = psum[pb][:].rearrange("p (k t) -> p k t", k=4)
                return e.tensor_copy(out=o, in_=src_) if ce == "dve" else e.copy(out=o, in_=src_)
            S.op(ce, cp, reads=[t_ps[pb]], writes=[t_xT[i]])
        S.dma("sp", lambda e, i=i, tt=tt: e.dma_start(
            out=XT[:, tt * 128:(tt + 1) * 128].rearrange("(k p) t -> p k t", p=128), in_=xT_sb[i][:]),
            reads=[t_xT[i]], writes=[t_XT[tt // 4]])
    S.dma_barrier("sp")

    S.op("pe", lambda e: e.matmul(psum[0][0:32, 0:64], lhsT=ones_bf[:, 0:32], rhs=ones_bf[:, 0:64], start=True, stop=True),
         reads=[t_const], writes=[t_ps[0]])

    T = 512
    xt = [sb(f"xt{i}", [128, 8, T], F32) for i in range(2)]
    t_xt = [Tok(), Tok()]
    hb = sb("hb", [128, 8, T], BF16)
    t_hb = Tok()
    ub = sb("ub", [128, 8, T], BF16)
    usq = sb("usq", [128, 8, T], BF16)
    t_ub = Tok()
    t_usq = Tok()
    stat = sb("stat", [128, 4, T], F32)
    t_stat = Tok()
    tmpn = [sb(f"tmpn{i}", [128, T], F32) for i in range(2)]
    t_tmpn = [Tok(), Tok()]
    NWB = 6
    wblk = [sb(f"wblk{i}", [128, 8 * 128], BF16) for i in range(NWB)]
    t_wblk = [Tok() for _ in range(NWB)]
    cnt = {"xt": 0, "w13": 0, "w2": 0, "sil": 0, "tmpn": 0, "ps": 0, "wblk": 0}
    ffn_bufs = {}

    def ffn_alloc():
        ffn_bufs["hid"] = sb("hid", [128, NJ, T], BF16)
        ffn_bufs["sil"] = [sb(f"sil{i}", [128, T], BF16) for i in range(2)]
        ffn_bufs["w13b"] = [sb(f"w13b{i}", [128, 2, 8, 128], BF16) for i in range(3)]
        ffn_bufs["w2b"] = [sb(f"w2b{i}", [128, NJ, 128], BF16) for i in range(2)]

    t_hid = [Tok() for _ in range(NJ)]
    t_sil = [Tok(), Tok()]
    t_w13b = [Tok() for _ in range(3)]
    t_w2b = [Tok(), Tok()]

    tiles = [(i * T, T) for i in range(L // T)] + [(L, CTX)]

    dbg = {}

    def dump(name, ap, reads, dt=BF16):
        if debug is None or name not in debug or name in dbg:
            return
        dbg[name] = 1
        shp = list(ap.shape)
        d_ = nc.dram_tensor("dbg_" + name, shp, dt, kind="ExternalOutput").ap()
        S.dma("sp", lambda e: e.dma_start(out=d_, in_=ap), reads=reads)

    def dsnap(name, ap, dt=BF16):
        if debug is None or name not in debug:
            return
        S.dma_barrier("sp")
        d_ = nc.dram_tensor("dbg_" + name, list(ap.shape), dt, kind="ExternalOutput").ap()
        S.dma("sp", lambda e: e.dma_start(out=d_, in_=ap))
        S.dma_barrier("sp")

    def snapshot(name):
        if debug is None or name not in debug:
            return
        S.dma_barrier("sp")
        d_ = nc.dram_tensor("dbg_" + name, [D, NT], F32, kind="ExternalOutput").ap()
        S.dma("sp", lambda e: e.dma_start(out=d_, in_=XT))
        S.dma_barrier("sp")

    def load_xt(t0, tw):
        i = cnt["xt"] % 2
        cnt["xt"] += 1
        S.dma("sp", lambda e: e.dma_start(out=xt[i][:, :, 0:tw],
                                          in_=XT[:, t0:t0 + tw].rearrange("(k p) t -> p k t", p=128)),
              reads=[t_XT[t0 // 512]], writes=[t_xt[i]])
        return i

    def ln_pieces(li, s, xi, t0, tw, which):
        S.op("pool", lambda e: e.tensor_copy(out=ub[:, :, 0:tw], in_=xt[xi][:, :, 0:tw]), reads=[t_xt[xi]], writes=[t_ub])
        S.op("act", lambda e: e.activation(out=usq[:, :, 0:tw], in_=xt[xi][:, :, 0:tw], func=AF.Square),
             reads=[t_xt[xi]], writes=[t_usq])
        yield
        p1, p2 = 6, 7
        eps_eff = EPS / (ALPHA * ALPHA)

        def mm1(e):
            r = None
            for kc in range(8):
                r = e.matmul(psum[p1][:, 0:tw], lhsT=ones_bf[:], rhs=ub[:, kc, 0:tw], start=(kc == 0), stop=(kc == 7))
            return r

        def mm2(e):
            r = None
            for kc in range(8):
                r = e.matmul(psum[p2][:, 0:tw], lhsT=ones_bf[:], rhs=usq[:, kc, 0:tw], start=(kc == 0), stop=(kc == 7))
            return r
        S.op("pe", mm1, reads=[t_ub, t_const], writes=[t_ps[p1]])
        S.op("pe", mm2, reads=[t_usq, t_const], writes=[t_ps[p2]])
        S.op("act", lambda e: e.activation(out=stat[:, 0, 0:tw], in_=psum[p1][:, 0:tw], func=AF.Copy, scale=1.0 / D),
             reads=[t_ps[p1]], writes=[t_stat])
        S.op("dve", lambda e: e.tensor_tensor(out=stat[:, 1, 0:tw], in0=stat[:, 0, 0:tw], in1=stat[:, 0, 0:tw], op=ALU.mult),
             reads=[t_stat], writes=[t_stat])
        S.op("dve", lambda e: e.scalar_tensor_tensor(out=stat[:, 1, 0:tw], in0=psum[p2][:, 0:tw], scalar=1.0 / D,
                                                     in1=stat[:, 1, 0:tw], op0=ALU.mult, op1=ALU.subtract),
             reads=[t_stat, t_ps[p2]], writes=[t_stat])
        S.op("dve", lambda e: e.tensor_scalar_add(out=stat[:, 1, 0:tw], in0=stat[:, 1, 0:tw], scalar1=eps_eff),
             reads=[t_stat], writes=[t_stat])
        S.op("act", lambda e: e.activation(out=stat[:, 1, 0:tw], in_=stat[:, 1, 0:tw], func=AF.Sqrt),
             reads=[t_stat], writes=[t_stat])
        S.op("dve", lambda e: e.reciprocal(out=stat[:, 2, 0:tw], in_=stat[:, 1, 0:tw]),
             reads=[t_stat], writes=[t_stat])
        S.op("dve", lambda e: e.scalar_tensor_tensor(out=stat[:, 3, 0:tw], in0=stat[:, 0, 0:tw], scalar=-1.0,
                                                     in1=stat[:, 2, 0:tw], op0=ALU.mult, op1=ALU.mult),
             reads=[t_stat], writes=[t_stat])
        yield
        for kc in range(8):
            if kc:
                yield
            ti = cnt["tmpn"] % 2
            cnt["tmpn"] += 1
            S.op("dve", lambda e, kc=kc, ti=ti: e.tensor_tensor(out=tmpn[ti][:, 0:tw], in0=xt[xi][:, kc, 0:tw],
                                                                in1=stat[:, 2, 0:tw], op=ALU.mult),
                 reads=[t_xt[xi], t_stat], writes=[t_tmpn[ti]])
            S.op("pool", lambda e, kc=kc, ti=ti: e.tensor_tensor(out=tmpn[ti][:, 0:tw], in0=tmpn[ti][:, 0:tw],
                                                                 in1=stat[:, 3, 0:tw], op=ALU.add),
                 reads=[t_tmpn[ti], t_stat], writes=[t_tmpn[ti]])
            S.op("act", lambda e, kc=kc, ti=ti: e.activation(out=xt[xi][:, kc, 0:tw], in_=tmpn[ti][:, 0:tw], func=AF.Identity,
                                                             scale=lnw_sb[:, li, s, kc:kc + 1], bias=lnb_sb[:, li, s, kc:kc + 1]),
                 reads=[t_tmpn[ti], t_small], writes=[t_xt[xi]])
        S.dma("sp", lambda e: e.dma_start(out=XT[:, t0:t0 + tw].rearrange("(k p) t -> p k t", p=128), in_=xt[xi][:, :, 0:tw]),
              reads=[t_xt[xi]], writes=[t_XT[t0 // 512]])

    def layer_norm_store(li, s, xi, t0, tw, which):
        for _ in ln_pieces(li, s, xi, t0, tw, which):
            pass

    def drain(gen):
        if gen is not None:
            for _ in gen:
                pass

    def step(gen):
        if gen is not None:
            try:
                next(gen)
            except StopIteration:
                return None
        return gen

    def ffn_tile(li, f, s, t0, tw, xi, prefetch, pending):
        which = 0 if t0 < L else 1
        hid, sil, w13b, w2b = ffn_bufs["hid"], ffn_bufs["sil"], ffn_bufs["w13b"], ffn_bufs["w2b"]
        for kc in range(8):
            S.op("act", lambda e, kc=kc: e.activation(out=hb[:, kc, 0:tw], in_=xt[xi][:, kc, 0:tw], func=AF.Identity,
                                                      scale=mods[:, li, s, 0, kc, which:which + 1],
                                                      bias=mods[:, li, s, 1, kc, which:which + 1]),
                 reads=[t_xt[xi], t_mod], writes=[t_hb])
        for j in range(NJ):
            wi = cnt["w13"] % 3
            cnt["w13"] += 1
            for ag in range(2):
                S.dma("sp", lambda e, wi=wi, ag=ag, j=j: e.dma_start(
                    out=w13b[wi][:, ag].rearrange("p k m -> p (k m)"), in_=W13S[li, f, ag, j]),
                    writes=[t_w13b[wi]], join=(ag == 1))
            pa = (cnt["ps"] % 3) * 2
            cnt["ps"] += 1
            for ag in range(2):
                def mm(e, wi=wi, ag=ag, pa=pa):
                    r = None
                    for kc in range(8):
                        r = e.matmul(psum[pa + ag][:, 0:tw], lhsT=w13b[wi][:, ag, kc, :], rhs=hb[:, kc, 0:tw],
                                     start=(kc == 0), stop=(kc == 7))
                    return r
                S.op("pe", mm, reads=[t_w13b[wi], t_hb], writes=[t_ps[pa + ag]])
            si = cnt["sil"] % 2
            cnt["sil"] += 1
            S.op("act", lambda e, si=si, pa=pa: e.activation(out=sil[si][:, 0:tw], in_=psum[pa][:, 0:tw], func=AF.Silu),
                 reads=[t_ps[pa]], writes=[t_sil[si]])
            S.op("dve", lambda e, si=si, pa=pa, j=j: e.tensor_tensor(out=hid[:, j, 0:tw], in0=psum[pa + 1][:, 0:tw],
                                                                     in1=sil[si][:, 0:tw], op=ALU.mult),
                 reads=[t_ps[pa + 1], t_sil[si]], writes=[t_hid[j]])
            if j >= 1:
                pending = step(pending)
        drain(pending)
        dump("hb", hb[:], [t_hb])
        dump("hid", hid[:], t_hid)
        nxt = prefetch()
        for m in range(8):
            wi = cnt["w2"] % 2
            cnt["w2"] += 1
            S.dma("sp", lambda e, wi=wi, m=m: e.dma_start(out=w2b[wi][:].rearrange("p j n -> p (j n)"), in_=W2S[li, f, m]),
                  writes=[t_w2b[wi]])
            pa = (cnt["ps"] % 3) * 2
            cnt["ps"] += 1

            def mm(e, wi=wi, pa=pa):
                r = None
                for j in range(NJ):
                    r = e.matmul(psum[pa][:, 0:tw], lhsT=w2b[wi][:, j, :], rhs=hid[:, j, 0:tw],
                                 start=(j == 0), stop=(j == NJ - 1))
                return r
            S.op("pe", mm, reads=[t_w2b[wi]] + t_hid, writes=[t_ps[pa]])
            S.op("dve", lambda e, m=m, pa=pa: e.scalar_tensor_tensor(
                out=xt[xi][:, m, 0:tw], in0=psum[pa][:, 0:tw], scalar=mods[:, li, s, 2, m, which:which + 1],
                in1=xt[xi][:, m, 0:tw], op0=ALU.mult, op1=ALU.add),
                reads=[t_ps[pa], t_xt[xi], t_mod], writes=[t_xt[xi]])
        dump("u", xt[xi][:], [t_xt[xi]], F32)
        return nxt, ln_pieces(li, s, xi, t0, tw, which)

    def ffn_phase(li, f, tile_list):
        s = 0 if f == 0 else 2
        m0 = amark()
        ffn_alloc()
        pend = None
        nxt = load_xt(*tile_list[0])
        for ti_, (t0, tw) in enumerate(tile_list):
            def prefetch(ti_=ti_):
                if ti_ + 1 < len(tile_list):
                    return load_xt(*tile_list[ti_ + 1])
                return None
            nxt, pend = ffn_tile(li, f, s, t0, tw, nxt, prefetch, pend)
        drain(pend)
        arelease(m0)

    def lin(wap, KC, M, rhs, rtoks, tw, evac):
        wi = cnt["wblk"] % NWB
        cnt["wblk"] += 1
        S.dma("sp", lambda e: e.dma_start(out=wblk[wi][:, 0:KC * 128], in_=wap), writes=[t_wblk[wi]])
        pb = cnt["ps"] % 6
        cnt["ps"] += 1
        wv_ = wblk[wi][:, 0:KC * 128].rearrange("p (k m) -> p k m", k=KC)

        def mm(e):
            r = None
            for kc in range(KC):
                r = e.matmul(psum[pb][0:M, 0:tw], lhsT=wv_[:, kc, 0:M], rhs=rhs[:, kc, 0:tw], start=(kc == 0), stop=(kc == KC - 1))
            return r
        S.op("pe", mm, reads=[t_wblk[wi]] + list(rtoks), writes=[t_ps[pb]])
        evac(pb)
        return pb

    def ev_copy(eng, out_ap, wtok, M, tw, scale=None, func=None):
        def evac(pb):
            src = psum[pb][0:M, 0:tw]
            if eng == "act":
                if func is not None or scale is not None:
                    S.op("act", lambda e: e.activation(out=out_ap, in_=src, func=(func or AF.Copy), scale=(scale or 1.0)),
                         reads=[t_ps[pb]], writes=[wtok])
                else:
                    S.op("act", lambda e: e.copy(out=out_ap, in_=src), reads=[t_ps[pb]], writes=[wtok])
            else:
                if scale is not None:
                    S.op(eng, lambda e: e.tensor_scalar_mul(out=out_ap, in0=src, scalar1=scale), reads=[t_ps[pb]], writes=[wtok])
                else:
                    S.op(eng, lambda e: e.tensor_copy(out=out_ap, in_=src), reads=[t_ps[pb]], writes=[wtok])
        return evac

    m1b = {}

    def m1_alloc():
        m1b["wv"] = sb("wv", [128, 8, 1024], BF16)
        m1b["wkvv"] = sb("wkvv", [128, 2, 1024], BF16)
        m1b["dwst"] = sb("dwst", [33, 2, 512], F32)
        m1b["dw"] = sb("dw", [33, 2, 512], BF16)
        m1b["glrT"] = sb("glrT", [33, T], BF16)
        m1b["st8"] = [sb(f"st8_{i}", [128, 8, T], BF16) for i in range(2)]
        m1b["z7"] = [sb(f"z7_{i}", [128, T], F32) for i in range(2)]
        m1b["cqf"] = sb("cqf", [128, 4, T], F32)
        m1b["cqn"] = sb("cqn", [128, 4, T], BF16)
        m1b["ckvn"] = sb("ckvn", [128, 2, T], BF16)
        m1b["cs"] = sb("cs", [128, 2, T], F32)
        m1b["etmp"] = [sb(f"etmp{i}", [128, 512], F32) for i in range(2)]
        m1b["gst"] = sb("gst", [128, 2, 4, 512], BF16)
        m1b["vst"] = sb("vst", [128, 4, 1024], BF16)
        m1b["rtmp"] = [sb(f"rtmp{i}", [128, T], F32) for i in range(2)]
        for k_ in ["wv", "wkvv", "dw", "glrT", "cqf", "cqn", "ckvn", "cs", "gst", "vst"]:
            m1b["t_" + k_] = Tok(k_)
        m1b["t_st8"] = [Tok(), Tok()]
        m1b["t_z7"] = [Tok(), Tok()]
        m1b["t_etmp"] = [Tok(), Tok()]
        m1b["t_rtmp"] = [Tok(), Tok()]
        m1b["n"] = {"st8": 0, "z7": 0, "etmp": 0, "rtmp": 0}

    def rms_apply(src_f, KC, tw, nfeat, norm_ap_fn, out_bf, t_src, t_out):
        S.op("act", lambda e: e.activation(out=usq[:, 0:KC, 0:tw], in_=src_f[:, 0:KC, 0:tw], func=AF.Square),
             reads=[t_src], writes=[t_usq])
        p2 = 7

        def mm2(e):
            r = None
            for kc in range(KC):
                r = e.matmul(psum[p2][:, 0:tw], lhsT=ones_bf[:], rhs=usq[:, kc, 0:tw], start=(kc == 0), stop=(kc == KC - 1))
            return r
        S.op("pe", mm2, reads=[t_usq, t_const], writes=[t_ps[p2]])
        S.op("dve", lambda e: e.tensor_scalar(out=stat[:, 1, 0:tw], in0=psum[p2][:, 0:tw], scalar1=1.0 / nfeat, scalar2=EPS,
                                              op0=ALU.mult, op1=ALU.add), reads=[t_ps[p2]], writes=[t_stat])
        S.op("act", lambda e: e.activation(out=stat[:, 1, 0:tw], in_=stat[:, 1, 0:tw], func=AF.Sqrt), reads=[t_stat], writes=[t_stat])
        S.op("dve", lambda e: e.reciprocal(out=stat[:, 2, 0:tw], in_=stat[:, 1, 0:tw]), reads=[t_stat], writes=[t_stat])
        for kc in range(KC):
            S.op("dve", lambda e, kc=kc: e.scalar_tensor_tensor(out=out_bf[:, kc, 0:tw], in0=src_f[:, kc, 0:tw], scalar=norm_ap_fn(kc),
                                                                in1=stat[:, 2, 0:tw], op0=ALU.mult, op1=ALU.mult),
                 reads=[t_src, t_stat, t_small], writes=[t_out])

    def rope_pair(wA, wB, KC, rhs, rtoks, tw, isx, out_ap, wtok):
        if not isx:
            lin(wA, KC, 128, rhs, rtoks, tw, ev_copy("act", out_ap, wtok, 128, tw))
            return
        n = m1b["n"]
        r1 = n["rtmp"] % 2
        n["rtmp"] += 1
        cs = m1b["cs"]
        rt, t_rt = m1b["rtmp"][r1], m1b["t_rtmp"][r1]

        def evA(pb):
            S.op("dve", lambda e: e.tensor_tensor(out=rt[:, 0:tw], in0=psum[pb][:, 0:tw], in1=cs[:, 0, 0:tw], op=ALU.mult),
                 reads=[t_ps[pb], m1b["t_cs"]], writes=[t_rt])

        def evB(pb):
            ti = cnt["tmpn"] % 2
            cnt["tmpn"] += 1
            S.op("dve", lambda e: e.tensor_tensor(out=tmpn[ti][:, 0:tw], in0=psum[pb][:, 0:tw], in1=cs[:, 1, 0:tw], op=ALU.mult),
                 reads=[t_ps[pb], m1b["t_cs"]], writes=[t_tmpn[ti]])
            S.op("pool", lambda e: e.tensor_tensor(out=out_ap, in0=rt[:, 0:tw], in1=tmpn[ti][:, 0:tw], op=ALU.add),
                 reads=[t_rt, t_tmpn[ti]], writes=[wtok])
        lin(wA, KC, 128, rhs, rtoks, tw, evA)
        lin(wB, KC, 128, rhs, rtoks, tw, evB)

    def m1_tile(li, t0, tw, xi, prefetch):
        which = 0 if t0 < L else 1
        isx = t0 < L
        n = m1b["n"]
        ns = tw // 128
        for kc in range(8):
            S.op("act", lambda e, kc=kc: e.activation(out=hb[:, kc, 0:tw], in_=xt[xi][:, kc, 0:tw], func=AF.Identity,
                                                      scale=mods[:, li, 1, 0, kc, which:which + 1],
                                                      bias=mods[:, li, 1, 1, kc, which:which + 1]),
                 reads=[t_xt[xi], t_mod], writes=[t_hb])
        if isx:
            S.dma("sp", lambda e: e.dma_start(out=m1b["cs"][:, :, 0:tw], in_=rope_cs[:, :, t0:t0 + tw]), writes=[m1b["t_cs"]])

        def st8_next():
            i = n["st8"] % 2
            n["st8"] += 1
            return m1b["st8"][i], m1b["t_st8"][i]
        stq, t_stq = st8_next()
        for h in range(4):
            lin(WIS[li, h], 8, 128, hb, [t_hb], tw, ev_copy("act", stq[:, h, 0:tw], t_stq, 128, tw, scale=128.0 ** -0.5))
        for h in range(4):
            lin(WIS[li, 4 + h], 8, 128, hb, [t_hb], tw, ev_copy("dve", stq[:, 4 + h, 0:tw], t_stq, 128, tw))
        S.dma("act", lambda e: e.dma_start(out=QT[:, t0:t0 + tw].rearrange("(h p) t -> p h t", p=128), in_=stq[:, 0:4, 0:tw]),
              reads=[t_stq], writes=[])
        S.dma("act", lambda e: e.dma_start(out=KT[:, t0:t0 + tw].rearrange("(h p) t -> p h t", p=128), in_=stq[:, 4:8, 0:tw]),
              reads=[t_stq], writes=[])
        str_, t_str = st8_next()
        for c_ in range(8):
            zi = n["z7"] % 2
            n["z7"] += 1
            z7b, t_z7b = m1b["z7"][zi], m1b["t_z7"][zi]

            def evr(pb, c_=c_, z7b=z7b, t_z7b=t_z7b):
                S.op("act", lambda e: e.activation(out=z7b[:, 0:tw], in_=psum[pb][:, 0:tw], func=AF.Silu), reads=[t_ps[pb]], writes=[t_z7b])
                S.op("dve", lambda e: e.tensor_scalar_mul(out=str_[:, c_, 0:tw], in0=z7b[:, 0:tw], scalar1=gnorm_sb[:, li, c_:c_ + 1]),
                     reads=[t_z7b, t_small], writes=[t_str])
            lin(WIS[li, 8 + c_], 8, 128, hb, [t_hb], tw, evr)
        S.dma("act", lambda e: e.dma_start(out=RT[:, t0:t0 + tw].rearrange("(h p) t -> p h t", p=128), in_=str_[:, :, 0:tw]),
              reads=[t_str], writes=[])
        stu, t_stu = st8_next()
        for c_ in range(8):
            zi = n["z7"] % 2
            n["z7"] += 1
            z7b, t_z7b = m1b["z7"][zi], m1b["t_z7"][zi]
            lin(WIS[li, 16 + c_], 8, 128, hb, [t_hb], tw, ev_copy("act", z7b[:, 0:tw], t_z7b, 128, tw))

            def ev8(pb, c_=c_, z7b=z7b, t_z7b=t_z7b):
                S.op("dve", lambda e: e.tensor_tensor(out=stu[:, c_, 0:tw], in0=psum[pb][:, 0:tw], in1=z7b[:, 0:tw], op=ALU.mult),
                     reads=[t_ps[pb], t_z7b], writes=[t_stu])
            lin(WIS[li, 24 + c_], 8, 128, hb, [t_hb], tw, ev8)
        S.dma("act", lambda e: e.dma_start(out=UT[:, t0:t0 + tw].rearrange("(h p) t -> p h t", p=128), in_=stu[:, :, 0:tw]),
              reads=[t_stu], writes=[])
        glrT, dw, gst = m1b["glrT"], m1b["dw"], m1b["gst"]
        lin(WIS[li, 70], 8, 32, hb, [t_hb], tw, ev_copy("dve", glrT[0:32, 0:tw], m1b["t_glrT"], 32, tw))
        for sub in range(ns):
            for d_ in range(2):
                pb = cnt["ps"] % 6
                cnt["ps"] += 1
                S.op("pe", lambda e, sub=sub, d_=d_, pb=pb: e.matmul(psum[pb][:, :], lhsT=glrT[0:33, sub * 128:(sub + 1) * 128],
                                                                     rhs=dw[0:33, d_, :], start=True, stop=True),
                     reads=[m1b["t_glrT"], m1b["t_dw"]], writes=[t_ps[pb]])
                ei = n["etmp"] % 2
                n["etmp"] += 1
                et, t_et = m1b["etmp"][ei], m1b["t_etmp"][ei]
                S.op("act", lambda e, pb=pb, et=et: e.activation(out=et[:], in_=psum[pb][:, :], func=AF.Exp, scale=-1.0),
                     reads=[t_ps[pb]], writes=[t_et])
                S.op("act", lambda e, et=et, sub=sub, d_=d_: e.activation(out=gst[:, d_, sub, :], in_=et[:], func=AF.Ln, bias=1.0),
                     reads=[t_et], writes=[m1b["t_gst"]])
        for d_ in range(2):
            S.dma("act", lambda e, d_=d_: e.dma_start(out=GP[d_, t0:t0 + tw, :].rearrange("(s p) f -> p s f", p=128),
                                                     in_=gst[:, d_, 0:ns, :]), reads=[m1b["t_gst"]], writes=[])
        wv, vst = m1b["wv"], m1b["vst"]
        for sub in range(ns):
            for half in range(2):
                pb = cnt["ps"] % 6
                cnt["ps"] += 1

                def mmv(e, sub=sub, half=half, pb=pb):
                    r = None
                    for kc in range(8):
                        r = e.matmul(psum[pb][:, :], lhsT=hb[:, kc, sub * 128:(sub + 1) * 128], rhs=wv[:, kc, half * 512:(half + 1) * 512],
                                     start=(kc == 0), stop=(kc == 7))
                    return r
                S.op("pe", mmv, reads=[t_hb, m1b["t_wv"]], writes=[t_ps[pb]])
                eng = "act" if half == 0 else "dve"
                ev_copy(eng, vst[:, sub, half * 512:(half + 1) * 512], m1b["t_vst"], 128, 512)(pb)
        S.dma("act", lambda e: e.dma_start(out=VG[t0:t0 + tw, :].rearrange("(s p) f -> p s f", p=128), in_=vst[:, 0:ns, :]),
              reads=[m1b["t_vst"]], writes=[])
        stk, t_stk = st8_next()
        rope_pair(WIS[li, 71], WIS[li, 72], 8, hb, [t_hb], tw, isx, stk[:, 0, 0:tw], t_stk)
        S.dma("act", lambda e: e.dma_start(out=KRT[:, t0:t0 + tw], in_=stk[:, 0, 0:tw]), reads=[t_stk], writes=[])
        cqf, cqn, ckvn = m1b["cqf"], m1b["cqn"], m1b["ckvn"]
        for c_ in range(4):
            lin(WIS[li, 32 + c_], 8, 128, hb, [t_hb], tw, ev_copy("act", cqf[:, c_, 0:tw], m1b["t_cqf"], 128, tw))
        rms_apply(cqf, 4, tw, 512, lambda kc: qnorm_sb[:, li, kc:kc + 1], cqn, m1b["t_cqf"], m1b["t_cqn"])
        for c_ in range(2):
            lin(WIS[li, 36 + c_], 8, 128, hb, [t_hb], tw, ev_copy("act", cqf[:, c_, 0:tw], m1b["t_cqf"], 128, tw))
        rms_apply(cqf, 2, tw, 256, lambda kc: kvnorm_sb[:, li, kc:kc + 1], ckvn, m1b["t_cqf"], m1b["t_ckvn"])
        nxt = prefetch()
        stn, t_stn = st8_next()
        for h in range(8):
            lin(WUQS[li, h], 4, 128, cqn, [m1b["t_cqn"]], tw, ev_copy("act" if h % 2 else "dve", stn[:, h, 0:tw], t_stn, 128, tw))
        S.dma("act", lambda e: e.dma_start(out=QNT[:, t0:t0 + tw].rearrange("(h p) t -> p h t", p=128), in_=stn[:, :, 0:tw]),
              reads=[t_stn], writes=[])
        stp, t_stp = st8_next()
        for a in range(4):
            rope_pair(WUQS[li, 8 + a], WUQS[li, 12 + a], 4, cqn, [m1b["t_cqn"]], tw, isx, stp[:, a, 0:tw], t_stp)
        S.dma("act", lambda e: e.dma_start(out=QRT[:, t0:t0 + tw].rearrange("(h p) t -> p h t", p=128), in_=stp[:, 0:4, 0:tw]),
              reads=[t_stp], writes=[])
        stkn, t_stkn = st8_next()
        for h in range(8):
            lin(WUKS[li, h], 2, 128, ckvn, [m1b["t_ckvn"]], tw, ev_copy("act" if h % 2 else "dve", stkn[:, h, 0:tw], t_stkn, 128, tw))
        S.dma("act", lambda e: e.dma_start(out=KNT[:, t0:t0 + tw].rearrange("(h p) t -> p h t", p=128), in_=stkn[:, :, 0:tw]),
              reads=[t_stkn], writes=[])
        wkvv = m1b["wkvv"]
        for sub in range(ns):
            for half in range(2):
                pb = cnt["ps"] % 6
                cnt["ps"] += 1

                def mmv2(e, sub=sub, half=half, pb=pb):
                    r = None
                    for kc in range(2):
                        r = e.matmul(psum[pb][:, :], lhsT=ckvn[:, kc, sub * 128:(sub + 1) * 128], rhs=wkvv[:, kc, half * 512:(half + 1) * 512],
                                     start=(kc == 0), stop=(kc == 1))
                    return r
                S.op("pe", mmv2, reads=[m1b["t_ckvn"], m1b["t_wkvv"]], writes=[t_ps[pb]])
                eng = "act" if half == 0 else "dve"
                ev_copy(eng, vst[:, sub, half * 512:(half + 1) * 512], m1b["t_vst"], 128, 512)(pb)
        c0 = t0 // 128
        for sub in range(ns):
            S.dma("act", lambda e, sub=sub: e.dma_start(out=VM[:, :, c0 + sub, :].rearrange("h p m -> p h m"),
                                                       in_=vst[:, sub, :].rearrange("p (h m) -> p h m", h=8)),
                  reads=[m1b["t_vst"]], writes=[])
        return nxt

    def mixer_in_phase(li, tile_list):
        m0 = amark()
        m1_alloc()
        S.dma("sp", lambda e: e.dma_start(out=m1b["wv"][:, :, 0:512], in_=WVS[li, 0].rearrange("p (k c) -> p k c", k=8)),
              writes=[m1b["t_wv"]])
        S.dma("sp", lambda e: e.dma_start(out=m1b["wv"][:, :, 512:1024], in_=WVS[li, 1].rearrange("p (k c) -> p k c", k=8)),
              writes=[m1b["t_wv"]], join=True)
        S.dma("sp", lambda e: e.dma_start(out=m1b["wkvv"][:].rearrange("p k c -> p (k c)"), in_=WUKVV[li]), writes=[m1b["t_wkvv"]])
        dwst, dw = m1b["dwst"], m1b["dw"]
        S.op("pool", lambda e: e.memset(dwst[:], 0.0), writes=[m1b["t_dw"]])
        S.op("pool", lambda e: e.memset(m1b["glrT"][32:33, :], 1.0), writes=[m1b["t_glrT"]])
        S.dma("sp", lambda e: e.dma_start(out=dwst[0:16, 0, :], in_=gdw_in[li, 0]), writes=[m1b["t_dw"]])
        S.dma("sp", lambda e: e.dma_start(out=dwst[16:32, 1, :], in_=gdw_in[li, 1]), writes=[m1b["t_dw"]])
        S.dma("sp", lambda e: e.dma_start(out=dwst[32:33, :, :], in_=gdb_in[li:li + 1]), writes=[m1b["t_dw"]])
        S.op("dve", lambda e: e.tensor_copy(out=dw[:], in_=dwst[:]), reads=[m1b["t_dw"]], writes=[m1b["t_dw"]])
        nxt = load_xt(*tile_list[0])
        for ti_, (t0, tw) in enumerate(tile_list):
            def prefetch(ti_=ti_):
                if ti_ + 1 < len(tile_list):
                    return load_xt(*tile_list[ti_ + 1])
                return None
            nxt = m1_tile(li, t0, tw, nxt, prefetch)
        arelease(m0)

    def gla_phase(li):
        m0 = amark()
        ld = []
        for i in range(3):
            b_ = dict(gp=sb(f"g_gp{i}", [128, 512], BF16), q=sb(f"g_q{i}", [128, 4, 128], BF16), k=sb(f"g_k{i}", [128, 4, 128], BF16),
                      v=sb(f"g_v{i}", [128, 1024], BF16), of=sb(f"g_of{i}", [128, 8, 128], F32), r=sb(f"g_r{i}", [128, 8, 128], BF16),
                      t_in=Tok(), t_of=Tok())
            ld.append(b_)
        wk = []
        for i in range(2):
            b_ = dict(eq=sb(f"g_eq{i}", [128, 4, 128], F32), ek=sb(f"g_ek{i}", [128, 4, 128], F32), qe=sb(f"g_qe{i}", [128, 4, 128], BF16),
                      ke=sb(f"g_ke{i}", [128, 4, 128], BF16), kd=sb(f"g_kd{i}", [128, 4, 128], BF16), At=sb(f"g_At{i}", [128, 4, 128], BF16),
                      kdt=sb(f"g_kdt{i}", [128, 4, 128], BF16), t_e=Tok(), t_qe=Tok(), t_ke=Tok(), t_kd=Tok(), t_At=Tok(), t_kdt=Tok())
            wk.append(b_)
        NR = 4
        Sr = [sb(f"g_S{i}", [128, 4, 256], F32) for i in range(NR)]
        t_Sr = [[Tok() for _ in range(4)] for _ in range(NR)]
        Sbr = [sb(f"g_Sb{i}", [128, 4, 256], BF16) for i in range(NR)]
        t_Sbr = [Tok() for _ in range(NR)]
        gst_ = {"n": 0}
        ost = [sb(f"g_ost{i}", [128, 8, 128], F32) for i in range(2)]
        t_ost = [Tok(), Tok()]
        sq = sb("g_sq", [128, 8, 128], BF16)
        t_sq = Tok()
        rst = sb("g_rst", [128, 4, 128], F32)
        t_rst = Tok()
        t1b = sb("g_t1", [128, 8, 128], F32)
        t_t1 = Tok()
        anb = [sb(f"g_an{i}", [128, 8, 128], BF16) for i in range(2)]
        t_an = [Tok(), Tok()]
        B_BC, B_A, B_T, B_ST, B_O0, B_O1 = 0, 0, 1, 1, 2, 3
        B_U = [(4, 5), (6, 7)]
        psT_bf = psum[B_T][:].bitcast(BF16)

        def loads(d_, t0, i):
            b_ = ld[i]
            S.dma("sp", lambda e: e.dma_start(out=b_["gp"][:], in_=GP[d_, t0:t0 + 128, :]), writes=[b_["t_in"]])
            S.dma("sp", lambda e: e.dma_start(out=b_["q"][:], in_=QT[:, t0:t0 + 128].rearrange("(h p) t -> p h t", p=128)),
                  writes=[b_["t_in"]], join=True)
            S.dma("sp", lambda e: e.dma_start(out=b_["k"][:], in_=KT[:, t0:t0 + 128].rearrange("(h p) t -> p h t", p=128)),
                  writes=[b_["t_in"]], join=True)
            S.dma("sp", lambda e: e.dma_start(out=b_["v"][:], in_=VG[t0:t0 + 128, :]), writes=[b_["t_in"]], join=True)
            if d_ == 1:
                S.dma("sp", lambda e: e.dma_start(out=b_["of"][:], in_=OF[:, t0:t0 + 128].rearrange("(k p) t -> p k t", p=128)),
                      writes=[b_["t_of"]])
                S.dma("sp", lambda e: e.dma_start(out=b_["r"][:], in_=RT[:, t0:t0 + 128].rearrange("(k p) t -> p k t", p=128)),
                      writes=[b_["t_of"]], join=True)

        def prologue(d_, i, wi_):
            b_, w_ = ld[i], wk[wi_]
            esel = 63 if d_ == 0 else 0

            def mm_bc(e):
                r = None
                for h in range(4):
                    r = e.matmul(psum[B_BC][:, h * 128:(h + 1) * 128], lhsT=b_["gp"][:, h * 128:(h + 1) * 128], rhs=gmask[:, d_, :],
                                 start=True, stop=True)
                return r
            S.op("pe", mm_bc, reads=[b_["t_in"], t_small], writes=[t_ps[B_BC]])
            bcv = psum[B_BC][:].rearrange("p (h t) -> p h t", h=4)
            S.op("act", lambda e: e.activation(out=w_["eq"][:], in_=bcv, func=AF.Exp), reads=[t_ps[B_BC]], writes=[w_["t_e"]])
            S.op("act", lambda e: e.activation(out=w_["ek"][:], in_=bcv, func=AF.Exp, scale=-1.0), reads=[t_ps[B_BC]], writes=[w_["t_e"]])
            S.op("dve", lambda e: e.tensor_tensor(out=w_["qe"][:], in0=b_["q"][:], in1=w_["eq"][:], op=ALU.mult),
                 reads=[b_["t_in"], w_["t_e"]], writes=[w_["t_qe"]])
            S.op("dve", lambda e: e.tensor_tensor(out=w_["ke"][:], in0=b_["k"][:], in1=w_["ek"][:], op=ALU.mult),
                 reads=[b_["t_in"], w_["t_e"]], writes=[w_["t_ke"]])
            S.op("pool", lambda e: e.tensor_tensor(
                out=w_["kd"][:].rearrange("p h (c j) -> p h c j", c=2), in0=w_["ke"][:].rearrange("p h (c j) -> p h c j", c=2),
                in1=w_["eq"][:, :, esel::64].unsqueeze(3).to_broadcast([128, 4, 2, 64]), op=ALU.mult),
                reads=[w_["t_ke"], w_["t_e"]], writes=[w_["t_kd"]])

            def mm_a(e):
                r = None
                for h in range(4):
                    r = e.matmul(psum[B_A][:, h * 128:(h + 1) * 128], lhsT=w_["ke"][:, h, :], rhs=w_["qe"][:, h, :], start=True, stop=True)
                return r
            S.op("pe", mm_a, reads=[w_["t_ke"], w_["t_qe"]], writes=[t_ps[B_A]])
            S.op("dve", lambda e: e.tensor_tensor(out=w_["At"][:], in0=psum[B_A][:].rearrange("p (h t) -> p h t", h=4),
                                                  in1=gmask[:, 2 + d_, :].unsqueeze(1).to_broadcast([128, 4, 128]), op=ALU.mult),
                 reads=[t_ps[B_A], t_small], writes=[w_["t_At"]])

            def mm_t(e):
                r = None
                for h in range(4):
                    r = e.transpose(psT_bf[:, h * 128:(h + 1) * 128], w_["kd"][:, h, :], ident_bf[:])
                return r
            S.op("pe", mm_t, reads=[w_["t_kd"], t_const], writes=[t_ps[B_T]])
            S.op("act", lambda e: e.copy(out=w_["kdt"][:], in_=psT_bf[:, 0:512].rearrange("p (h t) -> p h t", h=4)),
                 reads=[t_ps[B_T]], writes=[w_["t_kdt"]])

        def chain(d_, t0, i, oi):
            b_, w_ = ld[i], wk[oi]
            chunks = (0, 1) if d_ == 0 else (1, 0)
            esel = 63 if d_ == 0 else 0

            def oreg(h, c2):
                return psum[B_O0 + h // 2][:, ((h % 2) * 2 + c2) * 128:((h % 2) * 2 + c2 + 1) * 128]

            def mm_intra(e):
                r = None
                for h in range(4):
                    for c2 in range(2):
                        r = e.matmul(oreg(h, c2), lhsT=b_["v"][:, h * 256 + c2 * 128:h * 256 + (c2 + 1) * 128], rhs=w_["At"][:, h, :],
                                     start=(h % 2 == 0 and c2 == 0), stop=False, skip_group_check=True)
                return r
            n0 = gst_["n"]
            gst_["n"] += 2
            for ci, c in enumerate(chunks):
                nn = n0 + ci
                ub0, ub1 = B_U[nn % 2]

                def mm_upd(e, c=c, ub0=ub0, ub1=ub1):
                    r = None
                    for h in range(4):
                        r = e.matmul(psum[(ub0, ub1)[h // 2]][:, (h % 2) * 256:(h % 2 + 1) * 256], lhsT=w_["kdt"][c * 64:(c + 1) * 64, h, :],
                                     rhs=b_["v"][c * 64:(c + 1) * 64, h * 256:(h + 1) * 256], start=True, stop=True)
                    return r
                S.op("pe", mm_upd, reads=[w_["t_kdt"], b_["t_in"]], writes=[t_ps[ub0], t_ps[ub1]])
                cur, prv = nn % NR, (nn - 1) % NR
                for h in range(4):
                    S.op("dve", lambda e, h=h, c=c, cur=cur, prv=prv, ub0=ub0, ub1=ub1: e.scalar_tensor_tensor(
                        out=Sr[cur][:, h, :], in0=Sr[prv][:, h, :], scalar=w_["eq"][:, h, c * 64 + esel:c * 64 + esel + 1],
                        in1=psum[(ub0, ub1)[h // 2]][:, (h % 2) * 256:(h % 2 + 1) * 256], op0=ALU.mult, op1=ALU.add),
                        reads=[t_Sr[prv][h], w_["t_e"], t_ps[(ub0, ub1)[h // 2]]], writes=[t_Sr[cur][h]])
                S.op("act", lambda e, cur=cur: e.copy(out=Sbr[cur][:], in_=Sr[cur][:]), reads=t_Sr[cur], writes=[t_Sbr[cur]])
            S.op("pe", mm_intra, reads=[b_["t_in"], w_["t_At"]], writes=[t_ps[B_O0], t_ps[B_O1]])
            for ci, c in enumerate(chunks):
                prv = (n0 + ci - 1) % NR

                def mm_inter(e, c=c, ci=ci, prv=prv):
                    r = None
                    for h in range(4):
                        for c2 in range(2):
                            r = e.matmul(oreg(h, c2)[:, c * 64:(c + 1) * 64], lhsT=Sbr[prv][:, h, c2 * 128:(c2 + 1) * 128],
                                         rhs=w_["qe"][:, h, c * 64:(c + 1) * 64], start=False, stop=(ci == 1), skip_group_check=True)
                    return r
                S.op("pe", mm_inter, reads=[t_Sbr[prv], w_["t_qe"]], writes=[t_ps[B_O0], t_ps[B_O1]])
            o_t, t_o = ost[oi], t_ost[oi]
            if d_ == 0:
                S.op("act", lambda e: e.copy(out=o_t[:, 0:4, :], in_=psum[B_O0][:].rearrange("p (k t) -> p k t", k=4)),
                     reads=[t_ps[B_O0]], writes=[t_o])
                S.op("dve", lambda e: e.tensor_copy(out=o_t[:, 4:8, :], in_=psum[B_O1][:].rearrange("p (k t) -> p k t", k=4)),
                     reads=[t_ps[B_O1]], writes=[t_o])
                S.dma("sp", lambda e: e.dma_start(out=OF[:, t0:t0 + 128].rearrange("(k p) t -> p k t", p=128), in_=o_t[:]),
                      reads=[t_o], writes=[])
                return
            for half in range(2):
                S.op("dve", lambda e, half=half: e.tensor_tensor(
                    out=o_t[:, half * 4:half * 4 + 4, :], in0=psum[B_O0 + half][:].rearrange("p (k t) -> p k t", k=4),
                    in1=b_["of"][:, half * 4:half * 4 + 4, :], op=ALU.add), reads=[t_ps[B_O0 + half], b_["t_of"]], writes=[t_o])
            S.op("act", lambda e: e.activation(out=sq[:], in_=o_t[:], func=AF.Square), reads=[t_o], writes=[t_sq])

            def mm_st(e):
                r = None
                for h in range(4):
                    for c2 in range(2):
                        r = e.matmul(psum[B_ST][:, h * 128:(h + 1) * 128], lhsT=ones_bf[:], rhs=sq[:, h * 2 + c2, :],
                                     start=(h == 0 and c2 == 0), stop=(c2 == 1), skip_group_check=True)
                return r
            S.op("pe", mm_st, reads=[t_sq, t_const], writes=[t_ps[B_ST]])
            S.op("dve", lambda e: e.tensor_scalar(out=rst[:], in0=psum[B_ST][:].rearrange("p (h t) -> p h t", h=4), scalar1=1.0 / 256,
                                                  scalar2=EPS, op0=ALU.mult, op1=ALU.add), reads=[t_ps[B_ST]], writes=[t_rst])
            S.op("act", lambda e: e.activation(out=rst[:], in_=rst[:], func=AF.Sqrt), reads=[t_rst], writes=[t_rst])
            S.op("dve", lambda e: e.reciprocal(out=rst[:], in_=rst[:]), reads=[t_rst], writes=[t_rst])
            S.op("dve", lambda e: e.tensor_tensor(out=t1b[:].rearrange("p (h c) t -> p h c t", c=2),
                                                  in0=o_t[:].rearrange("p (h c) t -> p h c t", c=2),
                                                  in1=rst[:].unsqueeze(2).to_broadcast([128, 4, 2, 128]), op=ALU.mult),
                 reads=[t_o, t_rst], writes=[t_t1])
            a_t, t_a = anb[oi], t_an[oi]
            S.op("dve", lambda e: e.tensor_tensor(out=a_t[:], in0=t1b[:], in1=b_["r"][:], op=ALU.mult),
                 reads=[t_t1, b_["t_of"]], writes=[t_a])
            S.dma("sp", lambda e: e.dma_start(out=AN[:, t0:t0 + 128].rearrange("(k p) t -> p k t", p=128), in_=a_t[:]),
                  reads=[t_a], writes=[])

        for d_ in range(2):
            gst_["n"] = 0
            S.op("dve", lambda e: e.memset(Sr[NR - 1][:], 0.0), writes=t_Sr[NR - 1])
            S.op("pool", lambda e: e.memset(Sbr[NR - 1][:], 0.0), writes=[t_Sbr[NR - 1]])
            seq = [L, L + 128] + [i * 128 for i in range(L // 128)]
            if d_ == 1:
                seq = [L + 128, L] + [i * 128 for i in reversed(range(L // 128))]
            loads(d_, seq[0], 0)
            if len(seq) > 1:
                loads(d_, seq[1], 1)
            prologue(d_, 0, 0)
            for n_, t0 in enumerate(seq):
                if n_ + 2 < len(seq):
                    loads(d_, seq[n_ + 2], (n_ + 2) % 3)
                if n_ + 1 < len(seq):
                    prologue(d_, (n_ + 1) % 3, (n_ + 1) % 2)
                chain(d_, t0, n_ % 3, n_ % 2)
            if d_ == 0:
                S.full_barrier()
        arelease(m0)

    def attn_phase(li, qtiles):
        m0 = amark()
        kr2 = [sb(f"a_kr{i}", [128, NT], BF16) for i in range(2)]
        t_kr = Tok()
        kn = [sb(f"a_kn{i}", [128, NT], BF16) for i in range(2)]
        vh = [sb(f"a_vh{i}", [128, NT // 128, 128], BF16) for i in range(2)]
        t_kv = [Tok(), Tok()]
        qn = [sb(f"a_qn{i}", [128, T], BF16) for i in range(2)]
        qr = [sb(f"a_qr{i}", [128, T], BF16) for i in range(2)]
        t_q = [Tok(), Tok()]
        NP = 6
        pT = [sb(f"a_pT{i}", [128, T], BF16) for i in range(NP)]
        t_pT = [Tok() for _ in range(NP)]
        accd = [sb(f"a_accd{i}", [128, T], F32) for i in range(2)]
        accp = [sb(f"a_accp{i}", [128, T], F32) for i in range(2)]
        t_accd = [Tok(), Tok()]
        t_accp = [Tok(), Tok()]
        dhi = [sb(f"a_dhi{i}", [128, T], BF16) for i in range(2)]
        dlo = [sb(f"a_dlo{i}", [128, T], BF16) for i in range(2)]
        t_dh = [Tok(), Tok()]
        rden = sb("a_rden", [128, T], F32)
        t_rden = Tok()
        cst = [sb(f"a_cst{i}", [128, T], BF16) for i in range(2)]
        t_cst = [Tok(), Tok()]
        SCALE = 192.0 ** -0.5
        st = {"q": 0, "p": 0, "s": 0, "o": 0}

        S.op("pool", lambda e: e.memset(kr2[0][64:128, :], 0.0), writes=[t_kr])
        S.op("pool", lambda e: e.memset(kr2[1][0:64, :], 0.0), writes=[t_kr])
        S.dma("sp", lambda e: e.dma_start(out=kr2[0][0:64, :], in_=KRT[0:64, :]), writes=[t_kr])
        S.dma("sp", lambda e: e.dma_start(out=kr2[1][64:128, :], in_=KRT[64:128, :]), writes=[t_kr])

        def load_head(h, i):
            S.dma("sp", lambda e: e.dma_start(out=kn[i][:], in_=KNT[h * 128:(h + 1) * 128, :]), writes=[t_kv[i]])
            S.dma("sp", lambda e: e.dma_start(out=vh[i][:], in_=VM[h]), writes=[t_kv[i]], join=True)

        def load_q(h, t0, tw):
            i = st["q"] % 2
            st["q"] += 1
            S.dma("sp", lambda e: e.dma_start(out=qn[i][:, 0:tw], in_=QNT[h * 128:(h + 1) * 128, t0:t0 + tw]), writes=[t_q[i]])
            S.dma("sp", lambda e: e.dma_start(out=qr[i][:, 0:tw], in_=QRT[(h // 2) * 128:(h // 2 + 1) * 128, t0:t0 + tw]), writes=[t_q[i]], join=True)
            return i

        def attend(h, hi, t0, tw, qi):
            hp = h % 2
            chunks = list(range(NT // 128)) if t0 < L else list(range(L // 128, NT // 128))
            oset = st["o"] % 2
            st["o"] += 1
            B_O, B_D = 4 + oset, 6 + oset
            nch = len(chunks)
            sbank = {}
            a_d, a_p = accd[oset], accp[oset]
            t_ad, t_ap = t_accd[oset], t_accp[oset]
            used = {"dve": False, "pool": False, "pe": False}

            def qk(ci):
                c = chunks[ci]
                pb = st["s"] % 4
                st["s"] += 1
                sbank[ci] = pb

                def mm(e):
                    e.matmul(psum[pb][:, 0:tw], lhsT=kn[hi][:, c * 128:(c + 1) * 128], rhs=qn[qi][:, 0:tw], start=True, stop=False)
                    return e.matmul(psum[pb][:, 0:tw], lhsT=kr2[hp][:, c * 128:(c + 1) * 128], rhs=qr[qi][:, 0:tw], start=False, stop=True)
                S.op("pe", mm, reads=[t_kv[hi], t_kr, t_q[qi]], writes=[t_ps[pb]])

            def pv(ci):
                c = chunks[ci]
                pb = sbank[ci]
                pi = st["p"] % NP
                st["p"] += 1
                S.op("act", lambda e: e.activation(out=pT[pi][:, 0:tw], in_=psum[pb][:, 0:tw], func=AF.Exp, scale=SCALE),
                     reads=[t_ps[pb]], writes=[t_pT[pi]])
                S.op("pe", lambda e: e.matmul(psum[B_O][:, 0:tw], lhsT=vh[hi][:, c, :], rhs=pT[pi][:, 0:tw], start=(ci == 0), stop=(ci == nch - 1)),
                     reads=[t_kv[hi], t_pT[pi]], writes=[t_ps[B_O]])
                if ci % 4 == 3:
                    first = not used["pe"]
                    used["pe"] = True
                    S.op("pe", lambda e: e.matmul(psum[B_D][:, 0:tw], lhsT=ones_bf[:], rhs=pT[pi][:, 0:tw], start=first, stop=False),
                         reads=[t_pT[pi], t_const], writes=[t_ps[B_D]])
                    return
                eng = "dve"
                a_, t_a_ = a_d, t_ad
                if not used[eng]:
                    used[eng] = True
                    S.op(eng, lambda e: e.tensor_copy(out=a_[:, 0:tw], in_=pT[pi][:, 0:tw]), reads=[t_pT[pi]], writes=[t_a_])
                else:
                    S.op(eng, lambda e: e.tensor_tensor(out=a_[:, 0:tw], in0=a_[:, 0:tw], in1=pT[pi][:, 0:tw], op=ALU.add),
                         reads=[t_pT[pi], t_a_], writes=[t_a_])
            LOOK = 2
            for ci in range(min(LOOK, nch)):
                qk(ci)
            for ci in range(nch):
                if ci + LOOK < nch:
                    qk(ci + LOOK)
                pv(ci)
            if used["pool"]:
                S.op("dve", lambda e: e.tensor_tensor(out=a_d[:, 0:tw], in0=a_d[:, 0:tw], in1=a_p[:, 0:tw], op=ALU.add),
                     reads=[t_ad, t_ap], writes=[t_ad])
            S.op("dve", lambda e: e.tensor_copy(out=dhi[oset][:, 0:tw], in_=a_d[:, 0:tw]), reads=[t_ad], writes=[t_dh[oset]])
            S.op("dve", lambda e: e.tensor_tensor(out=dlo[oset][:, 0:tw], in0=a_d[:, 0:tw], in1=dhi[oset][:, 0:tw], op=ALU.subtract),
                 reads=[t_ad, t_dh[oset]], writes=[t_dh[oset]])

            def mmd(e):
                e.matmul(psum[B_D][:, 0:tw], lhsT=ones_bf[:], rhs=dhi[oset][:, 0:tw], start=(not used["pe"]), stop=False)
                return e.matmul(psum[B_D][:, 0:tw], lhsT=ones_bf[:], rhs=dlo[oset][:, 0:tw], start=False, stop=True)
            S.op("pe", mmd, reads=[t_dh[oset], t_const], writes=[t_ps[B_D]])
            S.op("dve", lambda e: e.reciprocal(out=rden[:, 0:tw], in_=psum[B_D][:, 0:tw]), reads=[t_ps[B_D]], writes=[t_rden])
            ci_ = oset
            S.op("dve", lambda e: e.tensor_tensor(out=cst[ci_][:, 0:tw], in0=psum[B_O][:, 0:tw], in1=rden[:, 0:tw], op=ALU.mult),
                 reads=[t_ps[B_O], t_rden], writes=[t_cst[ci_]])
            S.dma("sp", lambda e: e.dma_start(out=CT[h * 128:(h + 1) * 128, t0:t0 + tw], in_=cst[ci_][:, 0:tw]),
                  reads=[t_cst[ci_]], writes=[])

        load_head(0, 0)
        work = [(h, t0, tw) for h in range(8) for (t0, tw) in qtiles]
        nq = load_q(*work[0])
        for wi_, (h, t0, tw) in enumerate(work):
            hi = h % 2
            if t0 == qtiles[0][0] and h + 1 < 8:
                load_head(h + 1, 1 - hi)
            qi = nq
            if wi_ + 1 < len(work):
                nq = load_q(*work[wi_ + 1])
            attend(h, hi, t0, tw, qi)
        arelease(m0)

    def merge_phase(li, tile_list):
        m0 = amark()
        anb = [sb(f"m_an{i}", [128, 8, T], BF16) for i in range(2)]
        ctb = [sb(f"m_ct{i}", [128, 8, T], BF16) for i in range(2)]
        utb = [sb(f"m_ut{i}", [128, 8, T + 2], BF16) for i in range(2)]
        t_in = [Tok(), Tok()]
        bnb = sb("m_bn", [128, 8, T], BF16)
        t_bn = Tok()
        mbb = sb("m_mb", [128, 8, T], BF16)
        t_mb = Tok()
        gts = [sb(f"m_g{i}", [128, T], F32) for i in range(3)]
        t_g = [Tok() for _ in range(3)]
        acc = [sb(f"m_acc{i}", [128, T], F32) for i in range(2)]
        t_acc = [Tok(), Tok()]
        tt_ = [sb(f"m_t{i}", [128, T], F32) for i in range(2)]
        t_tt = [Tok(), Tok()]
        cvt = [sb(f"m_cv{i}", [128, T], F32) for i in range(2)]
        t_cv = [Tok(), Tok()]
        st = {"in": 0, "g": 0, "acc": 0, "t": 0, "cv": 0}

        def load_in(t0, tw):
            i = st["in"] % 2
            st["in"] += 1
            S.dma("sp", lambda e: e.dma_start(out=anb[i][:, :, 0:tw], in_=AN[:, t0:t0 + tw].rearrange("(k p) t -> p k t", p=128)),
                  writes=[t_in[i]])
            S.dma("sp", lambda e: e.dma_start(out=ctb[i][:, :, 0:tw], in_=CT[:, t0:t0 + tw].rearrange("(k p) t -> p k t", p=128)),
                  writes=[t_in[i]], join=True)
            first = (t0 == 0 or t0 == L)
            lastt = (t0 + tw == L or t0 + tw == NT)
            lo = t0 if first else t0 - 1
            hi_ = t0 + tw if lastt else t0 + tw + 1
            o0 = 1 if first else 0
            S.dma("sp", lambda e: e.dma_start(out=utb[i][:, :, o0:o0 + hi_ - lo], in_=UT[:, lo:hi_].rearrange("(k p) t -> p k t", p=128)),
                  writes=[t_in[i]], join=True)
            if first:
                S.op("pool", lambda e: e.memset(utb[i][:, :, 0:1], 0.0), writes=[t_in[i]])
            if lastt:
                S.op("pool", lambda e: e.memset(utb[i][:, :, tw + 1:tw + 2], 0.0), writes=[t_in[i]])
            return i

        def m4_tile(t0, tw, xi, ii, prefetch, pending):
            which = 0 if t0 < L else 1
            an_, ct_, ut_ = anb[ii], ctb[ii], utb[ii]
            for kc in range(8):
                S.op("act", lambda e, kc=kc: e.activation(out=hb[:, kc, 0:tw], in_=xt[xi][:, kc, 0:tw], func=AF.Identity,
                                                          scale=mods[:, li, 1, 0, kc, which:which + 1],
                                                          bias=mods[:, li, 1, 1, kc, which:which + 1]),
                     reads=[t_xt[xi], t_mod], writes=[t_hb])
            for c_ in range(8):
                vi = st["cv"] % 2
                st["cv"] += 1
                cv_, t_cv_ = cvt[vi], t_cv[vi]
                S.op("pool", lambda e, c_=c_, cv_=cv_: e.tensor_scalar_mul(out=cv_[:, 0:tw], in0=ut_[:, c_, 1:tw + 1],
                                                                           scalar1=convw_sb[:, li, 1, c_:c_ + 1]),
                     reads=[t_in[ii], t_small], writes=[t_cv_])
                S.op("dve", lambda e, c_=c_, cv_=cv_: e.scalar_tensor_tensor(out=cv_[:, 0:tw], in0=ut_[:, c_, 0:tw],
                                                                              scalar=convw_sb[:, li, 0, c_:c_ + 1], in1=cv_[:, 0:tw],
                                                                              op0=ALU.mult, op1=ALU.add),
                     reads=[t_in[ii], t_small, t_cv_], writes=[t_cv_])
                S.op("dve", lambda e, c_=c_, cv_=cv_: e.scalar_tensor_tensor(out=cv_[:, 0:tw], in0=ut_[:, c_, 2:tw + 2],
                                                                              scalar=convw_sb[:, li, 2, c_:c_ + 1], in1=cv_[:, 0:tw],
                                                                              op0=ALU.mult, op1=ALU.add),
                     reads=[t_in[ii], t_small, t_cv_], writes=[t_cv_])

                def ev6(pb, c_=c_, cv_=cv_, t_cv_=t_cv_):
                    S.op("dve", lambda e: e.tensor_tensor(out=bnb[:, c_, 0:tw], in0=psum[pb][:, 0:tw], in1=cv_[:, 0:tw], op=ALU.mult),
                         reads=[t_ps[pb], t_cv_], writes=[t_bn])
                lin(WIS[li, 38 + c_], 8, 128, hb, [t_hb], tw, ev6)
                pending = step(pending)
            srcs = [(an_, [t_in[ii]]), (bnb, [t_bn]), (ct_, [t_in[ii]])]
            def do_m(m):
                ai = st["acc"] % 2
                st["acc"] += 1
                acc_, t_acc_ = acc[ai], t_acc[ai]
                for br in range(3):
                    gi = st["g"] % 3
                    st["g"] += 1
                    g_, t_g_ = gts[gi], t_g[gi]

                    def evg(pb, g_=g_, t_g_=t_g_, br=br):
                        S.op("act", lambda e: e.activation(out=g_[:, 0:tw], in_=psum[pb][:, 0:tw], func=AF.Sigmoid,
                                                           bias=bgate_sb[:, li, br * 8 + m:br * 8 + m + 1]),
                             reads=[t_ps[pb], t_small], writes=[t_g_])
                    lin(WIS[li, 46 + br * 8 + m], 8, 128, hb, [t_hb], tw, evg)

                    def evp(pb, g_=g_, t_g_=t_g_, br=br):
                        if br == 0:
                            S.op("dve", lambda e: e.tensor_tensor(out=acc_[:, 0:tw], in0=psum[pb][:, 0:tw], in1=g_[:, 0:tw], op=ALU.mult),
                                 reads=[t_ps[pb], t_g_], writes=[t_acc_])
                            return
                        ti = st["t"] % 2
                        st["t"] += 1
                        S.op("dve", lambda e: e.tensor_tensor(out=tt_[ti][:, 0:tw], in0=psum[pb][:, 0:tw], in1=g_[:, 0:tw], op=ALU.mult),
                             reads=[t_ps[pb], t_g_], writes=[t_tt[ti]])
                        if br == 1:
                            S.op("pool", lambda e: e.tensor_tensor(out=acc_[:, 0:tw], in0=acc_[:, 0:tw], in1=tt_[ti][:, 0:tw], op=ALU.add),
                                 reads=[t_tt[ti], t_acc_], writes=[t_acc_])
                        else:
                            S.op("pool", lambda e: e.tensor_tensor(out=mbb[:, m, 0:tw], in0=acc_[:, 0:tw], in1=tt_[ti][:, 0:tw], op=ALU.add),
                                 reads=[t_tt[ti], t_acc_], writes=[t_mb])
                    src, stoks = srcs[br]
                    lin(PRJS[li, br, m], 8, 128, src, stoks, tw, evp)
            for m in range(8):
                do_m(m)
                if m < 3:
                    pending = step(pending)
            drain(pending)
            nxt = prefetch()
            for m in range(8):
                def evo(pb, m=m):
                    S.op("dve", lambda e: e.scalar_tensor_tensor(
                        out=xt[xi][:, m, 0:tw], in0=psum[pb][:, 0:tw], scalar=mods[:, li, 1, 2, m, which:which + 1],
                        in1=xt[xi][:, m, 0:tw], op0=ALU.mult, op1=ALU.add),
                        reads=[t_ps[pb], t_xt[xi], t_mod], writes=[t_xt[xi]])
                lin(PRJS[li, 3, m], 8, 128, mbb, [t_mb], tw, evo)
            return nxt, ln_pieces(li, 1, xi, t0, tw, which)

        pend = None
        nx = load_xt(*tile_list[0])
        ni = load_in(*tile_list[0])
        for ti_, (t0, tw) in enumerate(tile_list):
            def prefetch(ti_=ti_):
                if ti_ + 1 < len(tile_list):
                    return load_xt(*tile_list[ti_ + 1]), load_in(*tile_list[ti_ + 1])
                return None, None
            (nx, ni), pend = m4_tile(t0, tw, nx, ni, prefetch, pend)
        drain(pend)
        arelease(m0)

    if debug is not None and "mod" in debug:
        d_ = nc.dram_tensor("dbg_mod", [128, NL * 72 * 2], F32, kind="ExternalOutput").ap()
        S.dma("sp", lambda e: e.dma_start(out=d_, in_=modT[:].rearrange("p l i w -> p (l i w)")), reads=[t_mod])

    snapshot("x0")
    for li, l in enumerate(layers):
        last = (l == DEPTH - 1)
        ffn_phase(li, 0, tiles)
        snapshot("ffn1")
        if not n_ffn_only:
            mixer_in_phase(li, tiles)
            gla_phase(li)
            dsnap("OF", OF, F32)
            dsnap("AN", AN)
            attn_phase(li, tiles[:-1] if last else tiles)
            dsnap("CT", CT)
            merge_phase(li, tiles[:-1] if last else tiles)
            snapshot("mix")
            for nm_, ap_ in [("QT", QT), ("KT", KT), ("RT", RT), ("UT", UT), ("GP", GP), ("VG", VG), ("KRT", KRT),
                             ("QNT", QNT), ("QRT", QRT), ("KNT", KNT), ("VM", VM)]:
                dsnap(nm_, ap_)
        ffn_phase(li, 1, tiles[:-1] if last else tiles)

    t_y = Tok()
    for tt in range((NT if emit_ctx else L) // 128):
        i = tt % 2
        S.dma("sp", lambda e, i=i, tt=tt: e.dma_start(
            out=xT_sb[i][:], in_=XT[:, tt * 128:(tt + 1) * 128].rearrange("(k p) t -> p k t", p=128)),
            reads=[t_XT[tt // 4]], writes=[t_xT[i]])
        for half in range(2):
            pb = 2 * (tt % 2) + half

            def tr(e, i=i, half=half, pb=pb):
                r = None
                for q in range(4):
                    kc = half * 4 + q
                    r = e.transpose(psum[pb][:, q * 128:(q + 1) * 128], xT_sb[i][:, kc, :], ident[:])
                return r
            S.op("pe", tr, reads=[t_xT[i], t_const], writes=[t_ps[pb]])
            ce = "dve" if half == 0 else "act"

            def cp(e, i=i, half=half, pb=pb, ce=ce):
                o = xin_sb[i][:, half * 512:(half + 1) * 512]
                return e.tensor_copy(out=o, in_=psum[pb][:]) if ce == "dve" else e.copy(out=o, in_=psum[pb][:])
            S.op(ce, cp, reads=[t_ps[pb]], writes=[t_xin[i]])
        dst_ = y_out[tt * 128:(tt + 1) * 128, :] if tt < L // 128 else yc_out[(tt - L // 128) * 128:(tt - L // 128 + 1) * 128, :]
        S.dma("sp", lambda e, i=i, dst_=dst_: e.dma_start(out=dst_, in_=xin_sb[i][:]),
              reads=[t_xin[i]], writes=[t_y])
    S.dma_barrier("sp")

    S.emit(es)
    es.close()
    return nc


def make_consts():
    t = np.arange(L)
    rows, cols = t // 64, t % 64
    inv = 10000.0 ** (-np.arange(16, dtype=np.float32) / 16.0)
    cs = np.zeros((128, 2, L), np.float32)
    for p in range(128):
        d_ = p % 64
        pos = rows if d_ < 32 else cols
        ang = pos.astype(np.float32) * inv[d_ % 16]
        cs[p, 0] = np.cos(ang)
        cs[p, 1] = np.sin(ang)
    j = np.arange(128)[:, None]
    i = np.arange(128)[None, :]
    same = (j // 64) == (i // 64)
    gm = np.zeros((128, 4, 128), np.float32)
    gm[:, 0] = np.where(same & (j <= i), -1.0 / 16.0, 0.0)
    gm[:, 1] = np.where(same & (j >= i), -1.0 / 16.0, 0.0)
    gm[:, 2] = np.where(same & (j <= i), 1.0, 0.0)
    gm[:, 3] = np.where(same & (j >= i), 1.0, 0.0)
    return {"ident_in": np.eye(128, dtype=np.float32), "rope_cs": cs, "gmask": gm}


PER_LAYER = ["ada_w", "ada_b", "ln_w", "ln_b", "ffn_w13", "ffn_w2", "w_in", "b_gate", "gla_decay_w", "gla_decay_b",
             "gla_norm", "gla_proj", "conv_w", "conv_proj", "mla_q_norm", "mla_w_uq", "mla_kv_norm", "mla_w_ukv",
             "mla_proj", "w_out"]


def make_in_maps(inputs, layers, n_cores, xs=None, ctxs=None):
    consts = make_consts()
    shared = {k: np.ascontiguousarray(np.asarray(inputs[k])[layers]) for k in PER_LAYER}
    shared["c_ctx"] = np.ascontiguousarray(np.asarray(inputs["c_ctx"]))
    shared.update(consts)
    maps = []
    for b in range(n_cores):
        m = dict(shared)
        m["x"] = np.ascontiguousarray(np.asarray(inputs["x"][b]) if xs is None else xs[b])
        m["ctx"] = np.ascontiguousarray(np.asarray(inputs["ctx"][b]) if ctxs is None else ctxs[b])
        m["c"] = np.ascontiguousarray(np.asarray(inputs["c"][b]))
        maps.append(m)
    return maps


FUSED = True
_PROG = {}


def _prog(layers):
    key = tuple(layers)
    if key not in _PROG:
        _PROG[key] = build_program(list(layers))
    return _PROG[key]


def kernel(**inputs):
    inputs = {k: np.asarray(v) for k, v in inputs.items()}
    n = 8
    if FUSED:
        nc = _prog(range(DEPTH))
        maps = make_in_maps(inputs, list(range(DEPTH)), n)
        res = run_bass_kernel_spmd(nc, maps, core_ids=list(range(n)))
        return np.stack([np.asarray(r["y"]) for r in res.results], axis=0).astype(np.float32)
    xs = [inputs["x"][b] for b in range(n)]
    cs = [inputs["ctx"][b] for b in range(n)]
    for l in range(DEPTH):
        nc = _prog([l]) if l == DEPTH - 1 else _prog([0])
        maps = make_in_maps(inputs, [l], n, xs=xs, ctxs=cs)
        res = run_bass_kernel_spmd(nc, maps, core_ids=list(range(n)))
        xs = [np.asarray(r["y"]) for r in res.results]
        if l != DEPTH - 1:
            cs = [np.asarray(r["yc"]) for r in res.results]
    return np.stack(xs, axis=0).astype(np.float32)
```

```python
import contextlib
import numpy as np
import concourse.bass as bass
import concourse.mybir as mybir
from concourse.bass_utils import run_bass_kernel_spmd

F32 = mybir.dt.float32
BF16 = mybir.dt.bfloat16
AF = mybir.ActivationFunctionType
ALU = mybir.AluOpType
AX = mybir.AxisListType

D = 1024
L = 4096
CTX = 256
NT = L + CTX
DEPTH = 4
DFF = 2816
NJ = DFF // 128
ALPHA = (2.0 * DEPTH) ** 0.25
EPS = 1e-6
INW = 10080

SAME_ENGINE_SYNC = False
EPOCH = 8192
N_DMA_SEMS = 8


class Tok:
    __slots__ = ("ws", "r", "name")

    def __init__(self, name=""):
        self.ws = []
        self.r = []
        self.name = name


class Op:
    __slots__ = ("eng", "idx", "fn", "waits", "sig", "cnt", "is_dma", "dsem", "dcnt", "prev_dma")

    def __init__(self, eng, idx, fn, is_dma):
        self.eng = eng
        self.idx = idx
        self.fn = fn
        self.waits = []
        self.sig = False
        self.cnt = 0
        self.is_dma = is_dma
        self.dsem = None
        self.dcnt = 0
        self.prev_dma = None


class Sched:
    ENGS = ("pe", "act", "dve", "pool", "sp")

    def __init__(self, nc):
        self.nc = nc
        self.ops = {e: [] for e in self.ENGS}
        self.seen = {e: {f: -1 for f in self.ENGS} for e in self.ENGS}
        self.seen_dma = {e: {} for e in self.ENGS}
        self.dma_n = {e: 0 for e in self.ENGS}
        self.dma_last = {}

    def _add(self, eng, fn, reads, writes, is_dma, join=False):
        lst = self.ops[eng]
        op = Op(eng, len(lst), fn, is_dma)
        deps = []
        for t in reads:
            deps.extend(t.ws)
        for t in writes:
            if not join:
                deps.extend(t.ws)
            deps.extend(t.r)
        for p in deps:
            self._dep(op, p)
        for t in writes:
            if join:
                t.ws.append(op)
            else:
                t.ws = [op]
            t.r = []
        for t in reads:
            t.r.append(op)
        if is_dma:
            k = self.dma_n[eng]
            self.dma_n[eng] += 1
            slot = k % N_DMA_SEMS
            op.dsem = (eng, slot)
            op.dcnt = 16 * (k // N_DMA_SEMS + 1)
            op.prev_dma = self.dma_last.get((eng, slot))
            self.dma_last[(eng, slot)] = op
            if op.prev_dma is not None:
                self._dep(op, op.prev_dma)
        lst.append(op)
        return op

    def _dep(self, op, p):
        e = op.eng
        if p.is_dma:
            sd = self.seen_dma[e]
            if sd.get(p.dsem, 0) >= p.dcnt:
                return
            sd[p.dsem] = p.dcnt
            op.waits.append(p)
        else:
            if p.eng == e and not op.is_dma and (e == "pe" or not SAME_ENGINE_SYNC):
                return
            if self.seen[e][p.eng] >= p.idx:
                return
            self.seen[e][p.eng] = p.idx
            p.sig = True
            op.waits.append(p)

    def op(self, eng, fn, reads=(), writes=()):
        return self._add(eng, fn, reads, writes, False)

    def dma_barrier(self, eng, queues=("sp", "act", "pool")):
        op = Op(eng, len(self.ops[eng]), lambda e: e.nop(), False)
        for (q, slot), p in self.dma_last.items():
            if q in queues:
                self._dep(op, p)
        self.ops[eng].append(op)
        return op

    def dma(self, eng, fn, reads=(), writes=(), join=False):
        return self._add(eng, fn, reads, writes, True, join)

    def full_barrier(self):
        lasts = {e: (self.ops[e][-1] if self.ops[e] else None) for e in self.ENGS}
        dl = list(self.dma_last.values())
        for e in self.ENGS:
            op = Op(e, len(self.ops[e]), lambda en: en.nop(), False)
            for f in self.ENGS:
                p = lasts[f]
                if p is None or f == e:
                    continue
                if p.is_dma:
                    q = None
                    for cand in reversed(self.ops[f]):
                        if not cand.is_dma:
                            q = cand
                            break
                    p = q
                if p is not None:
                    self._dep(op, p)
            for p in dl:
                self._dep(op, p)
            self.ops[e].append(op)

    def emit(self, es):
        nc = self.nc
        nsig = {}
        for e in self.ENGS:
            c = 0
            for op in self.ops[e]:
                if (not op.is_dma) and op.sig:
                    c += 1
                    op.cnt = c
            nsig[e] = c
        csems = {}
        for e in self.ENGS:
            n_ep = (nsig[e] + EPOCH - 1) // EPOCH
            csems[e] = [es.enter_context(nc.semaphore(f"c_{e}_{i}")) for i in range(n_ep)]
        dsems = {}
        for e in self.ENGS:
            if self.dma_n[e]:
                for s in range(min(N_DMA_SEMS, self.dma_n[e])):
                    dsems[(e, s)] = es.enter_context(nc.semaphore(f"d_{e}_{s}"))
        fin = es.enter_context(nc.semaphore("fin"))
        block = es.enter_context(nc.Block())
        engs = {"pe": (block.tensor, nc.tensor), "act": (block.scalar, nc.scalar), "dve": (block.vector, nc.vector),
                "pool": (block.gpsimd, nc.gpsimd), "sp": (block.sync, nc.sync)}

        def run(e, eng):
            for op in self.ops[e]:
                for p in op.waits:
                    if p.is_dma:
                        eng.wait_ge(dsems[p.dsem], p.dcnt)
                    else:
                        ep = (p.cnt - 1) // EPOCH
                        eng.wait_ge(csems[p.eng][ep], (p.cnt - 1) % EPOCH + 1)
                inst = op.fn(eng)
                if op.is_dma:
                    inst.then_inc(dsems[op.dsem], 16)
                elif op.sig:
                    ep = (op.cnt - 1) // EPOCH
                    inst.then_inc(csems[e][ep], 1)

        for e in self.ENGS:
            if not self.ops[e]:
                continue
            deco, _ = engs[e]

            def f(eng, e=e):
                run(e, eng)

            deco(f)


class Ctx:
    pass


def build_program(layers, n_ffn_only=False, debug=None):
    nc = bass.Bass("TRN2", target_bir_lowering=False)
    es = contextlib.ExitStack()
    S = Sched(nc)
    NL = len(layers)

    def din(name, shape, dt=F32):
        return nc.dram_tensor(name, list(shape), dt, kind="ExternalInput").ap()

    def dint(name, shape, dt):
        return nc.dram_tensor(name, list(shape), dt, kind="Internal").ap()

    ARENA_WORDS = 52992
    arena = es.enter_context(nc.sbuf_tensor("arena", [128, ARENA_WORDS], F32))
    astate = {"off": 0, "peak": 0}

    def sb(name, shape, dt):
        n = 1
        for d_ in shape[1:]:
            n *= d_
        isz = 4 if dt == F32 else 2
        words = (n * isz + 63) // 64 * 16
        off = astate["off"]
        assert off + words <= ARENA_WORDS, f"arena overflow at {name}: {off}+{words}"
        astate["off"] = off + words
        astate["peak"] = max(astate["peak"], astate["off"])
        v = arena[0:shape[0], off:off + (n * isz + 3) // 4]
        if dt != F32:
            v = v.bitcast(dt)
            v = v[:, 0:n]
        if len(shape) == 3:
            v = v.rearrange("p (a b) -> p a b", a=shape[1])
        elif len(shape) == 4:
            v = v.rearrange("p (a b c) -> p a b c", a=shape[1], b=shape[2])
        elif len(shape) == 5:
            v = v.rearrange("p (a b c d) -> p a b c d", a=shape[1], b=shape[2], c=shape[3])
        elif len(shape) == 6:
            v = v.rearrange("p (a b c d e) -> p a b c d e", a=shape[1], b=shape[2], c=shape[3], d=shape[4])
        return v

    def amark():
        return astate["off"]

    def arelease(m):
        S.full_barrier()
        astate["off"] = m

    x_in = din("x", [L, D])
    ctx_in = din("ctx", [CTX, D])
    c_in = din("c", [D])
    cctx_in = din("c_ctx", [D])
    ada_w = din("ada_w", [NL, D, 9 * D])
    ada_b = din("ada_b", [NL, 9 * D])
    ln_w = din("ln_w", [NL, 3, D])
    ln_b = din("ln_b", [NL, 3, D])
    ffn_w13 = din("ffn_w13", [NL, 2, D, 2 * DFF])
    ffn_w2 = din("ffn_w2", [NL, 2, DFF, D])
    ident_in = din("ident_in", [128, 128])
    w_in = din("w_in", [NL, D, INW])
    b_gate = din("b_gate", [NL, 3 * D])
    gdw_in = din("gla_decay_w", [NL, 2, 16, 512])
    gdb_in = din("gla_decay_b", [NL, 2, 512])
    gnorm_in = din("gla_norm", [NL, D])
    gla_proj = din("gla_proj", [NL, D, D])
    conv_w_in = din("conv_w", [NL, 3, D])
    conv_proj = din("conv_proj", [NL, D, D])
    qnorm_in = din("mla_q_norm", [NL, 512])
    w_uq = din("mla_w_uq", [NL, 512, 1536])
    kvnorm_in = din("mla_kv_norm", [NL, 256])
    w_ukv = din("mla_w_ukv", [NL, 256, 2048])
    mla_proj = din("mla_proj", [NL, D, D])
    w_out = din("w_out", [NL, D, D])
    rope_cs = din("rope_cs", [128, 2, L])
    gmask_in = din("gmask", [128, 4, 128])
    y_out = nc.dram_tensor("y", [L, D], F32, kind="ExternalOutput").ap()
    emit_ctx = (layers[-1] != DEPTH - 1)
    if emit_ctx:
        yc_out = nc.dram_tensor("yc", [CTX, D], F32, kind="ExternalOutput").ap()

    XT = dint("XT", [D, NT], F32)
    W13S = dint("W13S", [NL, 2, 2, NJ, 128, 8 * 128], BF16)
    W2S = dint("W2S", [NL, 2, 8, 128, NJ * 128], BF16)

    NBLK = 73
    WIS = dint("WIS", [NL, NBLK, 128, 8 * 128], BF16)
    WVS = dint("WVS", [NL, 2, 128, 8 * 512], BF16)
    PRJS = dint("PRJS", [NL, 4, 8, 128, 8 * 128], BF16)
    WUQS = dint("WUQS", [NL, 16, 128, 4 * 128], BF16)
    WUKS = dint("WUKS", [NL, 8, 128, 2 * 128], BF16)
    WUKVV = dint("WUKVV", [NL, 128, 2 * 1024], BF16)
    QT = dint("QT", [512, NT], BF16)
    KT = dint("KT", [512, NT], BF16)
    RT = dint("RT", [1024, NT], BF16)
    GP = dint("GP", [2, NT, 512], BF16)
    UT = dint("UT", [1024, NT], BF16)
    KRT = dint("KRT", [128, NT], BF16)
    VG = dint("VG", [NT, 1024], BF16)
    QNT = dint("QNT", [1024, NT], BF16)
    QRT = dint("QRT", [512, NT], BF16)
    KNT = dint("KNT", [1024, NT], BF16)
    VM = dint("VM", [8, 128, NT // 128, 128], BF16)
    OF = dint("OF", [1024, NT], F32)
    AN = dint("AN", [1024, NT], BF16)
    CT = dint("CT", [1024, NT], BF16)

    ident = sb("ident", [128, 128], F32)
    ones_bf = sb("ones_bf", [128, 128], BF16)
    t_const = Tok()
    S.dma("sp", lambda e: e.dma_start(out=ident[:], in_=ident_in), writes=[t_const])
    S.op("pool", lambda e: e.memset(ones_bf[:], 1.0), writes=[t_const])

    psum = [es.enter_context(nc.psum_tensor(f"ps{i}", [128, 512], F32)) for i in range(8)]
    t_ps = [Tok(f"ps{i}") for i in range(8)]

    modT = sb("modT", [128, NL, 72, 2], F32)
    mods = sb("mods", [128, NL, 3, 3, 8, 2], F32)
    lnw_sb = sb("lnw_sb", [128, NL, 3, 8], F32)
    lnb_sb = sb("lnb_sb", [128, NL, 3, 8], F32)
    adab_sb = sb("adab_sb", [128, NL, 72], F32)
    sc_sb = sb("sc_sb", [128, 8, 2], F32)
    sc_tmp = sb("sc_tmp", [128, 8, 2], F32)
    t_small = Tok()
    t_mod = Tok()
    bgate_sb = sb("bgate_sb", [128, NL, 24], F32)
    gnorm_sb = sb("gnorm_sb", [128, NL, 8], F32)
    convw_sb = sb("convw_sb", [128, NL, 3, 8], F32)
    qnorm_sb = sb("qnorm_sb", [128, NL, 4], F32)
    kvnorm_sb = sb("kvnorm_sb", [128, NL, 2], F32)
    gmask_f = sb("gmask_f", [128, 4, 128], F32)
    gmask = sb("gmask", [128, 4, 128], BF16)
    ident_bf = sb("ident_bf", [128, 128], BF16)

    def ld1(out_ap, in_ap):
        def f(e):
            with nc.allow_non_contiguous_dma(reason="tiny vector loads"):
                return e.dma_start(out=out_ap, in_=in_ap)
        S.dma("sp", f, writes=[t_small])

    ld1(sc_tmp[:, :, 0], c_in.rearrange("(k p) -> p k", p=128))
    ld1(sc_tmp[:, :, 1], cctx_in.rearrange("(k p) -> p k", p=128))
    for l in range(NL):
        ld1(adab_sb[:, l, :], ada_b[l].rearrange("(k p) -> p k", p=128))
        for s in range(3):
            ld1(lnw_sb[:, l, s, :], ln_w[l, s].rearrange("(k p) -> p k", p=128))
            ld1(lnb_sb[:, l, s, :], ln_b[l, s].rearrange("(k p) -> p k", p=128))
    for l in range(NL):
        ld1(bgate_sb[:, l, :], b_gate[l].rearrange("(k p) -> p k", p=128))
        ld1(gnorm_sb[:, l, :], gnorm_in[l].rearrange("(k p) -> p k", p=128))
        for k3 in range(3):
            ld1(convw_sb[:, l, k3, :], conv_w_in[l, k3].rearrange("(k p) -> p k", p=128))
        ld1(qnorm_sb[:, l, :], qnorm_in[l].rearrange("(k p) -> p k", p=128))
        ld1(kvnorm_sb[:, l, :], kvnorm_in[l].rearrange("(k p) -> p k", p=128))
    ld1(gmask_f[:], gmask_in)
    S.op("dve", lambda e: e.tensor_copy(out=gmask[:], in_=gmask_f[:]), reads=[t_small], writes=[t_small])
    S.op("dve", lambda e: e.tensor_copy(out=ident_bf[:], in_=ident[:]), reads=[t_const], writes=[t_const])
    S.op("act", lambda e: e.activation(out=sc_sb[:], in_=sc_tmp[:], func=AF.Silu), reads=[t_small], writes=[t_mod])

    m_pre = amark()
    ACW = 1152
    adaw_buf = [sb(f"adaw{i}", [128, 8, ACW], F32) for i in range(2)]
    t_adaw = [Tok(), Tok()]
    piece = 0
    for l in range(NL):
        for pc in range(9 * D // ACW):
            b = piece % 2
            buf = adaw_buf[b]
            src = ada_w[l, :, pc * ACW:(pc + 1) * ACW].rearrange("(k p) c -> p k c", p=128)
            S.dma("sp", lambda e, buf=buf, src=src: e.dma_start(out=buf[:], in_=src), writes=[t_adaw[b]])
            pb = piece % 8
            for mi in range(ACW // 128):
                i = pc * (ACW // 128) + mi

                def mm(e, buf=buf, mi=mi, pb=pb, mcol=mi):
                    r = None
                    for kc in range(8):
                        r = e.matmul(psum[pb][:, mcol * 2:mcol * 2 + 2], lhsT=buf[:, kc, mi * 128:(mi + 1) * 128],
                                     rhs=sc_sb[:, kc, :], start=(kc == 0), stop=(kc == 7))
                    return r
                S.op("pe", mm, reads=[t_adaw[b], t_mod], writes=[t_ps[pb]])
            i0 = pc * (ACW // 128)
            n = ACW // 128

            def ev(e, l=l, i0=i0, n=n, pb=pb):
                return e.tensor_tensor(out=modT[:, l, i0:i0 + n, :],
                                       in0=psum[pb][:, 0:2 * n].rearrange("p (i w) -> p i w", w=2),
                                       in1=adab_sb[:, l, i0:i0 + n].unsqueeze(2).to_broadcast([128, n, 2]),
                                       op=ALU.add)
            S.op("dve", ev, reads=[t_ps[pb], t_small], writes=[t_mod])
            piece += 1
    for l in range(NL):
        for s in range(3):
            base = 3 * s * 8
            S.op("dve", lambda e, l=l, s=s, base=base: e.tensor_scalar_add(
                out=mods[:, l, s, 0], in0=modT[:, l, base + 8:base + 16, :], scalar1=1.0), reads=[t_mod], writes=[t_mod])
            S.op("dve", lambda e, l=l, s=s, base=base: e.tensor_copy(
                out=mods[:, l, s, 1], in_=modT[:, l, base:base + 8, :]), reads=[t_mod], writes=[t_mod])
            gsc = (1.0 / ALPHA) if s == 1 else (0.5 / ALPHA)
            S.op("dve", lambda e, l=l, s=s, base=base, gsc=gsc: e.tensor_scalar_mul(
                out=mods[:, l, s, 2], in0=modT[:, l, base + 16:base + 24, :], scalar1=gsc), reads=[t_mod], writes=[t_mod])

    PW = 512
    PB = {}

    def prep_alloc(tag, stq):
        PB["wld"] = [sb(f"wld{tag}{i}", [128, 8 * PW], F32) for i in range(2)]
        PB["wcv"] = [sb(f"wcv{tag}{i}", [128, 8 * PW], BF16) for i in range(2)]
        PB["t_wld"] = [Tok(), Tok()]
        PB["t_wcv"] = [Tok(), Tok()]
        PB["stq"] = stq

    prep_alloc("u", "sp")
    t_wscr = Tok("wscr")
    prep_n = [0]
    cast_engs = ["act", "dve", "pool"]

    def prep(src, KC, width, dst, mb):
        cw = (8 * PW // KC) // mb * mb
        for c0 in range(0, width, cw):
            w_ = min(cw, width - c0)
            nb = w_ // mb
            i = prep_n[0] % 2
            ce = cast_engs[prep_n[0] % 3]
            prep_n[0] += 1
            ld_v = PB["wld"][i][:, 0:KC * w_].rearrange("p (k c) -> p k c", k=KC)
            S.dma("sp", lambda e, ld_v=ld_v, c0=c0, w_=w_: e.dma_start(
                out=ld_v, in_=src[:, c0:c0 + w_].rearrange("(k p) c -> p k c", p=128)), writes=[PB["t_wld"][i]])
            cv_v = PB["wcv"][i][:, 0:KC * w_].rearrange("p (b k m) -> p b k m", b=nb, k=KC)
            in_v = PB["wld"][i][:, 0:KC * w_].rearrange("p (k b m) -> p b k m", k=KC, b=nb)
            if ce == "act":
                S.op("act", lambda e, cv_v=cv_v, in_v=in_v: e.copy(out=cv_v, in_=in_v), reads=[PB["t_wld"][i]], writes=[PB["t_wcv"][i]])
            else:
                S.op(ce, lambda e, cv_v=cv_v, in_v=in_v: e.tensor_copy(out=cv_v, in_=in_v), reads=[PB["t_wld"][i]], writes=[PB["t_wcv"][i]])
            st_v = PB["wcv"][i][:, 0:KC * w_].rearrange("p (b f) -> p b f", b=nb)
            b0 = c0 // mb
            S.dma(PB["stq"], lambda e, st_v=st_v, b0=b0, nb=nb: e.dma_start(
                out=dst[b0:b0 + nb].rearrange("b p f -> p b f"), in_=st_v), reads=[PB["t_wcv"][i]], writes=[t_wscr])
            yield

    def cast(ce, out_ap, in_ap, neg=False):
        i = None
        if neg:
            if ce == "act":
                return lambda e: e.activation(out=out_ap, in_=in_ap, func=AF.Copy, scale=-1.0)
            return lambda e: e.tensor_scalar_mul(out=out_ap, in0=in_ap, scalar1=-1.0)
        if ce == "act":
            return lambda e: e.copy(out=out_ap, in_=in_ap)
        return lambda e: e.tensor_copy(out=out_ap, in_=in_ap)

    def prep_custom(KC, src_cols, pieces, stores):
        raise NotImplementedError

    def prep_special(src, KC, w_, ops, nout, dst):
        i = prep_n[0] % 2
        prep_n[0] += 1
        ld_v = PB["wld"][i][:, 0:KC * w_].rearrange("p (k c) -> p k c", k=KC)
        S.dma("sp", lambda e: e.dma_start(out=ld_v, in_=src.rearrange("(k p) c -> p k c", p=128)), writes=[PB["t_wld"][i]])
        cv = PB["wcv"][i][:, 0:nout * KC * 128].rearrange("p (b k m) -> p b k m", b=nout, k=KC)
        for n_, (o_ap, i_ap, neg) in enumerate(ops(ld_v, cv)):
            ce = cast_engs[n_ % 3]
            S.op(ce, cast(ce, o_ap, i_ap, neg), reads=[PB["t_wld"][i]], writes=[PB["t_wcv"][i]])
        cv_f = PB["wcv"][i][:, 0:nout * KC * 128].rearrange("p (b f) -> p b f", b=nout)
        S.dma(PB["stq"], lambda e: e.dma_start(out=dst.rearrange("b p f -> p b f"), in_=cv_f),
              reads=[PB["t_wcv"][i]], writes=[t_wscr])
        yield

    def rot_ops(o_blk, i_cols):
        ov = o_blk.rearrange("p k (g two s) -> p k g two s", g=2, two=2)
        iv = i_cols.rearrange("p k (g two s) -> p k g two s", g=2, two=2)
        return [(ov[:, :, :, 0, :], iv[:, :, :, 1, :], True), (ov[:, :, :, 1, :], iv[:, :, :, 0, :], False)]

    def prep_layer(li):
        for f in range(2):
            for ag in range(2):
                yield from prep(ffn_w13[li, f, :, ag * DFF:(ag + 1) * DFF], 8, DFF, W13S[li, f, ag], 128)
            yield from prep(ffn_w2[li, f], NJ, D, W2S[li, f], 128)
        wl = w_in[li]
        yield from prep(wl[:, 0:512], 8, 512, WIS[li, 0:4], 128)
        yield from prep(wl[:, 512:1024], 8, 512, WIS[li, 4:8], 128)
        yield from prep(wl[:, 2048:3072], 8, 1024, WIS[li, 8:16], 128)
        yield from prep(wl[:, 4128:5152], 8, 1024, WIS[li, 16:24], 128)
        yield from prep(wl[:, 5152:6176], 8, 1024, WIS[li, 24:32], 128)
        yield from prep(wl[:, 6176:6688], 8, 512, WIS[li, 32:36], 128)
        yield from prep(wl[:, 6688:6944], 8, 256, WIS[li, 36:38], 128)
        yield from prep(wl[:, 3104:4128], 8, 1024, WIS[li, 38:46], 128)
        yield from prep(wl[:, 7008:10080], 8, 3072, WIS[li, 46:70], 128)
        yield from prep(wl[:, 1024:2048], 8, 1024, WVS[li], 512)
        yield from prep_special(wl[:, 3072:3104], 8, 32, lambda ld, cv: [(cv[:, 0, :, 0:32], ld, False)], 1, WIS[li, 70:71])
        yield from prep_special(wl[:, 6944:7008], 8, 64,
                     lambda ld, cv: [(cv[:, 0, :, 0:64], ld, False), (cv[:, 0, :, 64:128], ld, False)]
                     + rot_ops(cv[:, 1, :, 0:64], ld) + rot_ops(cv[:, 1, :, 64:128], ld), 2, WIS[li, 71:73])
        for pi, pw in enumerate([gla_proj, conv_proj, mla_proj, w_out]):
            yield from prep(pw[li], 8, 1024, PRJS[li, pi], 128)
        for a in range(4):
            def uq_ops(ld, cv):
                ops = [(cv[:, 0, :, :], ld[:, :, 0:128], False), (cv[:, 1, :, :], ld[:, :, 192:320], False),
                       (cv[:, 2, :, 0:64], ld[:, :, 128:192], False), (cv[:, 2, :, 64:128], ld[:, :, 320:384], False)]
                ops += rot_ops(cv[:, 3, :, 0:64], ld[:, :, 128:192]) + rot_ops(cv[:, 3, :, 64:128], ld[:, :, 320:384])
                return ops
            i = prep_n[0] % 2
            prep_n[0] += 1
            ld_v = PB["wld"][i][:, 0:4 * 384].rearrange("p (k c) -> p k c", k=4)
            S.dma("sp", lambda e, ld_v=ld_v, a=a: e.dma_start(
                out=ld_v, in_=w_uq[li, :, a * 384:(a + 1) * 384].rearrange("(k p) c -> p k c", p=128)), writes=[PB["t_wld"][i]])
            cv = PB["wcv"][i][:, 0:4 * 512].rearrange("p (b k m) -> p b k m", b=4, k=4)
            for n_, (o_ap, i_ap, neg) in enumerate(uq_ops(ld_v, cv)):
                ce = cast_engs[n_ % 3]
                S.op(ce, cast(ce, o_ap, i_ap, neg), reads=[PB["t_wld"][i]], writes=[PB["t_wcv"][i]])
            cvf = PB["wcv"][i][:, 0:4 * 512].rearrange("p (b f) -> p b f", b=4)
            S.dma(PB["stq"], lambda e, cvf=cvf, a=a: e.dma_start(out=WUQS[li, 2 * a:2 * a + 2].rearrange("b p f -> p b f"),
                                                        in_=cvf[:, 0:2, :]), reads=[PB["t_wcv"][i]], writes=[t_wscr])
            S.dma(PB["stq"], lambda e, cvf=cvf, a=a: e.dma_start(out=WUQS[li, 8 + a], in_=cvf[:, 2, :]),
                  reads=[PB["t_wcv"][i]], writes=[t_wscr])
            S.dma(PB["stq"], lambda e, cvf=cvf, a=a: e.dma_start(out=WUQS[li, 12 + a], in_=cvf[:, 3, :]),
                  reads=[PB["t_wcv"][i]], writes=[t_wscr])
            yield
        i = prep_n[0] % 2
        prep_n[0] += 1
        ld_v = PB["wld"][i][:, 0:2 * 2048].rearrange("p (k h two m) -> p k h two m", k=2, h=8, two=2)
        ld_full = PB["wld"][i][:, 0:4096].rearrange("p (k c) -> p k c", k=2)
        S.dma("sp", lambda e: e.dma_start(out=ld_full, in_=w_ukv[li].rearrange("(k p) c -> p k c", p=128)), writes=[PB["t_wld"][i]])
        cvk = PB["wcv"][i][:, 0:2048].rearrange("p (h k m) -> p h k m", h=8, k=2)
        cvv = PB["wcv"][i][:, 2048:4096].rearrange("p (k h m) -> p k h m", k=2, h=8)
        for kk in range(2):
            ce = cast_engs[kk]
            S.op(ce, cast(ce, cvk[:, :, kk, :], ld_v[:, kk, :, 0, :]), reads=[PB["t_wld"][i]], writes=[PB["t_wcv"][i]])
            ce = cast_engs[kk + 1]
            S.op(ce, cast(ce, cvv[:, kk, :, :], ld_v[:, kk, :, 1, :]), reads=[PB["t_wld"][i]], writes=[PB["t_wcv"][i]])
        cvk_f = PB["wcv"][i][:, 0:2048].rearrange("p (b f) -> p b f", b=8)
        cvv_f = PB["wcv"][i][:, 2048:4096]
        S.dma(PB["stq"], lambda e: e.dma_start(out=WUKS[li].rearrange("b p f -> p b f"), in_=cvk_f),
              reads=[PB["t_wcv"][i]], writes=[t_wscr])
        S.dma(PB["stq"], lambda e: e.dma_start(out=WUKVV[li], in_=cvv_f),
              reads=[PB["t_wcv"][i]], writes=[t_wscr])

    drain_ = lambda g: [None for _ in g]
    drain_(prep_layer(0))

    arelease(m_pre)

    xin_sb = [sb(f"xin{i}", [128, D], F32) for i in range(2)]
    xT_sb = [sb(f"xTs{i}", [128, 8, 128], F32) for i in range(2)]
    t_xin = [Tok(), Tok()]
    t_xT = [Tok(), Tok()]
    t_XT = [Tok(f"XT{i}") for i in range(9)]
    for tt in range(NT // 128):
        i = tt % 2
        src = x_in[tt * 128:(tt + 1) * 128, :] if tt < L // 128 else ctx_in[(tt - L // 128) * 128:(tt - L // 128 + 1) * 128, :]
        S.dma("sp", lambda e, i=i, src=src: e.dma_start(out=xin_sb[i][:], in_=src), writes=[t_xin[i]])
        for half in range(2):
            pb = 2 * (tt % 2) + half

            def tr(e, i=i, half=half, pb=pb):
                r = None
                for q in range(4):
                    kc = half * 4 + q
                    r = e.transpose(psum[pb][:, q * 128:(q + 1) * 128], xin_sb[i][:, kc * 128:(kc + 1) * 128], ident[:])
                return r
            S.op("pe", tr, reads=[t_xin[i], t_const], writes=[t_ps[pb]])
            ce = "dve" if half == 0 else "act"

            def cp(e, i=i, half=half, pb=pb, ce=ce):
                o = xT_sb[i][:, half * 4:half * 4 + 4, :]
                src_ = psum[pb][:].rearrange("p (k t) -> p k t", k=4)
                return e.tensor_copy(out=o, in_=src_) if ce == "dve" else e.copy(out=o, in_=src_)
            S.op(ce, cp, reads=[t_ps[pb]], writes=[t_xT[i]])
        S.dma("sp", lambda e, i=i, tt=tt: e.dma_start(
            out=XT[:, tt * 128:(tt + 1) * 128].rearrange("(k p) t -> p k t", p=128), in_=xT_sb[i][:]),
            reads=[t_xT[i]], writes=[t_XT[tt // 4]])
    S.dma_barrier("sp")

    S.op("pe", lambda e: e.matmul(psum[0][0:32, 0:64], lhsT=ones_bf[:, 0:32], rhs=ones_bf[:, 0:64], start=True, stop=True),
         reads=[t_const], writes=[t_ps[0]])

    T = 512
    xt = [sb(f"xt{i}", [128, 8, T], F32) for i in range(2)]
    t_xt = [Tok(), Tok()]
    hb = sb("hb", [128, 8, T], BF16)
    t_hb = Tok()
    ub = sb("ub", [128, 8, T], BF16)
    usq = sb("usq", [128, 8, T], BF16)
    t_ub = Tok()
    t_usq = Tok()
    stat = sb("stat", [128, 4, T], F32)
    t_stat = Tok()
    tmpn = [sb(f"tmpn{i}", [128, T], F32) for i in range(2)]
    t_tmpn = [Tok(), Tok()]
    NWB = 6
    wblk = [sb(f"wblk{i}", [128, 8 * 128], BF16) for i in range(NWB)]
    t_wblk = [Tok() for _ in range(NWB)]
    cnt = {"xt": 0, "w13": 0, "w2": 0, "sil": 0, "tmpn": 0, "ps": 0, "wblk": 0}
    ffn_bufs = {}

    def ffn_alloc():
        ffn_bufs["hid"] = sb("hid", [128, NJ, T], BF16)
        ffn_bufs["sil"] = [sb(f"sil{i}", [128, T], BF16) for i in range(2)]
        ffn_bufs["w13b"] = [sb(f"w13b{i}", [128, 2, 8, 128], BF16) for i in range(3)]
        ffn_bufs["w2b"] = [sb(f"w2b{i}", [128, NJ, 128], BF16) for i in range(2)]

    t_hid = [Tok() for _ in range(NJ)]
    t_sil = [Tok(), Tok()]
    t_w13b = [Tok() for _ in range(3)]
    t_w2b = [Tok(), Tok()]

    tiles = [(i * T, T) for i in range(L // T)] + [(L, CTX)]

    dbg = {}

    def dump(name, ap, reads, dt=BF16):
        if debug is None or name not in debug or name in dbg:
            return
        dbg[name] = 1
        shp = list(ap.shape)
        d_ = nc.dram_tensor("dbg_" + name, shp, dt, kind="ExternalOutput").ap()
        S.dma("sp", lambda e: e.dma_start(out=d_, in_=ap), reads=reads)

    def dsnap(name, ap, dt=BF16):
        if debug is None or name not in debug:
            return
        S.dma_barrier("sp")
        d_ = nc.dram_tensor("dbg_" + name, list(ap.shape), dt, kind="ExternalOutput").ap()
        S.dma("sp", lambda e: e.dma_start(out=d_, in_=ap))
        S.dma_barrier("sp")

    def snapshot(name):
        if debug is None or name not in debug:
            return
        S.dma_barrier("sp")
        d_ = nc.dram_tensor("dbg_" + name, [D, NT], F32, kind="ExternalOutput").ap()
        S.dma("sp", lambda e: e.dma_start(out=d_, in_=XT))
        S.dma_barrier("sp")

    def load_xt(t0, tw):
        i = cnt["xt"] % 2
        cnt["xt"] += 1
        S.dma("sp", lambda e: e.dma_start(out=xt[i][:, :, 0:tw],
                                          in_=XT[:, t0:t0 + tw].rearrange("(k p) t -> p k t", p=128)),
              reads=[t_XT[t0 // 512]], writes=[t_xt[i]])
        return i

    def ln_pieces(li, s, xi, t0, tw, which):
        S.op("pool", lambda e: e.tensor_copy(out=ub[:, :, 0:tw], in_=xt[xi][:, :, 0:tw]), reads=[t_xt[xi]], writes=[t_ub])
        S.op("act", lambda e: e.activation(out=usq[:, :, 0:tw], in_=xt[xi][:, :, 0:tw], func=AF.Square),
             reads=[t_xt[xi]], writes=[t_usq])
        yield
        p1, p2 = 6, 7
        eps_eff = EPS / (ALPHA * ALPHA)

        def mm1(e):
            r = None
            for kc in range(8):
                r = e.matmul(psum[p1][:, 0:tw], lhsT=ones_bf[:], rhs=ub[:, kc, 0:tw], start=(kc == 0), stop=(kc == 7))
            return r

        def mm2(e):
            r = None
            for kc in range(8):
                r = e.matmul(psum[p2][:, 0:tw], lhsT=ones_bf[:], rhs=usq[:, kc, 0:tw], start=(kc == 0), stop=(kc == 7))
            return r
        S.op("pe", mm1, reads=[t_ub, t_const], writes=[t_ps[p1]])
        S.op("pe", mm2, reads=[t_usq, t_const], writes=[t_ps[p2]])
        S.op("act", lambda e: e.activation(out=stat[:, 0, 0:tw], in_=psum[p1][:, 0:tw], func=AF.Copy, scale=1.0 / D),
             reads=[t_ps[p1]], writes=[t_stat])
        S.op("dve", lambda e: e.tensor_tensor(out=stat[:, 1, 0:tw], in0=stat[:, 0, 0:tw], in1=stat[:, 0, 0:tw], op=ALU.mult),
             reads=[t_stat], writes=[t_stat])
        S.op("dve", lambda e: e.scalar_tensor_tensor(out=stat[:, 1, 0:tw], in0=psum[p2][:, 0:tw], scalar=1.0 / D,
                                                     in1=stat[:, 1, 0:tw], op0=ALU.mult, op1=ALU.subtract),
             reads=[t_stat, t_ps[p2]], writes=[t_stat])
        S.op("dve", lambda e: e.tensor_scalar_add(out=stat[:, 1, 0:tw], in0=stat[:, 1, 0:tw], scalar1=eps_eff),
             reads=[t_stat], writes=[t_stat])
        S.op("act", lambda e: e.activation(out=stat[:, 1, 0:tw], in_=stat[:, 1, 0:tw], func=AF.Sqrt),
             reads=[t_stat], writes=[t_stat])
        S.op("dve", lambda e: e.reciprocal(out=stat[:, 2, 0:tw], in_=stat[:, 1, 0:tw]),
             reads=[t_stat], writes=[t_stat])
        S.op("dve", lambda e: e.scalar_tensor_tensor(out=stat[:, 3, 0:tw], in0=stat[:, 0, 0:tw], scalar=-1.0,
                                                     in1=stat[:, 2, 0:tw], op0=ALU.mult, op1=ALU.mult),
             reads=[t_stat], writes=[t_stat])
        yield
        for kc in range(8):
            if kc:
                yield
            ti = cnt["tmpn"] % 2
            cnt["tmpn"] += 1
            S.op("dve", lambda e, kc=kc, ti=ti: e.tensor_tensor(out=tmpn[ti][:, 0:tw], in0=xt[xi][:, kc, 0:tw],
                                                                in1=stat[:, 2, 0:tw], op=ALU.mult),
                 reads=[t_xt[xi], t_stat], writes=[t_tmpn[ti]])
            S.op("pool", lambda e, kc=kc, ti=ti: e.tensor_tensor(out=tmpn[ti][:, 0:tw], in0=tmpn[ti][:, 0:tw],
                                                                 in1=stat[:, 3, 0:tw], op=ALU.add),
                 reads=[t_tmpn[ti], t_stat], writes=[t_tmpn[ti]])
            S.op("act", lambda e, kc=kc, ti=ti: e.activation(out=xt[xi][:, kc, 0:tw], in_=tmpn[ti][:, 0:tw], func=AF.Identity,
                                                             scale=lnw_sb[:, li, s, kc:kc + 1], bias=lnb_sb[:, li, s, kc:kc + 1]),
                 reads=[t_tmpn[ti], t_small], writes=[t_xt[xi]])
        S.dma("sp", lambda e: e.dma_start(out=XT[:, t0:t0 + tw].rearrange("(k p) t -> p k t", p=128), in_=xt[xi][:, :, 0:tw]),
              reads=[t_xt[xi]], writes=[t_XT[t0 // 512]])

    def layer_norm_store(li, s, xi, t0, tw, which):
        for _ in ln_pieces(li, s, xi, t0, tw, which):
            pass

    def drain(gen):
        if gen is not None:
            for _ in gen:
                pass

    def step(gen):
        if gen is not None:
            try:
                next(gen)
            except StopIteration:
                return None
        return gen

    def ffn_tile(li, f, s, t0, tw, xi, prefetch, pending):
        which = 0 if t0 < L else 1
        hid, sil, w13b, w2b = ffn_bufs["hid"], ffn_bufs["sil"], ffn_bufs["w13b"], ffn_bufs["w2b"]
        for kc in range(8):
            S.op("act", lambda e, kc=kc: e.activation(out=hb[:, kc, 0:tw], in_=xt[xi][:, kc, 0:tw], func=AF.Identity,
                                                      scale=mods[:, li, s, 0, kc, which:which + 1],
                                                      bias=mods[:, li, s, 1, kc, which:which + 1]),
                 reads=[t_xt[xi], t_mod], writes=[t_hb])
        for j in range(NJ):
            wi = cnt["w13"] % 3
            cnt["w13"] += 1
            for ag in range(2):
                S.dma("sp", lambda e, wi=wi, ag=ag, j=j: e.dma_start(
                    out=w13b[wi][:, ag].rearrange("p k m -> p (k m)"), in_=W13S[li, f, ag, j]),
                    writes=[t_w13b[wi]], join=(ag == 1))
            pa = (cnt["ps"] % 3) * 2
            cnt["ps"] += 1
            for ag in range(2):
                def mm(e, wi=wi, ag=ag, pa=pa):
                    r = None
                    for kc in range(8):
                        r = e.matmul(psum[pa + ag][:, 0:tw], lhsT=w13b[wi][:, ag, kc, :], rhs=hb[:, kc, 0:tw],
                                     start=(kc == 0), stop=(kc == 7))
                    return r
                S.op("pe", mm, reads=[t_w13b[wi], t_hb], writes=[t_ps[pa + ag]])
            si = cnt["sil"] % 2
            cnt["sil"] += 1
            S.op("act", lambda e, si=si, pa=pa: e.activation(out=sil[si][:, 0:tw], in_=psum[pa][:, 0:tw], func=AF.Silu),
                 reads=[t_ps[pa]], writes=[t_sil[si]])
            S.op("dve", lambda e, si=si, pa=pa, j=j: e.tensor_tensor(out=hid[:, j, 0:tw], in0=psum[pa + 1][:, 0:tw],
                                                                     in1=sil[si][:, 0:tw], op=ALU.mult),
                 reads=[t_ps[pa + 1], t_sil[si]], writes=[t_hid[j]])
            if j >= 1:
                pending = step(pending)
            if j % 2 == 0:
                BG["g"] = step(BG["g"])
        drain(pending)
        dump("hb", hb[:], [t_hb])
        dump("hid", hid[:], t_hid)
        nxt = prefetch()
        for m in range(8):
            wi = cnt["w2"] % 2
            cnt["w2"] += 1
            S.dma("sp", lambda e, wi=wi, m=m: e.dma_start(out=w2b[wi][:].rearrange("p j n -> p (j n)"), in_=W2S[li, f, m]),
                  writes=[t_w2b[wi]])
            pa = (cnt["ps"] % 3) * 2
            cnt["ps"] += 1

            def mm(e, wi=wi, pa=pa):
                r = None
                for j in range(NJ):
                    r = e.matmul(psum[pa][:, 0:tw], lhsT=w2b[wi][:, j, :], rhs=hid[:, j, 0:tw],
                                 start=(j == 0), stop=(j == NJ - 1))
                return r
            S.op("pe", mm, reads=[t_w2b[wi]] + t_hid, writes=[t_ps[pa]])
            S.op("dve", lambda e, m=m, pa=pa: e.scalar_tensor_tensor(
                out=xt[xi][:, m, 0:tw], in0=psum[pa][:, 0:tw], scalar=mods[:, li, s, 2, m, which:which + 1],
                in1=xt[xi][:, m, 0:tw], op0=ALU.mult, op1=ALU.add),
                reads=[t_ps[pa], t_xt[xi], t_mod], writes=[t_xt[xi]])
        dump("u", xt[xi][:], [t_xt[xi]], F32)
        return nxt, ln_pieces(li, s, xi, t0, tw, which)

    BG = {"g": None}

    def ffn_phase(li, f, tile_list):
        s = 0 if f == 0 else 2
        m0 = amark()
        ffn_alloc()
        if f == 1 and li + 1 < NL:
            prep_alloc(f"b{li}", "act")
            BG["g"] = prep_layer(li + 1)
        pend = None
        nxt = load_xt(*tile_list[0])
        for ti_, (t0, tw) in enumerate(tile_list):
            def prefetch(ti_=ti_):
                if ti_ + 1 < len(tile_list):
                    return load_xt(*tile_list[ti_ + 1])
                return None
            nxt, pend = ffn_tile(li, f, s, t0, tw, nxt, prefetch, pend)
        drain(pend)
        drain(BG["g"])
        BG["g"] = None
        arelease(m0)

    def lin(wap, KC, M, rhs, rtoks, tw, evac):
        wi = cnt["wblk"] % NWB
        cnt["wblk"] += 1
        S.dma("sp", lambda e: e.dma_start(out=wblk[wi][:, 0:KC * 128], in_=wap), writes=[t_wblk[wi]])
        pb = cnt["ps"] % 6
        cnt["ps"] += 1
        wv_ = wblk[wi][:, 0:KC * 128].rearrange("p (k m) -> p k m", k=KC)

        def mm(e):
            r = None
            for kc in range(KC):
                r = e.matmul(psum[pb][0:M, 0:tw], lhsT=wv_[:, kc, 0:M], rhs=rhs[:, kc, 0:tw], start=(kc == 0), stop=(kc == KC - 1))
            return r
        S.op("pe", mm, reads=[t_wblk[wi]] + list(rtoks), writes=[t_ps[pb]])
        evac(pb)
        return pb

    def ev_copy(eng, out_ap, wtok, M, tw, scale=None, func=None):
        def evac(pb):
            src = psum[pb][0:M, 0:tw]
            if eng == "act":
                if func is not None or scale is not None:
                    S.op("act", lambda e: e.activation(out=out_ap, in_=src, func=(func or AF.Copy), scale=(scale or 1.0)),
                         reads=[t_ps[pb]], writes=[wtok])
                else:
                    S.op("act", lambda e: e.copy(out=out_ap, in_=src), reads=[t_ps[pb]], writes=[wtok])
            else:
                if scale is not None:
                    S.op(eng, lambda e: e.tensor_scalar_mul(out=out_ap, in0=src, scalar1=scale), reads=[t_ps[pb]], writes=[wtok])
                else:
                    S.op(eng, lambda e: e.tensor_copy(out=out_ap, in_=src), reads=[t_ps[pb]], writes=[wtok])
        return evac

    m1b = {}

    def m1_alloc():
        m1b["wv"] = sb("wv", [128, 8, 1024], BF16)
        m1b["wkvv"] = sb("wkvv", [128, 2, 1024], BF16)
        m1b["dwst"] = sb("dwst", [33, 2, 512], F32)
        m1b["dw"] = sb("dw", [33, 2, 512], BF16)
        m1b["glrT"] = sb("glrT", [33, T], BF16)
        m1b["st8"] = [sb(f"st8_{i}", [128, 8, T], BF16) for i in range(2)]
        m1b["z7"] = [sb(f"z7_{i}", [128, T], F32) for i in range(2)]
        m1b["cqf"] = sb("cqf", [128, 4, T], F32)
        m1b["cqn"] = sb("cqn", [128, 4, T], BF16)
        m1b["ckvn"] = sb("ckvn", [128, 2, T], BF16)
        m1b["cs"] = sb("cs", [128, 2, T], F32)
        m1b["etmp"] = [sb(f"etmp{i}", [128, 512], F32) for i in range(2)]
        m1b["gst"] = sb("gst", [128, 2, 4, 512], BF16)
        m1b["vst"] = sb("vst", [128, 4, 1024], BF16)
        m1b["rtmp"] = [sb(f"rtmp{i}", [128, T], F32) for i in range(2)]
        for k_ in ["wv", "wkvv", "dw", "glrT", "cqf", "cqn", "ckvn", "cs", "gst", "vst"]:
            m1b["t_" + k_] = Tok(k_)
        m1b["t_st8"] = [Tok(), Tok()]
        m1b["t_z7"] = [Tok(), Tok()]
        m1b["t_etmp"] = [Tok(), Tok()]
        m1b["t_rtmp"] = [Tok(), Tok()]
        m1b["n"] = {"st8": 0, "z7": 0, "etmp": 0, "rtmp": 0}

    def rms_apply(src_f, KC, tw, nfeat, norm_ap_fn, out_bf, t_src, t_out):
        S.op("act", lambda e: e.activation(out=usq[:, 0:KC, 0:tw], in_=src_f[:, 0:KC, 0:tw], func=AF.Square),
             reads=[t_src], writes=[t_usq])
        p2 = 7

        def mm2(e):
            r = None
            for kc in range(KC):
                r = e.matmul(psum[p2][:, 0:tw], lhsT=ones_bf[:], rhs=usq[:, kc, 0:tw], start=(kc == 0), stop=(kc == KC - 1))
            return r
        S.op("pe", mm2, reads=[t_usq, t_const], writes=[t_ps[p2]])
        S.op("dve", lambda e: e.tensor_scalar(out=stat[:, 1, 0:tw], in0=psum[p2][:, 0:tw], scalar1=1.0 / nfeat, scalar2=EPS,
                                              op0=ALU.mult, op1=ALU.add), reads=[t_ps[p2]], writes=[t_stat])
        S.op("act", lambda e: e.activation(out=stat[:, 1, 0:tw], in_=stat[:, 1, 0:tw], func=AF.Sqrt), reads=[t_stat], writes=[t_stat])
        S.op("dve", lambda e: e.reciprocal(out=stat[:, 2, 0:tw], in_=stat[:, 1, 0:tw]), reads=[t_stat], writes=[t_stat])
        for kc in range(KC):
            S.op("dve", lambda e, kc=kc: e.scalar_tensor_tensor(out=out_bf[:, kc, 0:tw], in0=src_f[:, kc, 0:tw], scalar=norm_ap_fn(kc),
                                                                in1=stat[:, 2, 0:tw], op0=ALU.mult, op1=ALU.mult),
                 reads=[t_src, t_stat, t_small], writes=[t_out])

    def rope_pair(wA, wB, KC, rhs, rtoks, tw, isx, out_ap, wtok):
        if not isx:
            lin(wA, KC, 128, rhs, rtoks, tw, ev_copy("act", out_ap, wtok, 128, tw))
            return
        n = m1b["n"]
        r1 = n["rtmp"] % 2
        n["rtmp"] += 1
        cs = m1b["cs"]
        rt, t_rt = m1b["rtmp"][r1], m1b["t_rtmp"][r1]

        def evA(pb):
            S.op("dve", lambda e: e.tensor_tensor(out=rt[:, 0:tw], in0=psum[pb][:, 0:tw], in1=cs[:, 0, 0:tw], op=ALU.mult),
                 reads=[t_ps[pb], m1b["t_cs"]], writes=[t_rt])

        def evB(pb):
            ti = cnt["tmpn"] % 2
            cnt["tmpn"] += 1
            S.op("dve", lambda e: e.tensor_tensor(out=tmpn[ti][:, 0:tw], in0=psum[pb][:, 0:tw], in1=cs[:, 1, 0:tw], op=ALU.mult),
                 reads=[t_ps[pb], m1b["t_cs"]], writes=[t_tmpn[ti]])
            S.op("pool", lambda e: e.tensor_tensor(out=out_ap, in0=rt[:, 0:tw], in1=tmpn[ti][:, 0:tw], op=ALU.add),
                 reads=[t_rt, t_tmpn[ti]], writes=[wtok])
        lin(wA, KC, 128, rhs, rtoks, tw, evA)
        lin(wB, KC, 128, rhs, rtoks, tw, evB)

    def m1_tile(li, t0, tw, xi, prefetch):
        which = 0 if t0 < L else 1
        isx = t0 < L
        n = m1b["n"]
        ns = tw // 128
        for kc in range(8):
            S.op("act", lambda e, kc=kc: e.activation(out=hb[:, kc, 0:tw], in_=xt[xi][:, kc, 0:tw], func=AF.Identity,
                                                      scale=mods[:, li, 1, 0, kc, which:which + 1],
                                                      bias=mods[:, li, 1, 1, kc, which:which + 1]),
                 reads=[t_xt[xi], t_mod], writes=[t_hb])
        if isx:
            S.dma("sp", lambda e: e.dma_start(out=m1b["cs"][:, :, 0:tw], in_=rope_cs[:, :, t0:t0 + tw]), writes=[m1b["t_cs"]])

        def st8_next():
            i = n["st8"] % 2
            n["st8"] += 1
            return m1b["st8"][i], m1b["t_st8"][i]
        stq, t_stq = st8_next()
        for h in range(4):
            lin(WIS[li, h], 8, 128, hb, [t_hb], tw, ev_copy("act", stq[:, h, 0:tw], t_stq, 128, tw, scale=128.0 ** -0.5))
        for h in range(4):
            lin(WIS[li, 4 + h], 8, 128, hb, [t_hb], tw, ev_copy("dve", stq[:, 4 + h, 0:tw], t_stq, 128, tw))
        S.dma("act", lambda e: e.dma_start(out=QT[:, t0:t0 + tw].rearrange("(h p) t -> p h t", p=128), in_=stq[:, 0:4, 0:tw]),
              reads=[t_stq], writes=[])
        S.dma("act", lambda e: e.dma_start(out=KT[:, t0:t0 + tw].rearrange("(h p) t -> p h t", p=128), in_=stq[:, 4:8, 0:tw]),
              reads=[t_stq], writes=[])
        str_, t_str = st8_next()
        for c_ in range(8):
            zi = n["z7"] % 2
            n["z7"] += 1
            z7b, t_z7b = m1b["z7"][zi], m1b["t_z7"][zi]

            def evr(pb, c_=c_, z7b=z7b, t_z7b=t_z7b):
                S.op("act", lambda e: e.activation(out=z7b[:, 0:tw], in_=psum[pb][:, 0:tw], func=AF.Silu), reads=[t_ps[pb]], writes=[t_z7b])
                S.op("dve", lambda e: e.tensor_scalar_mul(out=str_[:, c_, 0:tw], in0=z7b[:, 0:tw], scalar1=gnorm_sb[:, li, c_:c_ + 1]),
                     reads=[t_z7b, t_small], writes=[t_str])
            lin(WIS[li, 8 + c_], 8, 128, hb, [t_hb], tw, evr)
        S.dma("act", lambda e: e.dma_start(out=RT[:, t0:t0 + tw].rearrange("(h p) t -> p h t", p=128), in_=str_[:, :, 0:tw]),
              reads=[t_str], writes=[])
        stu, t_stu = st8_next()
        for c_ in range(8):
            zi = n["z7"] % 2
            n["z7"] += 1
            z7b, t_z7b = m1b["z7"][zi], m1b["t_z7"][zi]
            lin(WIS[li, 16 + c_], 8, 128, hb, [t_hb], tw, ev_copy("act", z7b[:, 0:tw], t_z7b, 128, tw))

            def ev8(pb, c_=c_, z7b=z7b, t_z7b=t_z7b):
                S.op("dve", lambda e: e.tensor_tensor(out=stu[:, c_, 0:tw], in0=psum[pb][:, 0:tw], in1=z7b[:, 0:tw], op=ALU.mult),
                     reads=[t_ps[pb], t_z7b], writes=[t_stu])
            lin(WIS[li, 24 + c_], 8, 128, hb, [t_hb], tw, ev8)
        S.dma("act", lambda e: e.dma_start(out=UT[:, t0:t0 + tw].rearrange("(h p) t -> p h t", p=128), in_=stu[:, :, 0:tw]),
              reads=[t_stu], writes=[])
        glrT, dw, gst = m1b["glrT"], m1b["dw"], m1b["gst"]
        lin(WIS[li, 70], 8, 32, hb, [t_hb], tw, ev_copy("dve", glrT[0:32, 0:tw], m1b["t_glrT"], 32, tw))
        for sub in range(ns):
            for d_ in range(2):
                pb = cnt["ps"] % 6
                cnt["ps"] += 1
                S.op("pe", lambda e, sub=sub, d_=d_, pb=pb: e.matmul(psum[pb][:, :], lhsT=glrT[0:33, sub * 128:(sub + 1) * 128],
                                                                     rhs=dw[0:33, d_, :], start=True, stop=True),
                     reads=[m1b["t_glrT"], m1b["t_dw"]], writes=[t_ps[pb]])
                ei = n["etmp"] % 2
                n["etmp"] += 1
                et, t_et = m1b["etmp"][ei], m1b["t_etmp"][ei]
                S.op("act", lambda e, pb=pb, et=et: e.activation(out=et[:], in_=psum[pb][:, :], func=AF.Exp, scale=-1.0),
                     reads=[t_ps[pb]], writes=[t_et])
                S.op("act", lambda e, et=et, sub=sub, d_=d_: e.activation(out=gst[:, d_, sub, :], in_=et[:], func=AF.Ln, bias=1.0),
                     reads=[t_et], writes=[m1b["t_gst"]])
        for d_ in range(2):
            S.dma("act", lambda e, d_=d_: e.dma_start(out=GP[d_, t0:t0 + tw, :].rearrange("(s p) f -> p s f", p=128),
                                                     in_=gst[:, d_, 0:ns, :]), reads=[m1b["t_gst"]], writes=[])
        wv, vst = m1b["wv"], m1b["vst"]
        for sub in range(ns):
            for half in range(2):
                pb = cnt["ps"] % 6
                cnt["ps"] += 1

                def mmv(e, sub=sub, half=half, pb=pb):
                    r = None
                    for kc in range(8):
                        r = e.matmul(psum[pb][:, :], lhsT=hb[:, kc, sub * 128:(sub + 1) * 128], rhs=wv[:, kc, half * 512:(half + 1) * 512],
                                     start=(kc == 0), stop=(kc == 7))
                    return r
                S.op("pe", mmv, reads=[t_hb, m1b["t_wv"]], writes=[t_ps[pb]])
                eng = "act" if half == 0 else "dve"
                ev_copy(eng, vst[:, sub, half * 512:(half + 1) * 512], m1b["t_vst"], 128, 512)(pb)
        S.dma("act", lambda e: e.dma_start(out=VG[t0:t0 + tw, :].rearrange("(s p) f -> p s f", p=128), in_=vst[:, 0:ns, :]),
              reads=[m1b["t_vst"]], writes=[])
        stk, t_stk = st8_next()
        rope_pair(WIS[li, 71], WIS[li, 72], 8, hb, [t_hb], tw, isx, stk[:, 0, 0:tw], t_stk)
        S.dma("act", lambda e: e.dma_start(out=KRT[:, t0:t0 + tw], in_=stk[:, 0, 0:tw]), reads=[t_stk], writes=[])
        cqf, cqn, ckvn = m1b["cqf"], m1b["cqn"], m1b["ckvn"]
        for c_ in range(4):
            lin(WIS[li, 32 + c_], 8, 128, hb, [t_hb], tw, ev_copy("act", cqf[:, c_, 0:tw], m1b["t_cqf"], 128, tw))
        rms_apply(cqf, 4, tw, 512, lambda kc: qnorm_sb[:, li, kc:kc + 1], cqn, m1b["t_cqf"], m1b["t_cqn"])
        for c_ in range(2):
            lin(WIS[li, 36 + c_], 8, 128, hb, [t_hb], tw, ev_copy("act", cqf[:, c_, 0:tw], m1b["t_cqf"], 128, tw))
        rms_apply(cqf, 2, tw, 256, lambda kc: kvnorm_sb[:, li, kc:kc + 1], ckvn, m1b["t_cqf"], m1b["t_ckvn"])
        nxt = prefetch()
        stn, t_stn = st8_next()
        for h in range(8):
            lin(WUQS[li, h], 4, 128, cqn, [m1b["t_cqn"]], tw, ev_copy("act" if h % 2 else "dve", stn[:, h, 0:tw], t_stn, 128, tw))
        S.dma("act", lambda e: e.dma_start(out=QNT[:, t0:t0 + tw].rearrange("(h p) t -> p h t", p=128), in_=stn[:, :, 0:tw]),
              reads=[t_stn], writes=[])
        stp, t_stp = st8_next()
        for a in range(4):
            rope_pair(WUQS[li, 8 + a], WUQS[li, 12 + a], 4, cqn, [m1b["t_cqn"]], tw, isx, stp[:, a, 0:tw], t_stp)
        S.dma("act", lambda e: e.dma_start(out=QRT[:, t0:t0 + tw].rearrange("(h p) t -> p h t", p=128), in_=stp[:, 0:4, 0:tw]),
              reads=[t_stp], writes=[])
        stkn, t_stkn = st8_next()
        for h in range(8):
            lin(WUKS[li, h], 2, 128, ckvn, [m1b["t_ckvn"]], tw, ev_copy("act" if h % 2 else "dve", stkn[:, h, 0:tw], t_stkn, 128, tw))
        S.dma("act", lambda e: e.dma_start(out=KNT[:, t0:t0 + tw].rearrange("(h p) t -> p h t", p=128), in_=stkn[:, :, 0:tw]),
              reads=[t_stkn], writes=[])
        wkvv = m1b["wkvv"]
        for sub in range(ns):
            for half in range(2):
                pb = cnt["ps"] % 6
                cnt["ps"] += 1

                def mmv2(e, sub=sub, half=half, pb=pb):
                    r = None
                    for kc in range(2):
                        r = e.matmul(psum[pb][:, :], lhsT=ckvn[:, kc, sub * 128:(sub + 1) * 128], rhs=wkvv[:, kc, half * 512:(half + 1) * 512],
                                     start=(kc == 0), stop=(kc == 1))
                    return r
                S.op("pe", mmv2, reads=[m1b["t_ckvn"], m1b["t_wkvv"]], writes=[t_ps[pb]])
                eng = "act" if half == 0 else "dve"
                ev_copy(eng, vst[:, sub, half * 512:(half + 1) * 512], m1b["t_vst"], 128, 512)(pb)
        c0 = t0 // 128
        for sub in range(ns):
            S.dma("act", lambda e, sub=sub: e.dma_start(out=VM[:, :, c0 + sub, :].rearrange("h p m -> p h m"),
                                                       in_=vst[:, sub, :].rearrange("p (h m) -> p h m", h=8)),
                  reads=[m1b["t_vst"]], writes=[])
        return nxt

    def mixer_in_phase(li, tile_list):
        m0 = amark()
        m1_alloc()
        S.dma("sp", lambda e: e.dma_start(out=m1b["wv"][:, :, 0:512], in_=WVS[li, 0].rearrange("p (k c) -> p k c", k=8)),
              writes=[m1b["t_wv"]])
        S.dma("sp", lambda e: e.dma_start(out=m1b["wv"][:, :, 512:1024], in_=WVS[li, 1].rearrange("p (k c) -> p k c", k=8)),
              writes=[m1b["t_wv"]], join=True)
        S.dma("sp", lambda e: e.dma_start(out=m1b["wkvv"][:].rearrange("p k c -> p (k c)"), in_=WUKVV[li]), writes=[m1b["t_wkvv"]])
        dwst, dw = m1b["dwst"], m1b["dw"]
        S.op("pool", lambda e: e.memset(dwst[:], 0.0), writes=[m1b["t_dw"]])
        S.op("pool", lambda e: e.memset(m1b["glrT"][32:33, :], 1.0), writes=[m1b["t_glrT"]])
        S.dma("sp", lambda e: e.dma_start(out=dwst[0:16, 0, :], in_=gdw_in[li, 0]), writes=[m1b["t_dw"]])
        S.dma("sp", lambda e: e.dma_start(out=dwst[16:32, 1, :], in_=gdw_in[li, 1]), writes=[m1b["t_dw"]])
        S.dma("sp", lambda e: e.dma_start(out=dwst[32:33, :, :], in_=gdb_in[li:li + 1]), writes=[m1b["t_dw"]])
        S.op("dve", lambda e: e.tensor_copy(out=dw[:], in_=dwst[:]), reads=[m1b["t_dw"]], writes=[m1b["t_dw"]])
        nxt = load_xt(*tile_list[0])
        for ti_, (t0, tw) in enumerate(tile_list):
            def prefetch(ti_=ti_):
                if ti_ + 1 < len(tile_list):
                    return load_xt(*tile_list[ti_ + 1])
                return None
            nxt = m1_tile(li, t0, tw, nxt, prefetch)
        arelease(m0)

    def gla_phase(li):
        m0 = amark()
        ld = []
        for i in range(3):
            b_ = dict(gp=sb(f"g_gp{i}", [128, 512], BF16), q=sb(f"g_q{i}", [128, 4, 128], BF16), k=sb(f"g_k{i}", [128, 4, 128], BF16),
                      v=sb(f"g_v{i}", [128, 1024], BF16), of=sb(f"g_of{i}", [128, 8, 128], F32), r=sb(f"g_r{i}", [128, 8, 128], BF16),
                      t_in=Tok(), t_of=Tok())
            ld.append(b_)
        wk = []
        for i in range(2):
            b_ = dict(eq=sb(f"g_eq{i}", [128, 4, 128], F32), ek=sb(f"g_ek{i}", [128, 4, 128], F32), qe=sb(f"g_qe{i}", [128, 4, 128], BF16),
                      ke=sb(f"g_ke{i}", [128, 4, 128], BF16), kd=sb(f"g_kd{i}", [128, 4, 128], BF16), At=sb(f"g_At{i}", [128, 4, 128], BF16),
                      kdt=sb(f"g_kdt{i}", [128, 4, 128], BF16), t_e=Tok(), t_qe=Tok(), t_ke=Tok(), t_kd=Tok(), t_At=Tok(), t_kdt=Tok())
            wk.append(b_)
        NR = 4
        Sr = [sb(f"g_S{i}", [128, 4, 256], F32) for i in range(NR)]
        t_Sr = [[Tok() for _ in range(4)] for _ in range(NR)]
        Sbr = [sb(f"g_Sb{i}", [128, 4, 256], BF16) for i in range(NR)]
        t_Sbr = [Tok() for _ in range(NR)]
        gst_ = {"n": 0}
        ost = [sb(f"g_ost{i}", [128, 8, 128], F32) for i in range(2)]
        t_ost = [Tok(), Tok()]
        sq = sb("g_sq", [128, 8, 128], BF16)
        t_sq = Tok()
        rst = sb("g_rst", [128, 4, 128], F32)
        t_rst = Tok()
        t1b = sb("g_t1", [128, 8, 128], F32)
        t_t1 = Tok()
        anb = [sb(f"g_an{i}", [128, 8, 128], BF16) for i in range(2)]
        t_an = [Tok(), Tok()]
        B_BC, B_A, B_T, B_ST, B_O0, B_O1 = 0, 0, 1, 1, 2, 3
        B_U = [(4, 5), (6, 7)]
        psT_bf = psum[B_T][:].bitcast(BF16)

        def loads(d_, t0, i):
            b_ = ld[i]
            S.dma("sp", lambda e: e.dma_start(out=b_["gp"][:], in_=GP[d_, t0:t0 + 128, :]), writes=[b_["t_in"]])
            S.dma("sp", lambda e: e.dma_start(out=b_["q"][:], in_=QT[:, t0:t0 + 128].rearrange("(h p) t -> p h t", p=128)),
                  writes=[b_["t_in"]], join=True)
            S.dma("sp", lambda e: e.dma_start(out=b_["k"][:], in_=KT[:, t0:t0 + 128].rearrange("(h p) t -> p h t", p=128)),
                  writes=[b_["t_in"]], join=True)
            S.dma("sp", lambda e: e.dma_start(out=b_["v"][:], in_=VG[t0:t0 + 128, :]), writes=[b_["t_in"]], join=True)
            if d_ == 1:
                S.dma("sp", lambda e: e.dma_start(out=b_["of"][:], in_=OF[:, t0:t0 + 128].rearrange("(k p) t -> p k t", p=128)),
                      writes=[b_["t_of"]])
                S.dma("sp", lambda e: e.dma_start(out=b_["r"][:], in_=RT[:, t0:t0 + 128].rearrange("(k p) t -> p k t", p=128)),
                      writes=[b_["t_of"]], join=True)

        def prologue(d_, i, wi_):
            b_, w_ = ld[i], wk[wi_]
            esel = 63 if d_ == 0 else 0

            def mm_bc(e):
                r = None
                for h in range(4):
                    r = e.matmul(psum[B_BC][:, h * 128:(h + 1) * 128], lhsT=b_["gp"][:, h * 128:(h + 1) * 128], rhs=gmask[:, d_, :],
                                 start=True, stop=True)
                return r
            S.op("pe", mm_bc, reads=[b_["t_in"], t_small], writes=[t_ps[B_BC]])
            bcv = psum[B_BC][:].rearrange("p (h t) -> p h t", h=4)
            S.op("act", lambda e: e.activation(out=w_["eq"][:], in_=bcv, func=AF.Exp), reads=[t_ps[B_BC]], writes=[w_["t_e"]])
            S.op("act", lambda e: e.activation(out=w_["ek"][:], in_=bcv, func=AF.Exp, scale=-1.0), reads=[t_ps[B_BC]], writes=[w_["t_e"]])
            S.op("dve", lambda e: e.tensor_tensor(out=w_["qe"][:], in0=b_["q"][:], in1=w_["eq"][:], op=ALU.mult),
                 reads=[b_["t_in"], w_["t_e"]], writes=[w_["t_qe"]])
            S.op("dve", lambda e: e.tensor_tensor(out=w_["ke"][:], in0=b_["k"][:], in1=w_["ek"][:], op=ALU.mult),
                 reads=[b_["t_in"], w_["t_e"]], writes=[w_["t_ke"]])
            S.op("pool", lambda e: e.tensor_tensor(
                out=w_["kd"][:].rearrange("p h (c j) -> p h c j", c=2), in0=w_["ke"][:].rearrange("p h (c j) -> p h c j", c=2),
                in1=w_["eq"][:, :, esel::64].unsqueeze(3).to_broadcast([128, 4, 2, 64]), op=ALU.mult),
                reads=[w_["t_ke"], w_["t_e"]], writes=[w_["t_kd"]])

            def mm_a(e):
                r = None
                for h in range(4):
                    r = e.matmul(psum[B_A][:, h * 128:(h + 1) * 128], lhsT=w_["ke"][:, h, :], rhs=w_["qe"][:, h, :], start=True, stop=True)
                return r
            S.op("pe", mm_a, reads=[w_["t_ke"], w_["t_qe"]], writes=[t_ps[B_A]])
            S.op("dve", lambda e: e.tensor_tensor(out=w_["At"][:], in0=psum[B_A][:].rearrange("p (h t) -> p h t", h=4),
                                                  in1=gmask[:, 2 + d_, :].unsqueeze(1).to_broadcast([128, 4, 128]), op=ALU.mult),
                 reads=[t_ps[B_A], t_small], writes=[w_["t_At"]])

            def mm_t(e):
                r = None
                for h in range(4):
                    r = e.transpose(psT_bf[:, h * 128:(h + 1) * 128], w_["kd"][:, h, :], ident_bf[:])
                return r
            S.op("pe", mm_t, reads=[w_["t_kd"], t_const], writes=[t_ps[B_T]])
            S.op("act", lambda e: e.copy(out=w_["kdt"][:], in_=psT_bf[:, 0:512].rearrange("p (h t) -> p h t", h=4)),
                 reads=[t_ps[B_T]], writes=[w_["t_kdt"]])

        def chain(d_, t0, i, oi):
            b_, w_ = ld[i], wk[oi]
            chunks = (0, 1) if d_ == 0 else (1, 0)
            esel = 63 if d_ == 0 else 0

            def oreg(h, c2):
                return psum[B_O0 + h // 2][:, ((h % 2) * 2 + c2) * 128:((h % 2) * 2 + c2 + 1) * 128]

            def mm_intra(e):
                r = None
                for h in range(4):
                    for c2 in range(2):
                        r = e.matmul(oreg(h, c2), lhsT=b_["v"][:, h * 256 + c2 * 128:h * 256 + (c2 + 1) * 128], rhs=w_["At"][:, h, :],
                                     start=(h % 2 == 0 and c2 == 0), stop=False, skip_group_check=True)
                return r
            n0 = gst_["n"]
            gst_["n"] += 2
            for ci, c in enumerate(chunks):
                nn = n0 + ci
                ub0, ub1 = B_U[nn % 2]

                def mm_upd(e, c=c, ub0=ub0, ub1=ub1):
                    r = None
                    for h in range(4):
                        r = e.matmul(psum[(ub0, ub1)[h // 2]][:, (h % 2) * 256:(h % 2 + 1) * 256], lhsT=w_["kdt"][c * 64:(c + 1) * 64, h, :],
                                     rhs=b_["v"][c * 64:(c + 1) * 64, h * 256:(h + 1) * 256], start=True, stop=True)
                    return r
                S.op("pe", mm_upd, reads=[w_["t_kdt"], b_["t_in"]], writes=[t_ps[ub0], t_ps[ub1]])
                cur, prv = nn % NR, (nn - 1) % NR
                for h in range(4):
                    S.op("dve", lambda e, h=h, c=c, cur=cur, prv=prv, ub0=ub0, ub1=ub1: e.scalar_tensor_tensor(
                        out=Sr[cur][:, h, :], in0=Sr[prv][:, h, :], scalar=w_["eq"][:, h, c * 64 + esel:c * 64 + esel + 1],
                        in1=psum[(ub0, ub1)[h // 2]][:, (h % 2) * 256:(h % 2 + 1) * 256], op0=ALU.mult, op1=ALU.add),
                        reads=[t_Sr[prv][h], w_["t_e"], t_ps[(ub0, ub1)[h // 2]]], writes=[t_Sr[cur][h]])
                S.op("act", lambda e, cur=cur: e.copy(out=Sbr[cur][:], in_=Sr[cur][:]), reads=t_Sr[cur], writes=[t_Sbr[cur]])
            S.op("pe", mm_intra, reads=[b_["t_in"], w_["t_At"]], writes=[t_ps[B_O0], t_ps[B_O1]])
            for ci, c in enumerate(chunks):
                prv = (n0 + ci - 1) % NR

                def mm_inter(e, c=c, ci=ci, prv=prv):
                    r = None
                    for h in range(4):
                        for c2 in range(2):
                            r = e.matmul(oreg(h, c2)[:, c * 64:(c + 1) * 64], lhsT=Sbr[prv][:, h, c2 * 128:(c2 + 1) * 128],
                                         rhs=w_["qe"][:, h, c * 64:(c + 1) * 64], start=False, stop=(ci == 1), skip_group_check=True)
                    return r
                S.op("pe", mm_inter, reads=[t_Sbr[prv], w_["t_qe"]], writes=[t_ps[B_O0], t_ps[B_O1]])
            o_t, t_o = ost[oi], t_ost[oi]
            if d_ == 0:
                S.op("act", lambda e: e.copy(out=o_t[:, 0:4, :], in_=psum[B_O0][:].rearrange("p (k t) -> p k t", k=4)),
                     reads=[t_ps[B_O0]], writes=[t_o])
                S.op("dve", lambda e: e.tensor_copy(out=o_t[:, 4:8, :], in_=psum[B_O1][:].rearrange("p (k t) -> p k t", k=4)),
                     reads=[t_ps[B_O1]], writes=[t_o])
                S.dma("sp", lambda e: e.dma_start(out=OF[:, t0:t0 + 128].rearrange("(k p) t -> p k t", p=128), in_=o_t[:]),
                      reads=[t_o], writes=[])
                return
            for half in range(2):
                S.op("dve", lambda e, half=half: e.tensor_tensor(
                    out=o_t[:, half * 4:half * 4 + 4, :], in0=psum[B_O0 + half][:].rearrange("p (k t) -> p k t", k=4),
                    in1=b_["of"][:, half * 4:half * 4 + 4, :], op=ALU.add), reads=[t_ps[B_O0 + half], b_["t_of"]], writes=[t_o])
            S.op("act", lambda e: e.activation(out=sq[:], in_=o_t[:], func=AF.Square), reads=[t_o], writes=[t_sq])

            def mm_st(e):
                r = None
                for h in range(4):
                    for c2 in range(2):
                        r = e.matmul(psum[B_ST][:, h * 128:(h + 1) * 128], lhsT=ones_bf[:], rhs=sq[:, h * 2 + c2, :],
                                     start=(h == 0 and c2 == 0), stop=(c2 == 1), skip_group_check=True)
                return r
            S.op("pe", mm_st, reads=[t_sq, t_const], writes=[t_ps[B_ST]])
            S.op("dve", lambda e: e.tensor_scalar(out=rst[:], in0=psum[B_ST][:].rearrange("p (h t) -> p h t", h=4), scalar1=1.0 / 256,
                                                  scalar2=EPS, op0=ALU.mult, op1=ALU.add), reads=[t_ps[B_ST]], writes=[t_rst])
            S.op("act", lambda e: e.activation(out=rst[:], in_=rst[:], func=AF.Sqrt), reads=[t_rst], writes=[t_rst])
            S.op("dve", lambda e: e.reciprocal(out=rst[:], in_=rst[:]), reads=[t_rst], writes=[t_rst])
            S.op("dve", lambda e: e.tensor_tensor(out=t1b[:].rearrange("p (h c) t -> p h c t", c=2),
                                                  in0=o_t[:].rearrange("p (h c) t -> p h c t", c=2),
                                                  in1=rst[:].unsqueeze(2).to_broadcast([128, 4, 2, 128]), op=ALU.mult),
                 reads=[t_o, t_rst], writes=[t_t1])
            a_t, t_a = anb[oi], t_an[oi]
            S.op("dve", lambda e: e.tensor_tensor(out=a_t[:], in0=t1b[:], in1=b_["r"][:], op=ALU.mult),
                 reads=[t_t1, b_["t_of"]], writes=[t_a])
            S.dma("sp", lambda e: e.dma_start(out=AN[:, t0:t0 + 128].rearrange("(k p) t -> p k t", p=128), in_=a_t[:]),
                  reads=[t_a], writes=[])

        for d_ in range(2):
            gst_["n"] = 0
            S.op("dve", lambda e: e.memset(Sr[NR - 1][:], 0.0), writes=t_Sr[NR - 1])
            S.op("pool", lambda e: e.memset(Sbr[NR - 1][:], 0.0), writes=[t_Sbr[NR - 1]])
            seq = [L, L + 128] + [i * 128 for i in range(L // 128)]
            if d_ == 1:
                seq = [L + 128, L] + [i * 128 for i in reversed(range(L // 128))]
            loads(d_, seq[0], 0)
            if len(seq) > 1:
                loads(d_, seq[1], 1)
            prologue(d_, 0, 0)
            for n_, t0 in enumerate(seq):
                if n_ + 2 < len(seq):
                    loads(d_, seq[n_ + 2], (n_ + 2) % 3)
                if n_ + 1 < len(seq):
                    prologue(d_, (n_ + 1) % 3, (n_ + 1) % 2)
                chain(d_, t0, n_ % 3, n_ % 2)
            if d_ == 0:
                S.full_barrier()
        arelease(m0)

    def attn_phase(li, qtiles):
        m0 = amark()
        kr2 = [sb(f"a_kr{i}", [128, NT], BF16) for i in range(2)]
        t_kr = Tok()
        kn = [sb(f"a_kn{i}", [128, NT], BF16) for i in range(2)]
        vh = [sb(f"a_vh{i}", [128, NT // 128, 128], BF16) for i in range(2)]
        t_kv = [Tok(), Tok()]
        qn = [sb(f"a_qn{i}", [128, T], BF16) for i in range(2)]
        qr = [sb(f"a_qr{i}", [128, T], BF16) for i in range(2)]
        t_q = [Tok(), Tok()]
        NP = 6
        pT = [sb(f"a_pT{i}", [128, T], BF16) for i in range(NP)]
        t_pT = [Tok() for _ in range(NP)]
        accd = [sb(f"a_accd{i}", [128, T], F32) for i in range(2)]
        accp = [sb(f"a_accp{i}", [128, T], F32) for i in range(2)]
        t_accd = [Tok(), Tok()]
        t_accp = [Tok(), Tok()]
        dhi = [sb(f"a_dhi{i}", [128, T], BF16) for i in range(2)]
        dlo = [sb(f"a_dlo{i}", [128, T], BF16) for i in range(2)]
        t_dh = [Tok(), Tok()]
        rden = sb("a_rden", [128, T], F32)
        t_rden = Tok()
        cst = [sb(f"a_cst{i}", [128, T], BF16) for i in range(2)]
        t_cst = [Tok(), Tok()]
        SCALE = 192.0 ** -0.5
        st = {"q": 0, "p": 0, "s": 0, "o": 0}

        S.op("pool", lambda e: e.memset(kr2[0][64:128, :], 0.0), writes=[t_kr])
        S.op("pool", lambda e: e.memset(kr2[1][0:64, :], 0.0), writes=[t_kr])
        S.dma("sp", lambda e: e.dma_start(out=kr2[0][0:64, :], in_=KRT[0:64, :]), writes=[t_kr])
        S.dma("sp", lambda e: e.dma_start(out=kr2[1][64:128, :], in_=KRT[64:128, :]), writes=[t_kr])

        def load_head(h, i):
            S.dma("sp", lambda e: e.dma_start(out=kn[i][:], in_=KNT[h * 128:(h + 1) * 128, :]), writes=[t_kv[i]])
            S.dma("sp", lambda e: e.dma_start(out=vh[i][:], in_=VM[h]), writes=[t_kv[i]], join=True)

        def load_q(h, t0, tw):
            i = st["q"] % 2
            st["q"] += 1
            S.dma("sp", lambda e: e.dma_start(out=qn[i][:, 0:tw], in_=QNT[h * 128:(h + 1) * 128, t0:t0 + tw]), writes=[t_q[i]])
            S.dma("sp", lambda e: e.dma_start(out=qr[i][:, 0:tw], in_=QRT[(h // 2) * 128:(h // 2 + 1) * 128, t0:t0 + tw]), writes=[t_q[i]], join=True)
            return i

        def attend(h, hi, t0, tw, qi):
            hp = h % 2
            chunks = list(range(NT // 128)) if t0 < L else list(range(L // 128, NT // 128))
            oset = st["o"] % 2
            st["o"] += 1
            B_O, B_D = 4 + oset, 6 + oset
            nch = len(chunks)
            sbank = {}
            a_d, a_p = accd[oset], accp[oset]
            t_ad, t_ap = t_accd[oset], t_accp[oset]
            used = {"dve": False, "pool": False, "pe": False}

            def qk(ci):
                c = chunks[ci]
                pb = st["s"] % 4
                st["s"] += 1
                sbank[ci] = pb

                def mm(e):
                    e.matmul(psum[pb][:, 0:tw], lhsT=kn[hi][:, c * 128:(c + 1) * 128], rhs=qn[qi][:, 0:tw], start=True, stop=False)
                    return e.matmul(psum[pb][:, 0:tw], lhsT=kr2[hp][:, c * 128:(c + 1) * 128], rhs=qr[qi][:, 0:tw], start=False, stop=True)
                S.op("pe", mm, reads=[t_kv[hi], t_kr, t_q[qi]], writes=[t_ps[pb]])

            def pv(ci):
                c = chunks[ci]
                pb = sbank[ci]
                pi = st["p"] % NP
                st["p"] += 1
                S.op("act", lambda e: e.activation(out=pT[pi][:, 0:tw], in_=psum[pb][:, 0:tw], func=AF.Exp, scale=SCALE),
                     reads=[t_ps[pb]], writes=[t_pT[pi]])
                S.op("pe", lambda e: e.matmul(psum[B_O][:, 0:tw], lhsT=vh[hi][:, c, :], rhs=pT[pi][:, 0:tw], start=(ci == 0), stop=(ci == nch - 1)),
                     reads=[t_kv[hi], t_pT[pi]], writes=[t_ps[B_O]])
                if ci % 4 == 3:
                    first = not used["pe"]
                    used["pe"] = True
                    S.op("pe", lambda e: e.matmul(psum[B_D][:, 0:tw], lhsT=ones_bf[:], rhs=pT[pi][:, 0:tw], start=first, stop=False),
                         reads=[t_pT[pi], t_const], writes=[t_ps[B_D]])
                    return
                eng = "dve"
                a_, t_a_ = a_d, t_ad
                if not used[eng]:
                    used[eng] = True
                    S.op(eng, lambda e: e.tensor_copy(out=a_[:, 0:tw], in_=pT[pi][:, 0:tw]), reads=[t_pT[pi]], writes=[t_a_])
                else:
                    S.op(eng, lambda e: e.tensor_tensor(out=a_[:, 0:tw], in0=a_[:, 0:tw], in1=pT[pi][:, 0:tw], op=ALU.add),
                         reads=[t_pT[pi], t_a_], writes=[t_a_])
            LOOK = 2
            for ci in range(min(LOOK, nch)):
                qk(ci)
            for ci in range(nch):
                if ci + LOOK < nch:
                    qk(ci + LOOK)
                pv(ci)
            if used["pool"]:
                S.op("dve", lambda e: e.tensor_tensor(out=a_d[:, 0:tw], in0=a_d[:, 0:tw], in1=a_p[:, 0:tw], op=ALU.add),
                     reads=[t_ad, t_ap], writes=[t_ad])
            S.op("dve", lambda e: e.tensor_copy(out=dhi[oset][:, 0:tw], in_=a_d[:, 0:tw]), reads=[t_ad], writes=[t_dh[oset]])
            S.op("dve", lambda e: e.tensor_tensor(out=dlo[oset][:, 0:tw], in0=a_d[:, 0:tw], in1=dhi[oset][:, 0:tw], op=ALU.subtract),
                 reads=[t_ad, t_dh[oset]], writes=[t_dh[oset]])

            def mmd(e):
                e.matmul(psum[B_D][:, 0:tw], lhsT=ones_bf[:], rhs=dhi[oset][:, 0:tw], start=(not used["pe"]), stop=False)
                return e.matmul(psum[B_D][:, 0:tw], lhsT=ones_bf[:], rhs=dlo[oset][:, 0:tw], start=False, stop=True)
            S.op("pe", mmd, reads=[t_dh[oset], t_const], writes=[t_ps[B_D]])
            S.op("dve", lambda e: e.reciprocal(out=rden[:, 0:tw], in_=psum[B_D][:, 0:tw]), reads=[t_ps[B_D]], writes=[t_rden])
            ci_ = oset
            S.op("dve", lambda e: e.tensor_tensor(out=cst[ci_][:, 0:tw], in0=psum[B_O][:, 0:tw], in1=rden[:, 0:tw], op=ALU.mult),
                 reads=[t_ps[B_O], t_rden], writes=[t_cst[ci_]])
            S.dma("sp", lambda e: e.dma_start(out=CT[h * 128:(h + 1) * 128, t0:t0 + tw], in_=cst[ci_][:, 0:tw]),
                  reads=[t_cst[ci_]], writes=[])

        load_head(0, 0)
        work = [(h, t0, tw) for h in range(8) for (t0, tw) in qtiles]
        nq = load_q(*work[0])
        for wi_, (h, t0, tw) in enumerate(work):
            hi = h % 2
            if t0 == qtiles[0][0] and h + 1 < 8:
                load_head(h + 1, 1 - hi)
            qi = nq
            if wi_ + 1 < len(work):
                nq = load_q(*work[wi_ + 1])
            attend(h, hi, t0, tw, qi)
        arelease(m0)

    def merge_phase(li, tile_list):
        m0 = amark()
        anb = [sb(f"m_an{i}", [128, 8, T], BF16) for i in range(2)]
        ctb = [sb(f"m_ct{i}", [128, 8, T], BF16) for i in range(2)]
        utb = [sb(f"m_ut{i}", [128, 8, T + 2], BF16) for i in range(2)]
        t_in = [Tok(), Tok()]
        bnb = sb("m_bn", [128, 8, T], BF16)
        t_bn = Tok()
        mbb = sb("m_mb", [128, 8, T], BF16)
        t_mb = Tok()
        gts = [sb(f"m_g{i}", [128, T], F32) for i in range(3)]
        t_g = [Tok() for _ in range(3)]
        acc = [sb(f"m_acc{i}", [128, T], F32) for i in range(2)]
        t_acc = [Tok(), Tok()]
        tt_ = [sb(f"m_t{i}", [128, T], F32) for i in range(2)]
        t_tt = [Tok(), Tok()]
        cvt = [sb(f"m_cv{i}", [128, T], F32) for i in range(2)]
        t_cv = [Tok(), Tok()]
        st = {"in": 0, "g": 0, "acc": 0, "t": 0, "cv": 0}

        def load_in(t0, tw):
            i = st["in"] % 2
            st["in"] += 1
            S.dma("sp", lambda e: e.dma_start(out=anb[i][:, :, 0:tw], in_=AN[:, t0:t0 + tw].rearrange("(k p) t -> p k t", p=128)),
                  writes=[t_in[i]])
            S.dma("sp", lambda e: e.dma_start(out=ctb[i][:, :, 0:tw], in_=CT[:, t0:t0 + tw].rearrange("(k p) t -> p k t", p=128)),
                  writes=[t_in[i]], join=True)
            first = (t0 == 0 or t0 == L)
            lastt = (t0 + tw == L or t0 + tw == NT)
            lo = t0 if first else t0 - 1
            hi_ = t0 + tw if lastt else t0 + tw + 1
            o0 = 1 if first else 0
            S.dma("sp", lambda e: e.dma_start(out=utb[i][:, :, o0:o0 + hi_ - lo], in_=UT[:, lo:hi_].rearrange("(k p) t -> p k t", p=128)),
                  writes=[t_in[i]], join=True)
            if first:
                S.op("pool", lambda e: e.memset(utb[i][:, :, 0:1], 0.0), writes=[t_in[i]])
            if lastt:
                S.op("pool", lambda e: e.memset(utb[i][:, :, tw + 1:tw + 2], 0.0), writes=[t_in[i]])
            return i

        def m4_tile(t0, tw, xi, ii, prefetch, pending):
            which = 0 if t0 < L else 1
            an_, ct_, ut_ = anb[ii], ctb[ii], utb[ii]
            for kc in range(8):
                S.op("act", lambda e, kc=kc: e.activation(out=hb[:, kc, 0:tw], in_=xt[xi][:, kc, 0:tw], func=AF.Identity,
                                                          scale=mods[:, li, 1, 0, kc, which:which + 1],
                                                          bias=mods[:, li, 1, 1, kc, which:which + 1]),
                     reads=[t_xt[xi], t_mod], writes=[t_hb])
            for c_ in range(8):
                vi = st["cv"] % 2
                st["cv"] += 1
                cv_, t_cv_ = cvt[vi], t_cv[vi]
                S.op("pool", lambda e, c_=c_, cv_=cv_: e.tensor_scalar_mul(out=cv_[:, 0:tw], in0=ut_[:, c_, 1:tw + 1],
                                                                           scalar1=convw_sb[:, li, 1, c_:c_ + 1]),
                     reads=[t_in[ii], t_small], writes=[t_cv_])
                S.op("dve", lambda e, c_=c_, cv_=cv_: e.scalar_tensor_tensor(out=cv_[:, 0:tw], in0=ut_[:, c_, 0:tw],
                                                                              scalar=convw_sb[:, li, 0, c_:c_ + 1], in1=cv_[:, 0:tw],
                                                                              op0=ALU.mult, op1=ALU.add),
                     reads=[t_in[ii], t_small, t_cv_], writes=[t_cv_])
                S.op("dve", lambda e, c_=c_, cv_=cv_: e.scalar_tensor_tensor(out=cv_[:, 0:tw], in0=ut_[:, c_, 2:tw + 2],
                                                                              scalar=convw_sb[:, li, 2, c_:c_ + 1], in1=cv_[:, 0:tw],
                                                                              op0=ALU.mult, op1=ALU.add),
                     reads=[t_in[ii], t_small, t_cv_], writes=[t_cv_])

                def ev6(pb, c_=c_, cv_=cv_, t_cv_=t_cv_):
                    S.op("dve", lambda e: e.tensor_tensor(out=bnb[:, c_, 0:tw], in0=psum[pb][:, 0:tw], in1=cv_[:, 0:tw], op=ALU.mult),
                         reads=[t_ps[pb], t_cv_], writes=[t_bn])
                lin(WIS[li, 38 + c_], 8, 128, hb, [t_hb], tw, ev6)
                pending = step(pending)
            srcs = [(an_, [t_in[ii]]), (bnb, [t_bn]), (ct_, [t_in[ii]])]
            def do_m(m):
                ai = st["acc"] % 2
                st["acc"] += 1
                acc_, t_acc_ = acc[ai], t_acc[ai]
                for br in range(3):
                    gi = st["g"] % 3
                    st["g"] += 1
                    g_, t_g_ = gts[gi], t_g[gi]

                    def evg(pb, g_=g_, t_g_=t_g_, br=br):
                        S.op("act", lambda e: e.activation(out=g_[:, 0:tw], in_=psum[pb][:, 0:tw], func=AF.Sigmoid,
                                                           bias=bgate_sb[:, li, br * 8 + m:br * 8 + m + 1]),
                             reads=[t_ps[pb], t_small], writes=[t_g_])
                    lin(WIS[li, 46 + br * 8 + m], 8, 128, hb, [t_hb], tw, evg)

                    def evp(pb, g_=g_, t_g_=t_g_, br=br):
                        if br == 0:
                            S.op("dve", lambda e: e.tensor_tensor(out=acc_[:, 0:tw], in0=psum[pb][:, 0:tw], in1=g_[:, 0:tw], op=ALU.mult),
                                 reads=[t_ps[pb], t_g_], writes=[t_acc_])
                            return
                        ti = st["t"] % 2
                        st["t"] += 1
                        S.op("dve", lambda e: e.tensor_tensor(out=tt_[ti][:, 0:tw], in0=psum[pb][:, 0:tw], in1=g_[:, 0:tw], op=ALU.mult),
                             reads=[t_ps[pb], t_g_], writes=[t_tt[ti]])
                        if br == 1:
                            S.op("pool", lambda e: e.tensor_tensor(out=acc_[:, 0:tw], in0=acc_[:, 0:tw], in1=tt_[ti][:, 0:tw], op=ALU.add),
                                 reads=[t_tt[ti], t_acc_], writes=[t_acc_])
                        else:
                            S.op("pool", lambda e: e.tensor_tensor(out=mbb[:, m, 0:tw], in0=acc_[:, 0:tw], in1=tt_[ti][:, 0:tw], op=ALU.add),
                                 reads=[t_tt[ti], t_acc_], writes=[t_mb])
                    src, stoks = srcs[br]
                    lin(PRJS[li, br, m], 8, 128, src, stoks, tw, evp)
            for m in range(8):
                do_m(m)
                if m < 3:
                    pending = step(pending)
            drain(pending)
            nxt = prefetch()
            for m in range(8):
                def evo(pb, m=m):
                    S.op("dve", lambda e: e.scalar_tensor_tensor(
                        out=xt[xi][:, m, 0:tw], in0=psum[pb][:, 0:tw], scalar=mods[:, li, 1, 2, m, which:which + 1],
                        in1=xt[xi][:, m, 0:tw], op0=ALU.mult, op1=ALU.add),
                        reads=[t_ps[pb], t_xt[xi], t_mod], writes=[t_xt[xi]])
                lin(PRJS[li, 3, m], 8, 128, mbb, [t_mb], tw, evo)
            return nxt, ln_pieces(li, 1, xi, t0, tw, which)

        pend = None
        nx = load_xt(*tile_list[0])
        ni = load_in(*tile_list[0])
        for ti_, (t0, tw) in enumerate(tile_list):
            def prefetch(ti_=ti_):
                if ti_ + 1 < len(tile_list):
                    return load_xt(*tile_list[ti_ + 1]), load_in(*tile_list[ti_ + 1])
                return None, None
            (nx, ni), pend = m4_tile(t0, tw, nx, ni, prefetch, pend)
        drain(pend)
        arelease(m0)

    if debug is not None and "mod" in debug:
        d_ = nc.dram_tensor("dbg_mod", [128, NL * 72 * 2], F32, kind="ExternalOutput").ap()
        S.dma("sp", lambda e: e.dma_start(out=d_, in_=modT[:].rearrange("p l i w -> p (l i w)")), reads=[t_mod])

    snapshot("x0")
    for li, l in enumerate(layers):
        last = (l == DEPTH - 1)
        ffn_phase(li, 0, tiles)
        snapshot("ffn1")
        if not n_ffn_only:
            mixer_in_phase(li, tiles)
            gla_phase(li)
            dsnap("OF", OF, F32)
            dsnap("AN", AN)
            attn_phase(li, tiles[:-1] if last else tiles)
            dsnap("CT", CT)
            merge_phase(li, tiles[:-1] if last else tiles)
            snapshot("mix")
            for nm_, ap_ in [("QT", QT), ("KT", KT), ("RT", RT), ("UT", UT), ("GP", GP), ("VG", VG), ("KRT", KRT),
                             ("QNT", QNT), ("QRT", QRT), ("KNT", KNT), ("VM", VM)]:
                dsnap(nm_, ap_)
        ffn_phase(li, 1, tiles[:-1] if last else tiles)

    t_y = Tok()
    for tt in range((NT if emit_ctx else L) // 128):
        i = tt % 2
        S.dma("sp", lambda e, i=i, tt=tt: e.dma_start(
            out=xT_sb[i][:], in_=XT[:, tt * 128:(tt + 1) * 128].rearrange("(k p) t -> p k t", p=128)),
            reads=[t_XT[tt // 4]], writes=[t_xT[i]])
        for half in range(2):
            pb = 2 * (tt % 2) + half

            def tr(e, i=i, half=half, pb=pb):
                r = None
                for q in range(4):
                    kc = half * 4 + q
                    r = e.transpose(psum[pb][:, q * 128:(q + 1) * 128], xT_sb[i][:, kc, :], ident[:])
                return r
            S.op("pe", tr, reads=[t_xT[i], t_const], writes=[t_ps[pb]])
            ce = "dve" if half == 0 else "act"

            def cp(e, i=i, half=half, pb=pb, ce=ce):
                o = xin_sb[i][:, half * 512:(half + 1) * 512]
                return e.tensor_copy(out=o, in_=psum[pb][:]) if ce == "dve" else e.copy(out=o, in_=psum[pb][:])
            S.op(ce, cp, reads=[t_ps[pb]], writes=[t_xin[i]])
        dst_ = y_out[tt * 128:(tt + 1) * 128, :] if tt < L // 128 else yc_out[(tt - L // 128) * 128:(tt - L // 128 + 1) * 128, :]
        S.dma("sp", lambda e, i=i, dst_=dst_: e.dma_start(out=dst_, in_=xin_sb[i][:]),
              reads=[t_xin[i]], writes=[t_y])
    S.dma_barrier("sp")

    S.emit(es)
    es.close()
    return nc


def make_consts():
    t = np.arange(L)
    rows, cols = t // 64, t % 64
    inv = 10000.0 ** (-np.arange(16, dtype=np.float32) / 16.0)
    cs = np.zeros((128, 2, L), np.float32)
    for p in range(128):
        d_ = p % 64
        pos = rows if d_ < 32 else cols
        ang = pos.astype(np.float32) * inv[d_ % 16]
        cs[p, 0] = np.cos(ang)
        cs[p, 1] = np.sin(ang)
    j = np.arange(128)[:, None]
    i = np.arange(128)[None, :]
    same = (j // 64) == (i // 64)
    gm = np.zeros((128, 4, 128), np.float32)
    gm[:, 0] = np.where(same & (j <= i), -1.0 / 16.0, 0.0)
    gm[:, 1] = np.where(same & (j >= i), -1.0 / 16.0, 0.0)
    gm[:, 2] = np.where(same & (j <= i), 1.0, 0.0)
    gm[:, 3] = np.where(same & (j >= i), 1.0, 0.0)
    return {"ident_in": np.eye(128, dtype=np.float32), "rope_cs": cs, "gmask": gm}


PER_LAYER = ["ada_w", "ada_b", "ln_w", "ln_b", "ffn_w13", "ffn_w2", "w_in", "b_gate", "gla_decay_w", "gla_decay_b",
             "gla_norm", "gla_proj", "conv_w", "conv_proj", "mla_q_norm", "mla_w_uq", "mla_kv_norm", "mla_w_ukv",
             "mla_proj", "w_out"]


def make_in_maps(inputs, layers, n_cores, xs=None, ctxs=None):
    consts = make_consts()
    shared = {k: np.ascontiguousarray(np.asarray(inputs[k])[layers]) for k in PER_LAYER}
    shared["c_ctx"] = np.ascontiguousarray(np.asarray(inputs["c_ctx"]))
    shared.update(consts)
    maps = []
    for b in range(n_cores):
        m = dict(shared)
        m["x"] = np.ascontiguousarray(np.asarray(inputs["x"][b]) if xs is None else xs[b])
        m["ctx"] = np.ascontiguousarray(np.asarray(inputs["ctx"][b]) if ctxs is None else ctxs[b])
        m["c"] = np.ascontiguousarray(np.asarray(inputs["c"][b]))
        maps.append(m)
    return maps


FUSED = True
_PROG = {}


def _prog(layers):
    key = tuple(layers)
    if key not in _PROG:
        _PROG[key] = build_program(list(layers))
    return _PROG[key]


def kernel(**inputs):
    inputs = {k: np.asarray(v) for k, v in inputs.items()}
    n = 8
    if FUSED:
        nc = _prog(range(DEPTH))
        maps = make_in_maps(inputs, list(range(DEPTH)), n)
        res = run_bass_kernel_spmd(nc, maps, core_ids=list(range(n)))
        return np.stack([np.asarray(r["y"]) for r in res.results], axis=0).astype(np.float32)
    xs = [inputs["x"][b] for b in range(n)]
    cs = [inputs["ctx"][b] for b in range(n)]
    for l in range(DEPTH):
        nc = _prog([l]) if l == DEPTH - 1 else _prog([0])
        maps = make_in_maps(inputs, [l], n, xs=xs, ctxs=cs)
        res = run_bass_kernel_spmd(nc, maps, core_ids=list(range(n)))
        xs = [np.asarray(r["y"]) for r in res.results]
        if l != DEPTH - 1:
            cs = [np.asarray(r["yc"]) for r in res.results]
    return np.stack(xs, axis=0).astype(np.float32)
```

```python
import contextlib
import numpy as np
import concourse.bass as bass
import concourse.mybir as mybir
from concourse.bass_utils import run_bass_kernel_spmd

F32 = mybir.dt.float32
BF16 = mybir.dt.bfloat16
AF = mybir.ActivationFunctionType
ALU = mybir.AluOpType
AX = mybir.AxisListType

D = 1024
L = 4096
CTX = 256
NT = L + CTX
DEPTH = 4
DFF = 2816
NJ = DFF // 128
ALPHA = (2.0 * DEPTH) ** 0.25
EPS = 1e-6
INW = 10080

SAME_ENGINE_SYNC = False
EPOCH = 8192
N_DMA_SEMS = 8


class Tok:
    __slots__ = ("ws", "r", "name")

    def __init__(self, name=""):
        self.ws = []
        self.r = []
        self.name = name


class Op:
    __slots__ = ("eng", "idx", "fn", "waits", "sig", "cnt", "is_dma", "dsem", "dcnt", "prev_dma")

    def __init__(self, eng, idx, fn, is_dma):
        self.eng = eng
        self.idx = idx
        self.fn = fn
        self.waits = []
        self.sig = False
        self.cnt = 0
        self.is_dma = is_dma
        self.dsem = None
        self.dcnt = 0
        self.prev_dma = None


class Sched:
    ENGS = ("pe", "act", "dve", "pool", "sp")

    def __init__(self, nc):
        self.nc = nc
        self.ops = {e: [] for e in self.ENGS}
        self.seen = {e: {f: -1 for f in self.ENGS} for e in self.ENGS}
        self.seen_dma = {e: {} for e in self.ENGS}
        self.dma_n = {e: 0 for e in self.ENGS}
        self.dma_last = {}

    def _add(self, eng, fn, reads, writes, is_dma, join=False):
        lst = self.ops[eng]
        op = Op(eng, len(lst), fn, is_dma)
        deps = []
        for t in reads:
            deps.extend(t.ws)
        for t in writes:
            if not join:
                deps.extend(t.ws)
            deps.extend(t.r)
        for p in deps:
            self._dep(op, p)
        for t in writes:
            if join:
                t.ws.append(op)
            else:
                t.ws = [op]
            t.r = []
        for t in reads:
            t.r.append(op)
        if is_dma:
            k = self.dma_n[eng]
            self.dma_n[eng] += 1
            slot = k % N_DMA_SEMS
            op.dsem = (eng, slot)
            op.dcnt = 16 * (k // N_DMA_SEMS + 1)
            op.prev_dma = self.dma_last.get((eng, slot))
            self.dma_last[(eng, slot)] = op
            if op.prev_dma is not None:
                self._dep(op, op.prev_dma)
        lst.append(op)
        return op

    def _dep(self, op, p):
        e = op.eng
        if p.is_dma:
            sd = self.seen_dma[e]
            if sd.get(p.dsem, 0) >= p.dcnt:
                return
            sd[p.dsem] = p.dcnt
            op.waits.append(p)
        else:
            if p.eng == e and not op.is_dma and (e == "pe" or not SAME_ENGINE_SYNC):
                return
            if self.seen[e][p.eng] >= p.idx:
                return
            self.seen[e][p.eng] = p.idx
            p.sig = True
            op.waits.append(p)

    def op(self, eng, fn, reads=(), writes=()):
        return self._add(eng, fn, reads, writes, False)

    def dma_barrier(self, eng, queues=("sp", "act", "pool")):
        op = Op(eng, len(self.ops[eng]), lambda e: e.nop(), False)
        for (q, slot), p in self.dma_last.items():
            if q in queues:
                self._dep(op, p)
        self.ops[eng].append(op)
        return op

    def dma(self, eng, fn, reads=(), writes=(), join=False):
        return self._add(eng, fn, reads, writes, True, join)

    def full_barrier(self):
        lasts = {e: (self.ops[e][-1] if self.ops[e] else None) for e in self.ENGS}
        dl = list(self.dma_last.values())
        for e in self.ENGS:
            op = Op(e, len(self.ops[e]), lambda en: en.nop(), False)
            for f in self.ENGS:
                p = lasts[f]
                if p is None or f == e:
                    continue
                if p.is_dma:
                    q = None
                    for cand in reversed(self.ops[f]):
                        if not cand.is_dma:
                            q = cand
                            break
                    p = q
                if p is not None:
                    self._dep(op, p)
            for p in dl:
                self._dep(op, p)
            self.ops[e].append(op)

    def emit(self, es):
        nc = self.nc
        nsig = {}
        for e in self.ENGS:
            c = 0
            for op in self.ops[e]:
                if (not op.is_dma) and op.sig:
                    c += 1
                    op.cnt = c
            nsig[e] = c
        csems = {}
        for e in self.ENGS:
            n_ep = (nsig[e] + EPOCH - 1) // EPOCH
            csems[e] = [es.enter_context(nc.semaphore(f"c_{e}_{i}")) for i in range(n_ep)]
        dsems = {}
        for e in self.ENGS:
            if self.dma_n[e]:
                for s in range(min(N_DMA_SEMS, self.dma_n[e])):
                    dsems[(e, s)] = es.enter_context(nc.semaphore(f"d_{e}_{s}"))
        fin = es.enter_context(nc.semaphore("fin"))
        block = es.enter_context(nc.Block())
        engs = {"pe": (block.tensor, nc.tensor), "act": (block.scalar, nc.scalar), "dve": (block.vector, nc.vector),
                "pool": (block.gpsimd, nc.gpsimd), "sp": (block.sync, nc.sync)}

        def run(e, eng):
            for op in self.ops[e]:
                for p in op.waits:
                    if p.is_dma:
                        eng.wait_ge(dsems[p.dsem], p.dcnt)
                    else:
                        ep = (p.cnt - 1) // EPOCH
                        eng.wait_ge(csems[p.eng][ep], (p.cnt - 1) % EPOCH + 1)
                inst = op.fn(eng)
                if op.is_dma:
                    inst.then_inc(dsems[op.dsem], 16)
                elif op.sig:
                    ep = (op.cnt - 1) // EPOCH
                    inst.then_inc(csems[e][ep], 1)

        for e in self.ENGS:
            if not self.ops[e]:
                continue
            deco, _ = engs[e]

            def f(eng, e=e):
                run(e, eng)

            deco(f)


class Ctx:
    pass


def build_program(layers, n_ffn_only=False, debug=None):
    nc = bass.Bass("TRN2", target_bir_lowering=False)
    es = contextlib.ExitStack()
    S = Sched(nc)
    NL = len(layers)

    def din(name, shape, dt=F32):
        return nc.dram_tensor(name, list(shape), dt, kind="ExternalInput").ap()

    def dint(name, shape, dt):
        return nc.dram_tensor(name, list(shape), dt, kind="Internal").ap()

    ARENA_WORDS = 52992
    arena = es.enter_context(nc.sbuf_tensor("arena", [128, ARENA_WORDS], F32))
    astate = {"off": 0, "peak": 0}

    def sb(name, shape, dt):
        n = 1
        for d_ in shape[1:]:
            n *= d_
        isz = 4 if dt == F32 else 2
        words = (n * isz + 63) // 64 * 16
        off = astate["off"]
        assert off + words <= ARENA_WORDS, f"arena overflow at {name}: {off}+{words}"
        astate["off"] = off + words
        astate["peak"] = max(astate["peak"], astate["off"])
        v = arena[0:shape[0], off:off + (n * isz + 3) // 4]
        if dt != F32:
            v = v.bitcast(dt)
            v = v[:, 0:n]
        if len(shape) == 3:
            v = v.rearrange("p (a b) -> p a b", a=shape[1])
        elif len(shape) == 4:
            v = v.rearrange("p (a b c) -> p a b c", a=shape[1], b=shape[2])
        elif len(shape) == 5:
            v = v.rearrange("p (a b c d) -> p a b c d", a=shape[1], b=shape[2], c=shape[3])
        elif len(shape) == 6:
            v = v.rearrange("p (a b c d e) -> p a b c d e", a=shape[1], b=shape[2], c=shape[3], d=shape[4])
        return v

    def amark():
        return astate["off"]

    def arelease(m):
        S.full_barrier()
        astate["off"] = m

    x_in = din("x", [L, D])
    ctx_in = din("ctx", [CTX, D])
    c_in = din("c", [D])
    cctx_in = din("c_ctx", [D])
    ada_w = din("ada_w", [NL, D, 9 * D])
    ada_b = din("ada_b", [NL, 9 * D])
    ln_w = din("ln_w", [NL, 3, D])
    ln_b = din("ln_b", [NL, 3, D])
    ffn_w13 = din("ffn_w13", [NL, 2, D, 2 * DFF])
    ffn_w2 = din("ffn_w2", [NL, 2, DFF, D])
    ident_in = din("ident_in", [128, 128])
    w_in = din("w_in", [NL, D, INW])
    b_gate = din("b_gate", [NL, 3 * D])
    gdw_in = din("gla_decay_w", [NL, 2, 16, 512])
    gdb_in = din("gla_decay_b", [NL, 2, 512])
    gnorm_in = din("gla_norm", [NL, D])
    gla_proj = din("gla_proj", [NL, D, D])
    conv_w_in = din("conv_w", [NL, 3, D])
    conv_proj = din("conv_proj", [NL, D, D])
    qnorm_in = din("mla_q_norm", [NL, 512])
    w_uq = din("mla_w_uq", [NL, 512, 1536])
    kvnorm_in = din("mla_kv_norm", [NL, 256])
    w_ukv = din("mla_w_ukv", [NL, 256, 2048])
    mla_proj = din("mla_proj", [NL, D, D])
    w_out = din("w_out", [NL, D, D])
    rope_cs = din("rope_cs", [128, 2, L])
    gmask_in = din("gmask", [128, 4, 128])
    y_out = nc.dram_tensor("y", [L, D], F32, kind="ExternalOutput").ap()
    emit_ctx = (layers[-1] != DEPTH - 1)
    if emit_ctx:
        yc_out = nc.dram_tensor("yc", [CTX, D], F32, kind="ExternalOutput").ap()

    XT = dint("XT", [D, NT], F32)
    W13S = dint("W13S", [NL, 2, 2, NJ, 128, 8 * 128], BF16)
    W2S = dint("W2S", [NL, 2, 8, 128, NJ * 128], BF16)

    NBLK = 73
    WIS = dint("WIS", [NL, NBLK, 128, 8 * 128], BF16)
    WVS = dint("WVS", [NL, 2, 128, 8 * 512], BF16)
    PRJS = dint("PRJS", [NL, 4, 8, 128, 8 * 128], BF16)
    WUQS = dint("WUQS", [NL, 16, 128, 4 * 128], BF16)
    WUKS = dint("WUKS", [NL, 8, 128, 2 * 128], BF16)
    WUKVV = dint("WUKVV", [NL, 128, 2 * 1024], BF16)
    QT = dint("QT", [512, NT], BF16)
    KT = dint("KT", [512, NT], BF16)
    RT = dint("RT", [1024, NT], BF16)
    GP = dint("GP", [2, NT, 512], BF16)
    UT = dint("UT", [1024, NT], BF16)
    KRT = dint("KRT", [128, NT], BF16)
    VG = dint("VG", [NT, 1024], BF16)
    QNT = dint("QNT", [1024, NT], BF16)
    QRT = dint("QRT", [512, NT], BF16)
    KNT = dint("KNT", [1024, NT], BF16)
    VM = dint("VM", [8, 128, NT // 128, 128], BF16)
    OF = dint("OF", [1024, NT], F32)
    AN = dint("AN", [1024, NT], BF16)
    CT = dint("CT", [1024, NT], BF16)

    ident = sb("ident", [128, 128], F32)
    ones_bf = sb("ones_bf", [128, 128], BF16)
    t_const = Tok()
    S.dma("sp", lambda e: e.dma_start(out=ident[:], in_=ident_in), writes=[t_const])
    S.op("pool", lambda e: e.memset(ones_bf[:], 1.0), writes=[t_const])

    psum = [es.enter_context(nc.psum_tensor(f"ps{i}", [128, 512], F32)) for i in range(8)]
    t_ps = [Tok(f"ps{i}") for i in range(8)]

    modT = sb("modT", [128, NL, 72, 2], F32)
    mods = sb("mods", [128, NL, 3, 3, 8, 2], F32)
    lnw_sb = sb("lnw_sb", [128, NL, 3, 8], F32)
    lnb_sb = sb("lnb_sb", [128, NL, 3, 8], F32)
    adab_sb = sb("adab_sb", [128, NL, 72], F32)
    sc_sb = sb("sc_sb", [128, 8, 2], F32)
    sc_tmp = sb("sc_tmp", [128, 8, 2], F32)
    t_small = Tok()
    t_mod = Tok()
    bgate_sb = sb("bgate_sb", [128, NL, 24], F32)
    gnorm_sb = sb("gnorm_sb", [128, NL, 8], F32)
    convw_sb = sb("convw_sb", [128, NL, 3, 8], F32)
    qnorm_sb = sb("qnorm_sb", [128, NL, 4], F32)
    kvnorm_sb = sb("kvnorm_sb", [128, NL, 2], F32)
    gmask_f = sb("gmask_f", [128, 4, 128], F32)
    gmask = sb("gmask", [128, 4, 128], BF16)
    ident_bf = sb("ident_bf", [128, 128], BF16)

    def ld1(out_ap, in_ap):
        def f(e):
            with nc.allow_non_contiguous_dma(reason="tiny vector loads"):
                return e.dma_start(out=out_ap, in_=in_ap)
        S.dma("sp", f, writes=[t_small])

    ld1(sc_tmp[:, :, 0], c_in.rearrange("(k p) -> p k", p=128))
    ld1(sc_tmp[:, :, 1], cctx_in.rearrange("(k p) -> p k", p=128))
    for l in range(NL):
        ld1(adab_sb[:, l, :], ada_b[l].rearrange("(k p) -> p k", p=128))
        for s in range(3):
            ld1(lnw_sb[:, l, s, :], ln_w[l, s].rearrange("(k p) -> p k", p=128))
            ld1(lnb_sb[:, l, s, :], ln_b[l, s].rearrange("(k p) -> p k", p=128))
    for l in range(NL):
        ld1(bgate_sb[:, l, :], b_gate[l].rearrange("(k p) -> p k", p=128))
        ld1(gnorm_sb[:, l, :], gnorm_in[l].rearrange("(k p) -> p k", p=128))
        for k3 in range(3):
            ld1(convw_sb[:, l, k3, :], conv_w_in[l, k3].rearrange("(k p) -> p k", p=128))
        ld1(qnorm_sb[:, l, :], qnorm_in[l].rearrange("(k p) -> p k", p=128))
        ld1(kvnorm_sb[:, l, :], kvnorm_in[l].rearrange("(k p) -> p k", p=128))
    ld1(gmask_f[:], gmask_in)
    S.op("dve", lambda e: e.tensor_copy(out=gmask[:], in_=gmask_f[:]), reads=[t_small], writes=[t_small])
    S.op("dve", lambda e: e.tensor_copy(out=ident_bf[:], in_=ident[:]), reads=[t_const], writes=[t_const])
    S.op("act", lambda e: e.activation(out=sc_sb[:], in_=sc_tmp[:], func=AF.Silu), reads=[t_small], writes=[t_mod])

    m_pre = amark()
    ACW = 1152
    adaw_buf = [sb(f"adaw{i}", [128, 8, ACW], F32) for i in range(2)]
    t_adaw = [Tok(), Tok()]
    piece = 0
    for l in range(NL):
        for pc in range(9 * D // ACW):
            b = piece % 2
            buf = adaw_buf[b]
            src = ada_w[l, :, pc * ACW:(pc + 1) * ACW].rearrange("(k p) c -> p k c", p=128)
            S.dma("sp", lambda e, buf=buf, src=src: e.dma_start(out=buf[:], in_=src), writes=[t_adaw[b]])
            pb = piece % 8
            for mi in range(ACW // 128):
                i = pc * (ACW // 128) + mi

                def mm(e, buf=buf, mi=mi, pb=pb, mcol=mi):
                    r = None
                    for kc in range(8):
                        r = e.matmul(psum[pb][:, mcol * 2:mcol * 2 + 2], lhsT=buf[:, kc, mi * 128:(mi + 1) * 128],
                                     rhs=sc_sb[:, kc, :], start=(kc == 0), stop=(kc == 7))
                    return r
                S.op("pe", mm, reads=[t_adaw[b], t_mod], writes=[t_ps[pb]])
            i0 = pc * (ACW // 128)
            n = ACW // 128

            def ev(e, l=l, i0=i0, n=n, pb=pb):
                return e.tensor_tensor(out=modT[:, l, i0:i0 + n, :],
                                       in0=psum[pb][:, 0:2 * n].rearrange("p (i w) -> p i w", w=2),
                                       in1=adab_sb[:, l, i0:i0 + n].unsqueeze(2).to_broadcast([128, n, 2]),
                                       op=ALU.add)
            S.op("dve", ev, reads=[t_ps[pb], t_small], writes=[t_mod])
            piece += 1
    for l in range(NL):
        for s in range(3):
            base = 3 * s * 8
            S.op("dve", lambda e, l=l, s=s, base=base: e.tensor_scalar_add(
                out=mods[:, l, s, 0], in0=modT[:, l, base + 8:base + 16, :], scalar1=1.0), reads=[t_mod], writes=[t_mod])
            S.op("dve", lambda e, l=l, s=s, base=base: e.tensor_copy(
                out=mods[:, l, s, 1], in_=modT[:, l, base:base + 8, :]), reads=[t_mod], writes=[t_mod])
            gsc = (1.0 / ALPHA) if s == 1 else (0.5 / ALPHA)
            S.op("dve", lambda e, l=l, s=s, base=base, gsc=gsc: e.tensor_scalar_mul(
                out=mods[:, l, s, 2], in0=modT[:, l, base + 16:base + 24, :], scalar1=gsc), reads=[t_mod], writes=[t_mod])

    PW = 512
    PB = {}

    def prep_alloc(tag, stq):
        PB["wld"] = [sb(f"wld{tag}{i}", [128, 8 * PW], F32) for i in range(2)]
        PB["wcv"] = [sb(f"wcv{tag}{i}", [128, 8 * PW], BF16) for i in range(2)]
        PB["t_wld"] = [Tok(), Tok()]
        PB["t_wcv"] = [Tok(), Tok()]
        PB["stq"] = stq

    prep_alloc("u", "sp")
    t_wscr = Tok("wscr")
    prep_n = [0]
    cast_engs = ["act", "dve", "pool"]

    def prep(src, KC, width, dst, mb):
        cw = (8 * PW // KC) // mb * mb
        for c0 in range(0, width, cw):
            w_ = min(cw, width - c0)
            nb = w_ // mb
            i = prep_n[0] % 2
            ce = cast_engs[prep_n[0] % 3]
            prep_n[0] += 1
            ld_v = PB["wld"][i][:, 0:KC * w_].rearrange("p (k c) -> p k c", k=KC)
            S.dma("sp", lambda e, ld_v=ld_v, c0=c0, w_=w_: e.dma_start(
                out=ld_v, in_=src[:, c0:c0 + w_].rearrange("(k p) c -> p k c", p=128)), writes=[PB["t_wld"][i]])
            cv_v = PB["wcv"][i][:, 0:KC * w_].rearrange("p (b k m) -> p b k m", b=nb, k=KC)
            in_v = PB["wld"][i][:, 0:KC * w_].rearrange("p (k b m) -> p b k m", k=KC, b=nb)
            if ce == "act":
                S.op("act", lambda e, cv_v=cv_v, in_v=in_v: e.copy(out=cv_v, in_=in_v), reads=[PB["t_wld"][i]], writes=[PB["t_wcv"][i]])
            else:
                S.op(ce, lambda e, cv_v=cv_v, in_v=in_v: e.tensor_copy(out=cv_v, in_=in_v), reads=[PB["t_wld"][i]], writes=[PB["t_wcv"][i]])
            st_v = PB["wcv"][i][:, 0:KC * w_].rearrange("p (b f) -> p b f", b=nb)
            b0 = c0 // mb
            S.dma(PB["stq"], lambda e, st_v=st_v, b0=b0, nb=nb: e.dma_start(
                out=dst[b0:b0 + nb].rearrange("b p f -> p b f"), in_=st_v), reads=[PB["t_wcv"][i]], writes=[t_wscr])
            yield

    def cast(ce, out_ap, in_ap, neg=False):
        i = None
        if neg:
            if ce == "act":
                return lambda e: e.activation(out=out_ap, in_=in_ap, func=AF.Copy, scale=-1.0)
            return lambda e: e.tensor_scalar_mul(out=out_ap, in0=in_ap, scalar1=-1.0)
        if ce == "act":
            return lambda e: e.copy(out=out_ap, in_=in_ap)
        return lambda e: e.tensor_copy(out=out_ap, in_=in_ap)

    def prep_custom(KC, src_cols, pieces, stores):
        raise NotImplementedError

    def prep_special(src, KC, w_, ops, nout, dst):
        i = prep_n[0] % 2
        prep_n[0] += 1
        ld_v = PB["wld"][i][:, 0:KC * w_].rearrange("p (k c) -> p k c", k=KC)
        S.dma("sp", lambda e: e.dma_start(out=ld_v, in_=src.rearrange("(k p) c -> p k c", p=128)), writes=[PB["t_wld"][i]])
        cv = PB["wcv"][i][:, 0:nout * KC * 128].rearrange("p (b k m) -> p b k m", b=nout, k=KC)
        for n_, (o_ap, i_ap, neg) in enumerate(ops(ld_v, cv)):
            ce = cast_engs[n_ % 3]
            S.op(ce, cast(ce, o_ap, i_ap, neg), reads=[PB["t_wld"][i]], writes=[PB["t_wcv"][i]])
        cv_f = PB["wcv"][i][:, 0:nout * KC * 128].rearrange("p (b f) -> p b f", b=nout)
        S.dma(PB["stq"], lambda e: e.dma_start(out=dst.rearrange("b p f -> p b f"), in_=cv_f),
              reads=[PB["t_wcv"][i]], writes=[t_wscr])
        yield

    def rot_ops(o_blk, i_cols):
        ov = o_blk.rearrange("p k (g two s) -> p k g two s", g=2, two=2)
        iv = i_cols.rearrange("p k (g two s) -> p k g two s", g=2, two=2)
        return [(ov[:, :, :, 0, :], iv[:, :, :, 1, :], True), (ov[:, :, :, 1, :], iv[:, :, :, 0, :], False)]

    def prep_layer(li):
        for f in range(2):
            for ag in range(2):
                yield from prep(ffn_w13[li, f, :, ag * DFF:(ag + 1) * DFF], 8, DFF, W13S[li, f, ag], 128)
            yield from prep(ffn_w2[li, f], NJ, D, W2S[li, f], 128)
        wl = w_in[li]
        yield from prep(wl[:, 0:512], 8, 512, WIS[li, 0:4], 128)
        yield from prep(wl[:, 512:1024], 8, 512, WIS[li, 4:8], 128)
        yield from prep(wl[:, 2048:3072], 8, 1024, WIS[li, 8:16], 128)
        yield from prep(wl[:, 4128:5152], 8, 1024, WIS[li, 16:24], 128)
        yield from prep(wl[:, 5152:6176], 8, 1024, WIS[li, 24:32], 128)
        yield from prep(wl[:, 6176:6688], 8, 512, WIS[li, 32:36], 128)
        yield from prep(wl[:, 6688:6944], 8, 256, WIS[li, 36:38], 128)
        yield from prep(wl[:, 3104:4128], 8, 1024, WIS[li, 38:46], 128)
        yield from prep(wl[:, 7008:10080], 8, 3072, WIS[li, 46:70], 128)
        yield from prep(wl[:, 1024:2048], 8, 1024, WVS[li], 512)
        yield from prep_special(wl[:, 3072:3104], 8, 32, lambda ld, cv: [(cv[:, 0, :, 0:32], ld, False)], 1, WIS[li, 70:71])
        yield from prep_special(wl[:, 6944:7008], 8, 64,
                     lambda ld, cv: [(cv[:, 0, :, 0:64], ld, False), (cv[:, 0, :, 64:128], ld, False)]
                     + rot_ops(cv[:, 1, :, 0:64], ld) + rot_ops(cv[:, 1, :, 64:128], ld), 2, WIS[li, 71:73])
        for pi, pw in enumerate([gla_proj, conv_proj, mla_proj, w_out]):
            yield from prep(pw[li], 8, 1024, PRJS[li, pi], 128)
        for a in range(4):
            def uq_ops(ld, cv):
                ops = [(cv[:, 0, :, :], ld[:, :, 0:128], False), (cv[:, 1, :, :], ld[:, :, 192:320], False),
                       (cv[:, 2, :, 0:64], ld[:, :, 128:192], False), (cv[:, 2, :, 64:128], ld[:, :, 320:384], False)]
                ops += rot_ops(cv[:, 3, :, 0:64], ld[:, :, 128:192]) + rot_ops(cv[:, 3, :, 64:128], ld[:, :, 320:384])
                return ops
            i = prep_n[0] % 2
            prep_n[0] += 1
            ld_v = PB["wld"][i][:, 0:4 * 384].rearrange("p (k c) -> p k c", k=4)
            S.dma("sp", lambda e, ld_v=ld_v, a=a: e.dma_start(
                out=ld_v, in_=w_uq[li, :, a * 384:(a + 1) * 384].rearrange("(k p) c -> p k c", p=128)), writes=[PB["t_wld"][i]])
            cv = PB["wcv"][i][:, 0:4 * 512].rearrange("p (b k m) -> p b k m", b=4, k=4)
            for n_, (o_ap, i_ap, neg) in enumerate(uq_ops(ld_v, cv)):
                ce = cast_engs[n_ % 3]
                S.op(ce, cast(ce, o_ap, i_ap, neg), reads=[PB["t_wld"][i]], writes=[PB["t_wcv"][i]])
            cvf = PB["wcv"][i][:, 0:4 * 512].rearrange("p (b f) -> p b f", b=4)
            S.dma(PB["stq"], lambda e, cvf=cvf, a=a: e.dma_start(out=WUQS[li, 2 * a:2 * a + 2].rearrange("b p f -> p b f"),
                                                        in_=cvf[:, 0:2, :]), reads=[PB["t_wcv"][i]], writes=[t_wscr])
            S.dma(PB["stq"], lambda e, cvf=cvf, a=a: e.dma_start(out=WUQS[li, 8 + a], in_=cvf[:, 2, :]),
                  reads=[PB["t_wcv"][i]], writes=[t_wscr])
            S.dma(PB["stq"], lambda e, cvf=cvf, a=a: e.dma_start(out=WUQS[li, 12 + a], in_=cvf[:, 3, :]),
                  reads=[PB["t_wcv"][i]], writes=[t_wscr])
            yield
        i = prep_n[0] % 2
        prep_n[0] += 1
        ld_v = PB["wld"][i][:, 0:2 * 2048].rearrange("p (k h two m) -> p k h two m", k=2, h=8, two=2)
        ld_full = PB["wld"][i][:, 0:4096].rearrange("p (k c) -> p k c", k=2)
        S.dma("sp", lambda e: e.dma_start(out=ld_full, in_=w_ukv[li].rearrange("(k p) c -> p k c", p=128)), writes=[PB["t_wld"][i]])
        cvk = PB["wcv"][i][:, 0:2048].rearrange("p (h k m) -> p h k m", h=8, k=2)
        cvv = PB["wcv"][i][:, 2048:4096].rearrange("p (k h m) -> p k h m", k=2, h=8)
        for kk in range(2):
            ce = cast_engs[kk]
            S.op(ce, cast(ce, cvk[:, :, kk, :], ld_v[:, kk, :, 0, :]), reads=[PB["t_wld"][i]], writes=[PB["t_wcv"][i]])
            ce = cast_engs[kk + 1]
            S.op(ce, cast(ce, cvv[:, kk, :, :], ld_v[:, kk, :, 1, :]), reads=[PB["t_wld"][i]], writes=[PB["t_wcv"][i]])
        cvk_f = PB["wcv"][i][:, 0:2048].rearrange("p (b f) -> p b f", b=8)
        cvv_f = PB["wcv"][i][:, 2048:4096]
        S.dma(PB["stq"], lambda e: e.dma_start(out=WUKS[li].rearrange("b p f -> p b f"), in_=cvk_f),
              reads=[PB["t_wcv"][i]], writes=[t_wscr])
        S.dma(PB["stq"], lambda e: e.dma_start(out=WUKVV[li], in_=cvv_f),
              reads=[PB["t_wcv"][i]], writes=[t_wscr])

    drain_ = lambda g: [None for _ in g]
    drain_(prep_layer(0))

    arelease(m_pre)

    xin_sb = [sb(f"xin{i}", [128, D], F32) for i in range(2)]
    xT_sb = [sb(f"xTs{i}", [128, 8, 128], F32) for i in range(2)]
    t_xin = [Tok(), Tok()]
    t_xT = [Tok(), Tok()]
    t_XT = [Tok(f"XT{i}") for i in range(9)]
    for tt in range(NT // 128):
        i = tt % 2
        src = x_in[tt * 128:(tt + 1) * 128, :] if tt < L // 128 else ctx_in[(tt - L // 128) * 128:(tt - L // 128 + 1) * 128, :]
        S.dma("sp", lambda e, i=i, src=src: e.dma_start(out=xin_sb[i][:], in_=src), writes=[t_xin[i]])
        for half in range(2):
            pb = 2 * (tt % 2) + half

            def tr(e, i=i, half=half, pb=pb):
                r = None
                for q in range(4):
                    kc = half * 4 + q
                    r = e.transpose(psum[pb][:, q * 128:(q + 1) * 128], xin_sb[i][:, kc * 128:(kc + 1) * 128], ident[:])
                return r
            S.op("pe", tr, reads=[t_xin[i], t_const], writes=[t_ps[pb]])
            ce = "dve" if half == 0 else "act"

            def cp(e, i=i, half=half, pb=pb, ce=ce):
                o = xT_sb[i][:, half * 4:half * 4 + 4, :]
                src_ = psum[pb][:].rearrange("p (k t) -> p k t", k=4)
                return e.tensor_copy(out=o, in_=src_) if ce == "dve" else e.copy(out=o, in_=src_)
            S.op(ce, cp, reads=[t_ps[pb]], writes=[t_xT[i]])
        S.dma("sp", lambda e, i=i, tt=tt: e.dma_start(
            out=XT[:, tt * 128:(tt + 1) * 128].rearrange("(k p) t -> p k t", p=128), in_=xT_sb[i][:]),
            reads=[t_xT[i]], writes=[t_XT[tt // 4]])
    S.dma_barrier("sp")

    S.op("pe", lambda e: e.matmul(psum[0][0:32, 0:64], lhsT=ones_bf[:, 0:32], rhs=ones_bf[:, 0:64], start=True, stop=True),
         reads=[t_const], writes=[t_ps[0]])

    T = 512
    xt = [sb(f"xt{i}", [128, 8, T], F32) for i in range(2)]
    t_xt = [Tok(), Tok()]
    hb = sb("hb", [128, 8, T], BF16)
    t_hb = Tok()
    ub = sb("ub", [128, 8, T], BF16)
    usq = sb("usq", [128, 8, T], BF16)
    t_ub = Tok()
    t_usq = Tok()
    stat = sb("stat", [128, 4, T], F32)
    t_stat = Tok()
    tmpn = [sb(f"tmpn{i}", [128, T], F32) for i in range(2)]
    t_tmpn = [Tok(), Tok()]
    NWB = 6
    wblk = [sb(f"wblk{i}", [128, 8 * 128], BF16) for i in range(NWB)]
    t_wblk = [Tok() for _ in range(NWB)]
    cnt = {"xt": 0, "w13": 0, "w2": 0, "sil": 0, "tmpn": 0, "ps": 0, "wblk": 0}
    ffn_bufs = {}

    def ffn_alloc():
        ffn_bufs["hid"] = sb("hid", [128, NJ, T], BF16)
        ffn_bufs["sil"] = [sb(f"sil{i}", [128, T], BF16) for i in range(2)]
        ffn_bufs["w13b"] = [sb(f"w13b{i}", [128, 2, 8, 128], BF16) for i in range(3)]
        ffn_bufs["w2b"] = [sb(f"w2b{i}", [128, NJ, 128], BF16) for i in range(2)]

    t_hid = [Tok() for _ in range(NJ)]
    t_sil = [Tok(), Tok()]
    t_w13b = [Tok() for _ in range(3)]
    t_w2b = [Tok(), Tok()]

    tiles = [(i * T, T) for i in range(L // T)] + [(L, CTX)]

    dbg = {}

    def dump(name, ap, reads, dt=BF16):
        if debug is None or name not in debug or name in dbg:
            return
        dbg[name] = 1
        shp = list(ap.shape)
        d_ = nc.dram_tensor("dbg_" + name, shp, dt, kind="ExternalOutput").ap()
        S.dma("sp", lambda e: e.dma_start(out=d_, in_=ap), reads=reads)

    def dsnap(name, ap, dt=BF16):
        if debug is None or name not in debug:
            return
        S.dma_barrier("sp")
        d_ = nc.dram_tensor("dbg_" + name, list(ap.shape), dt, kind="ExternalOutput").ap()
        S.dma("sp", lambda e: e.dma_start(out=d_, in_=ap))
        S.dma_barrier("sp")

    def snapshot(name):
        if debug is None or name not in debug:
            return
        S.dma_barrier("sp")
        d_ = nc.dram_tensor("dbg_" + name, [D, NT], F32, kind="ExternalOutput").ap()
        S.dma("sp", lambda e: e.dma_start(out=d_, in_=XT))
        S.dma_barrier("sp")

    def load_xt(t0, tw):
        i = cnt["xt"] % 2
        cnt["xt"] += 1
        S.dma("sp", lambda e: e.dma_start(out=xt[i][:, :, 0:tw],
                                          in_=XT[:, t0:t0 + tw].rearrange("(k p) t -> p k t", p=128)),
              reads=[t_XT[t0 // 512]], writes=[t_xt[i]])
        return i

    def ln_pieces(li, s, xi, t0, tw, which):
        S.op("act", lambda e: e.copy(out=ub[:, :, 0:tw], in_=xt[xi][:, :, 0:tw]), reads=[t_xt[xi]], writes=[t_ub])
        S.op("act", lambda e: e.activation(out=usq[:, :, 0:tw], in_=xt[xi][:, :, 0:tw], func=AF.Square),
             reads=[t_xt[xi]], writes=[t_usq])
        yield
        p1, p2 = 6, 7
        eps_eff = EPS / (ALPHA * ALPHA)

        def mm1(e):
            r = None
            for kc in range(8):
                r = e.matmul(psum[p1][:, 0:tw], lhsT=ones_bf[:], rhs=ub[:, kc, 0:tw], start=(kc == 0), stop=(kc == 7))
            return r

        def mm2(e):
            r = None
            for kc in range(8):
                r = e.matmul(psum[p2][:, 0:tw], lhsT=ones_bf[:], rhs=usq[:, kc, 0:tw], start=(kc == 0), stop=(kc == 7))
            return r
        S.op("pe", mm1, reads=[t_ub, t_const], writes=[t_ps[p1]])
        S.op("pe", mm2, reads=[t_usq, t_const], writes=[t_ps[p2]])
        S.op("act", lambda e: e.activation(out=stat[:, 0, 0:tw], in_=psum[p1][:, 0:tw], func=AF.Copy, scale=1.0 / D),
             reads=[t_ps[p1]], writes=[t_stat])
        S.op("dve", lambda e: e.tensor_tensor(out=stat[:, 1, 0:tw], in0=stat[:, 0, 0:tw], in1=stat[:, 0, 0:tw], op=ALU.mult),
             reads=[t_stat], writes=[t_stat])
        S.op("dve", lambda e: e.scalar_tensor_tensor(out=stat[:, 1, 0:tw], in0=psum[p2][:, 0:tw], scalar=1.0 / D,
                                                     in1=stat[:, 1, 0:tw], op0=ALU.mult, op1=ALU.subtract),
             reads=[t_stat, t_ps[p2]], writes=[t_stat])
        S.op("dve", lambda e: e.tensor_scalar_add(out=stat[:, 1, 0:tw], in0=stat[:, 1, 0:tw], scalar1=eps_eff),
             reads=[t_stat], writes=[t_stat])
        S.op("act", lambda e: e.activation(out=stat[:, 1, 0:tw], in_=stat[:, 1, 0:tw], func=AF.Sqrt),
             reads=[t_stat], writes=[t_stat])
        S.op("dve", lambda e: e.reciprocal(out=stat[:, 2, 0:tw], in_=stat[:, 1, 0:tw]),
             reads=[t_stat], writes=[t_stat])
        S.op("dve", lambda e: e.scalar_tensor_tensor(out=stat[:, 3, 0:tw], in0=stat[:, 0, 0:tw], scalar=-1.0,
                                                     in1=stat[:, 2, 0:tw], op0=ALU.mult, op1=ALU.mult),
             reads=[t_stat], writes=[t_stat])
        yield
        for kc in range(8):
            if kc:
                yield
            ti = cnt["tmpn"] % 2
            cnt["tmpn"] += 1
            S.op("dve", lambda e, kc=kc, ti=ti: e.tensor_tensor(out=tmpn[ti][:, 0:tw], in0=xt[xi][:, kc, 0:tw],
                                                                in1=stat[:, 2, 0:tw], op=ALU.mult),
                 reads=[t_xt[xi], t_stat], writes=[t_tmpn[ti]])
            S.op("dve", lambda e, kc=kc, ti=ti: e.tensor_tensor(out=tmpn[ti][:, 0:tw], in0=tmpn[ti][:, 0:tw],
                                                                in1=stat[:, 3, 0:tw], op=ALU.add),
                 reads=[t_tmpn[ti], t_stat], writes=[t_tmpn[ti]])
            S.op("act", lambda e, kc=kc, ti=ti: e.activation(out=xt[xi][:, kc, 0:tw], in_=tmpn[ti][:, 0:tw], func=AF.Identity,
                                                             scale=lnw_sb[:, li, s, kc:kc + 1], bias=lnb_sb[:, li, s, kc:kc + 1]),
                 reads=[t_tmpn[ti], t_small], writes=[t_xt[xi]])
        S.dma("sp", lambda e: e.dma_start(out=XT[:, t0:t0 + tw].rearrange("(k p) t -> p k t", p=128), in_=xt[xi][:, :, 0:tw]),
              reads=[t_xt[xi]], writes=[t_XT[t0 // 512]])

    def layer_norm_store(li, s, xi, t0, tw, which):
        for _ in ln_pieces(li, s, xi, t0, tw, which):
            pass

    def drain(gen):
        if gen is not None:
            for _ in gen:
                pass

    def step(gen):
        if gen is not None:
            try:
                next(gen)
            except StopIteration:
                return None
        return gen

    def ffn_tile(li, f, s, t0, tw, xi, prefetch, pending):
        which = 0 if t0 < L else 1
        hid, sil, w13b, w2b = ffn_bufs["hid"], ffn_bufs["sil"], ffn_bufs["w13b"], ffn_bufs["w2b"]
        for kc in range(8):
            S.op("act", lambda e, kc=kc: e.activation(out=hb[:, kc, 0:tw], in_=xt[xi][:, kc, 0:tw], func=AF.Identity,
                                                      scale=mods[:, li, s, 0, kc, which:which + 1],
                                                      bias=mods[:, li, s, 1, kc, which:which + 1]),
                 reads=[t_xt[xi], t_mod], writes=[t_hb])
        for j in range(NJ):
            wi = cnt["w13"] % 3
            cnt["w13"] += 1
            for ag in range(2):
                S.dma("sp", lambda e, wi=wi, ag=ag, j=j: e.dma_start(
                    out=w13b[wi][:, ag].rearrange("p k m -> p (k m)"), in_=W13S[li, f, ag, j]),
                    writes=[t_w13b[wi]], join=(ag == 1))
            pa = (cnt["ps"] % 3) * 2
            cnt["ps"] += 1
            for ag in range(2):
                def mm(e, wi=wi, ag=ag, pa=pa):
                    r = None
                    for kc in range(8):
                        r = e.matmul(psum[pa + ag][:, 0:tw], lhsT=w13b[wi][:, ag, kc, :], rhs=hb[:, kc, 0:tw],
                                     start=(kc == 0), stop=(kc == 7))
                    return r
                S.op("pe", mm, reads=[t_w13b[wi], t_hb], writes=[t_ps[pa + ag]])
            si = cnt["sil"] % 2
            cnt["sil"] += 1
            S.op("act", lambda e, si=si, pa=pa: e.activation(out=sil[si][:, 0:tw], in_=psum[pa][:, 0:tw], func=AF.Silu),
                 reads=[t_ps[pa]], writes=[t_sil[si]])
            S.op("dve", lambda e, si=si, pa=pa, j=j: e.tensor_tensor(out=hid[:, j, 0:tw], in0=psum[pa + 1][:, 0:tw],
                                                                     in1=sil[si][:, 0:tw], op=ALU.mult),
                 reads=[t_ps[pa + 1], t_sil[si]], writes=[t_hid[j]])
            if j >= 1:
                pending = step(pending)
            if j % 2 == 0:
                BG["g"] = step(BG["g"])
        drain(pending)
        dump("hb", hb[:], [t_hb])
        dump("hid", hid[:], t_hid)
        nxt = prefetch()
        for m in range(8):
            wi = cnt["w2"] % 2
            cnt["w2"] += 1
            S.dma("sp", lambda e, wi=wi, m=m: e.dma_start(out=w2b[wi][:].rearrange("p j n -> p (j n)"), in_=W2S[li, f, m]),
                  writes=[t_w2b[wi]])
            pa = (cnt["ps"] % 3) * 2
            cnt["ps"] += 1

            def mm(e, wi=wi, pa=pa):
                r = None
                for j in range(NJ):
                    r = e.matmul(psum[pa][:, 0:tw], lhsT=w2b[wi][:, j, :], rhs=hid[:, j, 0:tw],
                                 start=(j == 0), stop=(j == NJ - 1))
                return r
            S.op("pe", mm, reads=[t_w2b[wi]] + t_hid, writes=[t_ps[pa]])
            S.op("dve", lambda e, m=m, pa=pa: e.scalar_tensor_tensor(
                out=xt[xi][:, m, 0:tw], in0=psum[pa][:, 0:tw], scalar=mods[:, li, s, 2, m, which:which + 1],
                in1=xt[xi][:, m, 0:tw], op0=ALU.mult, op1=ALU.add),
                reads=[t_ps[pa], t_xt[xi], t_mod], writes=[t_xt[xi]])
        dump("u", xt[xi][:], [t_xt[xi]], F32)
        return nxt, ln_pieces(li, s, xi, t0, tw, which)

    BG = {"g": None}

    def ffn_phase(li, f, tile_list):
        s = 0 if f == 0 else 2
        m0 = amark()
        ffn_alloc()
        if f == 1 and li + 1 < NL:
            prep_alloc(f"b{li}", "act")
            BG["g"] = prep_layer(li + 1)
        pend = None
        nxt = load_xt(*tile_list[0])
        for ti_, (t0, tw) in enumerate(tile_list):
            def prefetch(ti_=ti_):
                if ti_ + 1 < len(tile_list):
                    return load_xt(*tile_list[ti_ + 1])
                return None
            nxt, pend = ffn_tile(li, f, s, t0, tw, nxt, prefetch, pend)
        drain(pend)
        drain(BG["g"])
        BG["g"] = None
        arelease(m0)

    def lin(wap, KC, M, rhs, rtoks, tw, evac):
        wi = cnt["wblk"] % NWB
        cnt["wblk"] += 1
        S.dma("sp", lambda e: e.dma_start(out=wblk[wi][:, 0:KC * 128], in_=wap), writes=[t_wblk[wi]])
        pb = cnt["ps"] % 6
        cnt["ps"] += 1
        wv_ = wblk[wi][:, 0:KC * 128].rearrange("p (k m) -> p k m", k=KC)

        def mm(e):
            r = None
            for kc in range(KC):
                r = e.matmul(psum[pb][0:M, 0:tw], lhsT=wv_[:, kc, 0:M], rhs=rhs[:, kc, 0:tw], start=(kc == 0), stop=(kc == KC - 1))
            return r
        S.op("pe", mm, reads=[t_wblk[wi]] + list(rtoks), writes=[t_ps[pb]])
        evac(pb)
        return pb

    def ev_copy(eng, out_ap, wtok, M, tw, scale=None, func=None):
        def evac(pb):
            src = psum[pb][0:M, 0:tw]
            if eng == "act":
                if func is not None or scale is not None:
                    S.op("act", lambda e: e.activation(out=out_ap, in_=src, func=(func or AF.Copy), scale=(scale or 1.0)),
                         reads=[t_ps[pb]], writes=[wtok])
                else:
                    S.op("act", lambda e: e.copy(out=out_ap, in_=src), reads=[t_ps[pb]], writes=[wtok])
            else:
                if scale is not None:
                    S.op(eng, lambda e: e.tensor_scalar_mul(out=out_ap, in0=src, scalar1=scale), reads=[t_ps[pb]], writes=[wtok])
                else:
                    S.op(eng, lambda e: e.tensor_copy(out=out_ap, in_=src), reads=[t_ps[pb]], writes=[wtok])
        return evac

    m1b = {}

    def m1_alloc():
        m1b["wv"] = sb("wv", [128, 8, 1024], BF16)
        m1b["wkvv"] = sb("wkvv", [128, 2, 1024], BF16)
        m1b["dwst"] = sb("dwst", [33, 2, 512], F32)
        m1b["dw"] = sb("dw", [33, 2, 512], BF16)
        m1b["glrT"] = sb("glrT", [33, T], BF16)
        m1b["st8"] = [sb(f"st8_{i}", [128, 8, T], BF16) for i in range(2)]
        m1b["z7"] = [sb(f"z7_{i}", [128, T], F32) for i in range(2)]
        m1b["cqf"] = sb("cqf", [128, 4, T], F32)
        m1b["cqn"] = sb("cqn", [128, 4, T], BF16)
        m1b["ckvn"] = sb("ckvn", [128, 2, T], BF16)
        m1b["cs"] = sb("cs", [128, 2, T], F32)
        m1b["etmp"] = [sb(f"etmp{i}", [128, 512], F32) for i in range(2)]
        m1b["gst"] = sb("gst", [128, 2, 4, 512], BF16)
        m1b["vst"] = sb("vst", [128, 4, 1024], BF16)
        m1b["rtmp"] = [sb(f"rtmp{i}", [128, T], F32) for i in range(2)]
        for k_ in ["wv", "wkvv", "dw", "glrT", "cqf", "cqn", "ckvn", "cs", "gst", "vst"]:
            m1b["t_" + k_] = Tok(k_)
        m1b["t_st8"] = [Tok(), Tok()]
        m1b["t_z7"] = [Tok(), Tok()]
        m1b["t_etmp"] = [Tok(), Tok()]
        m1b["t_rtmp"] = [Tok(), Tok()]
        m1b["n"] = {"st8": 0, "z7": 0, "etmp": 0, "rtmp": 0}

    def rms_apply(src_f, KC, tw, nfeat, norm_ap_fn, out_bf, t_src, t_out):
        S.op("act", lambda e: e.activation(out=usq[:, 0:KC, 0:tw], in_=src_f[:, 0:KC, 0:tw], func=AF.Square),
             reads=[t_src], writes=[t_usq])
        p2 = 7

        def mm2(e):
            r = None
            for kc in range(KC):
                r = e.matmul(psum[p2][:, 0:tw], lhsT=ones_bf[:], rhs=usq[:, kc, 0:tw], start=(kc == 0), stop=(kc == KC - 1))
            return r
        S.op("pe", mm2, reads=[t_usq, t_const], writes=[t_ps[p2]])
        S.op("dve", lambda e: e.tensor_scalar(out=stat[:, 1, 0:tw], in0=psum[p2][:, 0:tw], scalar1=1.0 / nfeat, scalar2=EPS,
                                              op0=ALU.mult, op1=ALU.add), reads=[t_ps[p2]], writes=[t_stat])
        S.op("act", lambda e: e.activation(out=stat[:, 1, 0:tw], in_=stat[:, 1, 0:tw], func=AF.Sqrt), reads=[t_stat], writes=[t_stat])
        S.op("dve", lambda e: e.reciprocal(out=stat[:, 2, 0:tw], in_=stat[:, 1, 0:tw]), reads=[t_stat], writes=[t_stat])
        for kc in range(KC):
            S.op("dve", lambda e, kc=kc: e.scalar_tensor_tensor(out=out_bf[:, kc, 0:tw], in0=src_f[:, kc, 0:tw], scalar=norm_ap_fn(kc),
                                                                in1=stat[:, 2, 0:tw], op0=ALU.mult, op1=ALU.mult),
                 reads=[t_src, t_stat, t_small], writes=[t_out])

    def rope_pair(wA, wB, KC, rhs, rtoks, tw, isx, out_ap, wtok):
        if not isx:
            lin(wA, KC, 128, rhs, rtoks, tw, ev_copy("act", out_ap, wtok, 128, tw))
            return
        n = m1b["n"]
        r1 = n["rtmp"] % 2
        n["rtmp"] += 1
        cs = m1b["cs"]
        rt, t_rt = m1b["rtmp"][r1], m1b["t_rtmp"][r1]

        def evA(pb):
            S.op("dve", lambda e: e.tensor_tensor(out=rt[:, 0:tw], in0=psum[pb][:, 0:tw], in1=cs[:, 0, 0:tw], op=ALU.mult),
                 reads=[t_ps[pb], m1b["t_cs"]], writes=[t_rt])

        def evB(pb):
            ti = cnt["tmpn"] % 2
            cnt["tmpn"] += 1
            S.op("dve", lambda e: e.tensor_tensor(out=tmpn[ti][:, 0:tw], in0=psum[pb][:, 0:tw], in1=cs[:, 1, 0:tw], op=ALU.mult),
                 reads=[t_ps[pb], m1b["t_cs"]], writes=[t_tmpn[ti]])
            S.op("pool", lambda e: e.tensor_tensor(out=out_ap, in0=rt[:, 0:tw], in1=tmpn[ti][:, 0:tw], op=ALU.add),
                 reads=[t_rt, t_tmpn[ti]], writes=[wtok])
        lin(wA, KC, 128, rhs, rtoks, tw, evA)
        lin(wB, KC, 128, rhs, rtoks, tw, evB)

    def m1_tile(li, t0, tw, xi, prefetch):
        which = 0 if t0 < L else 1
        isx = t0 < L
        n = m1b["n"]
        ns = tw // 128
        for kc in range(8):
            S.op("act", lambda e, kc=kc: e.activation(out=hb[:, kc, 0:tw], in_=xt[xi][:, kc, 0:tw], func=AF.Identity,
                                                      scale=mods[:, li, 1, 0, kc, which:which + 1],
                                                      bias=mods[:, li, 1, 1, kc, which:which + 1]),
                 reads=[t_xt[xi], t_mod], writes=[t_hb])
        if isx:
            S.dma("sp", lambda e: e.dma_start(out=m1b["cs"][:, :, 0:tw], in_=rope_cs[:, :, t0:t0 + tw]), writes=[m1b["t_cs"]])

        def st8_next():
            i = n["st8"] % 2
            n["st8"] += 1
            return m1b["st8"][i], m1b["t_st8"][i]
        stq, t_stq = st8_next()
        for h in range(4):
            lin(WIS[li, h], 8, 128, hb, [t_hb], tw, ev_copy("act", stq[:, h, 0:tw], t_stq, 128, tw, scale=128.0 ** -0.5))
        for h in range(4):
            lin(WIS[li, 4 + h], 8, 128, hb, [t_hb], tw, ev_copy("dve", stq[:, 4 + h, 0:tw], t_stq, 128, tw))
        S.dma("act", lambda e: e.dma_start(out=QT[:, t0:t0 + tw].rearrange("(h p) t -> p h t", p=128), in_=stq[:, 0:4, 0:tw]),
              reads=[t_stq], writes=[])
        S.dma("act", lambda e: e.dma_start(out=KT[:, t0:t0 + tw].rearrange("(h p) t -> p h t", p=128), in_=stq[:, 4:8, 0:tw]),
              reads=[t_stq], writes=[])
        str_, t_str = st8_next()
        for c_ in range(8):
            zi = n["z7"] % 2
            n["z7"] += 1
            z7b, t_z7b = m1b["z7"][zi], m1b["t_z7"][zi]

            def evr(pb, c_=c_, z7b=z7b, t_z7b=t_z7b):
                S.op("act", lambda e: e.activation(out=z7b[:, 0:tw], in_=psum[pb][:, 0:tw], func=AF.Silu), reads=[t_ps[pb]], writes=[t_z7b])
                S.op("dve", lambda e: e.tensor_scalar_mul(out=str_[:, c_, 0:tw], in0=z7b[:, 0:tw], scalar1=gnorm_sb[:, li, c_:c_ + 1]),
                     reads=[t_z7b, t_small], writes=[t_str])
            lin(WIS[li, 8 + c_], 8, 128, hb, [t_hb], tw, evr)
        S.dma("act", lambda e: e.dma_start(out=RT[:, t0:t0 + tw].rearrange("(h p) t -> p h t", p=128), in_=str_[:, :, 0:tw]),
              reads=[t_str], writes=[])
        stu, t_stu = st8_next()
        for c_ in range(8):
            zi = n["z7"] % 2
            n["z7"] += 1
            z7b, t_z7b = m1b["z7"][zi], m1b["t_z7"][zi]
            lin(WIS[li, 16 + c_], 8, 128, hb, [t_hb], tw, ev_copy("act", z7b[:, 0:tw], t_z7b, 128, tw))

            def ev8(pb, c_=c_, z7b=z7b, t_z7b=t_z7b):
                S.op("dve", lambda e: e.tensor_tensor(out=stu[:, c_, 0:tw], in0=psum[pb][:, 0:tw], in1=z7b[:, 0:tw], op=ALU.mult),
                     reads=[t_ps[pb], t_z7b], writes=[t_stu])
            lin(WIS[li, 24 + c_], 8, 128, hb, [t_hb], tw, ev8)
        S.dma("act", lambda e: e.dma_start(out=UT[:, t0:t0 + tw].rearrange("(h p) t -> p h t", p=128), in_=stu[:, :, 0:tw]),
              reads=[t_stu], writes=[])
        glrT, dw, gst = m1b["glrT"], m1b["dw"], m1b["gst"]
        lin(WIS[li, 70], 8, 32, hb, [t_hb], tw, ev_copy("dve", glrT[0:32, 0:tw], m1b["t_glrT"], 32, tw))
        for sub in range(ns):
            for d_ in range(2):
                pb = cnt["ps"] % 6
                cnt["ps"] += 1
                S.op("pe", lambda e, sub=sub, d_=d_, pb=pb: e.matmul(psum[pb][:, :], lhsT=glrT[0:33, sub * 128:(sub + 1) * 128],
                                                                     rhs=dw[0:33, d_, :], start=True, stop=True),
                     reads=[m1b["t_glrT"], m1b["t_dw"]], writes=[t_ps[pb]])
                ei = n["etmp"] % 2
                n["etmp"] += 1
                et, t_et = m1b["etmp"][ei], m1b["t_etmp"][ei]
                S.op("act", lambda e, pb=pb, et=et: e.activation(out=et[:], in_=psum[pb][:, :], func=AF.Exp, scale=-1.0),
                     reads=[t_ps[pb]], writes=[t_et])
                S.op("act", lambda e, et=et, sub=sub, d_=d_: e.activation(out=gst[:, d_, sub, :], in_=et[:], func=AF.Ln, bias=1.0),
                     reads=[t_et], writes=[m1b["t_gst"]])
        for d_ in range(2):
            S.dma("act", lambda e, d_=d_: e.dma_start(out=GP[d_, t0:t0 + tw, :].rearrange("(s p) f -> p s f", p=128),
                                                     in_=gst[:, d_, 0:ns, :]), reads=[m1b["t_gst"]], writes=[])
        wv, vst = m1b["wv"], m1b["vst"]
        for sub in range(ns):
            for half in range(2):
                pb = cnt["ps"] % 6
                cnt["ps"] += 1

                def mmv(e, sub=sub, half=half, pb=pb):
                    r = None
                    for kc in range(8):
                        r = e.matmul(psum[pb][:, :], lhsT=hb[:, kc, sub * 128:(sub + 1) * 128], rhs=wv[:, kc, half * 512:(half + 1) * 512],
                                     start=(kc == 0), stop=(kc == 7))
                    return r
                S.op("pe", mmv, reads=[t_hb, m1b["t_wv"]], writes=[t_ps[pb]])
                eng = "act" if half == 0 else "dve"
                ev_copy(eng, vst[:, sub, half * 512:(half + 1) * 512], m1b["t_vst"], 128, 512)(pb)
        S.dma("act", lambda e: e.dma_start(out=VG[t0:t0 + tw, :].rearrange("(s p) f -> p s f", p=128), in_=vst[:, 0:ns, :]),
              reads=[m1b["t_vst"]], writes=[])
        stk, t_stk = st8_next()
        rope_pair(WIS[li, 71], WIS[li, 72], 8, hb, [t_hb], tw, isx, stk[:, 0, 0:tw], t_stk)
        S.dma("act", lambda e: e.dma_start(out=KRT[:, t0:t0 + tw], in_=stk[:, 0, 0:tw]), reads=[t_stk], writes=[])
        cqf, cqn, ckvn = m1b["cqf"], m1b["cqn"], m1b["ckvn"]
        for c_ in range(4):
            lin(WIS[li, 32 + c_], 8, 128, hb, [t_hb], tw, ev_copy("act", cqf[:, c_, 0:tw], m1b["t_cqf"], 128, tw))
        rms_apply(cqf, 4, tw, 512, lambda kc: qnorm_sb[:, li, kc:kc + 1], cqn, m1b["t_cqf"], m1b["t_cqn"])
        for c_ in range(2):
            lin(WIS[li, 36 + c_], 8, 128, hb, [t_hb], tw, ev_copy("act", cqf[:, c_, 0:tw], m1b["t_cqf"], 128, tw))
        rms_apply(cqf, 2, tw, 256, lambda kc: kvnorm_sb[:, li, kc:kc + 1], ckvn, m1b["t_cqf"], m1b["t_ckvn"])
        nxt = prefetch()
        stn, t_stn = st8_next()
        for h in range(8):
            lin(WUQS[li, h], 4, 128, cqn, [m1b["t_cqn"]], tw, ev_copy("act" if h % 2 else "dve", stn[:, h, 0:tw], t_stn, 128, tw))
        S.dma("act", lambda e: e.dma_start(out=QNT[:, t0:t0 + tw].rearrange("(h p) t -> p h t", p=128), in_=stn[:, :, 0:tw]),
              reads=[t_stn], writes=[])
        stp, t_stp = st8_next()
        for a in range(4):
            rope_pair(WUQS[li, 8 + a], WUQS[li, 12 + a], 4, cqn, [m1b["t_cqn"]], tw, isx, stp[:, a, 0:tw], t_stp)
        S.dma("act", lambda e: e.dma_start(out=QRT[:, t0:t0 + tw].rearrange("(h p) t -> p h t", p=128), in_=stp[:, 0:4, 0:tw]),
              reads=[t_stp], writes=[])
        stkn, t_stkn = st8_next()
        for h in range(8):
            lin(WUKS[li, h], 2, 128, ckvn, [m1b["t_ckvn"]], tw, ev_copy("act" if h % 2 else "dve", stkn[:, h, 0:tw], t_stkn, 128, tw))
        S.dma("act", lambda e: e.dma_start(out=KNT[:, t0:t0 + tw].rearrange("(h p) t -> p h t", p=128), in_=stkn[:, :, 0:tw]),
              reads=[t_stkn], writes=[])
        wkvv = m1b["wkvv"]
        for sub in range(ns):
            for half in range(2):
                pb = cnt["ps"] % 6
                cnt["ps"] += 1

                def mmv2(e, sub=sub, half=half, pb=pb):
                    r = None
                    for kc in range(2):
                        r = e.matmul(psum[pb][:, :], lhsT=ckvn[:, kc, sub * 128:(sub + 1) * 128], rhs=wkvv[:, kc, half * 512:(half + 1) * 512],
                                     start=(kc == 0), stop=(kc == 1))
                    return r
                S.op("pe", mmv2, reads=[m1b["t_ckvn"], m1b["t_wkvv"]], writes=[t_ps[pb]])
                eng = "act" if half == 0 else "dve"
                ev_copy(eng, vst[:, sub, half * 512:(half + 1) * 512], m1b["t_vst"], 128, 512)(pb)
        c0 = t0 // 128
        for sub in range(ns):
            S.dma("act", lambda e, sub=sub: e.dma_start(out=VM[:, :, c0 + sub, :].rearrange("h p m -> p h m"),
                                                       in_=vst[:, sub, :].rearrange("p (h m) -> p h m", h=8)),
                  reads=[m1b["t_vst"]], writes=[])
        return nxt

    def mixer_in_phase(li, tile_list):
        m0 = amark()
        m1_alloc()
        S.dma("sp", lambda e: e.dma_start(out=m1b["wv"][:, :, 0:512], in_=WVS[li, 0].rearrange("p (k c) -> p k c", k=8)),
              writes=[m1b["t_wv"]])
        S.dma("sp", lambda e: e.dma_start(out=m1b["wv"][:, :, 512:1024], in_=WVS[li, 1].rearrange("p (k c) -> p k c", k=8)),
              writes=[m1b["t_wv"]], join=True)
        S.dma("sp", lambda e: e.dma_start(out=m1b["wkvv"][:].rearrange("p k c -> p (k c)"), in_=WUKVV[li]), writes=[m1b["t_wkvv"]])
        dwst, dw = m1b["dwst"], m1b["dw"]
        S.op("pool", lambda e: e.memset(dwst[:], 0.0), writes=[m1b["t_dw"]])
        S.op("pool", lambda e: e.memset(m1b["glrT"][32:33, :], 1.0), writes=[m1b["t_glrT"]])
        S.dma("sp", lambda e: e.dma_start(out=dwst[0:16, 0, :], in_=gdw_in[li, 0]), writes=[m1b["t_dw"]])
        S.dma("sp", lambda e: e.dma_start(out=dwst[16:32, 1, :], in_=gdw_in[li, 1]), writes=[m1b["t_dw"]])
        S.dma("sp", lambda e: e.dma_start(out=dwst[32:33, :, :], in_=gdb_in[li:li + 1]), writes=[m1b["t_dw"]])
        S.op("dve", lambda e: e.tensor_copy(out=dw[:], in_=dwst[:]), reads=[m1b["t_dw"]], writes=[m1b["t_dw"]])
        nxt = load_xt(*tile_list[0])
        for ti_, (t0, tw) in enumerate(tile_list):
            def prefetch(ti_=ti_):
                if ti_ + 1 < len(tile_list):
                    return load_xt(*tile_list[ti_ + 1])
                return None
            nxt = m1_tile(li, t0, tw, nxt, prefetch)
        arelease(m0)

    def gla_phase(li):
        m0 = amark()
        ld = []
        for i in range(3):
            b_ = dict(gp=sb(f"g_gp{i}", [128, 512], BF16), q=sb(f"g_q{i}", [128, 4, 128], BF16), k=sb(f"g_k{i}", [128, 4, 128], BF16),
                      v=sb(f"g_v{i}", [128, 1024], BF16), of=sb(f"g_of{i}", [128, 8, 128], F32), r=sb(f"g_r{i}", [128, 8, 128], BF16),
                      t_in=Tok(), t_of=Tok())
            ld.append(b_)
        wk = []
        for i in range(2):
            b_ = dict(eq=sb(f"g_eq{i}", [128, 4, 128], F32), ek=sb(f"g_ek{i}", [128, 4, 128], F32), qe=sb(f"g_qe{i}", [128, 4, 128], BF16),
                      ke=sb(f"g_ke{i}", [128, 4, 128], BF16), kd=sb(f"g_kd{i}", [128, 4, 128], BF16), At=sb(f"g_At{i}", [128, 4, 128], BF16),
                      kdt=sb(f"g_kdt{i}", [128, 4, 128], BF16), t_e=Tok(), t_qe=Tok(), t_ke=Tok(), t_kd=Tok(), t_At=Tok(), t_kdt=Tok())
            wk.append(b_)
        NR = 4
        Sr = [sb(f"g_S{i}", [128, 4, 256], F32) for i in range(NR)]
        t_Sr = [[Tok() for _ in range(4)] for _ in range(NR)]
        Sbr = [sb(f"g_Sb{i}", [128, 4, 256], BF16) for i in range(NR)]
        t_Sbr = [Tok() for _ in range(NR)]
        gst_ = {"n": 0}
        ost = [sb(f"g_ost{i}", [128, 8, 128], F32) for i in range(2)]
        t_ost = [Tok(), Tok()]
        sq = sb("g_sq", [128, 8, 128], BF16)
        t_sq = Tok()
        rst = sb("g_rst", [128, 4, 128], F32)
        t_rst = Tok()
        t1b = sb("g_t1", [128, 8, 128], F32)
        t_t1 = Tok()
        anb = [sb(f"g_an{i}", [128, 8, 128], BF16) for i in range(2)]
        t_an = [Tok(), Tok()]
        B_BC, B_A, B_T, B_ST, B_O0, B_O1 = 0, 0, 1, 1, 2, 3
        B_U = [(4, 5), (6, 7)]
        psT_bf = psum[B_T][:].bitcast(BF16)

        def loads(d_, t0, i):
            b_ = ld[i]
            S.dma("sp", lambda e: e.dma_start(out=b_["gp"][:], in_=GP[d_, t0:t0 + 128, :]), writes=[b_["t_in"]])
            S.dma("sp", lambda e: e.dma_start(out=b_["q"][:], in_=QT[:, t0:t0 + 128].rearrange("(h p) t -> p h t", p=128)),
                  writes=[b_["t_in"]], join=True)
            S.dma("sp", lambda e: e.dma_start(out=b_["k"][:], in_=KT[:, t0:t0 + 128].rearrange("(h p) t -> p h t", p=128)),
                  writes=[b_["t_in"]], join=True)
            S.dma("sp", lambda e: e.dma_start(out=b_["v"][:], in_=VG[t0:t0 + 128, :]), writes=[b_["t_in"]], join=True)
            if d_ == 1:
                S.dma("sp", lambda e: e.dma_start(out=b_["of"][:], in_=OF[:, t0:t0 + 128].rearrange("(k p) t -> p k t", p=128)),
                      writes=[b_["t_of"]])
                S.dma("sp", lambda e: e.dma_start(out=b_["r"][:], in_=RT[:, t0:t0 + 128].rearrange("(k p) t -> p k t", p=128)),
                      writes=[b_["t_of"]], join=True)

        def prologue(d_, i, wi_):
            b_, w_ = ld[i], wk[wi_]
            esel = 63 if d_ == 0 else 0

            def mm_bc(e):
                r = None
                for h in range(4):
                    r = e.matmul(psum[B_BC][:, h * 128:(h + 1) * 128], lhsT=b_["gp"][:, h * 128:(h + 1) * 128], rhs=gmask[:, d_, :],
                                 start=True, stop=True)
                return r
            S.op("pe", mm_bc, reads=[b_["t_in"], t_small], writes=[t_ps[B_BC]])
            bcv = psum[B_BC][:].rearrange("p (h t) -> p h t", h=4)
            S.op("act", lambda e: e.activation(out=w_["eq"][:], in_=bcv, func=AF.Exp), reads=[t_ps[B_BC]], writes=[w_["t_e"]])
            S.op("act", lambda e: e.activation(out=w_["ek"][:], in_=bcv, func=AF.Exp, scale=-1.0), reads=[t_ps[B_BC]], writes=[w_["t_e"]])
            S.op("dve", lambda e: e.tensor_tensor(out=w_["qe"][:], in0=b_["q"][:], in1=w_["eq"][:], op=ALU.mult),
                 reads=[b_["t_in"], w_["t_e"]], writes=[w_["t_qe"]])
            S.op("dve", lambda e: e.tensor_tensor(out=w_["ke"][:], in0=b_["k"][:], in1=w_["ek"][:], op=ALU.mult),
                 reads=[b_["t_in"], w_["t_e"]], writes=[w_["t_ke"]])
            S.op("pool", lambda e: e.tensor_tensor(
                out=w_["kd"][:].rearrange("p h (c j) -> p h c j", c=2), in0=w_["ke"][:].rearrange("p h (c j) -> p h c j", c=2),
                in1=w_["eq"][:, :, esel::64].unsqueeze(3).to_broadcast([128, 4, 2, 64]), op=ALU.mult),
                reads=[w_["t_ke"], w_["t_e"]], writes=[w_["t_kd"]])

            def mm_a(e):
                r = None
                for h in range(4):
                    r = e.matmul(psum[B_A][:, h * 128:(h + 1) * 128], lhsT=w_["ke"][:, h, :], rhs=w_["qe"][:, h, :], start=True, stop=True)
                return r
            S.op("pe", mm_a, reads=[w_["t_ke"], w_["t_qe"]], writes=[t_ps[B_A]])
            S.op("dve", lambda e: e.tensor_tensor(out=w_["At"][:], in0=psum[B_A][:].rearrange("p (h t) -> p h t", h=4),
                                                  in1=gmask[:, 2 + d_, :].unsqueeze(1).to_broadcast([128, 4, 128]), op=ALU.mult),
                 reads=[t_ps[B_A], t_small], writes=[w_["t_At"]])

            def mm_t(e):
                r = None
                for h in range(4):
                    r = e.transpose(psT_bf[:, h * 128:(h + 1) * 128], w_["kd"][:, h, :], ident_bf[:])
                return r
            S.op("pe", mm_t, reads=[w_["t_kd"], t_const], writes=[t_ps[B_T]])
            S.op("act", lambda e: e.copy(out=w_["kdt"][:], in_=psT_bf[:, 0:512].rearrange("p (h t) -> p h t", h=4)),
                 reads=[t_ps[B_T]], writes=[w_["t_kdt"]])

        def chain(d_, t0, i, oi):
            b_, w_ = ld[i], wk[oi]
            chunks = (0, 1) if d_ == 0 else (1, 0)
            esel = 63 if d_ == 0 else 0

            def oreg(h, c2):
                return psum[B_O0 + h // 2][:, ((h % 2) * 2 + c2) * 128:((h % 2) * 2 + c2 + 1) * 128]

            def mm_intra(e):
                r = None
                for h in range(4):
                    for c2 in range(2):
                        r = e.matmul(oreg(h, c2), lhsT=b_["v"][:, h * 256 + c2 * 128:h * 256 + (c2 + 1) * 128], rhs=w_["At"][:, h, :],
                                     start=(h % 2 == 0 and c2 == 0), stop=False, skip_group_check=True)
                return r
            n0 = gst_["n"]
            gst_["n"] += 2
            for ci, c in enumerate(chunks):
                nn = n0 + ci
                ub0, ub1 = B_U[nn % 2]

                def mm_upd(e, c=c, ub0=ub0, ub1=ub1):
                    r = None
                    for h in range(4):
                        r = e.matmul(psum[(ub0, ub1)[h // 2]][:, (h % 2) * 256:(h % 2 + 1) * 256], lhsT=w_["kdt"][c * 64:(c + 1) * 64, h, :],
                                     rhs=b_["v"][c * 64:(c + 1) * 64, h * 256:(h + 1) * 256], start=True, stop=True)
                    return r
                S.op("pe", mm_upd, reads=[w_["t_kdt"], b_["t_in"]], writes=[t_ps[ub0], t_ps[ub1]])
                cur, prv = nn % NR, (nn - 1) % NR
                for h in range(4):
                    S.op("dve", lambda e, h=h, c=c, cur=cur, prv=prv, ub0=ub0, ub1=ub1: e.scalar_tensor_tensor(
                        out=Sr[cur][:, h, :], in0=Sr[prv][:, h, :], scalar=w_["eq"][:, h, c * 64 + esel:c * 64 + esel + 1],
                        in1=psum[(ub0, ub1)[h // 2]][:, (h % 2) * 256:(h % 2 + 1) * 256], op0=ALU.mult, op1=ALU.add),
                        reads=[t_Sr[prv][h], w_["t_e"], t_ps[(ub0, ub1)[h // 2]]], writes=[t_Sr[cur][h]])
                S.op("act", lambda e, cur=cur: e.copy(out=Sbr[cur][:], in_=Sr[cur][:]), reads=t_Sr[cur], writes=[t_Sbr[cur]])
            S.op("pe", mm_intra, reads=[b_["t_in"], w_["t_At"]], writes=[t_ps[B_O0], t_ps[B_O1]])
            for ci, c in enumerate(chunks):
                prv = (n0 + ci - 1) % NR

                def mm_inter(e, c=c, ci=ci, prv=prv):
                    r = None
                    for h in range(4):
                        for c2 in range(2):
                            r = e.matmul(oreg(h, c2)[:, c * 64:(c + 1) * 64], lhsT=Sbr[prv][:, h, c2 * 128:(c2 + 1) * 128],
                                         rhs=w_["qe"][:, h, c * 64:(c + 1) * 64], start=False, stop=(ci == 1), skip_group_check=True)
                    return r
                S.op("pe", mm_inter, reads=[t_Sbr[prv], w_["t_qe"]], writes=[t_ps[B_O0], t_ps[B_O1]])
            o_t, t_o = ost[oi], t_ost[oi]
            if d_ == 0:
                S.op("act", lambda e: e.copy(out=o_t[:, 0:4, :], in_=psum[B_O0][:].rearrange("p (k t) -> p k t", k=4)),
                     reads=[t_ps[B_O0]], writes=[t_o])
                S.op("dve", lambda e: e.tensor_copy(out=o_t[:, 4:8, :], in_=psum[B_O1][:].rearrange("p (k t) -> p k t", k=4)),
                     reads=[t_ps[B_O1]], writes=[t_o])
                S.dma("sp", lambda e: e.dma_start(out=OF[:, t0:t0 + 128].rearrange("(k p) t -> p k t", p=128), in_=o_t[:]),
                      reads=[t_o], writes=[])
                return
            for half in range(2):
                S.op("dve", lambda e, half=half: e.tensor_tensor(
                    out=o_t[:, half * 4:half * 4 + 4, :], in0=psum[B_O0 + half][:].rearrange("p (k t) -> p k t", k=4),
                    in1=b_["of"][:, half * 4:half * 4 + 4, :], op=ALU.add), reads=[t_ps[B_O0 + half], b_["t_of"]], writes=[t_o])
            S.op("act", lambda e: e.activation(out=sq[:], in_=o_t[:], func=AF.Square), reads=[t_o], writes=[t_sq])

            def mm_st(e):
                r = None
                for h in range(4):
                    for c2 in range(2):
                        r = e.matmul(psum[B_ST][:, h * 128:(h + 1) * 128], lhsT=ones_bf[:], rhs=sq[:, h * 2 + c2, :],
                                     start=(h == 0 and c2 == 0), stop=(c2 == 1), skip_group_check=True)
                return r
            S.op("pe", mm_st, reads=[t_sq, t_const], writes=[t_ps[B_ST]])
            S.op("dve", lambda e: e.tensor_scalar(out=rst[:], in0=psum[B_ST][:].rearrange("p (h t) -> p h t", h=4), scalar1=1.0 / 256,
                                                  scalar2=EPS, op0=ALU.mult, op1=ALU.add), reads=[t_ps[B_ST]], writes=[t_rst])
            S.op("act", lambda e: e.activation(out=rst[:], in_=rst[:], func=AF.Sqrt), reads=[t_rst], writes=[t_rst])
            S.op("dve", lambda e: e.reciprocal(out=rst[:], in_=rst[:]), reads=[t_rst], writes=[t_rst])
            S.op("dve", lambda e: e.tensor_tensor(out=t1b[:].rearrange("p (h c) t -> p h c t", c=2),
                                                  in0=o_t[:].rearrange("p (h c) t -> p h c t", c=2),
                                                  in1=rst[:].unsqueeze(2).to_broadcast([128, 4, 2, 128]), op=ALU.mult),
                 reads=[t_o, t_rst], writes=[t_t1])
            a_t, t_a = anb[oi], t_an[oi]
            S.op("dve", lambda e: e.tensor_tensor(out=a_t[:], in0=t1b[:], in1=b_["r"][:], op=ALU.mult),
                 reads=[t_t1, b_["t_of"]], writes=[t_a])
            S.dma("sp", lambda e: e.dma_start(out=AN[:, t0:t0 + 128].rearrange("(k p) t -> p k t", p=128), in_=a_t[:]),
                  reads=[t_a], writes=[])

        for d_ in range(2):
            gst_["n"] = 0
            S.op("dve", lambda e: e.memset(Sr[NR - 1][:], 0.0), writes=t_Sr[NR - 1])
            S.op("pool", lambda e: e.memset(Sbr[NR - 1][:], 0.0), writes=[t_Sbr[NR - 1]])
            seq = [L, L + 128] + [i * 128 for i in range(L // 128)]
            if d_ == 1:
                seq = [L + 128, L] + [i * 128 for i in reversed(range(L // 128))]
            loads(d_, seq[0], 0)
            if len(seq) > 1:
                loads(d_, seq[1], 1)
            prologue(d_, 0, 0)
            for n_, t0 in enumerate(seq):
                if n_ + 2 < len(seq):
                    loads(d_, seq[n_ + 2], (n_ + 2) % 3)
                if n_ + 1 < len(seq):
                    prologue(d_, (n_ + 1) % 3, (n_ + 1) % 2)
                chain(d_, t0, n_ % 3, n_ % 2)
            if d_ == 0:
                S.full_barrier()
        arelease(m0)

    def attn_phase(li, qtiles):
        m0 = amark()
        kr2 = [sb(f"a_kr{i}", [128, NT], BF16) for i in range(2)]
        t_kr = Tok()
        kn = [sb(f"a_kn{i}", [128, NT], BF16) for i in range(2)]
        vh = [sb(f"a_vh{i}", [128, NT // 128, 128], BF16) for i in range(2)]
        t_kv = [Tok(), Tok()]
        qn = [sb(f"a_qn{i}", [128, T], BF16) for i in range(2)]
        qr = [sb(f"a_qr{i}", [128, T], BF16) for i in range(2)]
        t_q = [Tok(), Tok()]
        NP = 6
        pT = [sb(f"a_pT{i}", [128, T], BF16) for i in range(NP)]
        t_pT = [Tok() for _ in range(NP)]
        accd = [sb(f"a_accd{i}", [128, T], F32) for i in range(2)]
        accp = [sb(f"a_accp{i}", [128, T], F32) for i in range(2)]
        t_accd = [Tok(), Tok()]
        t_accp = [Tok(), Tok()]
        dhi = [sb(f"a_dhi{i}", [128, T], BF16) for i in range(2)]
        dlo = [sb(f"a_dlo{i}", [128, T], BF16) for i in range(2)]
        t_dh = [Tok(), Tok()]
        rden = sb("a_rden", [128, T], F32)
        t_rden = Tok()
        cst = [sb(f"a_cst{i}", [128, T], BF16) for i in range(2)]
        t_cst = [Tok(), Tok()]
        SCALE = 192.0 ** -0.5
        st = {"q": 0, "p": 0, "s": 0, "o": 0}

        S.op("pool", lambda e: e.memset(kr2[0][64:128, :], 0.0), writes=[t_kr])
        S.op("pool", lambda e: e.memset(kr2[1][0:64, :], 0.0), writes=[t_kr])
        S.dma("sp", lambda e: e.dma_start(out=kr2[0][0:64, :], in_=KRT[0:64, :]), writes=[t_kr])
        S.dma("sp", lambda e: e.dma_start(out=kr2[1][64:128, :], in_=KRT[64:128, :]), writes=[t_kr])

        def load_head(h, i):
            S.dma("sp", lambda e: e.dma_start(out=kn[i][:], in_=KNT[h * 128:(h + 1) * 128, :]), writes=[t_kv[i]])
            S.dma("sp", lambda e: e.dma_start(out=vh[i][:], in_=VM[h]), writes=[t_kv[i]], join=True)

        def load_q(h, t0, tw):
            i = st["q"] % 2
            st["q"] += 1
            S.dma("sp", lambda e: e.dma_start(out=qn[i][:, 0:tw], in_=QNT[h * 128:(h + 1) * 128, t0:t0 + tw]), writes=[t_q[i]])
            S.dma("sp", lambda e: e.dma_start(out=qr[i][:, 0:tw], in_=QRT[(h // 2) * 128:(h // 2 + 1) * 128, t0:t0 + tw]), writes=[t_q[i]], join=True)
            return i

        def attend(h, hi, t0, tw, qi):
            hp = h % 2
            chunks = list(range(NT // 128)) if t0 < L else list(range(L // 128, NT // 128))
            oset = st["o"] % 2
            st["o"] += 1
            B_O, B_D = 4 + oset, 6 + oset
            nch = len(chunks)
            sbank = {}
            a_d, a_p = accd[oset], accp[oset]
            t_ad, t_ap = t_accd[oset], t_accp[oset]
            used = {"dve": False, "pool": False, "pe": False}

            def qk(ci):
                c = chunks[ci]
                pb = st["s"] % 4
                st["s"] += 1
                sbank[ci] = pb

                def mm(e):
                    e.matmul(psum[pb][:, 0:tw], lhsT=kn[hi][:, c * 128:(c + 1) * 128], rhs=qn[qi][:, 0:tw], start=True, stop=False)
                    return e.matmul(psum[pb][:, 0:tw], lhsT=kr2[hp][:, c * 128:(c + 1) * 128], rhs=qr[qi][:, 0:tw], start=False, stop=True)
                S.op("pe", mm, reads=[t_kv[hi], t_kr, t_q[qi]], writes=[t_ps[pb]])

            def pv(ci):
                c = chunks[ci]
                pb = sbank[ci]
                pi = st["p"] % NP
                st["p"] += 1
                S.op("act", lambda e: e.activation(out=pT[pi][:, 0:tw], in_=psum[pb][:, 0:tw], func=AF.Exp, scale=SCALE),
                     reads=[t_ps[pb]], writes=[t_pT[pi]])
                S.op("pe", lambda e: e.matmul(psum[B_O][:, 0:tw], lhsT=vh[hi][:, c, :], rhs=pT[pi][:, 0:tw], start=(ci == 0), stop=(ci == nch - 1)),
                     reads=[t_kv[hi], t_pT[pi]], writes=[t_ps[B_O]])
                if ci % 4 == 3:
                    first = not used["pe"]
                    used["pe"] = True
                    S.op("pe", lambda e: e.matmul(psum[B_D][:, 0:tw], lhsT=ones_bf[:], rhs=pT[pi][:, 0:tw], start=first, stop=False),
                         reads=[t_pT[pi], t_const], writes=[t_ps[B_D]])
                    return
                eng = "dve"
                a_, t_a_ = a_d, t_ad
                if not used[eng]:
                    used[eng] = True
                    S.op(eng, lambda e: e.tensor_copy(out=a_[:, 0:tw], in_=pT[pi][:, 0:tw]), reads=[t_pT[pi]], writes=[t_a_])
                else:
                    S.op(eng, lambda e: e.tensor_tensor(out=a_[:, 0:tw], in0=a_[:, 0:tw], in1=pT[pi][:, 0:tw], op=ALU.add),
                         reads=[t_pT[pi], t_a_], writes=[t_a_])
            LOOK = 2
            for ci in range(min(LOOK, nch)):
                qk(ci)
            for ci in range(nch):
                if ci + LOOK < nch:
                    qk(ci + LOOK)
                pv(ci)
            if used["pool"]:
                S.op("dve", lambda e: e.tensor_tensor(out=a_d[:, 0:tw], in0=a_d[:, 0:tw], in1=a_p[:, 0:tw], op=ALU.add),
                     reads=[t_ad, t_ap], writes=[t_ad])
            S.op("dve", lambda e: e.tensor_copy(out=dhi[oset][:, 0:tw], in_=a_d[:, 0:tw]), reads=[t_ad], writes=[t_dh[oset]])
            S.op("dve", lambda e: e.tensor_tensor(out=dlo[oset][:, 0:tw], in0=a_d[:, 0:tw], in1=dhi[oset][:, 0:tw], op=ALU.subtract),
                 reads=[t_ad, t_dh[oset]], writes=[t_dh[oset]])

            def mmd(e):
                e.matmul(psum[B_D][:, 0:tw], lhsT=ones_bf[:], rhs=dhi[oset][:, 0:tw], start=(not used["pe"]), stop=False)
                return e.matmul(psum[B_D][:, 0:tw], lhsT=ones_bf[:], rhs=dlo[oset][:, 0:tw], start=False, stop=True)
            S.op("pe", mmd, reads=[t_dh[oset], t_const], writes=[t_ps[B_D]])
            S.op("dve", lambda e: e.reciprocal(out=rden[:, 0:tw], in_=psum[B_D][:, 0:tw]), reads=[t_ps[B_D]], writes=[t_rden])
            ci_ = oset
            S.op("dve", lambda e: e.tensor_tensor(out=cst[ci_][:, 0:tw], in0=psum[B_O][:, 0:tw], in1=rden[:, 0:tw], op=ALU.mult),
                 reads=[t_ps[B_O], t_rden], writes=[t_cst[ci_]])
            S.dma("sp", lambda e: e.dma_start(out=CT[h * 128:(h + 1) * 128, t0:t0 + tw], in_=cst[ci_][:, 0:tw]),
                  reads=[t_cst[ci_]], writes=[])

        load_head(0, 0)
        work = [(h, t0, tw) for h in range(8) for (t0, tw) in qtiles]
        nq = load_q(*work[0])
        for wi_, (h, t0, tw) in enumerate(work):
            hi = h % 2
            if t0 == qtiles[0][0] and h + 1 < 8:
                load_head(h + 1, 1 - hi)
            qi = nq
            if wi_ + 1 < len(work):
                nq = load_q(*work[wi_ + 1])
            attend(h, hi, t0, tw, qi)
        arelease(m0)

    def merge_phase(li, tile_list):
        m0 = amark()
        anb = [sb(f"m_an{i}", [128, 8, T], BF16) for i in range(2)]
        ctb = [sb(f"m_ct{i}", [128, 8, T], BF16) for i in range(2)]
        utb = [sb(f"m_ut{i}", [128, 8, T + 2], BF16) for i in range(2)]
        t_in = [Tok(), Tok()]
        bnb = sb("m_bn", [128, 8, T], BF16)
        t_bn = Tok()
        mbb = sb("m_mb", [128, 8, T], BF16)
        t_mb = Tok()
        gts = [sb(f"m_g{i}", [128, T], F32) for i in range(3)]
        t_g = [Tok() for _ in range(3)]
        acc = [sb(f"m_acc{i}", [128, T], F32) for i in range(2)]
        t_acc = [Tok(), Tok()]
        tt_ = [sb(f"m_t{i}", [128, T], F32) for i in range(2)]
        t_tt = [Tok(), Tok()]
        cvt = [sb(f"m_cv{i}", [128, T], F32) for i in range(2)]
        t_cv = [Tok(), Tok()]
        st = {"in": 0, "g": 0, "acc": 0, "t": 0, "cv": 0}

        def load_in(t0, tw):
            i = st["in"] % 2
            st["in"] += 1
            S.dma("sp", lambda e: e.dma_start(out=anb[i][:, :, 0:tw], in_=AN[:, t0:t0 + tw].rearrange("(k p) t -> p k t", p=128)),
                  writes=[t_in[i]])
            S.dma("sp", lambda e: e.dma_start(out=ctb[i][:, :, 0:tw], in_=CT[:, t0:t0 + tw].rearrange("(k p) t -> p k t", p=128)),
                  writes=[t_in[i]], join=True)
            first = (t0 == 0 or t0 == L)
            lastt = (t0 + tw == L or t0 + tw == NT)
            lo = t0 if first else t0 - 1
            hi_ = t0 + tw if lastt else t0 + tw + 1
            o0 = 1 if first else 0
            S.dma("sp", lambda e: e.dma_start(out=utb[i][:, :, o0:o0 + hi_ - lo], in_=UT[:, lo:hi_].rearrange("(k p) t -> p k t", p=128)),
                  writes=[t_in[i]], join=True)
            if first:
                S.op("pool", lambda e: e.memset(utb[i][:, :, 0:1], 0.0), writes=[t_in[i]])
            if lastt:
                S.op("pool", lambda e: e.memset(utb[i][:, :, tw + 1:tw + 2], 0.0), writes=[t_in[i]])
            return i

        def m4_tile(t0, tw, xi, ii, prefetch, pending):
            which = 0 if t0 < L else 1
            an_, ct_, ut_ = anb[ii], ctb[ii], utb[ii]
            for kc in range(8):
                S.op("act", lambda e, kc=kc: e.activation(out=hb[:, kc, 0:tw], in_=xt[xi][:, kc, 0:tw], func=AF.Identity,
                                                          scale=mods[:, li, 1, 0, kc, which:which + 1],
                                                          bias=mods[:, li, 1, 1, kc, which:which + 1]),
                     reads=[t_xt[xi], t_mod], writes=[t_hb])
            for c_ in range(8):
                vi = st["cv"] % 2
                st["cv"] += 1
                cv_, t_cv_ = cvt[vi], t_cv[vi]
                S.op("act", lambda e, c_=c_, cv_=cv_: e.activation(out=cv_[:, 0:tw], in_=ut_[:, c_, 1:tw + 1], func=AF.Copy,
                                                                   scale=convw_sb[:, li, 1, c_:c_ + 1]),
                     reads=[t_in[ii], t_small], writes=[t_cv_])
                S.op("dve", lambda e, c_=c_, cv_=cv_: e.scalar_tensor_tensor(out=cv_[:, 0:tw], in0=ut_[:, c_, 0:tw],
                                                                              scalar=convw_sb[:, li, 0, c_:c_ + 1], in1=cv_[:, 0:tw],
                                                                              op0=ALU.mult, op1=ALU.add),
                     reads=[t_in[ii], t_small, t_cv_], writes=[t_cv_])
                S.op("dve", lambda e, c_=c_, cv_=cv_: e.scalar_tensor_tensor(out=cv_[:, 0:tw], in0=ut_[:, c_, 2:tw + 2],
                                                                              scalar=convw_sb[:, li, 2, c_:c_ + 1], in1=cv_[:, 0:tw],
                                                                              op0=ALU.mult, op1=ALU.add),
                     reads=[t_in[ii], t_small, t_cv_], writes=[t_cv_])

                def ev6(pb, c_=c_, cv_=cv_, t_cv_=t_cv_):
                    S.op("dve", lambda e: e.tensor_tensor(out=bnb[:, c_, 0:tw], in0=psum[pb][:, 0:tw], in1=cv_[:, 0:tw], op=ALU.mult),
                         reads=[t_ps[pb], t_cv_], writes=[t_bn])
                lin(WIS[li, 38 + c_], 8, 128, hb, [t_hb], tw, ev6)
                pending = step(pending)
            srcs = [(an_, [t_in[ii]]), (bnb, [t_bn]), (ct_, [t_in[ii]])]
            def do_m(m):
                ai = st["acc"] % 2
                st["acc"] += 1
                acc_, t_acc_ = acc[ai], t_acc[ai]
                for br in range(3):
                    gi = st["g"] % 3
                    st["g"] += 1
                    g_, t_g_ = gts[gi], t_g[gi]

                    def evg(pb, g_=g_, t_g_=t_g_, br=br):
                        S.op("act", lambda e: e.activation(out=g_[:, 0:tw], in_=psum[pb][:, 0:tw], func=AF.Sigmoid,
                                                           bias=bgate_sb[:, li, br * 8 + m:br * 8 + m + 1]),
                             reads=[t_ps[pb], t_small], writes=[t_g_])
                    lin(WIS[li, 46 + br * 8 + m], 8, 128, hb, [t_hb], tw, evg)

                    def evp(pb, g_=g_, t_g_=t_g_, br=br):
                        if br == 0:
                            S.op("dve", lambda e: e.tensor_tensor(out=acc_[:, 0:tw], in0=psum[pb][:, 0:tw], in1=g_[:, 0:tw], op=ALU.mult),
                                 reads=[t_ps[pb], t_g_], writes=[t_acc_])
                            return
                        ti = st["t"] % 2
                        st["t"] += 1
                        S.op("dve", lambda e: e.tensor_tensor(out=tt_[ti][:, 0:tw], in0=psum[pb][:, 0:tw], in1=g_[:, 0:tw], op=ALU.mult),
                             reads=[t_ps[pb], t_g_], writes=[t_tt[ti]])
                        if br == 1:
                            S.op("dve", lambda e: e.tensor_tensor(out=acc_[:, 0:tw], in0=acc_[:, 0:tw], in1=tt_[ti][:, 0:tw], op=ALU.add),
                                 reads=[t_tt[ti], t_acc_], writes=[t_acc_])
                        else:
                            S.op("dve", lambda e: e.tensor_tensor(out=mbb[:, m, 0:tw], in0=acc_[:, 0:tw], in1=tt_[ti][:, 0:tw], op=ALU.add),
                                 reads=[t_tt[ti], t_acc_], writes=[t_mb])
                    src, stoks = srcs[br]
                    lin(PRJS[li, br, m], 8, 128, src, stoks, tw, evp)
            for m in range(8):
                do_m(m)
                if m < 3:
                    pending = step(pending)
            drain(pending)
            nxt = prefetch()
            for m in range(8):
                def evo(pb, m=m):
                    S.op("dve", lambda e: e.scalar_tensor_tensor(
                        out=xt[xi][:, m, 0:tw], in0=psum[pb][:, 0:tw], scalar=mods[:, li, 1, 2, m, which:which + 1],
                        in1=xt[xi][:, m, 0:tw], op0=ALU.mult, op1=ALU.add),
                        reads=[t_ps[pb], t_xt[xi], t_mod], writes=[t_xt[xi]])
                lin(PRJS[li, 3, m], 8, 128, mbb, [t_mb], tw, evo)
            return nxt, ln_pieces(li, 1, xi, t0, tw, which)

        pend = None
        nx = load_xt(*tile_list[0])
        ni = load_in(*tile_list[0])
        for ti_, (t0, tw) in enumerate(tile_list):
            def prefetch(ti_=ti_):
                if ti_ + 1 < len(tile_list):
                    return load_xt(*tile_list[ti_ + 1]), load_in(*tile_list[ti_ + 1])
                return None, None
            (nx, ni), pend = m4_tile(t0, tw, nx, ni, prefetch, pend)
        drain(pend)
        arelease(m0)

    if debug is not None and "mod" in debug:
        d_ = nc.dram_tensor("dbg_mod", [128, NL * 72 * 2], F32, kind="ExternalOutput").ap()
        S.dma("sp", lambda e: e.dma_start(out=d_, in_=modT[:].rearrange("p l i w -> p (l i w)")), reads=[t_mod])

    snapshot("x0")
    for li, l in enumerate(layers):
        last = (l == DEPTH - 1)
        ffn_phase(li, 0, tiles)
        snapshot("ffn1")
        if not n_ffn_only:
            mixer_in_phase(li, tiles)
            gla_phase(li)
            dsnap("OF", OF, F32)
            dsnap("AN", AN)
            attn_phase(li, tiles[:-1] if last else tiles)
            dsnap("CT", CT)
            merge_phase(li, tiles[:-1] if last else tiles)
            snapshot("mix")
            for nm_, ap_ in [("QT", QT), ("KT", KT), ("RT", RT), ("UT", UT), ("GP", GP), ("VG", VG), ("KRT", KRT),
                             ("QNT", QNT), ("QRT", QRT), ("KNT", KNT), ("VM", VM)]:
                dsnap(nm_, ap_)
        ffn_phase(li, 1, tiles[:-1] if last else tiles)

    t_y = Tok()
    for tt in range((NT if emit_ctx else L) // 128):
        i = tt % 2
        S.dma("sp", lambda e, i=i, tt=tt: e.dma_start(
            out=xT_sb[i][:], in_=XT[:, tt * 128:(tt + 1) * 128].rearrange("(k p) t -> p k t", p=128)),
            reads=[t_XT[tt // 4]], writes=[t_xT[i]])
        for half in range(2):
            pb = 2 * (tt % 2) + half

            def tr(e, i=i, half=half, pb=pb):
                r = None
                for q in range(4):
                    kc = half * 4 + q
                    r = e.transpose(psum[pb][:, q * 128:(q + 1) * 128], xT_sb[i][:, kc, :], ident[:])
                return r
            S.op("pe", tr, reads=[t_xT[i], t_const], writes=[t_ps[pb]])
            ce = "dve" if half == 0 else "act"

            def cp(e, i=i, half=half, pb=pb, ce=ce):
                o = xin_sb[i][:, half * 512:(half + 1) * 512]
                return e.tensor_copy(out=o, in_=psum[pb][:]) if ce == "dve" else e.copy(out=o, in_=psum[pb][:])
            S.op(ce, cp, reads=[t_ps[pb]], writes=[t_xin[i]])
        dst_ = y_out[tt * 128:(tt + 1) * 128, :] if tt < L // 128 else yc_out[(tt - L // 128) * 128:(tt - L // 128 + 1) * 128, :]
        S.dma("sp", lambda e, i=i, dst_=dst_: e.dma_start(out=dst_, in_=xin_sb[i][:]),
              reads=[t_xin[i]], writes=[t_y])
    S.dma_barrier("sp")

    S.emit(es)
    es.close()
    return nc


def make_consts():
    t = np.arange(L)
    rows, cols = t // 64, t % 64
    inv = 10000.0 ** (-np.arange(16, dtype=np.float32) / 16.0)
    cs = np.zeros((128, 2, L), np.float32)
    for p in range(128):
        d_ = p % 64
        pos = rows if d_ < 32 else cols
        ang = pos.astype(np.float32) * inv[d_ % 16]
        cs[p, 0] = np.cos(ang)
        cs[p, 1] = np.sin(ang)
    j = np.arange(128)[:, None]
    i = np.arange(128)[None, :]
    same = (j // 64) == (i // 64)
    gm = np.zeros((128, 4, 128), np.float32)
    gm[:, 0] = np.where(same & (j <= i), -1.0 / 16.0, 0.0)
    gm[:, 1] = np.where(same & (j >= i), -1.0 / 16.0, 0.0)
    gm[:, 2] = np.where(same & (j <= i), 1.0, 0.0)
    gm[:, 3] = np.where(same & (j >= i), 1.0, 0.0)
    return {"ident_in": np.eye(128, dtype=np.float32), "rope_cs": cs, "gmask": gm}


PER_LAYER = ["ada_w", "ada_b", "ln_w", "ln_b", "ffn_w13", "ffn_w2", "w_in", "b_gate", "gla_decay_w", "gla_decay_b",
             "gla_norm", "gla_proj", "conv_w", "conv_proj", "mla_q_norm", "mla_w_uq", "mla_kv_norm", "mla_w_ukv",
             "mla_proj", "w_out"]


def make_in_maps(inputs, layers, n_cores, xs=None, ctxs=None):
    consts = make_consts()
    shared = {k: np.ascontiguousarray(np.asarray(inputs[k])[layers]) for k in PER_LAYER}
    shared["c_ctx"] = np.ascontiguousarray(np.asarray(inputs["c_ctx"]))
    shared.update(consts)
    maps = []
    for b in range(n_cores):
        m = dict(shared)
        m["x"] = np.ascontiguousarray(np.asarray(inputs["x"][b]) if xs is None else xs[b])
        m["ctx"] = np.ascontiguousarray(np.asarray(inputs["ctx"][b]) if ctxs is None else ctxs[b])
        m["c"] = np.ascontiguousarray(np.asarray(inputs["c"][b]))
        maps.append(m)
    return maps


FUSED = True
_PROG = {}


def _prog(layers):
    key = tuple(layers)
    if key not in _PROG:
        _PROG[key] = build_program(list(layers))
    return _PROG[key]


def kernel(**inputs):
    inputs = {k: np.asarray(v) for k, v in inputs.items()}
    n = 8
    if FUSED:
        nc = _prog(range(DEPTH))
        maps = make_in_maps(inputs, list(range(DEPTH)), n)
        res = run_bass_kernel_spmd(nc, maps, core_ids=list(range(n)))
        return np.stack([np.asarray(r["y"]) for r in res.results], axis=0).astype(np.float32)
    xs = [inputs["x"][b] for b in range(n)]
    cs = [inputs["ctx"][b] for b in range(n)]
    for l in range(DEPTH):
        nc = _prog([l]) if l == DEPTH - 1 else _prog([0])
        maps = make_in_maps(inputs, [l], n, xs=xs, ctxs=cs)
        res = run_bass_kernel_spmd(nc, maps, core_ids=list(range(n)))
        xs = [np.asarray(r["y"]) for r in res.results]
        if l != DEPTH - 1:
            cs = [np.asarray(r["yc"]) for r in res.results]
    return np.stack(xs, axis=0).astype(np.float32)
```
